# Optimizing a Trainium2 kernel written in Bass

```python
import math
import jax, jax.numpy as jnp
from jax import lax
import numpy as np

D_MODEL = 1024
BATCH = 2
SEQ = 8192
DEPTH = 2
DEC_BATCH = 32
DEC_SEQ = 8
PAST_LEN = 16384
PAGE_SIZE = 128

D_MIX = D_MODEL
D_ATT = D_MIX // 2
N_HEADS = 8
HEAD_DIM = D_ATT // N_HEADS
DILATED_GROUPS = ((128, 1), (512, 4), (2048, 16))
ATT_WINDOW = max(w for w, _ in DILATED_GROUPS)
ATT_BLOCK = 128
D_SSM = D_MIX // 4
SSM_GROUP = 16
N_SSM_GROUPS = D_SSM // SSM_GROUP
SSM_STATE = 64
DT_MIN = 1e-3
DT_MAX = 1e-1
D_POOL = D_MIX // 4
POOL_WINDOWS = (2, 4, 8, 16)
N_POOL_GROUPS = len(POOL_WINDOWS)
POOL_GROUP = D_POOL // N_POOL_GROUPS
POOL_BUF = max(POOL_WINDOWS) - 1
D_IN = 3 * D_ATT + D_SSM + D_POOL
D_FF = 2816
CONV_WIDTH = 3
CONV_BUF = CONV_WIDTH - 1
EPS = 1e-6
NEG = -1e30

kernel_name = 'hybrid_dilated_s5_pool_decoder_step'


def _rmsnorm(x, g):
    xf = x.astype(jnp.float32)
    return xf * lax.rsqrt(jnp.mean(xf * xf, axis=-1, keepdims=True) + EPS) * g.astype(jnp.float32)


def _dilated_prompt(q, k, v, window, dilation):
    b, s, h, e = q.shape
    n_back = window // dilation
    blk = ATT_BLOCK
    span = dilation * blk
    s_pad = -(-s // span) * span
    nb = s_pad // span
    pad = ((0, 0), (0, s_pad - s), (0, 0), (0, 0))

    def blocks(t):
        return jnp.pad(t, pad).reshape(b, nb, blk, dilation, h, e)

    def with_prev(t):
        prev = jnp.concatenate([jnp.zeros_like(t[:, :1]), t[:, :-1]], axis=1)
        return jnp.concatenate([prev, t], axis=2)

    qb = blocks(q)
    kc = with_prev(blocks(k))
    vc = with_prev(blocks(v))
    sc = jnp.einsum('bnqrhe,bnkrhe->bnrhqk', qb, kc) * (e ** -0.5)
    qi = jnp.arange(blk)[:, None] + blk
    ki = jnp.arange(2 * blk)[None, :]
    rel = qi - ki
    band = (rel >= 0) & (rel <= n_back)
    has_prev = (jnp.arange(nb)[:, None, None] > 0) | (ki[None] >= blk)
    mask = band[None] & has_prev
    sc = jnp.where(mask[None, :, None, None], sc, NEG)
    m = jnp.max(sc, axis=-1, keepdims=True)
    p = jnp.exp(sc - m)
    l = jnp.sum(p, axis=-1)
    o = jnp.einsum('bnrhqk,bnkrhe->bnqrhe', p, vc) / jnp.transpose(l, (0, 1, 4, 2, 3))[..., None]
    lse = jnp.transpose(m[..., 0] + jnp.log(l), (0, 1, 4, 2, 3))
    return o.reshape(b, s_pad, h, e)[:, :s], lse.reshape(b, s_pad, h)[:, :s]


def _dilated_decode(q, k_all, v_all, n_past, window, dilation):
    t = q.shape[1]
    n_back = window // dilation
    idx = n_past + jnp.arange(t)[:, None] - dilation * jnp.arange(n_back + 1)[None, :]
    valid = idx >= 0
    idx = jnp.maximum(idx, 0)
    kg = k_all[:, idx]
    vg = v_all[:, idx]
    sc = jnp.einsum('bthe,btnhe->bhtn', q, kg) * (q.shape[-1] ** -0.5)
    sc = jnp.where(valid[None, None], sc, NEG)
    m = jnp.max(sc, axis=-1, keepdims=True)
    p = jnp.exp(sc - m)
    l = jnp.sum(p, axis=-1)
    o = jnp.einsum('bhtn,btnhe->bthe', p, vg) / jnp.transpose(l, (0, 2, 1))[..., None]
    lse = jnp.transpose(m[..., 0] + jnp.log(l), (0, 2, 1))
    return o, lse


def _merge_by_denominator(outs, lses):
    w = jax.nn.softmax(jnp.stack(lses, 0), axis=0)
    return jnp.einsum('gbth,gbthe->bthe', w, jnp.stack(outs, 0))


def _cplx_affine_op(e1, e2):
    a1r, a1i, b1r, b1i = e1
    a2r, a2i, b2r, b2i = e2
    return (a2r * a1r - a2i * a1i,
            a2r * a1i + a2i * a1r,
            a2r * b1r - a2i * b1i + b2r,
            a2r * b1i + a2i * b1r + b2i)


def _s5(u, h0_re, h0_im, log_dt, a_re, a_im, b_re, b_im, c_re, c_im, d_skip, w_glu, b_glu):
    f32 = jnp.float32
    bsz, t, _ = u.shape
    a_re = a_re.astype(f32)
    a_im = a_im.astype(f32)
    dt = jnp.exp(log_dt.astype(f32))[:, None]
    mag = jnp.exp(dt * a_re)
    ab_re = mag * jnp.cos(dt * a_im)
    ab_im = mag * jnp.sin(dt * a_im)
    den = a_re * a_re + a_im * a_im
    n_re = ab_re - 1.0
    n_im = ab_im
    co_re = (n_re * a_re + n_im * a_im) / den
    co_im = (n_im * a_re - n_re * a_im) / den
    b_re = b_re.astype(f32)
    b_im = b_im.astype(f32)
    bb_re = co_re[..., None] * b_re - co_im[..., None] * b_im
    bb_im = co_re[..., None] * b_im + co_im[..., None] * b_re
    ug = u.reshape(bsz, t, N_SSM_GROUPS, SSM_GROUP)
    bu_re = jnp.einsum('btgc,gpc->btgp', ug, bb_re)
    bu_im = jnp.einsum('btgc,gpc->btgp', ug, bb_im)
    h0_re = h0_re.astype(f32)
    h0_im = h0_im.astype(f32)
    bu_re = bu_re.at[:, 0].add(ab_re * h0_re - ab_im * h0_im)
    bu_im = bu_im.at[:, 0].add(ab_re * h0_im + ab_im * h0_re)
    shape = bu_re.shape
    elems = (jnp.broadcast_to(ab_re, shape), jnp.broadcast_to(ab_im, shape), bu_re, bu_im)
    _, _, h_re, h_im = lax.associative_scan(_cplx_affine_op, elems, axis=1)
    y = (jnp.einsum('btgp,gcp->btgc', h_re, c_re.astype(f32))
         - jnp.einsum('btgp,gcp->btgc', h_im, c_im.astype(f32))).reshape(bsz, t, D_SSM)
    y = y + d_skip.astype(f32) * u
    g = jax.nn.gelu(y)
    out = g * jax.nn.sigmoid(g @ w_glu.astype(f32) + b_glu.astype(f32))
    return out, h_re[:, -1], h_im[:, -1]


def _pool_mix(u, prefix, start_pos, pool_w, pool_scale):
    f32 = jnp.float32
    b, t, _ = u.shape
    uc = jnp.concatenate([prefix.astype(f32), u.astype(f32)], axis=1)
    cs = jnp.concatenate([jnp.zeros((b, 1, D_POOL), f32), jnp.cumsum(uc, axis=1)], axis=1)
    end = cs[:, POOL_BUF + 1:]
    pos = start_pos + jnp.arange(t)
    cur = uc[:, POOL_BUF:]
    outs = []
    for gi, w in enumerate(POOL_WINDOWS):
        sl = slice(gi * POOL_GROUP, (gi + 1) * POOL_GROUP)
        win_sum = end[..., sl] - cs[:, POOL_BUF + 1 - w:POOL_BUF + 1 - w + t, sl]
        cnt = jnp.minimum(pos + 1, w).astype(f32)[None, :, None]
        outs.append(win_sum / cnt - cur[..., sl])
    pooled = jnp.stack(outs, axis=2)
    mixed = jnp.einsum('btgc,gcd->btgd', pooled, pool_w.astype(f32)).reshape(b, t, D_POOL)
    return mixed * pool_scale.astype(f32), uc[:, -POOL_BUF:]


def _causal_dwconv(h, prefix, w, bias):
    t = h.shape[1]
    hc = jnp.concatenate([prefix.astype(h.dtype), h], axis=1)
    w = w.astype(h.dtype)
    y = bias.astype(h.dtype) + hc[:, 0:t] * w[0]
    for j in range(1, CONV_WIDTH):
        y = y + hc[:, j:j + t] * w[j]
    return y, hc[:, -CONV_BUF:]


def _layer(x, lw, k_past, v_past, h0_re, h0_im, pool_prefix, conv_prefix, start_pos, prompt):
    f32 = jnp.float32
    b, t, _ = x.shape
    hn = _rmsnorm(x, lw['norm1_g'])
    z = hn @ lw['w_in'].astype(f32)
    q, k, v, u_ssm, u_pool = jnp.split(z, [D_ATT, 2 * D_ATT, 3 * D_ATT, 3 * D_ATT + D_SSM], axis=-1)
    q = q.reshape(b, t, N_HEADS, HEAD_DIM)
    k = k.reshape(b, t, N_HEADS, HEAD_DIM)
    v = v.reshape(b, t, N_HEADS, HEAD_DIM)
    outs, lses = [], []
    if prompt:
        for (w, d) in DILATED_GROUPS:
            o, l = _dilated_prompt(q, k, v, w, d)
            outs.append(o)
            lses.append(l)
        n_keep = min(ATT_WINDOW, t)
        k_new, v_new = k[:, t - n_keep:], v[:, t - n_keep:]
    else:
        n_past = k_past.shape[1]
        k_all = jnp.concatenate([k_past.astype(f32), k], axis=1)
        v_all = jnp.concatenate([v_past.astype(f32), v], axis=1)
        for (w, d) in DILATED_GROUPS:
            o, l = _dilated_decode(q, k_all, v_all, n_past, w, d)
            outs.append(o)
            lses.append(l)
        k_new, v_new = k, v
    att = _merge_by_denominator(outs, lses).reshape(b, t, D_ATT)
    ssm_out, h_re, h_im = _s5(u_ssm, h0_re, h0_im, lw['ssm_log_dt'], lw['ssm_a_re'], lw['ssm_a_im'],
                              lw['ssm_b_re'], lw['ssm_b_im'], lw['ssm_c_re'], lw['ssm_c_im'],
                              lw['ssm_d'], lw['ssm_w_glu'], lw['ssm_b_glu'])
    pool_out, pool_state = _pool_mix(u_pool, pool_prefix, start_pos, lw['pool_w'], lw['pool_scale'])
    mix = jnp.concatenate([_rmsnorm(att, lw['out_norm_att']),
                           _rmsnorm(ssm_out, lw['out_norm_ssm']),
                           _rmsnorm(pool_out, lw['out_norm_pool'])], axis=-1)
    x = x.astype(f32) + mix @ lw['w_out'].astype(f32)
    hn2 = _rmsnorm(x, lw['norm2_g'])
    up = hn2 @ lw['w_up'].astype(f32)
    up_c, conv_state = _causal_dwconv(up, conv_prefix, lw['conv_w'], lw['conv_b'])
    a, g = jnp.split(up_c, 2, axis=-1)
    x = x + (jax.nn.silu(g) * a) @ lw['w_down'].astype(f32)
    return x, (k_new, v_new, h_re, h_im, pool_state, conv_state)


def _normal(k, shape, scale):
    return scale * jax.random.normal(k, shape, jnp.float32)


def setup_inputs(seed: int = 0) -> dict:
    key = jax.random.key(seed)
    ks = jax.random.split(key, 32)
    att_buf = min(ATT_WINDOW, PAST_LEN)
    x_prompt = _normal(ks[0], (BATCH, SEQ, D_MODEL), 1.0)
    x_sample = _normal(ks[1], (DEC_BATCH, DEC_SEQ, D_MODEL), 1.0)
    cache_k = _normal(ks[2], (DEPTH, DEC_BATCH, att_buf, N_HEADS, HEAD_DIM), 1.0)
    cache_v = _normal(ks[3], (DEPTH, DEC_BATCH, att_buf, N_HEADS, HEAD_DIM), 1.0)
    state_ssm_re = _normal(ks[4], (DEPTH, DEC_BATCH, N_SSM_GROUPS, SSM_STATE), 0.1)
    state_ssm_im = _normal(ks[5], (DEPTH, DEC_BATCH, N_SSM_GROUPS, SSM_STATE), 0.1)
    state_pool = _normal(ks[6], (DEPTH, DEC_BATCH, POOL_BUF, D_POOL), 1.0)
    state_conv = _normal(ks[7], (DEPTH, DEC_BATCH, CONV_BUF, 2 * D_FF), 1.0)
    norm1_g = 1.0 + _normal(ks[8], (DEPTH, D_MODEL), 0.02)
    w_in = _normal(ks[9], (DEPTH, D_MODEL, D_IN), D_MODEL ** -0.5)
    ssm_log_dt = jax.random.uniform(ks[10], (DEPTH, N_SSM_GROUPS), jnp.float32,
                                    math.log(DT_MIN), math.log(DT_MAX))
    ssm_a_re = -0.5 + _normal(ks[11], (DEPTH, N_SSM_GROUPS, SSM_STATE), 0.01)
    ssm_a_im = (math.pi * jnp.arange(SSM_STATE, dtype=jnp.float32)
                + _normal(ks[12], (DEPTH, N_SSM_GROUPS, SSM_STATE), 0.01))
    ssm_b_re = _normal(ks[13], (DEPTH, N_SSM_GROUPS, SSM_STATE, SSM_GROUP), (2 * SSM_GROUP) ** -0.5)
    ssm_b_im = _normal(ks[14], (DEPTH, N_SSM_GROUPS, SSM_STATE, SSM_GROUP), (2 * SSM_GROUP) ** -0.5)
    ssm_c_re = _normal(ks[15], (DEPTH, N_SSM_GROUPS, SSM_GROUP, SSM_STATE), SSM_STATE ** -0.5)
    ssm_c_im = _normal(ks[16], (DEPTH, N_SSM_GROUPS, SSM_GROUP, SSM_STATE), SSM_STATE ** -0.5)
    ssm_d = _normal(ks[17], (DEPTH, D_SSM), 1.0)
    ssm_w_glu = _normal(ks[18], (DEPTH, D_SSM, D_SSM), D_SSM ** -0.5)
    ssm_b_glu = _normal(ks[19], (DEPTH, D_SSM), 0.02)
    pool_w = _normal(ks[20], (DEPTH, N_POOL_GROUPS, POOL_GROUP, POOL_GROUP), POOL_GROUP ** -0.5)
    pool_scale = 1.0 + _normal(ks[21], (DEPTH, D_POOL), 0.1)
    out_norm_att = 1.0 + _normal(ks[22], (DEPTH, D_ATT), 0.02)
    out_norm_ssm = 1.0 + _normal(ks[23], (DEPTH, D_SSM), 0.02)
    out_norm_pool = 1.0 + _normal(ks[24], (DEPTH, D_POOL), 0.02)
    w_out = _normal(ks[25], (DEPTH, D_MIX, D_MODEL), D_MIX ** -0.5)
    norm2_g = 1.0 + _normal(ks[26], (DEPTH, D_MODEL), 0.02)
    w_up = _normal(ks[27], (DEPTH, D_MODEL, 2 * D_FF), D_MODEL ** -0.5)
    conv_w = _normal(ks[28], (DEPTH, CONV_WIDTH, 2 * D_FF), CONV_WIDTH ** -0.5)
    conv_b = _normal(ks[29], (DEPTH, 2 * D_FF), 0.02)
    w_down = _normal(ks[30], (DEPTH, D_FF, D_MODEL), D_FF ** -0.5)
    norm_f_g = 1.0 + _normal(ks[31], (D_MODEL,), 0.02)
    return {'x_prompt': x_prompt, 'x_sample': x_sample, 'cache_k': cache_k, 'cache_v': cache_v,
            'state_ssm_re': state_ssm_re, 'state_ssm_im': state_ssm_im, 'state_pool': state_pool,
            'state_conv': state_conv, 'norm1_g': norm1_g, 'w_in': w_in, 'ssm_log_dt': ssm_log_dt,
            'ssm_a_re': ssm_a_re, 'ssm_a_im': ssm_a_im, 'ssm_b_re': ssm_b_re, 'ssm_b_im': ssm_b_im,
            'ssm_c_re': ssm_c_re, 'ssm_c_im': ssm_c_im, 'ssm_d': ssm_d, 'ssm_w_glu': ssm_w_glu,
            'ssm_b_glu': ssm_b_glu, 'pool_w': pool_w, 'pool_scale': pool_scale,
            'out_norm_att': out_norm_att, 'out_norm_ssm': out_norm_ssm, 'out_norm_pool': out_norm_pool,
            'w_out': w_out, 'norm2_g': norm2_g, 'w_up': w_up, 'conv_w': conv_w, 'conv_b': conv_b,
            'w_down': w_down, 'norm_f_g': norm_f_g}


def reference(x_prompt, x_sample, cache_k, cache_v, state_ssm_re, state_ssm_im, state_pool, state_conv,
              norm1_g, w_in, ssm_log_dt, ssm_a_re, ssm_a_im, ssm_b_re, ssm_b_im, ssm_c_re, ssm_c_im,
              ssm_d, ssm_w_glu, ssm_b_glu, pool_w, pool_scale, out_norm_att, out_norm_ssm,
              out_norm_pool, w_out, norm2_g, w_up, conv_w, conv_b, w_down, norm_f_g):
    f32 = jnp.float32
    xp = x_prompt.astype(f32)
    xs = x_sample.astype(f32)
    bp = xp.shape[0]
    st_p, st_s = [], []
    for i in range(DEPTH):
        lw = {'norm1_g': norm1_g[i], 'w_in': w_in[i], 'ssm_log_dt': ssm_log_dt[i],
              'ssm_a_re': ssm_a_re[i], 'ssm_a_im': ssm_a_im[i], 'ssm_b_re': ssm_b_re[i],
              'ssm_b_im': ssm_b_im[i], 'ssm_c_re': ssm_c_re[i], 'ssm_c_im': ssm_c_im[i],
              'ssm_d': ssm_d[i], 'ssm_w_glu': ssm_w_glu[i], 'ssm_b_glu': ssm_b_glu[i],
              'pool_w': pool_w[i], 'pool_scale': pool_scale[i], 'out_norm_att': out_norm_att[i],
              'out_norm_ssm': out_norm_ssm[i], 'out_norm_pool': out_norm_pool[i], 'w_out': w_out[i],
              'norm2_g': norm2_g[i], 'w_up': w_up[i], 'conv_w': conv_w[i], 'conv_b': conv_b[i],
              'w_down': w_down[i]}
        zero_h = jnp.zeros((bp, N_SSM_GROUPS, SSM_STATE), f32)
        xp, sp = _layer(xp, lw, None, None, zero_h, zero_h,
                        jnp.zeros((bp, POOL_BUF, D_POOL), f32),
                        jnp.zeros((bp, CONV_BUF, 2 * D_FF), f32), 0, True)
        xs, ss = _layer(xs, lw, cache_k[i], cache_v[i], state_ssm_re[i], state_ssm_im[i],
                        state_pool[i], state_conv[i], PAST_LEN, False)
        st_p.append(sp)
        st_s.append(ss)
    y_prompt = _rmsnorm(xp, norm_f_g).astype(x_prompt.dtype)
    y_sample = _rmsnorm(xs, norm_f_g).astype(x_sample.dtype)
    new_k_prompt = jnp.stack([s[0] for s in st_p], 0)
    new_v_prompt = jnp.stack([s[1] for s in st_p], 0)
    new_ssm_re_prompt = jnp.stack([s[2] for s in st_p], 0)
    new_ssm_im_prompt = jnp.stack([s[3] for s in st_p], 0)
    new_pool_prompt = jnp.stack([s[4] for s in st_p], 0)
    new_conv_prompt = jnp.stack([s[5] for s in st_p], 0)
    new_k_sample = jnp.stack([s[0] for s in st_s], 0)
    new_v_sample = jnp.stack([s[1] for s in st_s], 0)
    new_ssm_re_sample = jnp.stack([s[2] for s in st_s], 0)
    new_ssm_im_sample = jnp.stack([s[3] for s in st_s], 0)
    new_pool_sample = jnp.stack([s[4] for s in st_s], 0)
    new_conv_sample = jnp.stack([s[5] for s in st_s], 0)
    return (y_prompt, y_sample, new_k_prompt, new_v_prompt, new_ssm_re_prompt, new_ssm_im_prompt,
            new_pool_prompt, new_conv_prompt, new_k_sample, new_v_sample, new_ssm_re_sample,
            new_ssm_im_sample, new_pool_sample, new_conv_sample)
```

```python
import contextlib
import math
import numpy as np
import concourse.bass as bass
import concourse.mybir as mybir
from concourse.bass_utils import run_bass_kernel_spmd

F32 = mybir.dt.float32
BF16 = mybir.dt.bfloat16
I32 = mybir.dt.int32
ALU = mybir.AluOpType
AF = mybir.ActivationFunctionType

D = 1024
KC = 8
DIN = 2048
DFF = 2816
FC = 44
AC = 22
NSEQ = 4
TD = 8
NS = NSEQ * TD
PAST = 2048
EPS = 1e-6
NCORES = 8
TWO_PI = 2.0 * math.pi


class Sched:
    ENGS = ("pe", "act", "dve", "pool", "sp")

    def __init__(self, nc, stack, n_dma_sems=32):
        self.nc = nc
        self.ops = {e: [] for e in self.ENGS}
        self.count = {e: 0 for e in self.ENGS}
        self.known = {e: {} for e in self.ENGS}
        self.buf = {}
        self.esem = {e: stack.enter_context(nc.semaphore("s_" + e)) for e in self.ENGS}
        self.KD = n_dma_sems
        self.dsem = [stack.enter_context(nc.semaphore("d%d" % i)) for i in range(self.KD)]
        self.ndma = 0
        self.dlast = [0] * self.KD
        self.dead = False

    def _deps(self, reads, writes):
        deps = []
        for k in reads:
            st = self.buf.get(k)
            if st and st["w"] is not None:
                deps.append(st["w"] + ("raw",))
        for k in writes:
            st = self.buf.get(k)
            if st:
                if st["w"] is not None:
                    deps.append(st["w"])
                deps.extend(st["r"])
        return deps

    def _waits(self, eng, deps):
        need = {}
        for t in deps:
            if t[0] == "E":
                if t[1] == eng and (eng == "pe" or len(t) < 4):
                    continue
                key = ("E", t[1])
            else:
                key = ("D", t[1])
            if t[2] > need.get(key, 0):
                need[key] = t[2]
        out = []
        kn = self.known[eng]
        for key, v in need.items():
            if kn.get(key, 0) >= v:
                continue
            kn[key] = v
            sem = self.esem[key[1]] if key[0] == "E" else self.dsem[key[1]]
            out.append((sem, v))
        return out

    def _commit(self, tok, reads, writes):
        for k in reads:
            st = self.buf.setdefault(k, {"w": None, "r": []})
            st["r"].append(tok)
            if len(st["r"]) > 64:
                st["r"] = self._prune(st["r"])
        for k in writes:
            self.buf[k] = {"w": tok, "r": []}

    @staticmethod
    def _prune(lst):
        best = {}
        for t in lst:
            key = (t[0], t[1])
            if key not in best or t[2] > best[key][2]:
                best[key] = t
        return list(best.values())

    def op(self, eng, fn, reads=(), writes=()):
        if self.dead:
            return None
        psr = [k for k in reads if k.startswith("ps") and k[2:].isdigit()]
        if psr:
            reads = [k for k in reads if k not in psr]
            writes = list(writes) + psr
        waits = self._waits(eng, self._deps(reads, writes))
        self.count[eng] += 1
        tok = ("E", eng, self.count[eng])
        self.ops[eng].append((fn, waits, (self.esem[eng], 1)))
        self._commit(tok, reads, writes)
        return tok

    def dma(self, fn, reads=(), writes=(), q="sp"):
        if self.dead:
            return None
        slot = self.ndma % self.KD
        self.ndma += 1
        deps = self._deps(reads, writes)
        if self.dlast[slot] > 0:
            deps.append(("D", slot, self.dlast[slot]))
        waits = self._waits(q, deps)
        self.dlast[slot] += 16
        tok = ("D", slot, self.dlast[slot])
        self.ops[q].append((fn, waits, (self.dsem[slot], 16)))
        self._commit(tok, reads, writes)
        return tok

    def barrier(self):
        if self.dead:
            return
        dd = [("D", s, v) for s, v in enumerate(self.dlast) if v > 0]
        for e in self.ENGS:
            deps = [("E", e2, self.count[e2]) for e2 in self.ENGS if e2 != e and self.count[e2] > 0] + dd
            waits = self._waits(e, deps)
            self.ops[e].append((None, waits, None))
        self.buf = {}

    def emit(self):
        nc = self.nc
        engmap = {"pe": "tensor", "act": "scalar", "dve": "vector", "pool": "gpsimd", "sp": "sync"}
        with nc.Block() as block:
            for e in self.ENGS:
                lst = self.ops[e]

                def body(engine, lst=lst):
                    for fn, waits, inc in lst:
                        for sem, v in waits:
                            engine.wait_ge(sem, v)
                        if fn is not None:
                            fn(engine).then_inc(inc[0], inc[1])

                getattr(block, engmap[e])(body)


class Ring:
    def __init__(self, items):
        self.items = items
        self.i = 0

    def next(self):
        it = self.items[self.i % len(self.items)]
        self.i += 1
        return it


class _Stop(Exception):
    pass


class Builder:
    stop_at = None

    def ckpt(self, name):
        if self.stop_at is not None and name == self.stop_at and not self.S_.dead:
            self.S_.barrier()
            self.S_.dead = True

    def __init__(self, S=8192, TSM=128, TSF=256, dbg=False):
        assert S % TSM == 0 and S % TSF == 0 and TSM % 128 == 0
        self.S, self.TSM, self.TSF, self.dbg = S, TSM, TSF, dbg
        self.NSUB = TSM // 128
        self.JMIN = -(self.NSUB - 1)
        self.NJ = 17 + self.NSUB - 1
        self.NSLOT = 16 + self.NSUB
        self.NKEEP = min(2048, S)
        self.nc = bass.Bass("TRN2", target_bir_lowering=False)

    def MM(self, out, lhsT, rhs, start, stop, r, w):
        self.S_.op("pe", lambda e: e.matmul(out, lhsT=lhsT, rhs=rhs, start=start, stop=stop,
                                            skip_group_check=True), r, w)

    def TR(self, out, in_, ident, r, w):
        self.S_.op("pe", lambda e: e.transpose(out, in_, ident), r, w)

    def ACT(self, out, in_, func, r, w, bias=None, scale=None, accum=None):
        kw = {}
        if bias is not None:
            kw["bias"] = bias
        if scale is not None:
            kw["scale"] = scale
        if accum is not None:
            kw["accum_out"] = accum
        self.S_.op("act", lambda e: e.activation(out=out, in_=in_, func=func, **kw), r, w)

    def TT(self, eng, out, in0, in1, op, r, w):
        self.S_.op(eng, lambda e: e.tensor_tensor(out=out, in0=in0, in1=in1, op=op), r, w)

    def TS(self, eng, out, in0, s1, s2, op0, op1, r, w):
        if s2 is None:
            self.S_.op(eng, lambda e: e.tensor_scalar(out=out, in0=in0, scalar1=s1, scalar2=None, op0=op0), r, w)
        else:
            self.S_.op(eng, lambda e: e.tensor_scalar(out=out, in0=in0, scalar1=s1, scalar2=s2, op0=op0, op1=op1), r, w)

    def STT(self, out, in0, scalar, in1, op0, op1, r, w):
        self.S_.op("dve", lambda e: e.scalar_tensor_tensor(out=out, in0=in0, scalar=scalar, in1=in1,
                                                          op0=op0, op1=op1), r, w)

    def CP(self, eng, out, in_, r, w):
        if eng == "act":
            self.S_.op("act", lambda e: e.activation(out=out, in_=in_, func=AF.Copy), r, w)
        else:
            self.S_.op(eng, lambda e: e.tensor_copy(out=out, in_=in_), r, w)

    def MS(self, eng, ap, val, w):
        self.S_.op(eng, lambda e: e.memset(ap, val), (), w)

    def DMA(self, out, in_, r, w, q="sp", slow=False):
        if slow:
            self.S_.dma(lambda e: e.dma_start(out=out, in_=in_, allow_slow_non_contiguous=True), r, w, q=q)
        else:
            self.S_.dma(lambda e: e.dma_start(out=out, in_=in_), r, w, q=q)

    def sb(self, stack, name, shape, dt):
        self._uid = getattr(self, "_uid", 0) + 1
        return stack.enter_context(self.nc.sbuf_tensor("%s_%d" % (name, self._uid), shape, dt))

    def declare(self):
        nc, S = self.nc, self.S
        din = lambda n, s: nc.dram_tensor(n, s, F32, kind="ExternalInput").ap()
        dout = lambda n, s: nc.dram_tensor(n, s, F32, kind="ExternalOutput").ap()
        I = {}
        I["xp"] = din("xp", [S, D])
        I["xs"] = din("xs", [NS, D])
        I["ck"] = din("ck", [2, NSEQ, PAST, 512])
        I["cv"] = din("cv", [2, NSEQ, PAST, 512])
        I["sre"] = din("sre", [2, NSEQ, 16, 64])
        I["sim"] = din("sim", [2, NSEQ, 16, 64])
        I["spool"] = din("spool", [2, NSEQ, 15, 256])
        I["sconv"] = din("sconv", [2, NSEQ, 2, 2 * DFF])
        for n, s in (("norm1_g", [2, D]), ("w_in", [2, D, DIN]), ("ssm_log_dt", [2, 16]), ("ssm_a_re", [2, 16, 64]),
                     ("ssm_a_im", [2, 16, 64]), ("ssm_b_re", [2, 16, 64, 16]), ("ssm_b_im", [2, 16, 64, 16]),
                     ("ssm_c_re", [2, 16, 16, 64]), ("ssm_c_im", [2, 16, 16, 64]), ("ssm_d", [2, 256]),
                     ("ssm_w_glu", [2, 256, 256]), ("ssm_b_glu", [2, 256]), ("pool_w", [2, 4, 64, 64]),
                     ("pool_scale", [2, 256]), ("out_norm_att", [2, 512]), ("out_norm_ssm", [2, 256]),
                     ("out_norm_pool", [2, 256]), ("w_out", [2, D, D]), ("norm2_g", [2, D]),
                     ("w_up", [2, D, 2 * DFF]), ("conv_w", [2, 3, 2 * DFF]), ("conv_b", [2, 2 * DFF]),
                     ("w_down", [2, DFF, D]), ("norm_f_g", [D])):
            I[n] = din(n, s)
        O = {}
        O["yp"] = dout("yp", [S, D])
        O["ys"] = dout("ys", [NS, D])
        O["kp"] = dout("kp", [2, self.NKEEP, 512])
        O["vp"] = dout("vp", [2, self.NKEEP, 512])
        O["rep"] = dout("rep", [2, 16, 64])
        O["imp"] = dout("imp", [2, 16, 64])
        O["poolp"] = dout("poolp", [2, 15, 256])
        O["convp"] = dout("convp", [2, 2, 2 * DFF])
        O["ks"] = dout("ks", [2, NSEQ, TD, 512])
        O["vs"] = dout("vs", [2, NSEQ, TD, 512])
        O["res"] = dout("res", [2, NSEQ, 16, 64])
        O["ims"] = dout("ims", [2, NSEQ, 16, 64])
        O["pools"] = dout("pools", [2, NSEQ, 15, 256])
        O["convs"] = dout("convs", [2, NSEQ, 2, 2 * DFF])
        if self.dbg:
            O["dbg"] = dout("dbg", [12, KC, 128, S + NS])
        self.I, self.O = I, O
        self.scr = nc.dram_tensor("scr", [2, KC, 128, S + NS], F32, kind="Internal").ap()

    def sincos(self, st, X, n, tag, want_cos=True):
        ki = self.sb(st, tag + "_ki", [128, n], I32)
        r = self.sb(st, tag + "_r", [128, n], F32)
        sn = self.sb(st, tag + "_sn", [128, n], F32)
        outs = []
        for which, shift, dst in (("s", 0.0, sn),) + ((("c", math.pi / 2, None),) if want_cos else ()):
            if dst is None:
                dst = self.sb(st, tag + "_cs", [128, n], F32)
            src = X
            if shift != 0.0:
                self.TS("dve", r[:], X, shift, None, ALU.add, None, [tag + "X"], [tag + "r"])
                src = r[:]
            self.TS("dve", ki[:], src, 1.0 / TWO_PI, None, ALU.mult, None, [tag + "X", tag + "r"], [tag + "ki"])
            self.STT(r[:], ki[:], -TWO_PI, src, ALU.mult, ALU.add, [tag + "ki", tag + "X", tag + "r"], [tag + "r"])
            self.TS("dve", r[:], r[:], 3.14159, -3.14159, ALU.min, ALU.max, [tag + "r"], [tag + "r"])
            self.ACT(dst[:], r[:], AF.Sin, [tag + "r"], [tag + which])
            outs.append(dst)
        return outs

    def rms_feat(self, src, C, NT, gains, dst, nfeat, keys_r, key_w, tmp):
        sq, sd, rstd = tmp
        self.ACT(sq[:, 0:C, 0:NT], src[:, 0:C, 0:NT], AF.Square, keys_r, ["sq"])
        ps, pk = self.psC.next()
        for c in range(C):
            self.MM(ps[:, 0:NT], self.ones_b[:], sq[:, c, 0:NT], c == 0, c == C - 1, ["sq", "ones"], [pk])
        self.ACT(sd[:, 0:NT], ps[:, 0:NT], AF.Sqrt, [pk], ["sd"], bias=self.eps_col[:, 0:1], scale=1.0 / nfeat)
        self.S_.op("dve", lambda e: e.reciprocal(out=rstd[:, 0:NT], in_=sd[:, 0:NT]), ["sd"], ["rstd"])
        for c in range(C):
            self.STT(dst[:, c, 0:NT], src[:, c, 0:NT], gains[:, c:c + 1], rstd[:, 0:NT], ALU.mult, ALU.mult,
                     keys_r + ["rstd", "gains"], [key_w])

    def load_rows_T(self, st, rows, tag):
        stage = self.sb(st, tag + "_stg", [128, 128], F32)
        cols = self.sb(st, tag + "_cols", [128, 128], F32)
        self.MS("pool", stage[:], 0.0, [tag + "stg"])
        for ap, base in rows:
            n = ap.shape[0]
            self.DMA(stage[base:base + n, :], ap, [], [tag + "stg"])
        ps, pk = self.psC.next()
        self.TR(ps[:, 0:128], stage[:], self.ident_f[:], [tag + "stg", "ident"], [pk])
        self.CP("dve", cols[:], ps[:, 0:128], [pk], [tag + "cols"])
        return cols

    def set_rings(self, mix):
        rg = lambda ids: Ring([(self.ps[i], "ps%d" % i) for i in ids])
        if mix:
            self.psA, self.psS, self.psB, self.psC = rg((0, 1)), rg((2, 3, 4, 5)), rg((6,)), rg((7,))
        else:
            self.psA, self.psC = rg((0, 1, 2, 3, 4, 5)), rg((6, 7))

    def setup(self, st):
        nc = self.nc
        self.ps = [st.enter_context(nc.psum_tensor("ps%d" % i, [128, 512], F32)) for i in range(8)]
        self.set_rings(True)
        self.ident_f = self.sb(st, "ident_f", [128, 128], F32)
        self.ones_b = self.sb(st, "ones_b", [128, 128], BF16)
        self.eps_col = self.sb(st, "eps_col", [128, 1], F32)
        with contextlib.ExitStack() as t:
            onesf = self.sb(t, "onesf", [128, 128], F32)
            self.MS("pool", onesf[:], 1.0, ["onesf"])
            self.MS("pool", self.ones_b[:], 1.0, ["ones"])
            self.MS("pool", self.eps_col[:], EPS, ["eps"])
            self.S_.op("pool", lambda e: e.affine_select(out=self.ident_f[:], in_=onesf[:], pattern=[[-1, 128]],
                                                         compare_op=ALU.is_equal, fill=0.0, base=0,
                                                         channel_multiplier=1), ["onesf"], ["ident"])
            self.S_.barrier()

    def mix_phase(self, l, ph):
        nc, S_, I, O = self.nc, self.S_, self.I, self.O
        S, TS_, NSUB, NJ, JMIN, NSLOT = self.S, self.TSM, self.NSUB, self.NJ, self.JMIN, self.NSLOT
        TW = TS_ + 1
        rd, wr = (ph - 1) % 2, ph % 2
        self.set_rings(True)
        with contextlib.ExitStack() as st:
            sb = lambda n, s, d: self.sb(st, n, s, d)
            win = sb("win", [128, KC, DIN], BF16)
            wout = sb("wout", [128, KC, D], BF16)
            wglu = sb("wglu", [128, 2, 256], BF16)
            poolw = sb("poolw", [128, 2, 128], BF16)
            Bre = sb("Bre", [128, 8, 128], BF16)
            Bim = sb("Bim", [128, 8, 128], BF16)
            Cre = sb("Cre", [128, 8, 128], BF16)
            Cim = sb("Cim", [128, 8, 128], BF16)
            Tre = sb("Tre", [128, 8, TW], F32)
            Tim = sb("Tim", [128, 8, TW], F32)
            cs = sb("cs", [128, 8, TW], F32)
            sn = sb("sn", [128, 8, TW], F32)
            rho = sb("rho", [128, 8], F32)
            mask = sb("mask", [128, NJ, TS_], BF16)
            icnt_tab = sb("icnt_tab", [128, 2, 16], F32)
            icnt = sb("icnt", [128, 2], F32)
            for c in range(KC):
                self.DMA(win[:, c, :], I["w_in"][l, c * 128:(c + 1) * 128, :], [], ["win"], q="pool")
            for c in range(KC):
                self.DMA(wout[:, c, :], I["w_out"][l, c * 128:(c + 1) * 128, :], [], ["wout"], q="pool")
            for c in range(2):
                self.DMA(wglu[:, c, :], I["ssm_w_glu"][l, c * 128:(c + 1) * 128, :], [], ["wglu"], q="pool")
            self.MS("pool", poolw[:], 0.0, ["poolw"])
            for gi in range(4):
                h0 = (gi % 2) * 64
                self.DMA(poolw[h0:h0 + 64, gi // 2, h0:h0 + 64], I["pool_w"][l, gi], [], ["poolw"], q="pool")
            rows = [(I["norm1_g"][l].rearrange("(c p) -> c p", p=128), 0),
                    (I["out_norm_att"][l].rearrange("(c p) -> c p", p=128), 8),
                    (I["out_norm_ssm"][l].rearrange("(c p) -> c p", p=128), 12),
                    (I["out_norm_pool"][l].rearrange("(c p) -> c p", p=128), 14),
                    (I["pool_scale"][l].rearrange("(c p) -> c p", p=128), 16),
                    (I["ssm_d"][l].rearrange("(c p) -> c p", p=128), 18),
                    (I["ssm_b_glu"][l].rearrange("(c p) -> c p", p=128), 20)]
            pc = self.load_rows_T(st, rows, "pc")
            self.ckpt("m_pc")
            g1, gatt, gssm, gpool = pc[:, 0:8], pc[:, 8:12], pc[:, 12:14], pc[:, 14:16]
            pscale, dskip, bglu = pc[:, 16:18], pc[:, 18:20], pc[:, 20:22]
            with contextlib.ExitStack() as t:
                tb = lambda n, s, d: self.sb(t, n, s, d)
                arr = lambda nm: I[nm][l].rearrange("(gp gl) p -> gp (gl p)", gl=2)
                sc = self.load_rows_T(t, [(arr("ssm_a_re"), 0), (arr("ssm_a_im"), 32)], "sc")
                are, aim = sc[:, 0:8], sc[:, 32:40]
                dtr = tb("dtr", [128, 2], F32)
                dtrow = tb("dtrow", [128, 128], F32)
                self.MS("pool", dtr[:], 0.0, ["dtr"])
                self.MS("pool", dtrow[:], 0.0, ["dtrow"])
                self.DMA(dtr[0:8, :], I["ssm_log_dt"][l].rearrange("(gp gl) -> gp gl", gl=2), [], ["dtr"])
                self.ACT(dtr[0:8, :], dtr[0:8, :], AF.Exp, ["dtr"], ["dtr"])
                self.CP("dve", dtrow[0:8, :].rearrange("p (a b) -> p a b", b=64),
                        dtr[0:8, :].unsqueeze(2).to_broadcast([8, 2, 64]), ["dtr"], ["dtrow"])
                psd, pkd = self.psC.next()
                self.TR(psd[:, 0:128], dtrow[:], self.ident_f[:], ["dtrow", "ident"], [pkd])
                dtc = tb("dtc", [128, 8], F32)
                self.CP("dve", dtc[:], psd[:, 0:8], [pkd], ["dtc"])
                th = tb("th", [128, 8], F32)
                lre = tb("lre", [128, 8], F32)
                self.TT("dve", lre[:], are, dtc[:], ALU.mult, ["sccols", "dtc"], ["lre"])
                self.TT("dve", th[:], aim, dtc[:], ALU.mult, ["sccols", "dtc"], ["thX"])
                self.ACT(rho[:], lre[:], AF.Exp, ["lre"], ["rho"])
                s1, c1 = self.sincos(t, th[:], 8, "th")
                abre = tb("abre", [128, 8], F32)
                abim = tb("abim", [128, 8], F32)
                self.TT("dve", abre[:], rho[:], c1[:], ALU.mult, ["rho", "thc"], ["abre"])
                self.TT("dve", abim[:], rho[:], s1[:], ALU.mult, ["rho", "ths"], ["abim"])
                self.TS("dve", abre[:], abre[:], -1.0, None, ALU.add, None, ["abre"], ["abre"])
                den = tb("den", [128, 8], F32)
                t0 = tb("t0", [128, 8], F32)
                core_ = tb("core", [128, 8], F32)
                coim = tb("coim", [128, 8], F32)
                self.TT("dve", den[:], are, are, ALU.mult, ["sccols"], ["den"])
                self.TT("dve", t0[:], aim, aim, ALU.mult, ["sccols"], ["t0"])
                self.TT("dve", den[:], den[:], t0[:], ALU.add, ["den", "t0"], ["den"])
                self.S_.op("dve", lambda e: e.reciprocal(out=den[:], in_=den[:]), ["den"], ["den"])
                self.TT("dve", core_[:], abre[:], are, ALU.mult, ["abre", "sccols"], ["core"])
                self.TT("dve", t0[:], abim[:], aim, ALU.mult, ["abim", "sccols"], ["t0"])
                self.TT("dve", core_[:], core_[:], t0[:], ALU.add, ["core", "t0"], ["core"])
                self.TT("dve", core_[:], core_[:], den[:], ALU.mult, ["core", "den"], ["core"])
                self.TT("dve", coim[:], abim[:], are, ALU.mult, ["abim", "sccols"], ["coim"])
                self.TT("dve", t0[:], abre[:], aim, ALU.mult, ["abre", "sccols"], ["t0"])
                self.TT("dve", coim[:], coim[:], t0[:], ALU.subtract, ["coim", "t0"], ["coim"])
                self.TT("dve", coim[:], coim[:], den[:], ALU.mult, ["coim", "den"], ["coim"])
                self.ckpt("m_co")
                iot = tb("iot", [128, TW], F32)
                self.S_.op("pool", lambda e: e.iota(iot[:], pattern=[[1, TW]], base=0, channel_multiplier=0,
                                                    allow_small_or_imprecise_dtypes=True), [], ["iot"])
                ang = tb("ang", [128, 8, TW], F32)
                self.TT("dve", ang[:], th[:].unsqueeze(2).to_broadcast([128, 8, TW]),
                        iot[:].unsqueeze(1).to_broadcast([128, 8, TW]), ALU.mult, ["thX", "iot"], ["angX"])
                sA, cA = self.sincos(t, ang[:].rearrange("p a b -> p (a b)"), 8 * TW, "ang")
                self.CP("pool", sn[:].rearrange("p a b -> p (a b)"), sA[:], ["angs"], ["sn"])
                self.CP("pool", cs[:].rearrange("p a b -> p (a b)"), cA[:], ["angc"], ["cs"])
                tmpT = tb("tmpT", [128, 8, TW], F32)
                cob = lambda x: x[:].unsqueeze(2).to_broadcast([128, 8, TW])
                self.TT("dve", Tre[:], cs[:], cob(core_), ALU.mult, ["cs", "core"], ["Tre"])
                self.TT("dve", tmpT[:], sn[:], cob(coim), ALU.mult, ["sn", "coim"], ["tmpT"])
                self.TT("dve", Tre[:], Tre[:], tmpT[:], ALU.add, ["Tre", "tmpT"], ["Tre"])
                self.TT("dve", Tim[:], cs[:], cob(coim), ALU.mult, ["cs", "coim"], ["Tim"])
                self.TT("dve", tmpT[:], sn[:], cob(core_), ALU.mult, ["sn", "core"], ["tmpT"])
                self.TT("dve", Tim[:], Tim[:], tmpT[:], ALU.subtract, ["Tim", "tmpT"], ["Tim"])
                self.ckpt("m_tab")
                X4 = tb("X4", [128, 8, 128], F32)
                for nm, dst, isB in (("ssm_b_re", Bre, True), ("ssm_b_im", Bim, True),
                                     ("ssm_c_re", Cre, False), ("ssm_c_im", Cim, False)):
                    self.MS("pool", X4[:], 0.0, ["X4"])
                    for g in range(16):
                        gp, gl = g // 2, g % 2
                        r0 = (gp % 4) * 32
                        if isB:
                            self.DMA(X4[gl * 64:(gl + 1) * 64, gp, r0 + gl * 16:r0 + gl * 16 + 16], I[nm][l, g],
                                     [], ["X4"])
                        else:
                            self.DMA(X4[r0 + gl * 16:r0 + gl * 16 + 16, gp, gl * 64:(gl + 1) * 64], I[nm][l, g],
                                     [], ["X4"])
                    for gp in range(8):
                        psx, pkx = self.psA.next()
                        self.TR(psx[:, 0:128], X4[:, gp, :], self.ident_f[:], ["X4", "ident"], [pkx])
                        self.CP("act", dst[:, gp, :], psx[:, 0:128], [pkx], [nm])
                S_.barrier()
                self.ckpt("m_bc")
            with contextlib.ExitStack() as t:
                tb = lambda n, s, d: self.sb(t, n, s, d)
                NM = NJ * TS_
                v = tb("mv", [128, NM], F32)
                m0 = tb("m0", [128, NM], F32)
                cc = tb("mc", [128, NM], F32)
                qi = tb("mq", [128, NM], I32)
                rr = tb("mr", [128, NM], F32)
                self.S_.op("pool", lambda e: e.iota(v[:], pattern=[[128, NJ], [1, TS_]], base=128 * JMIN,
                                                    channel_multiplier=-1, allow_small_or_imprecise_dtypes=True),
                           [], ["mv"])
                self.TS("dve", m0[:], v[:], 0.0, None, ALU.is_ge, None, ["mv"], ["m0"])
                self.TS("dve", cc[:], v[:], 128.0, None, ALU.is_le, None, ["mv"], ["mc"])
                self.TT("dve", cc[:], cc[:], m0[:], ALU.mult, ["mc", "m0"], ["mc"])
                for dil, lim in ((4, 512.0), (16, 2048.0)):
                    self.TS("dve", qi[:], v[:], 1.0 / dil, None, ALU.mult, None, ["mv"], ["mq"])
                    self.STT(rr[:], qi[:], -float(dil), v[:], ALU.mult, ALU.add, ["mq", "mv"], ["mr"])
                    self.TS("dve", rr[:], rr[:], 0.0, None, ALU.is_equal, None, ["mr"], ["mr"])
                    self.TT("dve", rr[:], rr[:], m0[:], ALU.mult, ["mr", "m0"], ["mr"])
                    self.TS("dve", qi[:], v[:], lim, None, ALU.is_le, None, ["mv"], ["mq"])
                    self.TT("dve", rr[:], rr[:], qi[:], ALU.mult, ["mr", "mq"], ["mr"])
                    self.TT("dve", cc[:], cc[:], rr[:], ALU.add, ["mc", "mr"], ["mc"])
                self.CP("dve", mask[:].rearrange("p a b -> p (a b)"), cc[:], ["mc"], ["mask"])
                self.ckpt("m_mask")
                wcol = tb("wcol", [128, 2], F32)
                for ch, (wa, wb) in enumerate(((2.0, 4.0), (8.0, 16.0))):
                    self.MS("pool", wcol[0:64, ch:ch + 1], wa, ["wcol"])
                    self.MS("pool", wcol[64:128, ch:ch + 1], wb, ["wcol"])
                self.S_.op("dve", lambda e: e.reciprocal(out=icnt[:], in_=wcol[:]), ["wcol"], ["icnt"])
                i16 = tb("i16", [128, 16], F32)
                self.S_.op("pool", lambda e: e.iota(i16[:], pattern=[[1, 16]], base=1, channel_multiplier=0,
                                                    allow_small_or_imprecise_dtypes=True), [], ["i16"])
                self.TT("dve", icnt_tab[:], i16[:].unsqueeze(1).to_broadcast([128, 2, 16]),
                        wcol[:].unsqueeze(2).to_broadcast([128, 2, 16]), ALU.min, ["i16", "wcol"], ["icnt_tab"])
                self.S_.op("dve", lambda e: e.reciprocal(out=icnt_tab[:], in_=icnt_tab[:]), ["icnt_tab"], ["icnt_tab"])
                S_.barrier()
            xT = [sb("xT%d" % i, [128, KC, TS_], F32) for i in range(2)]
            xtok = sb("xtok", [128, NSUB, D], F32)
            sq = sb("sq", [128, KC, TS_], BF16)
            sd = sb("sd", [128, TS_], F32)
            rstd = sb("rstd", [128, TS_], F32)
            hnT = sb("hnT", [128, KC, TS_], BF16)
            qT = sb("qT", [128, 4, TS_], BF16)
            KT = sb("KT", [128, 4, NSLOT * 128], BF16)
            Vr = sb("Vr", [128, NSLOT, 8, 65], BF16)
            usb = sb("usb", [128, 2, TS_], BF16)
            usf = sb("usf", [128, 2, TS_], F32)
            upl = sb("upl", [128, 2, 16 + TS_], F32)
            Pr = Ring([(sb("P%d" % i, [128, 2 * TS_], BF16), "P%d" % i) for i in range(6)])
            rec = sb("rec", [128, NSUB], F32)
            att = sb("att", [128, NSUB, 512], F32)
            ssa = sb("ssa", [128, NSUB], F32)
            junk = sb("junk", [128, 512], BF16)
            mixT = sb("mixT", [128, KC, TS_], BF16)
            Gre = sb("Gre", [128, 8, TS_], F32)
            Gim = sb("Gim", [128, 8, TS_], F32)
            ta = sb("ta", [128, 8, TS_], F32)
            tb2 = sb("tb2", [128, 8, TS_], F32)
            hre = sb("hre", [128, 8, TS_], BF16)
            himn = sb("himn", [128, 8, TS_], BF16)
            wtmp = Ring([(sb("wt%d" % i, [128, TS_], F32), "wt%d" % i) for i in range(6)])
            gst = sb("gst", [128, 2, 8], F32)
            gl_t = sb("gl_t", [128, 4, 8], F32)
            hl = sb("hl", [128, 2, 8], F32)
            yss = sb("yss", [128, 2, TS_], F32)
            gact = sb("gact", [128, 2, TS_], F32)
            gactb = sb("gactb", [128, 2, TS_], BF16)
            sgl = sb("sgl", [128, 2, TS_], F32)
            pl1 = sb("pl1", [128, 2, 16 + TS_], F32)
            pl2 = sb("pl2", [128, 2, 16 + TS_], F32)
            wsel = sb("wsel", [128, 2, TS_], F32)
            pooled = sb("pooled", [128, 2, TS_], BF16)
            pout = sb("pout", [128, 2, TS_], F32)
            t16 = sb("t16", [128, 2, 16], F32)
            kvst = Ring([(sb("kvst%d" % i, [128, 512], F32), "kvst%d" % i) for i in range(2)])
            kts = sb("kts", [128, 4, NS], BF16)
            h0c = sb("h0c", [128, 2, 8], F32)
            ups = sb("ups", [128, 2, NS], F32)
            kcs = Ring([(sb("kcs%d" % i, [128, 512], F32), "kcs%d" % i) for i in range(2)])
            rmstmp = (sq, sd, rstd)
            self.ckpt("m_icnt")
            self.MS("pool", Vr[:], 1.0, ["vall"])
            S_.barrier()
            self.ckpt("m_work")

            def load_xT(dst, tok0, NT, from_tok, src_tok=None):
                key = "xT%d" % xT.index(dst)
                if from_tok:
                    nsub = (NT + 127) // 128
                    w = min(128, NT)
                    for sub in range(nsub):
                        self.DMA(xtok[0:w, sub, :], src_tok[tok0 + sub * 128: tok0 + sub * 128 + w, :], [], ["xtok"])
                    for c in range(KC):
                        ps, pk = self.psA.next()
                        for sub in range(nsub):
                            self.TR(ps[:, sub * 128: sub * 128 + w], xtok[0:w, sub, c * 128:(c + 1) * 128],
                                    self.ident_f[0:w, 0:w], ["xtok", "ident"], [pk])
                        self.CP("act" if c % 2 else "dve", dst[:, c, 0:NT], ps[:, 0:NT], [pk], [key])
                else:
                    self.DMA(dst[:, :, 0:NT], self.scr[rd, :, :, tok0:tok0 + NT].rearrange("c p t -> p c t"),
                             [], [key])
                return key

            def in_proj(xk, x, NT):
                self.rms_feat(x, KC, NT, g1, hnT, D, [xk], "hnT", rmstmp)
                self.ckpt("t_rms")
                outs = {}
                for oc in (0, 1, 2, 3, 4, 5, 6, 7, 12, 13, 14, 15):
                    ps, pk = self.psA.next()
                    for kc in range(KC):
                        self.MM(ps[:, 0:NT], win[:, kc, oc * 128:(oc + 1) * 128], hnT[:, kc, 0:NT], kc == 0,
                                kc == KC - 1, ["win", "hnT"], [pk])
                    outs[oc] = (ps, pk)
                    yield oc, ps, pk

            def attention(qcol0, NQ, ktiles, LA=4):
                nsub = (NQ + 127) // 128
                qw = min(128, NQ)
                pending = []

                def flush_one():
                    item = pending.pop(0)
                    if item[0] == "pv":
                        _, acc, acck, P, Pk, nk, slot, h, subs, first, last, pc0 = item
                        for i_, s_ in enumerate(subs):
                            self.MM(acc[0:qw, s_ * 65:(s_ + 1) * 65], P[0:nk, pc0 + s_ * 128: pc0 + s_ * 128 + qw],
                                    Vr[0:nk, slot, h, :], first and i_ == 0, last and i_ == len(subs) - 1,
                                    [Pk, "v%d" % slot, "vall"], [acck])
                    else:
                        _, acc, acck, h = item
                        accv = acc[0:qw, 0:nsub * 65].rearrange("p (s e) -> p s e", e=65)
                        self.S_.op("dve", lambda e, accv=accv: e.reciprocal(out=rec[0:qw, 0:nsub], in_=accv[:, :, 64]),
                                   [acck], ["rec"])
                        self.TT("dve", att[0:qw, 0:nsub, h * 64:(h + 1) * 64], accv[:, :, 0:64],
                                rec[0:qw, 0:nsub].unsqueeze(2).to_broadcast([qw, nsub, 64]), ALU.mult,
                                [acck, "rec"], ["att"])

                mm_i = [0]
                gq = []
                for h in range(8):
                    hp, hc = (h % 2) * 64, h // 2
                    acc, acck = self.psB.next()
                    plan = []
                    for (slot, nk, jj) in ktiles:
                        subs = [s_ for s_ in range(nsub) if 0 <= jj + s_ <= 16]
                        if subs:
                            plan.append((slot, nk, jj, subs))
                    groups = []
                    i_ = 0
                    while i_ < len(plan):
                        if (nsub == 1 and i_ + 1 < len(plan) and plan[i_][1] == 128 and plan[i_ + 1][1] == 128
                                and plan[i_ + 1][2] == plan[i_][2] - 1):
                            groups.append([plan[i_ + 1], plan[i_]])
                            i_ += 2
                        else:
                            groups.append([plan[i_]])
                            i_ += 1
                    npv = len(plan)
                    ipv = 0
                    for grp in groups:
                        ng = len(grp)
                        nk = grp[0][1]
                        scp, sck = self.psS.next()
                        P, Pk = Pr.next()
                        for gi_, (slot, nk_, jj, subs) in enumerate(grp):
                            self.MM(scp[0:nk, gi_ * NQ:(gi_ + 1) * NQ], KT[hp:hp + 64, hc, slot * 128: slot * 128 + nk],
                                    qT[hp:hp + 64, hc, qcol0:qcol0 + NQ], gi_ == 0, gi_ == ng - 1,
                                    ["kt%d" % slot, "qT"], [sck])
                        self.ACT(P[0:nk, 0:ng * NQ], scp[0:nk, 0:ng * NQ], AF.Exp, [sck], [Pk], scale=0.125)
                        m0_ = grp[0][2] - JMIN
                        mm_i[0] += 1
                        self.TT("pool" if mm_i[0] % 3 == 0 else "dve",
                                P[0:nk, 0:ng * NQ].rearrange("p (a q) -> p a q", a=ng),
                                P[0:nk, 0:ng * NQ].rearrange("p (a q) -> p a q", a=ng),
                                mask[0:nk, m0_:m0_ + ng, 0:NQ], ALU.mult, [Pk, "mask"], [Pk])
                        for gi_, (slot, nk_, jj, subs) in enumerate(grp):
                            pending.append(("pv", acc, acck, P, Pk, nk, slot, h, subs, ipv == 0, ipv == npv - 1, gi_ * NQ))
                            ipv += 1
                        if ipv == npv:
                            pending.append(("fin", acc, acck, h))
                        gq.append(ng)
                        while len(gq) > LA:
                            for _ in range(gq.pop(0)):
                                while pending[0][0] != "pv":
                                    flush_one()
                                flush_one()
                        yield
                while pending:
                    flush_one()
                yield

            def interleave(gens, weights):
                alive = list(gens)
                wts = list(weights)
                while alive:
                    for i in range(len(alive) - 1, -1, -1):
                        pass
                    nxt_alive, nxt_w = [], []
                    for g_, w_ in zip(alive, wts):
                        ok = True
                        for _ in range(w_):
                            try:
                                next(g_)
                            except StopIteration:
                                ok = False
                                break
                        if ok:
                            nxt_alive.append(g_)
                            nxt_w.append(w_)
                    alive, wts = nxt_alive, nxt_w

            def att_finish(col0, NQ):
                nsub = (NQ + 127) // 128
                qw = min(128, NQ)
                for s_ in range(nsub):
                    self.ACT(junk[0:qw, :], att[0:qw, s_, :], AF.Square, ["att"], ["junk", "ssa"], accum=ssa[0:qw, s_:s_ + 1])
                self.ACT(ssa[0:qw, 0:nsub], ssa[0:qw, 0:nsub], AF.Sqrt, ["junk", "ssa"], ["ssa"], bias=self.eps_col[0:qw, 0:1],
                         scale=1.0 / 512)
                self.S_.op("dve", lambda e: e.reciprocal(out=ssa[0:qw, 0:nsub], in_=ssa[0:qw, 0:nsub]), ["ssa"], ["ssa"])
                self.TT("dve", att[0:qw, 0:nsub, :], att[0:qw, 0:nsub, :],
                        ssa[0:qw, 0:nsub].unsqueeze(2).to_broadcast([qw, nsub, 512]), ALU.mult, ["att", "ssa"], ["att"])
                for c in range(4):
                    ps, pk = self.psA.next()
                    for s_ in range(nsub):
                        self.TR(ps[:, s_ * 128: s_ * 128 + qw], att[0:qw, s_, c * 128:(c + 1) * 128],
                                self.ident_f[0:qw, 0:qw], ["att", "ident"], [pk])
                    self.TS("dve", mixT[:, c, col0:col0 + NQ], ps[:, 0:NQ], gatt[:, c:c + 1], None, ALU.mult, None,
                            [pk, "pccols"], ["mixT"])

            def ssm(col0, NT, first_zero):
                for gp in range(8):
                    ch = gp // 4
                    pre, prk = self.psA.next()
                    self.MM(pre[:, 0:NT], Bre[:, gp, :], usb[:, ch, col0:col0 + NT], True, True, ["usb", "ssm_b_re"], [prk])
                    pim, pik = self.psA.next()
                    self.MM(pim[:, 0:NT], Bim[:, gp, :], usb[:, ch, col0:col0 + NT], True, True, ["usb", "ssm_b_im"], [pik])
                    wre, wrk = wtmp.next()
                    wim, wik = wtmp.next()
                    t1, t1k = wtmp.next()
                    self.TT("dve", wre[:, 0:NT], pre[:, 0:NT], Tre[:, gp, 0:NT], ALU.mult, [prk, "Tre"], [wrk])
                    self.TT("dve", t1[:, 0:NT], pim[:, 0:NT], Tim[:, gp, 0:NT], ALU.mult, [pik, "Tim"], [t1k])
                    self.TT("pool", wre[:, 0:NT], wre[:, 0:NT], t1[:, 0:NT], ALU.subtract, [wrk, t1k], [wrk])
                    t2, t2k = wtmp.next()
                    self.TT("dve", wim[:, 0:NT], pim[:, 0:NT], Tre[:, gp, 0:NT], ALU.mult, [pik, "Tre"], [wik])
                    self.TT("dve", t2[:, 0:NT], pre[:, 0:NT], Tim[:, gp, 0:NT], ALU.mult, [prk, "Tim"], [t2k])
                    self.TT("pool", wim[:, 0:NT], wim[:, 0:NT], t2[:, 0:NT], ALU.add, [wik, t2k], [wik])
                    for pl, (G, wv, wk) in enumerate(((Gre, wre, wrk), (Gim, wim, wik))):
                        self.S_.op("dve", lambda e, G=G, wv=wv, pl=pl, gp=gp: e.tensor_tensor_scan(
                            out=G[:, gp, 0:NT], data0=rho[:, gp:gp + 1].to_broadcast([128, NT]), data1=wv[:, 0:NT],
                            initial=gst[:, pl, gp:gp + 1], op0=ALU.mult, op1=ALU.add),
                            [wk, "rho", "gst"], ["G%d" % pl])
                    yield
                csn, snn = cs[:, :, 0:NT], sn[:, :, 0:NT]
                self.TT("dve", ta[:, :, 0:NT], Gre[:, :, 0:NT], csn, ALU.mult, ["G0", "cs"], ["ta"])
                self.TT("pool", tb2[:, :, 0:NT], Gim[:, :, 0:NT], snn, ALU.mult, ["G1", "sn"], ["tb2"])
                self.TT("dve", hre[:, :, 0:NT], ta[:, :, 0:NT], tb2[:, :, 0:NT], ALU.subtract, ["ta", "tb2"], ["hre"])
                yield
                self.TT("pool", ta[:, :, 0:NT], Gim[:, :, 0:NT], csn, ALU.mult, ["G1", "cs"], ["ta"])
                self.TT("dve", tb2[:, :, 0:NT], Gre[:, :, 0:NT], snn, ALU.mult, ["G0", "sn"], ["tb2"])
                self.STT(himn[:, :, 0:NT], ta[:, :, 0:NT], -1.0, tb2[:, :, 0:NT], ALU.mult, ALU.subtract,
                         ["ta", "tb2"], ["himn"])
                yield
                glr, gli = Gre[:, :, NT - 1], Gim[:, :, NT - 1]
                for dst, col in ((gst, NT), (hl, NT - 1)):
                    cN, sN = cs[:, :, col], sn[:, :, col]
                    self.TT("pool", gl_t[:, 0, :], glr, cN, ALU.mult, ["G0", "cs"], ["glt0"])
                    self.TT("pool", gl_t[:, 1, :], gli, sN, ALU.mult, ["G1", "sn"], ["glt1"])
                    self.TT("pool", gl_t[:, 2, :], gli, cN, ALU.mult, ["G1", "cs"], ["glt2"])
                    self.TT("pool", gl_t[:, 3, :], glr, sN, ALU.mult, ["G0", "sn"], ["glt3"])
                    dk = "gst" if dst is gst else "hl"
                    self.TT("pool", dst[:, 0, :], gl_t[:, 0, :], gl_t[:, 1, :], ALU.subtract, ["glt0", "glt1"], [dk])
                    self.TT("pool", dst[:, 1, :], gl_t[:, 2, :], gl_t[:, 3, :], ALU.add, ["glt2", "glt3"], [dk])
                if self.dbg and ph == 0 and col0 == 0 and not getattr(self, "_dumped", False):
                    self._dumped = True
                    dd = self.O["dbg"]
                    for i, (tt, kk) in enumerate(((Tre, "Tre"), (Tim, "Tim"), (cs, "cs"), (sn, "sn"))):
                        self.DMA(dd[5 + i, :, :, 0:TW].rearrange("c p t -> p c t"), tt[:, :, :], [kk], [])
                    self.DMA(dd[9, :, :, 0:NT].rearrange("c p t -> p c t"), Gre[:, :, 0:NT], ["G0"], [])
                    self.DMA(dd[10, :, :, 0:NT].rearrange("c p t -> p c t"), Gim[:, :, 0:NT], ["G1"], [])
                    self.DMA(dd[11, 0, :, 0:8], rho[:, :], ["rho"], [])
                    self.DMA(dd[11, 1, :, 0:NT], usf[:, 0, 0:NT], ["usf"], [])
                    self.DMA(dd[11, 2, :, 0:NT], usf[:, 1, 0:NT], ["usf"], [])
                yield
                for ch in range(2):
                    yp_, ypk = self.psC.next()
                    for i, gp in enumerate(range(ch * 4, ch * 4 + 4)):
                        self.MM(yp_[:, 0:NT], Cre[:, gp, :], hre[:, gp, 0:NT], i == 0, False, ["hre", "ssm_c_re"], [ypk])
                        self.MM(yp_[:, 0:NT], Cim[:, gp, :], himn[:, gp, 0:NT], False, i == 3, ["himn", "ssm_c_im"], [ypk])
                    self.STT(yss[:, ch, 0:NT], usf[:, ch, col0:col0 + NT], dskip[:, ch:ch + 1], yp_[:, 0:NT], ALU.mult,
                             ALU.add, [ypk, "usf", "pccols"], ["yss"])
                    yield
                yv, gv, sv = yss[:, :, 0:NT], gact[:, :, 0:NT], sgl[:, :, 0:NT]
                self.TT("pool", gv, yv, yv, ALU.mult, ["yss"], ["gact"])
                self.TS("dve", gv, gv, 0.044715, 1.0, ALU.mult, ALU.add, ["gact"], ["gact"])
                self.TT("pool", gv, gv, yv, ALU.mult, ["gact", "yss"], ["gact"])
                self.ACT(sv, gv, AF.Sigmoid, ["gact"], ["sgl"], scale=1.5957691216057308)
                self.TT("dve", gv, yv, sv, ALU.mult, ["yss", "sgl"], ["gact"])
                self.CP("pool", gactb[:, :, 0:NT], gv, ["gact"], ["gactb"])
                yield
                for oc in range(2):
                    zp, zk = self.psC.next()
                    for kc in range(2):
                        self.MM(zp[:, 0:NT], wglu[:, kc, oc * 128:(oc + 1) * 128], gactb[:, kc, 0:NT], kc == 0, kc == 1,
                                ["wglu", "gactb"], [zk])
                    self.ACT(sgl[:, oc, 0:NT], zp[:, 0:NT], AF.Sigmoid, [zk], ["sgl"], bias=bglu[:, oc:oc + 1])
                self.TT("dve", yss[:, :, 0:NT], gv, sv, ALU.mult, ["gact", "sgl"], ["yss"])
                yield
                self.rms_feat(yss, 2, NT, gssm, _View(mixT, 4, col0), 256, ["yss"], "mixT", rmstmp)
                yield

            def pool_mix(col0, NT, first):
                W = 16 + NT
                self.TT("pool", pl1[:, :, 1:W], upl[:, :, 1:W], upl[:, :, 0:W - 1], ALU.add, ["upl"], ["pl1"])
                self.TT("dve", pl2[:, :, 3:W], pl1[:, :, 3:W], pl1[:, :, 1:W - 2], ALU.add, ["pl1"], ["pl2"])
                self.CP("pool", wsel[0:64, 0, 0:NT], pl1[0:64, 0, 16:W], ["pl1"], ["wsel"])
                self.CP("pool", wsel[64:128, 0, 0:NT], pl2[64:128, 0, 16:W], ["pl2"], ["wsel"])
                yield
                self.TT("dve", pl1[:, :, 7:W], pl2[:, :, 7:W], pl2[:, :, 3:W - 4], ALU.add, ["pl2", "wsel"], ["pl1"])
                self.TT("pool", pl2[:, :, 15:W], pl1[:, :, 15:W], pl1[:, :, 7:W - 8], ALU.add, ["pl1", "wsel"], ["pl2"])
                self.CP("dve", wsel[0:64, 1, 0:NT], pl1[0:64, 1, 16:W], ["pl1"], ["wsel"])
                self.CP("dve", wsel[64:128, 1, 0:NT], pl2[64:128, 1, 16:W], ["pl2"], ["wsel"])
                yield
                for ch in range(2):
                    self.STT(pooled[:, ch, 0:NT], wsel[:, ch, 0:NT], icnt[:, ch:ch + 1], upl[:, ch, 16:W], ALU.mult,
                             ALU.subtract, ["wsel", "icnt", "upl"], ["pooled"])
                if first:
                    self.TT("dve", t16[:], wsel[:, :, 0:16], icnt_tab[:], ALU.mult, ["wsel", "icnt_tab"], ["t16"])
                    self.TT("dve", pooled[:, :, 0:16], t16[:], upl[:, :, 16:32], ALU.subtract, ["t16", "upl"], ["pooled"])
                for ch in range(2):
                    pp, ppk = self.psC.next()
                    self.MM(pp[:, 0:NT], poolw[:, ch, :], pooled[:, ch, 0:NT], True, True, ["poolw", "pooled"], [ppk])
                    self.TS("dve", pout[:, ch, 0:NT], pp[:, 0:NT], pscale[:, ch:ch + 1], None, ALU.mult, None,
                            [ppk, "pccols"], ["pout"])
                yield
                self.rms_feat(pout, 2, NT, gpool, _View(mixT, 6, col0), 256, ["pout"], "mixT", rmstmp)
                yield

            def out_proj(xk, x, NT, tok0):
                for oc in range(KC):
                    ps, pk = self.psA.next()
                    for kc in range(KC):
                        self.MM(ps[:, 0:NT], wout[:, kc, oc * 128:(oc + 1) * 128], mixT[:, kc, 0:NT], kc == 0,
                                kc == KC - 1, ["wout", "mixT"], [pk])
                    self.TT("dve", x[:, oc, 0:NT], ps[:, 0:NT], x[:, oc, 0:NT], ALU.add, [pk, xk], [xk])
                self.DMA(self.scr[wr, :, :, tok0:tok0 + NT].rearrange("c p t -> p c t"), x[:, :, 0:NT], [xk], [])
                if self.dbg:
                    self.DMA(self.O["dbg"][ph, :, :, tok0:tok0 + NT].rearrange("c p t -> p c t"), x[:, :, 0:NT], [xk], [])
                    if ph == 0:
                        self.DMA(self.O["dbg"][4, :, :, tok0:tok0 + NT].rearrange("c p t -> p c t"), mixT[:, :, 0:NT], ["mixT"], [], q="pool")

            def kv_tokmajor(NT, cols, is_v, slots, out_aps):
                nsub = len(cols)
                for s_, (c0, w) in enumerate(cols):
                    ps, pk = self.psA.next()
                    off = 1024 if is_v else 512
                    for kc in range(KC):
                        self.MM(ps[0:w, 0:512], hnT[:, kc, c0:c0 + w], win[:, kc, off:off + 512], kc == 0, kc == KC - 1,
                                ["hnT", "win"], [pk])
                    if is_v and slots[s_] is not None:
                        self.CP("act", Vr[0:w, slots[s_], :, 0:64], ps[0:w, 0:512].rearrange("p (h e) -> p h e", e=64),
                                [pk], ["v%d" % slots[s_]])
                    if out_aps[s_] is not None:
                        stg, sk = kvst.next()
                        self.CP("dve", stg[0:w, :], ps[0:w, 0:512], [pk], [sk])
                        self.DMA(out_aps[s_], stg[0:w, :], [sk], [])

            self.MS("pool", gst[:], 0.0, ["gst"])
            self.MS("pool", upl[:], 0.0, ["upl"])
            nst = S // TS_
            nxt = load_xT(xT[0], 0, TS_, ph == 0, I["xp"])
            self.ckpt("t_load")
            for sti in range(nst):
                x = xT[sti % 2]
                xk = nxt
                tok0 = sti * TS_
                if sti + 1 < nst:
                    nxt = load_xT(xT[(sti + 1) % 2], tok0 + TS_, TS_, ph == 0, I["xp"])
                a0 = tok0 // 128
                slots = [(a0 + s_) % NSLOT for s_ in range(NSUB)]
                for oc, ps, pk in in_proj(xk, x, TS_):
                    if oc < 4:
                        self.CP("act", qT[:, oc, 0:TS_], ps[:, 0:TS_], [pk], ["qT"])
                    elif oc < 8:
                        for s_ in range(NSUB):
                            self.CP("act" if s_ % 2 else "dve", KT[:, oc - 4, slots[s_] * 128:(slots[s_] + 1) * 128],
                                    ps[:, s_ * 128:(s_ + 1) * 128], [pk], ["kt%d" % slots[s_]])
                    elif oc < 14:
                        self.CP("act", usb[:, oc - 12, 0:TS_], ps[:, 0:TS_], [pk], ["usb"])
                        self.CP("dve", usf[:, oc - 12, 0:TS_], ps[:, 0:TS_], [pk], ["usf"])
                    else:
                        self.CP("dve", upl[:, oc - 14, 16:16 + TS_], ps[:, 0:TS_], [pk], ["upl"])
                keep0 = S - self.NKEEP
                cols = [(s_ * 128, 128) for s_ in range(NSUB)]
                outs_v = [O["vp"][l, tok0 + s_ * 128 - keep0: tok0 + s_ * 128 - keep0 + 128, :]
                          if tok0 + s_ * 128 >= keep0 else None for s_ in range(NSUB)]
                outs_k = [O["kp"][l, tok0 + s_ * 128 - keep0: tok0 + s_ * 128 - keep0 + 128, :]
                          if tok0 + s_ * 128 >= keep0 else None for s_ in range(NSUB)]
                self.ckpt("t_inproj")
                kv_tokmajor(TS_, cols, True, slots, outs_v)
                if any(o is not None for o in outs_k):
                    kv_tokmajor(TS_, cols, False, [None] * NSUB, outs_k)
                ktiles = [(a % NSLOT, 128, a0 - a) for a in range(max(0, a0 - 16), a0 + NSUB)]
                interleave([attention(0, TS_, ktiles), ssm(0, TS_, sti == 0), pool_mix(0, TS_, sti == 0)], [6, 1, 1])
                att_finish(0, TS_)
                if sti + 1 < nst:
                    self.CP("pool", upl[:, :, 1:16], upl[:, :, TS_ + 1:TS_ + 16], ["upl"], ["upl"])
                out_proj(xk, x, TS_, tok0)
            self.ckpt("m_prompt")
            for gp in range(8):
                for pl, nm in ((0, "rep"), (1, "imp")):
                    dst = bass.AP(O[nm].tensor, O[nm][l].offset + gp * 128, [[1, 128], [1, 1]])
                    self.DMA(dst, hl[:, pl, gp:gp + 1], ["hl"], [])
            for ch in range(2):
                for r_ in range(15):
                    dst = bass.AP(O["poolp"].tensor, O["poolp"][l, r_].offset + ch * 128, [[1, 128], [1, 1]])
                    self.DMA(dst, upl[:, ch, TS_ + 1 + r_: TS_ + 2 + r_], ["upl"], [], q="act")

            self.ckpt("m_pout")
            xs_ = xT[nst % 2]
            xk = load_xT(xs_, 0 if ph == 0 else S, NS, ph == 0, I["xs"])
            for oc, ps, pk in in_proj(xk, xs_, NS):
                if oc < 4:
                    self.CP("act", qT[:, oc, 0:NS], ps[:, 0:NS], [pk], ["qT"])
                elif oc < 8:
                    self.CP("act", kts[:, oc - 4, :], ps[:, 0:NS], [pk], ["kts"])
                elif oc < 14:
                    self.CP("act", usb[:, oc - 12, 0:NS], ps[:, 0:NS], [pk], ["usb"])
                    self.CP("dve", usf[:, oc - 12, 0:NS], ps[:, 0:NS], [pk], ["usf"])
                else:
                    self.CP("dve", ups[:, oc - 14, 0:NS], ps[:, 0:NS], [pk], ["ups"])
            for s in range(NSEQ):
                c0 = s * TD
                for m in range(16):
                    kc_, kck = kcs.next()
                    self.DMA(kc_[:], I["ck"][l, s, m * 128:(m + 1) * 128, :], [], [kck])
                    psk, pkk = self.psA.next()
                    for c in range(4):
                        self.TR(psk[:, c * 128:(c + 1) * 128], kc_[:, c * 128:(c + 1) * 128], self.ident_f[:],
                                [kck, "ident"], [pkk])
                    self.CP("act" if m % 2 else "dve", KT[:, :, m * 128:(m + 1) * 128],
                            psk[:, 0:512].rearrange("p (c k) -> p c k", k=128), [pkk], ["kt%d" % m])
                    self.DMA(Vr[:, m, :, 0:64], I["cv"][l, s, m * 128:(m + 1) * 128, :].rearrange("k (h e) -> k h e", e=64),
                             [], ["v%d" % m], q="pool")
                self.CP("dve", KT[:, :, 16 * 128:16 * 128 + TD], kts[:, :, c0:c0 + TD], ["kts"], ["kt16"])
                kv_tokmajor(TD, [(c0, TD)], True, [16], [O["vs"][l, s]])
                kv_tokmajor(TD, [(c0, TD)], False, [None], [O["ks"][l, s]])
                ktiles = [(m, 128, 16 - m) for m in range(16)] + [(16, TD, 0)]
                interleave([attention(c0, TD, ktiles)], [1])
                att_finish(c0, TD)
                for gp in range(8):
                    for pl, nm in ((0, "sre"), (1, "sim")):
                        src = bass.AP(I[nm].tensor, I[nm][l, s].offset + gp * 128, [[1, 128], [1, 1]])
                        self.DMA(h0c[:, pl, gp:gp + 1], src, [], ["h0c"])
                c1_, s1_ = cs[:, :, 1], sn[:, :, 1]
                self.TT("pool", gl_t[:, 0, :], h0c[:, 0, :], c1_, ALU.mult, ["h0c", "cs"], ["glt0"])
                self.TT("pool", gl_t[:, 1, :], h0c[:, 1, :], s1_, ALU.mult, ["h0c", "sn"], ["glt1"])
                self.TT("pool", gl_t[:, 2, :], h0c[:, 1, :], c1_, ALU.mult, ["h0c", "cs"], ["glt2"])
                self.TT("pool", gl_t[:, 3, :], h0c[:, 0, :], s1_, ALU.mult, ["h0c", "sn"], ["glt3"])
                self.TT("pool", gst[:, 0, :], gl_t[:, 0, :], gl_t[:, 1, :], ALU.subtract, ["glt0", "glt1"], ["gst"])
                self.TT("pool", gst[:, 1, :], gl_t[:, 2, :], gl_t[:, 3, :], ALU.add, ["glt2", "glt3"], ["gst"])
                interleave([ssm(c0, TD, False)], [1])
                for gp in range(8):
                    for pl, nm in ((0, "res"), (1, "ims")):
                        dst = bass.AP(O[nm].tensor, O[nm][l, s].offset + gp * 128, [[1, 128], [1, 1]])
                        self.DMA(dst, hl[:, pl, gp:gp + 1], ["hl"], [])
                for ch in range(2):
                    for r_ in range(15):
                        src = bass.AP(I["spool"].tensor, I["spool"][l, s, r_].offset + ch * 128, [[1, 128], [1, 1]])
                        self.DMA(upl[:, ch, 1 + r_:2 + r_], src, [], ["upl"], q="act")
                self.CP("dve", upl[:, :, 16:16 + TD], ups[:, :, c0:c0 + TD], ["ups"], ["upl"])
                for ch in range(2):
                    for r_ in range(15):
                        dst = bass.AP(O["pools"].tensor, O["pools"][l, s, r_].offset + ch * 128, [[1, 128], [1, 1]])
                        self.DMA(dst, upl[:, ch, TD + 1 + r_: TD + 2 + r_], ["upl"], [], q="act")
                interleave([pool_mix(c0, TD, False)], [1])
            out_proj(xk, xs_, NS, S)
            S_.barrier()

    def ffn_phase(self, l, ph, last):
        nc, S_, I, O = self.nc, self.S_, self.I, self.O
        S, TS_ = self.S, self.TSF
        rd, wr = (ph - 1) % 2, ph % 2
        self.set_rings(False)
        with contextlib.ExitStack() as st:
            sb = lambda n, s, d: self.sb(st, n, s, d)
            wup = sb("wup", [128, KC, 2 * DFF], BF16)
            wdn = sb("wdn", [128, AC, D], BF16)
            for c in range(KC):
                for hh in range(4):
                    self.DMA(wup[:, c, hh * 1408:(hh + 1) * 1408], I["w_up"][l, c * 128:(c + 1) * 128, hh * 1408:(hh + 1) * 1408],
                             [], ["wup"], q="pool")
            for c in range(AC):
                self.DMA(wdn[:, c, :], I["w_down"][l, c * 128:(c + 1) * 128, :], [], ["wdn"], q="pool")
            pa = self.load_rows_T(st, [(I["norm2_g"][l].rearrange("(c p) -> c p", p=128), 0),
                                       (I["norm_f_g"].rearrange("(c p) -> c p", p=128), 8),
                                       (I["conv_b"][l].rearrange("(c p) -> c p", p=128), 16),
                                       (I["conv_w"][l, 0].rearrange("(c p) -> c p", p=128), 64)], "pa")
            pb = self.load_rows_T(st, [(I["conv_w"][l, 1].rearrange("(c p) -> c p", p=128), 0),
                                       (I["conv_w"][l, 2].rearrange("(c p) -> c p", p=128), 64)], "pb")
            g2, gf, cb, cw0, cw1, cw2 = pa[:, 0:8], pa[:, 8:16], pa[:, 16:60], pa[:, 64:108], pb[:, 0:44], pb[:, 64:108]
            xT = [sb("fx%d" % i, [128, KC, TS_], F32) for i in range(2)]
            sq = sb("fsq", [128, KC, TS_], BF16)
            sd = sb("fsd", [128, TS_], F32)
            rstd = sb("frstd", [128, TS_], F32)
            hnT = sb("fhn", [128, KC, TS_], BF16)
            actT = sb("factT", [128, AC, TS_], BF16)
            raws = Ring([(sb("raw%d" % i, [128, TS_ + 2 * NSEQ], F32), "raw%d" % i) for i in range(4)])
            cvs = Ring([(sb("cv%d" % i, [128, TS_], F32), "cv%d" % i) for i in range(4)])
            sgs = Ring([(sb("sg%d" % i, [128, TS_], F32), "sg%d" % i) for i in range(2)])
            uptail = sb("uptail", [128, FC, 2], F32)
            stail = sb("stail", [128, FC, NSEQ, 2], F32)
            ynT = sb("ynT", [128, KC, TS_], F32) if last else None
            ytok = Ring([(sb("ytok%d" % i, [128, D], F32), "ytok%d" % i) for i in range(2)]) if last else None
            rmstmp = (sq, sd, rstd)
            self.MS("pool", uptail[:], 0.0, ["uptail"])
            for c in range(FC):
                src = I["sconv"][l, :, :, c * 128:(c + 1) * 128].rearrange("s r p -> p s r")
                self.DMA(stail[:, c, :, :], src, [], ["stail"], q="act", slow=True)
            S_.barrier()

            def load_x(dst, tok0, NT):
                key = "fx%d" % xT.index(dst)
                self.DMA(dst[:, :, 0:NT], self.scr[rd, :, :, tok0:tok0 + NT].rearrange("c p t -> p c t"), [], [key])
                return key

            def tile(xk, x, tok0, NT, nseq, tl, tail, tailk):
                self.rms_feat(x, KC, NT, g2, hnT, D, [xk], "fhn", rmstmp)
                for fc in range(AC):
                    st_ = []
                    for cidx in (fc, fc + AC):
                        ps, pk = self.psA.next()
                        for kc in range(KC):
                            self.MM(ps[:, 0:NT], wup[:, kc, cidx * 128:(cidx + 1) * 128], hnT[:, kc, 0:NT], kc == 0,
                                    kc == KC - 1, ["wup", "fhn"], [pk])
                        raw, rk = raws.next()
                        r3 = raw[:, 0:nseq * (tl + 2)].rearrange("p (s t) -> p s t", t=tl + 2)
                        tl_ap = tail[:, cidx, :].unsqueeze(1) if nseq == 1 else tail[:, cidx, :, :]
                        cv, ck = cvs.next()
                        c3 = cv[:, 0:NT].rearrange("p (s t) -> p s t", t=tl)
                        p3 = ps[:, 0:NT].rearrange("p (s t) -> p s t", t=tl)
                        tk_ = "%s%d" % (tailk, cidx)
                        self.CP("pool", r3[:, :, 0:2], tl_ap, [tailk, tk_], [rk])
                        self.CP("act", r3[:, :, 2:2 + tl], p3, [pk], [rk])
                        self.ACT(c3, p3, AF.Identity, [pk, "pacols", "pbcols"], [ck], bias=cb[:, cidx:cidx + 1],
                                 scale=cw2[:, cidx:cidx + 1])
                        st_.append((cidx, r3, c3, rk, ck, tl_ap, cv, tk_))
                    for (cidx, r3, c3, rk, ck, tl_ap, cv, tk_) in st_:
                        self.STT(c3, r3[:, :, 1:1 + tl], cw1[:, cidx:cidx + 1], c3, ALU.mult, ALU.add, [rk, ck, "pbcols"], [ck])
                    for (cidx, r3, c3, rk, ck, tl_ap, cv, tk_) in st_:
                        self.STT(c3, r3[:, :, 0:tl], cw0[:, cidx:cidx + 1], c3, ALU.mult, ALU.add, [rk, ck, "pacols"], [ck])
                        self.CP("pool", tl_ap, r3[:, :, tl:tl + 2], [rk], [tk_])
                    (_, _, _, _, cak, _, ca, _), (_, _, _, _, cgk, _, cg, _) = st_
                    sg, sgk = sgs.next()
                    self.ACT(sg[:, 0:NT], cg[:, 0:NT], AF.Silu, [cgk], [sgk])
                    self.TT("pool", actT[:, fc, 0:NT], sg[:, 0:NT], ca[:, 0:NT], ALU.mult, [sgk, cak], ["actT"])
                for oc in range(KC):
                    ps, pk = self.psA.next()
                    for kc in range(AC):
                        self.MM(ps[:, 0:NT], wdn[:, kc, oc * 128:(oc + 1) * 128], actT[:, kc, 0:NT], kc == 0, kc == AC - 1,
                                ["wdn", "actT"], [pk])
                    self.TT("dve", x[:, oc, 0:NT], ps[:, 0:NT], x[:, oc, 0:NT], ALU.add, [pk, xk], [xk])
                if self.dbg:
                    self.DMA(self.O["dbg"][ph, :, :, tok0:tok0 + NT].rearrange("c p t -> p c t"), x[:, :, 0:NT], [xk], [])
                if not last:
                    self.DMA(self.scr[wr, :, :, tok0:tok0 + NT].rearrange("c p t -> p c t"), x[:, :, 0:NT], [xk], [])
                    return
                self.rms_feat(x, KC, NT, gf, ynT, D, [xk], "ynT", rmstmp)
                nsub = (NT + 127) // 128
                w = min(128, NT)
                ydst = O["yp"] if nseq == 1 else O["ys"]
                t0_ = tok0 if nseq == 1 else 0
                for s_ in range(nsub):
                    yt, ytk = ytok.next()
                    for half in range(2):
                        ps, pk = self.psA.next()
                        for c4 in range(4):
                            c = half * 4 + c4
                            self.TR(ps[0:w, c4 * 128:(c4 + 1) * 128], ynT[:, c, s_ * 128: s_ * 128 + w], self.ident_f[:],
                                    ["ynT", "ident"], [pk])
                        self.CP("act" if half else "dve", yt[0:w, half * 512:(half + 1) * 512], ps[0:w, 0:512], [pk], [ytk])
                    self.DMA(ydst[t0_ + s_ * 128: t0_ + s_ * 128 + w, :], yt[0:w, :], [ytk], [])

            nst = S // TS_
            nxt = load_x(xT[0], 0, TS_)
            for sti in range(nst):
                x, xk = xT[sti % 2], nxt
                if sti + 1 < nst:
                    nxt = load_x(xT[(sti + 1) % 2], (sti + 1) * TS_, TS_)
                else:
                    nxt = load_x(xT[(sti + 1) % 2], S, NS)
                tile(xk, x, sti * TS_, TS_, 1, TS_, uptail, "uptail")
            for c in range(FC):
                for r_ in range(2):
                    dst = bass.AP(O["convp"].tensor, O["convp"][l, r_].offset + c * 128, [[1, 128], [1, 1]])
                    self.DMA(dst, uptail[:, c, r_:r_ + 1], ["uptail", "uptail%d" % c], [], q="act")
            tile(nxt, xT[nst % 2], S, NS, NSEQ, TD, stail, "stail")
            for c in range(FC):
                dst = O["convs"][l, :, :, c * 128:(c + 1) * 128].rearrange("s r p -> p s r")
                self.DMA(dst, stail[:, c, :, :], ["stail", "stail%d" % c], [], q="act", slow=True)
            S_.barrier()

    def build(self):
        self.declare()
        with contextlib.ExitStack() as st:
            self.S_ = Sched(self.nc, st)
            try:
                self.setup(st)
                self.ckpt("setup")
                ph = 0
                for l in range(2):
                    self.mix_phase(l, ph)
                    self.ckpt("mix%d" % l)
                    ph += 1
                    self.ffn_phase(l, ph, last=(l == 1))
                    self.ckpt("ffn%d" % l)
                    ph += 1
            except _Stop:
                self.S_.barrier()
            self.S_.emit()
        return self.nc


class _View:
    def __init__(self, t, c0, col0):
        self.t, self.c0, self.col0 = t, c0, col0

    def __getitem__(self, key):
        p, c, cols = key
        return self.t[p, self.c0 + c, self.col0 + cols.start: self.col0 + cols.stop]


_NC_CACHE = {}


def _get_nc():
    if "nc" not in _NC_CACHE:
        _NC_CACHE["nc"] = Builder().build()
    return _NC_CACHE["nc"]


_WNAMES = ("norm1_g", "w_in", "ssm_log_dt", "ssm_a_re", "ssm_a_im", "ssm_b_re", "ssm_b_im", "ssm_c_re", "ssm_c_im",
           "ssm_d", "ssm_w_glu", "ssm_b_glu", "pool_w", "pool_scale", "out_norm_att", "out_norm_ssm", "out_norm_pool",
           "w_out", "norm2_g", "w_up", "conv_w", "conv_b", "w_down", "norm_f_g")


def make_in_maps(inp, S=8192):
    f = lambda a: np.ascontiguousarray(np.asarray(a, dtype=np.float32))
    maps = []
    zeros_p = np.zeros((S, D), np.float32)
    for c in range(NCORES):
        m = {}
        m["xp"] = f(inp["x_prompt"][c]) if c < inp["x_prompt"].shape[0] else zeros_p
        sl = slice(c * NSEQ, (c + 1) * NSEQ)
        m["xs"] = f(inp["x_sample"][sl]).reshape(NS, D)
        m["ck"] = f(inp["cache_k"][:, sl]).reshape(2, NSEQ, PAST, 512)
        m["cv"] = f(inp["cache_v"][:, sl]).reshape(2, NSEQ, PAST, 512)
        m["sre"] = f(inp["state_ssm_re"][:, sl])
        m["sim"] = f(inp["state_ssm_im"][:, sl])
        m["spool"] = f(inp["state_pool"][:, sl])
        m["sconv"] = f(inp["state_conv"][:, sl])
        for n in _WNAMES:
            m[n] = f(inp[n])
        maps.append(m)
    return maps


def gather(res, S=8192, nkeep=2048, nb=2):
    r = res
    cat = lambda n: np.concatenate([r[c][n] for c in range(NCORES)], axis=1)
    y_prompt = np.stack([r[c]["yp"] for c in range(nb)], 0)
    y_sample = np.concatenate([r[c]["ys"].reshape(NSEQ, TD, D) for c in range(NCORES)], 0)
    pk = lambda n: np.stack([r[c][n] for c in range(nb)], 1)
    new_k_prompt = pk("kp").reshape(2, nb, nkeep, 8, 64)
    new_v_prompt = pk("vp").reshape(2, nb, nkeep, 8, 64)
    outs = (y_prompt, y_sample, new_k_prompt, new_v_prompt, pk("rep"), pk("imp"), pk("poolp"), pk("convp"),
            cat("ks").reshape(2, NCORES * NSEQ, TD, 8, 64), cat("vs").reshape(2, NCORES * NSEQ, TD, 8, 64),
            cat("res"), cat("ims"), cat("pools"), cat("convs"))
    return tuple(np.ascontiguousarray(o, dtype=np.float32) for o in outs)


def kernel(**inputs):
    nc = _get_nc()
    in_maps = make_in_maps(inputs)
    res = run_bass_kernel_spmd(nc, in_maps, core_ids=list(range(NCORES)))
    return gather(res.results)
```

```python
import contextlib
import math
import numpy as np
import concourse.bass as bass
import concourse.mybir as mybir
from concourse.bass_utils import run_bass_kernel_spmd

F32 = mybir.dt.float32
BF16 = mybir.dt.bfloat16
I32 = mybir.dt.int32
ALU = mybir.AluOpType
AF = mybir.ActivationFunctionType

D = 1024
KC = 8
DIN = 2048
DFF = 2816
FC = 44
AC = 22
NSEQ = 4
TD = 8
NS = NSEQ * TD
PAST = 2048
EPS = 1e-6
NCORES = 8
TWO_PI = 2.0 * math.pi


class Sched:
    ENGS = ("pe", "act", "dve", "pool", "sp")

    def __init__(self, nc, stack, n_dma_sems=32):
        self.nc = nc
        self.ops = {e: [] for e in self.ENGS}
        self.count = {e: 0 for e in self.ENGS}
        self.known = {e: {} for e in self.ENGS}
        self.buf = {}
        self.esem = {e: stack.enter_context(nc.semaphore("s_" + e)) for e in self.ENGS}
        self.KD = n_dma_sems
        self.dsem = [stack.enter_context(nc.semaphore("d%d" % i)) for i in range(self.KD)]
        self.ndma = 0
        self.dlast = [0] * self.KD
        self.dead = False

    def _deps(self, reads, writes):
        deps = []
        for k in reads:
            st = self.buf.get(k)
            if st and st["w"] is not None:
                deps.append(st["w"] + ("raw",))
        for k in writes:
            st = self.buf.get(k)
            if st:
                if st["w"] is not None:
                    deps.append(st["w"])
                deps.extend(st["r"])
        return deps

    def _waits(self, eng, deps):
        need = {}
        for t in deps:
            if t[0] == "E":
                if t[1] == eng and (eng == "pe" or len(t) < 4):
                    continue
                key = ("E", t[1])
            else:
                key = ("D", t[1])
            if t[2] > need.get(key, 0):
                need[key] = t[2]
        out = []
        kn = self.known[eng]
        for key, v in need.items():
            if kn.get(key, 0) >= v:
                continue
            kn[key] = v
            sem = self.esem[key[1]] if key[0] == "E" else self.dsem[key[1]]
            out.append((sem, v))
        return out

    def _commit(self, tok, reads, writes):
        for k in reads:
            st = self.buf.setdefault(k, {"w": None, "r": []})
            st["r"].append(tok)
            if len(st["r"]) > 64:
                st["r"] = self._prune(st["r"])
        for k in writes:
            self.buf[k] = {"w": tok, "r": []}

    @staticmethod
    def _prune(lst):
        best = {}
        for t in lst:
            key = (t[0], t[1])
            if key not in best or t[2] > best[key][2]:
                best[key] = t
        return list(best.values())

    def op(self, eng, fn, reads=(), writes=()):
        if self.dead:
            return None
        psr = [k for k in reads if k.startswith("ps") and k[2:].isdigit()]
        if psr:
            reads = [k for k in reads if k not in psr]
            writes = list(writes) + psr
        waits = self._waits(eng, self._deps(reads, writes))
        self.count[eng] += 1
        tok = ("E", eng, self.count[eng])
        self.ops[eng].append((fn, waits, (self.esem[eng], 1)))
        self._commit(tok, reads, writes)
        return tok

    def dma(self, fn, reads=(), writes=(), q="sp"):
        if self.dead:
            return None
        slot = self.ndma % self.KD
        self.ndma += 1
        deps = self._deps(reads, writes)
        if self.dlast[slot] > 0:
            deps.append(("D", slot, self.dlast[slot]))
        waits = self._waits(q, deps)
        self.dlast[slot] += 16
        tok = ("D", slot, self.dlast[slot])
        self.ops[q].append((fn, waits, (self.dsem[slot], 16)))
        self._commit(tok, reads, writes)
        return tok

    def barrier(self):
        if self.dead:
            return
        dd = [("D", s, v) for s, v in enumerate(self.dlast) if v > 0]
        for e in self.ENGS:
            deps = [("E", e2, self.count[e2]) for e2 in self.ENGS if e2 != e and self.count[e2] > 0] + dd
            waits = self._waits(e, deps)
            self.ops[e].append((None, waits, None))
        self.buf = {}

    def emit(self):
        nc = self.nc
        engmap = {"pe": "tensor", "act": "scalar", "dve": "vector", "pool": "gpsimd", "sp": "sync"}
        with nc.Block() as block:
            for e in self.ENGS:
                lst = self.ops[e]

                def body(engine, lst=lst):
                    for fn, waits, inc in lst:
                        for sem, v in waits:
                            engine.wait_ge(sem, v)
                        if fn is not None:
                            fn(engine).then_inc(inc[0], inc[1])

                getattr(block, engmap[e])(body)


class Ring:
    def __init__(self, items):
        self.items = items
        self.i = 0

    def next(self):
        it = self.items[self.i % len(self.items)]
        self.i += 1
        return it


class _Stop(Exception):
    pass


class Builder:
    stop_at = None

    def ckpt(self, name):
        if self.stop_at is not None and name == self.stop_at and not self.S_.dead:
            self.S_.barrier()
            self.S_.dead = True

    def __init__(self, S=8192, TSM=128, TSF=256, dbg=False):
        assert S % TSM == 0 and S % TSF == 0 and TSM % 128 == 0
        self.S, self.TSM, self.TSF, self.dbg = S, TSM, TSF, dbg
        self.NSUB = TSM // 128
        self.JMIN = -(self.NSUB - 1)
        self.NJ = 17 + self.NSUB - 1
        self.NSLOT = 16 + self.NSUB
        self.NKEEP = min(2048, S)
        self.nc = bass.Bass("TRN2", target_bir_lowering=False)

    def MM(self, out, lhsT, rhs, start, stop, r, w):
        self.S_.op("pe", lambda e: e.matmul(out, lhsT=lhsT, rhs=rhs, start=start, stop=stop,
                                            skip_group_check=True), r, w)

    def TR(self, out, in_, ident, r, w):
        self.S_.op("pe", lambda e: e.transpose(out, in_, ident), r, w)

    def ACT(self, out, in_, func, r, w, bias=None, scale=None, accum=None):
        kw = {}
        if bias is not None:
            kw["bias"] = bias
        if scale is not None:
            kw["scale"] = scale
        if accum is not None:
            kw["accum_out"] = accum
        self.S_.op("act", lambda e: e.activation(out=out, in_=in_, func=func, **kw), r, w)

    def TT(self, eng, out, in0, in1, op, r, w):
        self.S_.op(eng, lambda e: e.tensor_tensor(out=out, in0=in0, in1=in1, op=op), r, w)

    def TS(self, eng, out, in0, s1, s2, op0, op1, r, w):
        if s2 is None:
            self.S_.op(eng, lambda e: e.tensor_scalar(out=out, in0=in0, scalar1=s1, scalar2=None, op0=op0), r, w)
        else:
            self.S_.op(eng, lambda e: e.tensor_scalar(out=out, in0=in0, scalar1=s1, scalar2=s2, op0=op0, op1=op1), r, w)

    def STT(self, out, in0, scalar, in1, op0, op1, r, w):
        self.S_.op("dve", lambda e: e.scalar_tensor_tensor(out=out, in0=in0, scalar=scalar, in1=in1,
                                                          op0=op0, op1=op1), r, w)

    def CP(self, eng, out, in_, r, w):
        if eng == "act":
            self.S_.op("act", lambda e: e.activation(out=out, in_=in_, func=AF.Copy), r, w)
        else:
            self.S_.op(eng, lambda e: e.tensor_copy(out=out, in_=in_), r, w)

    def MS(self, eng, ap, val, w):
        self.S_.op(eng, lambda e: e.memset(ap, val), (), w)

    def DMA(self, out, in_, r, w, q="sp", slow=False):
        if slow:
            self.S_.dma(lambda e: e.dma_start(out=out, in_=in_, allow_slow_non_contiguous=True), r, w, q=q)
        else:
            self.S_.dma(lambda e: e.dma_start(out=out, in_=in_), r, w, q=q)

    def sb(self, stack, name, shape, dt):
        self._uid = getattr(self, "_uid", 0) + 1
        return stack.enter_context(self.nc.sbuf_tensor("%s_%d" % (name, self._uid), shape, dt))

    def declare(self):
        nc, S = self.nc, self.S
        din = lambda n, s: nc.dram_tensor(n, s, F32, kind="ExternalInput").ap()
        dout = lambda n, s: nc.dram_tensor(n, s, F32, kind="ExternalOutput").ap()
        I = {}
        I["xp"] = din("xp", [S, D])
        I["xs"] = din("xs", [NS, D])
        I["ck"] = din("ck", [2, NSEQ, PAST, 512])
        I["cv"] = din("cv", [2, NSEQ, PAST, 512])
        I["sre"] = din("sre", [2, NSEQ, 16, 64])
        I["sim"] = din("sim", [2, NSEQ, 16, 64])
        I["spool"] = din("spool", [2, NSEQ, 15, 256])
        I["sconv"] = din("sconv", [2, NSEQ, 2, 2 * DFF])
        for n, s in (("norm1_g", [2, D]), ("w_in", [2, D, DIN]), ("ssm_log_dt", [2, 16]), ("ssm_a_re", [2, 16, 64]),
                     ("ssm_a_im", [2, 16, 64]), ("ssm_b_re", [2, 16, 64, 16]), ("ssm_b_im", [2, 16, 64, 16]),
                     ("ssm_c_re", [2, 16, 16, 64]), ("ssm_c_im", [2, 16, 16, 64]), ("ssm_d", [2, 256]),
                     ("ssm_w_glu", [2, 256, 256]), ("ssm_b_glu", [2, 256]), ("pool_w", [2, 4, 64, 64]),
                     ("pool_scale", [2, 256]), ("out_norm_att", [2, 512]), ("out_norm_ssm", [2, 256]),
                     ("out_norm_pool", [2, 256]), ("w_out", [2, D, D]), ("norm2_g", [2, D]),
                     ("w_up", [2, D, 2 * DFF]), ("conv_w", [2, 3, 2 * DFF]), ("conv_b", [2, 2 * DFF]),
                     ("w_down", [2, DFF, D]), ("norm_f_g", [D])):
            I[n] = din(n, s)
        O = {}
        O["yp"] = dout("yp", [S, D])
        O["ys"] = dout("ys", [NS, D])
        O["kp"] = dout("kp", [2, self.NKEEP, 512])
        O["vp"] = dout("vp", [2, self.NKEEP, 512])
        O["rep"] = dout("rep", [2, 16, 64])
        O["imp"] = dout("imp", [2, 16, 64])
        O["poolp"] = dout("poolp", [2, 15, 256])
        O["convp"] = dout("convp", [2, 2, 2 * DFF])
        O["ks"] = dout("ks", [2, NSEQ, TD, 512])
        O["vs"] = dout("vs", [2, NSEQ, TD, 512])
        O["res"] = dout("res", [2, NSEQ, 16, 64])
        O["ims"] = dout("ims", [2, NSEQ, 16, 64])
        O["pools"] = dout("pools", [2, NSEQ, 15, 256])
        O["convs"] = dout("convs", [2, NSEQ, 2, 2 * DFF])
        if self.dbg:
            O["dbg"] = dout("dbg", [12, KC, 128, S + NS])
        self.I, self.O = I, O
        self.scr = nc.dram_tensor("scr", [2, KC, 128, S + NS], F32, kind="Internal").ap()

    def sincos(self, st, X, n, tag, want_cos=True):
        ki = self.sb(st, tag + "_ki", [128, n], I32)
        r = self.sb(st, tag + "_r", [128, n], F32)
        sn = self.sb(st, tag + "_sn", [128, n], F32)
        outs = []
        for which, shift, dst in (("s", 0.0, sn),) + ((("c", math.pi / 2, None),) if want_cos else ()):
            if dst is None:
                dst = self.sb(st, tag + "_cs", [128, n], F32)
            src = X
            if shift != 0.0:
                self.TS("dve", r[:], X, shift, None, ALU.add, None, [tag + "X"], [tag + "r"])
                src = r[:]
            self.TS("dve", ki[:], src, 1.0 / TWO_PI, None, ALU.mult, None, [tag + "X", tag + "r"], [tag + "ki"])
            self.STT(r[:], ki[:], -TWO_PI, src, ALU.mult, ALU.add, [tag + "ki", tag + "X", tag + "r"], [tag + "r"])
            self.TS("dve", r[:], r[:], 3.14159, -3.14159, ALU.min, ALU.max, [tag + "r"], [tag + "r"])
            self.ACT(dst[:], r[:], AF.Sin, [tag + "r"], [tag + which])
            outs.append(dst)
        return outs

    def rms_feat(self, src, C, NT, gains, dst, nfeat, keys_r, key_w, tmp):
        sq, sd, rstd = tmp
        self.ACT(sq[:, 0:C, 0:NT], src[:, 0:C, 0:NT], AF.Square, keys_r, ["sq"])
        ps, pk = self.psC.next()
        for c in range(C):
            self.MM(ps[:, 0:NT], self.ones_b[:], sq[:, c, 0:NT], c == 0, c == C - 1, ["sq", "ones"], [pk])
        self.ACT(sd[:, 0:NT], ps[:, 0:NT], AF.Sqrt, [pk], ["sd"], bias=self.eps_col[:, 0:1], scale=1.0 / nfeat)
        self.S_.op("dve", lambda e: e.reciprocal(out=rstd[:, 0:NT], in_=sd[:, 0:NT]), ["sd"], ["rstd"])
        for c in range(C):
            self.STT(dst[:, c, 0:NT], src[:, c, 0:NT], gains[:, c:c + 1], rstd[:, 0:NT], ALU.mult, ALU.mult,
                     keys_r + ["rstd", "gains"], [key_w])

    def load_rows_T(self, st, rows, tag):
        stage = self.sb(st, tag + "_stg", [128, 128], F32)
        cols = self.sb(st, tag + "_cols", [128, 128], F32)
        self.MS("pool", stage[:], 0.0, [tag + "stg"])
        for ap, base in rows:
            n = ap.shape[0]
            self.DMA(stage[base:base + n, :], ap, [], [tag + "stg"])
        ps, pk = self.psC.next()
        self.TR(ps[:, 0:128], stage[:], self.ident_f[:], [tag + "stg", "ident"], [pk])
        self.CP("dve", cols[:], ps[:, 0:128], [pk], [tag + "cols"])
        return cols

    def set_rings(self, mix):
        rg = lambda ids: Ring([(self.ps[i], "ps%d" % i) for i in ids])
        if mix:
            self.psA, self.psS, self.psB, self.psC = rg((0, 1)), rg((2, 3, 4, 5)), rg((6,)), rg((7,))
        else:
            self.psA, self.psC = rg((0, 1, 2, 3, 4, 5)), rg((6, 7))

    def setup(self, st):
        nc = self.nc
        self.ps = [st.enter_context(nc.psum_tensor("ps%d" % i, [128, 512], F32)) for i in range(8)]
        self.set_rings(True)
        self.ident_f = self.sb(st, "ident_f", [128, 128], F32)
        self.ones_b = self.sb(st, "ones_b", [128, 128], BF16)
        self.eps_col = self.sb(st, "eps_col", [128, 1], F32)
        with contextlib.ExitStack() as t:
            onesf = self.sb(t, "onesf", [128, 128], F32)
            self.MS("pool", onesf[:], 1.0, ["onesf"])
            self.MS("pool", self.ones_b[:], 1.0, ["ones"])
            self.MS("pool", self.eps_col[:], EPS, ["eps"])
            self.S_.op("pool", lambda e: e.affine_select(out=self.ident_f[:], in_=onesf[:], pattern=[[-1, 128]],
                                                         compare_op=ALU.is_equal, fill=0.0, base=0,
                                                         channel_multiplier=1), ["onesf"], ["ident"])
            self.S_.barrier()

    def mix_phase(self, l, ph):
        nc, S_, I, O = self.nc, self.S_, self.I, self.O
        S, TS_, NSUB, NJ, JMIN, NSLOT = self.S, self.TSM, self.NSUB, self.NJ, self.JMIN, self.NSLOT
        TW = TS_ + 1
        rd, wr = (ph - 1) % 2, ph % 2
        self.set_rings(True)
        with contextlib.ExitStack() as st:
            sb = lambda n, s, d: self.sb(st, n, s, d)
            win = sb("win", [128, KC, DIN], BF16)
            wout = sb("wout", [128, KC, D], BF16)
            wglu = sb("wglu", [128, 2, 256], BF16)
            poolw = sb("poolw", [128, 2, 128], BF16)
            Bre = sb("Bre", [128, 8, 128], BF16)
            Bim = sb("Bim", [128, 8, 128], BF16)
            Cre = sb("Cre", [128, 8, 128], BF16)
            Cim = sb("Cim", [128, 8, 128], BF16)
            Tre = sb("Tre", [128, 8, TW], F32)
            Tim = sb("Tim", [128, 8, TW], F32)
            cs = sb("cs", [128, 8, TW], F32)
            sn = sb("sn", [128, 8, TW], F32)
            rho = sb("rho", [128, 8], F32)
            mask = sb("mask", [128, NJ, TS_], BF16)
            icnt_tab = sb("icnt_tab", [128, 2, 16], F32)
            icnt = sb("icnt", [128, 2], F32)
            for c in range(KC):
                self.DMA(win[:, c, :], I["w_in"][l, c * 128:(c + 1) * 128, :], [], ["win"], q="pool")
            for c in range(KC):
                self.DMA(wout[:, c, :], I["w_out"][l, c * 128:(c + 1) * 128, :], [], ["wout"], q="pool")
            for c in range(2):
                self.DMA(wglu[:, c, :], I["ssm_w_glu"][l, c * 128:(c + 1) * 128, :], [], ["wglu"], q="pool")
            self.MS("pool", poolw[:], 0.0, ["poolw"])
            for gi in range(4):
                h0 = (gi % 2) * 64
                self.DMA(poolw[h0:h0 + 64, gi // 2, h0:h0 + 64], I["pool_w"][l, gi], [], ["poolw"], q="pool")
            rows = [(I["norm1_g"][l].rearrange("(c p) -> c p", p=128), 0),
                    (I["out_norm_att"][l].rearrange("(c p) -> c p", p=128), 8),
                    (I["out_norm_ssm"][l].rearrange("(c p) -> c p", p=128), 12),
                    (I["out_norm_pool"][l].rearrange("(c p) -> c p", p=128), 14),
                    (I["pool_scale"][l].rearrange("(c p) -> c p", p=128), 16),
                    (I["ssm_d"][l].rearrange("(c p) -> c p", p=128), 18),
                    (I["ssm_b_glu"][l].rearrange("(c p) -> c p", p=128), 20)]
            pc = self.load_rows_T(st, rows, "pc")
            self.ckpt("m_pc")
            g1, gatt, gssm, gpool = pc[:, 0:8], pc[:, 8:12], pc[:, 12:14], pc[:, 14:16]
            pscale, dskip, bglu = pc[:, 16:18], pc[:, 18:20], pc[:, 20:22]
            with contextlib.ExitStack() as t:
                tb = lambda n, s, d: self.sb(t, n, s, d)
                arr = lambda nm: I[nm][l].rearrange("(gp gl) p -> gp (gl p)", gl=2)
                sc = self.load_rows_T(t, [(arr("ssm_a_re"), 0), (arr("ssm_a_im"), 32)], "sc")
                are, aim = sc[:, 0:8], sc[:, 32:40]
                dtr = tb("dtr", [128, 2], F32)
                dtrow = tb("dtrow", [128, 128], F32)
                self.MS("pool", dtr[:], 0.0, ["dtr"])
                self.MS("pool", dtrow[:], 0.0, ["dtrow"])
                self.DMA(dtr[0:8, :], I["ssm_log_dt"][l].rearrange("(gp gl) -> gp gl", gl=2), [], ["dtr"])
                self.ACT(dtr[0:8, :], dtr[0:8, :], AF.Exp, ["dtr"], ["dtr"])
                self.CP("dve", dtrow[0:8, :].rearrange("p (a b) -> p a b", b=64),
                        dtr[0:8, :].unsqueeze(2).to_broadcast([8, 2, 64]), ["dtr"], ["dtrow"])
                psd, pkd = self.psC.next()
                self.TR(psd[:, 0:128], dtrow[:], self.ident_f[:], ["dtrow", "ident"], [pkd])
                dtc = tb("dtc", [128, 8], F32)
                self.CP("dve", dtc[:], psd[:, 0:8], [pkd], ["dtc"])
                th = tb("th", [128, 8], F32)
                lre = tb("lre", [128, 8], F32)
                self.TT("dve", lre[:], are, dtc[:], ALU.mult, ["sccols", "dtc"], ["lre"])
                self.TT("dve", th[:], aim, dtc[:], ALU.mult, ["sccols", "dtc"], ["thX"])
                self.ACT(rho[:], lre[:], AF.Exp, ["lre"], ["rho"])
                s1, c1 = self.sincos(t, th[:], 8, "th")
                abre = tb("abre", [128, 8], F32)
                abim = tb("abim", [128, 8], F32)
                self.TT("dve", abre[:], rho[:], c1[:], ALU.mult, ["rho", "thc"], ["abre"])
                self.TT("dve", abim[:], rho[:], s1[:], ALU.mult, ["rho", "ths"], ["abim"])
                self.TS("dve", abre[:], abre[:], -1.0, None, ALU.add, None, ["abre"], ["abre"])
                den = tb("den", [128, 8], F32)
                t0 = tb("t0", [128, 8], F32)
                core_ = tb("core", [128, 8], F32)
                coim = tb("coim", [128, 8], F32)
                self.TT("dve", den[:], are, are, ALU.mult, ["sccols"], ["den"])
                self.TT("dve", t0[:], aim, aim, ALU.mult, ["sccols"], ["t0"])
                self.TT("dve", den[:], den[:], t0[:], ALU.add, ["den", "t0"], ["den"])
                self.S_.op("dve", lambda e: e.reciprocal(out=den[:], in_=den[:]), ["den"], ["den"])
                self.TT("dve", core_[:], abre[:], are, ALU.mult, ["abre", "sccols"], ["core"])
                self.TT("dve", t0[:], abim[:], aim, ALU.mult, ["abim", "sccols"], ["t0"])
                self.TT("dve", core_[:], core_[:], t0[:], ALU.add, ["core", "t0"], ["core"])
                self.TT("dve", core_[:], core_[:], den[:], ALU.mult, ["core", "den"], ["core"])
                self.TT("dve", coim[:], abim[:], are, ALU.mult, ["abim", "sccols"], ["coim"])
                self.TT("dve", t0[:], abre[:], aim, ALU.mult, ["abre", "sccols"], ["t0"])
                self.TT("dve", coim[:], coim[:], t0[:], ALU.subtract, ["coim", "t0"], ["coim"])
                self.TT("dve", coim[:], coim[:], den[:], ALU.mult, ["coim", "den"], ["coim"])
                self.ckpt("m_co")
                iot = tb("iot", [128, TW], F32)
                self.S_.op("pool", lambda e: e.iota(iot[:], pattern=[[1, TW]], base=0, channel_multiplier=0,
                                                    allow_small_or_imprecise_dtypes=True), [], ["iot"])
                ang = tb("ang", [128, 8, TW], F32)
                self.TT("dve", ang[:], th[:].unsqueeze(2).to_broadcast([128, 8, TW]),
                        iot[:].unsqueeze(1).to_broadcast([128, 8, TW]), ALU.mult, ["thX", "iot"], ["angX"])
                sA, cA = self.sincos(t, ang[:].rearrange("p a b -> p (a b)"), 8 * TW, "ang")
                self.CP("pool", sn[:].rearrange("p a b -> p (a b)"), sA[:], ["angs"], ["sn"])
                self.CP("pool", cs[:].rearrange("p a b -> p (a b)"), cA[:], ["angc"], ["cs"])
                tmpT = tb("tmpT", [128, 8, TW], F32)
                cob = lambda x: x[:].unsqueeze(2).to_broadcast([128, 8, TW])
                self.TT("dve", Tre[:], cs[:], cob(core_), ALU.mult, ["cs", "core"], ["Tre"])
                self.TT("dve", tmpT[:], sn[:], cob(coim), ALU.mult, ["sn", "coim"], ["tmpT"])
                self.TT("dve", Tre[:], Tre[:], tmpT[:], ALU.add, ["Tre", "tmpT"], ["Tre"])
                self.TT("dve", Tim[:], cs[:], cob(coim), ALU.mult, ["cs", "coim"], ["Tim"])
                self.TT("dve", tmpT[:], sn[:], cob(core_), ALU.mult, ["sn", "core"], ["tmpT"])
                self.TT("dve", Tim[:], Tim[:], tmpT[:], ALU.subtract, ["Tim", "tmpT"], ["Tim"])
                self.ckpt("m_tab")
                X4 = tb("X4", [128, 8, 128], F32)
                for nm, dst, isB in (("ssm_b_re", Bre, True), ("ssm_b_im", Bim, True),
                                     ("ssm_c_re", Cre, False), ("ssm_c_im", Cim, False)):
                    self.MS("pool", X4[:], 0.0, ["X4"])
                    for g in range(16):
                        gp, gl = g // 2, g % 2
                        r0 = (gp % 4) * 32
                        if isB:
                            self.DMA(X4[gl * 64:(gl + 1) * 64, gp, r0 + gl * 16:r0 + gl * 16 + 16], I[nm][l, g],
                                     [], ["X4"])
                        else:
                            self.DMA(X4[r0 + gl * 16:r0 + gl * 16 + 16, gp, gl * 64:(gl + 1) * 64], I[nm][l, g],
                                     [], ["X4"])
                    for gp in range(8):
                        psx, pkx = self.psA.next()
                        self.TR(psx[:, 0:128], X4[:, gp, :], self.ident_f[:], ["X4", "ident"], [pkx])
                        self.CP("act", dst[:, gp, :], psx[:, 0:128], [pkx], [nm])
                S_.barrier()
                self.ckpt("m_bc")
            with contextlib.ExitStack() as t:
                tb = lambda n, s, d: self.sb(t, n, s, d)
                NM = NJ * TS_
                v = tb("mv", [128, NM], F32)
                m0 = tb("m0", [128, NM], F32)
                cc = tb("mc", [128, NM], F32)
                qi = tb("mq", [128, NM], I32)
                rr = tb("mr", [128, NM], F32)
                self.S_.op("pool", lambda e: e.iota(v[:], pattern=[[128, NJ], [1, TS_]], base=128 * JMIN,
                                                    channel_multiplier=-1, allow_small_or_imprecise_dtypes=True),
                           [], ["mv"])
                self.TS("dve", m0[:], v[:], 0.0, None, ALU.is_ge, None, ["mv"], ["m0"])
                self.TS("dve", cc[:], v[:], 128.0, None, ALU.is_le, None, ["mv"], ["mc"])
                self.TT("dve", cc[:], cc[:], m0[:], ALU.mult, ["mc", "m0"], ["mc"])
                for dil, lim in ((4, 512.0), (16, 2048.0)):
                    self.TS("dve", qi[:], v[:], 1.0 / dil, None, ALU.mult, None, ["mv"], ["mq"])
                    self.STT(rr[:], qi[:], -float(dil), v[:], ALU.mult, ALU.add, ["mq", "mv"], ["mr"])
                    self.TS("dve", rr[:], rr[:], 0.0, None, ALU.is_equal, None, ["mr"], ["mr"])
                    self.TT("dve", rr[:], rr[:], m0[:], ALU.mult, ["mr", "m0"], ["mr"])
                    self.TS("dve", qi[:], v[:], lim, None, ALU.is_le, None, ["mv"], ["mq"])
                    self.TT("dve", rr[:], rr[:], qi[:], ALU.mult, ["mr", "mq"], ["mr"])
                    self.TT("dve", cc[:], cc[:], rr[:], ALU.add, ["mc", "mr"], ["mc"])
                self.CP("dve", mask[:].rearrange("p a b -> p (a b)"), cc[:], ["mc"], ["mask"])
                self.ckpt("m_mask")
                wcol = tb("wcol", [128, 2], F32)
                for ch, (wa, wb) in enumerate(((2.0, 4.0), (8.0, 16.0))):
                    self.MS("pool", wcol[0:64, ch:ch + 1], wa, ["wcol"])
                    self.MS("pool", wcol[64:128, ch:ch + 1], wb, ["wcol"])
                self.S_.op("dve", lambda e: e.reciprocal(out=icnt[:], in_=wcol[:]), ["wcol"], ["icnt"])
                i16 = tb("i16", [128, 16], F32)
                self.S_.op("pool", lambda e: e.iota(i16[:], pattern=[[1, 16]], base=1, channel_multiplier=0,
                                                    allow_small_or_imprecise_dtypes=True), [], ["i16"])
                self.TT("dve", icnt_tab[:], i16[:].unsqueeze(1).to_broadcast([128, 2, 16]),
                        wcol[:].unsqueeze(2).to_broadcast([128, 2, 16]), ALU.min, ["i16", "wcol"], ["icnt_tab"])
                self.S_.op("dve", lambda e: e.reciprocal(out=icnt_tab[:], in_=icnt_tab[:]), ["icnt_tab"], ["icnt_tab"])
                S_.barrier()
            xT = [sb("xT%d" % i, [128, KC, TS_], F32) for i in range(2)]
            xtok = sb("xtok", [128, NSUB, D], F32)
            sq = sb("sq", [128, KC, TS_], BF16)
            sd = sb("sd", [128, TS_], F32)
            rstd = sb("rstd", [128, TS_], F32)
            hnT = sb("hnT", [128, KC, TS_], BF16)
            qT = sb("qT", [128, 4, TS_], BF16)
            KT = sb("KT", [128, 4, NSLOT * 128], BF16)
            Vr = sb("Vr", [128, NSLOT, 8, 65], BF16)
            usb = sb("usb", [128, 2, TS_], BF16)
            usf = sb("usf", [128, 2, TS_], F32)
            upl = sb("upl", [128, 2, 16 + TS_], F32)
            Pr = Ring([(sb("P%d" % i, [128, 2 * TS_], BF16), "P%d" % i) for i in range(6)])
            rec = sb("rec", [128, NSUB], F32)
            accs = sb("accs", [128, 8, NSUB * 65], F32)
            rec8 = sb("rec8", [128, 8, NSUB], F32)
            att = sb("att", [128, NSUB, 512], F32)
            ssa = sb("ssa", [128, NSUB], F32)
            junk = sb("junk", [128, 512], BF16)
            mixT = sb("mixT", [128, KC, TS_], BF16)
            Gre = sb("Gre", [128, 8, TS_], F32)
            Gim = sb("Gim", [128, 8, TS_], F32)
            ta = sb("ta", [128, 8, TS_], F32)
            tb2 = sb("tb2", [128, 8, TS_], F32)
            hre = sb("hre", [128, 8, TS_], BF16)
            himn = sb("himn", [128, 8, TS_], BF16)
            wtmp = Ring([(sb("wt%d" % i, [128, TS_], F32), "wt%d" % i) for i in range(6)])
            gst = sb("gst", [128, 2, 8], F32)
            gl_t = sb("gl_t", [128, 4, 8], F32)
            hl = sb("hl", [128, 2, 8], F32)
            yss = sb("yss", [128, 2, TS_], F32)
            gact = sb("gact", [128, 2, TS_], F32)
            gactb = sb("gactb", [128, 2, TS_], BF16)
            sgl = sb("sgl", [128, 2, TS_], F32)
            pl1 = sb("pl1", [128, 2, 16 + TS_], F32)
            pl2 = sb("pl2", [128, 2, 16 + TS_], F32)
            wsel = sb("wsel", [128, 2, TS_], F32)
            pooled = sb("pooled", [128, 2, TS_], BF16)
            pout = sb("pout", [128, 2, TS_], F32)
            t16 = sb("t16", [128, 2, 16], F32)
            kvst = Ring([(sb("kvst%d" % i, [128, 512], F32), "kvst%d" % i) for i in range(2)])
            kts = sb("kts", [128, 4, NS], BF16)
            h0c = sb("h0c", [128, 2, 8], F32)
            ups = sb("ups", [128, 2, NS], F32)
            kcs = Ring([(sb("kcs%d" % i, [128, 512], F32), "kcs%d" % i) for i in range(2)])
            rmstmp = (sq, sd, rstd)
            self.ckpt("m_icnt")
            if l == 0:
                print("SBUF remaining after mix alloc:", self.nc.sbuf_bytes_remaining)
            self.MS("pool", Vr[:], 1.0, ["vall"])
            S_.barrier()
            self.ckpt("m_work")

            def load_xT(dst, tok0, NT, from_tok, src_tok=None):
                key = "xT%d" % xT.index(dst)
                if from_tok:
                    nsub = (NT + 127) // 128
                    w = min(128, NT)
                    for sub in range(nsub):
                        self.DMA(xtok[0:w, sub, :], src_tok[tok0 + sub * 128: tok0 + sub * 128 + w, :], [], ["xtok"])
                    for c in range(KC):
                        ps, pk = self.psA.next()
                        for sub in range(nsub):
                            self.TR(ps[:, sub * 128: sub * 128 + w], xtok[0:w, sub, c * 128:(c + 1) * 128],
                                    self.ident_f[0:w, 0:w], ["xtok", "ident"], [pk])
                        self.CP("act" if c % 2 else "dve", dst[:, c, 0:NT], ps[:, 0:NT], [pk], [key])
                else:
                    self.DMA(dst[:, :, 0:NT], self.scr[rd, :, :, tok0:tok0 + NT].rearrange("c p t -> p c t"),
                             [], [key])
                return key

            def in_proj(xk, x, NT):
                self.rms_feat(x, KC, NT, g1, hnT, D, [xk], "hnT", rmstmp)
                self.ckpt("t_rms")
                outs = {}
                for oc in (0, 1, 2, 3, 4, 5, 6, 7, 12, 13, 14, 15):
                    ps, pk = self.psA.next()
                    for kc in range(KC):
                        self.MM(ps[:, 0:NT], win[:, kc, oc * 128:(oc + 1) * 128], hnT[:, kc, 0:NT], kc == 0,
                                kc == KC - 1, ["win", "hnT"], [pk])
                    outs[oc] = (ps, pk)
                    yield oc, ps, pk

            def attention(qcol0, NQ, ktiles, LA=4):
                nsub = (NQ + 127) // 128
                qw = min(128, NQ)
                pending = []

                def flush_one():
                    item = pending.pop(0)
                    if item[0] == "pv":
                        _, acc, acck, P, Pk, nk, slot, h, subs, first, last, pc0 = item
                        for i_, s_ in enumerate(subs):
                            self.MM(acc[0:qw, s_ * 65:(s_ + 1) * 65], P[0:nk, pc0 + s_ * 128: pc0 + s_ * 128 + qw],
                                    Vr[0:nk, slot, h, :], first and i_ == 0, last and i_ == len(subs) - 1,
                                    [Pk, "v%d" % slot, "vall"], [acck])
                    else:
                        _, acc, acck, h = item
                        self.CP("act", accs[0:qw, h, 0:nsub * 65], acc[0:qw, 0:nsub * 65], [acck], ["accs"])

                mm_i = [0]
                gq = []
                for h in range(8):
                    hp, hc = (h % 2) * 64, h // 2
                    acc, acck = self.psB.next()
                    plan = []
                    for (slot, nk, jj) in ktiles:
                        subs = [s_ for s_ in range(nsub) if 0 <= jj + s_ <= 16]
                        if subs:
                            plan.append((slot, nk, jj, subs))
                    groups = []
                    i_ = 0
                    while i_ < len(plan):
                        if (nsub == 1 and i_ + 1 < len(plan) and plan[i_][1] == 128 and plan[i_ + 1][1] == 128
                                and plan[i_ + 1][2] == plan[i_][2] - 1):
                            groups.append([plan[i_ + 1], plan[i_]])
                            i_ += 2
                        else:
                            groups.append([plan[i_]])
                            i_ += 1
                    npv = len(plan)
                    ipv = 0
                    for grp in groups:
                        ng = len(grp)
                        nk = grp[0][1]
                        scp, sck = self.psS.next()
                        P, Pk = Pr.next()
                        for gi_, (slot, nk_, jj, subs) in enumerate(grp):
                            self.MM(scp[0:nk, gi_ * NQ:(gi_ + 1) * NQ], KT[hp:hp + 64, hc, slot * 128: slot * 128 + nk],
                                    qT[hp:hp + 64, hc, qcol0:qcol0 + NQ], gi_ == 0, gi_ == ng - 1,
                                    ["kt%d" % slot, "qT"], [sck])
                        self.ACT(P[0:nk, 0:ng * NQ], scp[0:nk, 0:ng * NQ], AF.Exp, [sck], [Pk], scale=0.125)
                        m0_ = grp[0][2] - JMIN
                        mm_i[0] += 1
                        self.TT("pool",
                                P[0:nk, 0:ng * NQ].rearrange("p (a q) -> p a q", a=ng),
                                P[0:nk, 0:ng * NQ].rearrange("p (a q) -> p a q", a=ng),
                                mask[0:nk, m0_:m0_ + ng, 0:NQ], ALU.mult, [Pk, "mask"], [Pk])
                        for gi_, (slot, nk_, jj, subs) in enumerate(grp):
                            pending.append(("pv", acc, acck, P, Pk, nk, slot, h, subs, ipv == 0, ipv == npv - 1, gi_ * NQ))
                            ipv += 1
                        if ipv == npv:
                            pending.append(("fin", acc, acck, h))
                        gq.append(ng)
                        while len(gq) > LA:
                            for _ in range(gq.pop(0)):
                                while pending[0][0] != "pv":
                                    flush_one()
                                flush_one()
                        yield
                while pending:
                    flush_one()
                a4 = accs[0:qw, :, 0:nsub * 65].rearrange("p h (s e) -> p h s e", e=65)
                for s_ in range(nsub):
                    self.S_.op("dve", lambda e, s_=s_: e.reciprocal(out=rec8[0:qw, :, s_:s_ + 1], in_=a4[:, :, s_, 64:65]),
                               ["accs"], ["rec8"])
                    self.TT("dve", att[0:qw, s_, :].rearrange("p (h e) -> p h e", e=64), a4[:, :, s_, 0:64],
                            rec8[0:qw, :, s_:s_ + 1].to_broadcast([qw, 8, 64]), ALU.mult, ["accs", "rec8"], ["att"])
                yield

            def interleave(gens, weights):
                alive = list(gens)
                wts = list(weights)
                while alive:
                    for i in range(len(alive) - 1, -1, -1):
                        pass
                    nxt_alive, nxt_w = [], []
                    for g_, w_ in zip(alive, wts):
                        ok = True
                        for _ in range(w_):
                            try:
                                next(g_)
                            except StopIteration:
                                ok = False
                                break
                        if ok:
                            nxt_alive.append(g_)
                            nxt_w.append(w_)
                    alive, wts = nxt_alive, nxt_w

            def sched_tile(att_g, n_att, others):
                late_start = int(0.5 * n_att)
                held = [False] * len(others)
                alive = [True] * len(others)
                i = 0

                def adv(j):
                    if not alive[j]:
                        return
                    try:
                        tag = next(others[j])
                        if tag == "late":
                            held[j] = True
                    except StopIteration:
                        alive[j] = False

                for _ in att_g:
                    i += 1
                    for j in range(len(others)):
                        if i % (3 + 2 * j) == 0 and alive[j] and not (held[j] and i < late_start):
                            adv(j)
                for j in range(len(others)):
                    while alive[j]:
                        adv(j)

            def att_finish(col0, NQ):
                nsub = (NQ + 127) // 128
                qw = min(128, NQ)
                for s_ in range(nsub):
                    self.ACT(junk[0:qw, :], att[0:qw, s_, :], AF.Square, ["att"], ["junk", "ssa"], accum=ssa[0:qw, s_:s_ + 1])
                self.ACT(ssa[0:qw, 0:nsub], ssa[0:qw, 0:nsub], AF.Sqrt, ["junk", "ssa"], ["ssa"], bias=self.eps_col[0:qw, 0:1],
                         scale=1.0 / 512)
                self.S_.op("dve", lambda e: e.reciprocal(out=ssa[0:qw, 0:nsub], in_=ssa[0:qw, 0:nsub]), ["ssa"], ["ssa"])
                self.TT("dve", att[0:qw, 0:nsub, :], att[0:qw, 0:nsub, :],
                        ssa[0:qw, 0:nsub].unsqueeze(2).to_broadcast([qw, nsub, 512]), ALU.mult, ["att", "ssa"], ["att"])
                for c in range(4):
                    ps, pk = self.psA.next()
                    for s_ in range(nsub):
                        self.TR(ps[:, s_ * 128: s_ * 128 + qw], att[0:qw, s_, c * 128:(c + 1) * 128],
                                self.ident_f[0:qw, 0:qw], ["att", "ident"], [pk])
                    self.TS("dve", mixT[:, c, col0:col0 + NQ], ps[:, 0:NQ], gatt[:, c:c + 1], None, ALU.mult, None,
                            [pk, "pccols"], ["mixT"])

            def ssm(col0, NT, first_zero):
                for gp in range(8):
                    ch = gp // 4
                    pre, prk = self.psA.next()
                    self.MM(pre[:, 0:NT], Bre[:, gp, :], usb[:, ch, col0:col0 + NT], True, True, ["usb", "ssm_b_re"], [prk])
                    pim, pik = self.psA.next()
                    self.MM(pim[:, 0:NT], Bim[:, gp, :], usb[:, ch, col0:col0 + NT], True, True, ["usb", "ssm_b_im"], [pik])
                    wre, wrk = wtmp.next()
                    wim, wik = wtmp.next()
                    t1, t1k = wtmp.next()
                    self.TT("dve", wre[:, 0:NT], pre[:, 0:NT], Tre[:, gp, 0:NT], ALU.mult, [prk, "Tre"], [wrk])
                    self.TT("dve", t1[:, 0:NT], pim[:, 0:NT], Tim[:, gp, 0:NT], ALU.mult, [pik, "Tim"], [t1k])
                    self.TT("dve", wre[:, 0:NT], wre[:, 0:NT], t1[:, 0:NT], ALU.subtract, [wrk, t1k], [wrk])
                    t2, t2k = wtmp.next()
                    self.TT("dve", wim[:, 0:NT], pim[:, 0:NT], Tre[:, gp, 0:NT], ALU.mult, [pik, "Tre"], [wik])
                    self.TT("dve", t2[:, 0:NT], pre[:, 0:NT], Tim[:, gp, 0:NT], ALU.mult, [prk, "Tim"], [t2k])
                    self.TT("dve", wim[:, 0:NT], wim[:, 0:NT], t2[:, 0:NT], ALU.add, [wik, t2k], [wik])
                    for pl, (G, wv, wk) in enumerate(((Gre, wre, wrk), (Gim, wim, wik))):
                        self.S_.op("dve", lambda e, G=G, wv=wv, pl=pl, gp=gp: e.tensor_tensor_scan(
                            out=G[:, gp, 0:NT], data0=rho[:, gp:gp + 1].to_broadcast([128, NT]), data1=wv[:, 0:NT],
                            initial=gst[:, pl, gp:gp + 1], op0=ALU.mult, op1=ALU.add),
                            [wk, "rho", "gst"], ["G%d" % pl])
                    yield
                csn, snn = cs[:, :, 0:NT], sn[:, :, 0:NT]
                self.TT("dve", ta[:, :, 0:NT], Gre[:, :, 0:NT], csn, ALU.mult, ["G0", "cs"], ["ta"])
                self.TT("dve", tb2[:, :, 0:NT], Gim[:, :, 0:NT], snn, ALU.mult, ["G1", "sn"], ["tb2"])
                self.TT("dve", hre[:, :, 0:NT], ta[:, :, 0:NT], tb2[:, :, 0:NT], ALU.subtract, ["ta", "tb2"], ["hre"])
                yield
                self.TT("dve", ta[:, :, 0:NT], Gim[:, :, 0:NT], csn, ALU.mult, ["G1", "cs"], ["ta"])
                self.TT("dve", tb2[:, :, 0:NT], Gre[:, :, 0:NT], snn, ALU.mult, ["G0", "sn"], ["tb2"])
                self.STT(himn[:, :, 0:NT], ta[:, :, 0:NT], -1.0, tb2[:, :, 0:NT], ALU.mult, ALU.subtract,
                         ["ta", "tb2"], ["himn"])
                yield
                glr, gli = Gre[:, :, NT - 1], Gim[:, :, NT - 1]
                for dst, col in ((gst, NT), (hl, NT - 1)):
                    cN, sN = cs[:, :, col], sn[:, :, col]
                    self.TT("dve", gl_t[:, 0, :], glr, cN, ALU.mult, ["G0", "cs"], ["glt0"])
                    self.TT("dve", gl_t[:, 1, :], gli, sN, ALU.mult, ["G1", "sn"], ["glt1"])
                    self.TT("dve", gl_t[:, 2, :], gli, cN, ALU.mult, ["G1", "cs"], ["glt2"])
                    self.TT("dve", gl_t[:, 3, :], glr, sN, ALU.mult, ["G0", "sn"], ["glt3"])
                    dk = "gst" if dst is gst else "hl"
                    self.TT("dve", dst[:, 0, :], gl_t[:, 0, :], gl_t[:, 1, :], ALU.subtract, ["glt0", "glt1"], [dk])
                    self.TT("dve", dst[:, 1, :], gl_t[:, 2, :], gl_t[:, 3, :], ALU.add, ["glt2", "glt3"], [dk])
                if self.dbg and ph == 0 and col0 == 0 and not getattr(self, "_dumped", False):
                    self._dumped = True
                    dd = self.O["dbg"]
                    for i, (tt, kk) in enumerate(((Tre, "Tre"), (Tim, "Tim"), (cs, "cs"), (sn, "sn"))):
                        self.DMA(dd[5 + i, :, :, 0:TW].rearrange("c p t -> p c t"), tt[:, :, :], [kk], [])
                    self.DMA(dd[9, :, :, 0:NT].rearrange("c p t -> p c t"), Gre[:, :, 0:NT], ["G0"], [])
                    self.DMA(dd[10, :, :, 0:NT].rearrange("c p t -> p c t"), Gim[:, :, 0:NT], ["G1"], [])
                    self.DMA(dd[11, 0, :, 0:8], rho[:, :], ["rho"], [])
                    self.DMA(dd[11, 1, :, 0:NT], usf[:, 0, 0:NT], ["usf"], [])
                    self.DMA(dd[11, 2, :, 0:NT], usf[:, 1, 0:NT], ["usf"], [])
                yield "late"
                for ch in range(2):
                    yp_, ypk = self.psC.next()
                    for i, gp in enumerate(range(ch * 4, ch * 4 + 4)):
                        self.MM(yp_[:, 0:NT], Cre[:, gp, :], hre[:, gp, 0:NT], i == 0, False, ["hre", "ssm_c_re"], [ypk])
                        self.MM(yp_[:, 0:NT], Cim[:, gp, :], himn[:, gp, 0:NT], False, i == 3, ["himn", "ssm_c_im"], [ypk])
                    self.STT(yss[:, ch, 0:NT], usf[:, ch, col0:col0 + NT], dskip[:, ch:ch + 1], yp_[:, 0:NT], ALU.mult,
                             ALU.add, [ypk, "usf", "pccols"], ["yss"])
                    yield
                yv, gv, sv = yss[:, :, 0:NT], gact[:, :, 0:NT], sgl[:, :, 0:NT]
                self.TT("dve", gv, yv, yv, ALU.mult, ["yss"], ["gact"])
                self.TS("dve", gv, gv, 0.044715, 1.0, ALU.mult, ALU.add, ["gact"], ["gact"])
                self.TT("dve", gv, gv, yv, ALU.mult, ["gact", "yss"], ["gact"])
                self.ACT(sv, gv, AF.Sigmoid, ["gact"], ["sgl"], scale=1.5957691216057308)
                self.TT("dve", gv, yv, sv, ALU.mult, ["yss", "sgl"], ["gact"])
                self.CP("dve", gactb[:, :, 0:NT], gv, ["gact"], ["gactb"])
                yield
                for oc in range(2):
                    zp, zk = self.psC.next()
                    for kc in range(2):
                        self.MM(zp[:, 0:NT], wglu[:, kc, oc * 128:(oc + 1) * 128], gactb[:, kc, 0:NT], kc == 0, kc == 1,
                                ["wglu", "gactb"], [zk])
                    self.ACT(sgl[:, oc, 0:NT], zp[:, 0:NT], AF.Sigmoid, [zk], ["sgl"], bias=bglu[:, oc:oc + 1])
                self.TT("dve", yss[:, :, 0:NT], gv, sv, ALU.mult, ["gact", "sgl"], ["yss"])
                yield
                self.rms_feat(yss, 2, NT, gssm, _View(mixT, 4, col0), 256, ["yss"], "mixT", rmstmp)
                yield

            def pool_mix(col0, NT, first):
                W = 16 + NT
                self.TT("dve", pl1[:, :, 1:W], upl[:, :, 1:W], upl[:, :, 0:W - 1], ALU.add, ["upl"], ["pl1"])
                self.TT("dve", pl2[:, :, 3:W], pl1[:, :, 3:W], pl1[:, :, 1:W - 2], ALU.add, ["pl1"], ["pl2"])
                self.CP("dve", wsel[0:64, 0, 0:NT], pl1[0:64, 0, 16:W], ["pl1"], ["wsel"])
                self.CP("dve", wsel[64:128, 0, 0:NT], pl2[64:128, 0, 16:W], ["pl2"], ["wsel"])
                yield
                self.TT("dve", pl1[:, :, 7:W], pl2[:, :, 7:W], pl2[:, :, 3:W - 4], ALU.add, ["pl2", "wsel"], ["pl1"])
                self.TT("dve", pl2[:, :, 15:W], pl1[:, :, 15:W], pl1[:, :, 7:W - 8], ALU.add, ["pl1", "wsel"], ["pl2"])
                self.CP("dve", wsel[0:64, 1, 0:NT], pl1[0:64, 1, 16:W], ["pl1"], ["wsel"])
                self.CP("dve", wsel[64:128, 1, 0:NT], pl2[64:128, 1, 16:W], ["pl2"], ["wsel"])
                yield
                for ch in range(2):
                    self.STT(pooled[:, ch, 0:NT], wsel[:, ch, 0:NT], icnt[:, ch:ch + 1], upl[:, ch, 16:W], ALU.mult,
                             ALU.subtract, ["wsel", "icnt", "upl"], ["pooled"])
                if first:
                    self.TT("dve", t16[:], wsel[:, :, 0:16], icnt_tab[:], ALU.mult, ["wsel", "icnt_tab"], ["t16"])
                    self.TT("dve", pooled[:, :, 0:16], t16[:], upl[:, :, 16:32], ALU.subtract, ["t16", "upl"], ["pooled"])
                yield "late"
                for ch in range(2):
                    pp, ppk = self.psC.next()
                    self.MM(pp[:, 0:NT], poolw[:, ch, :], pooled[:, ch, 0:NT], True, True, ["poolw", "pooled"], [ppk])
                    self.TS("dve", pout[:, ch, 0:NT], pp[:, 0:NT], pscale[:, ch:ch + 1], None, ALU.mult, None,
                            [ppk, "pccols"], ["pout"])
                yield
                self.rms_feat(pout, 2, NT, gpool, _View(mixT, 6, col0), 256, ["pout"], "mixT", rmstmp)
                yield

            def out_proj(xk, x, NT, tok0):
                for oc in range(KC):
                    ps, pk = self.psA.next()
                    for kc in range(KC):
                        self.MM(ps[:, 0:NT], wout[:, kc, oc * 128:(oc + 1) * 128], mixT[:, kc, 0:NT], kc == 0,
                                kc == KC - 1, ["wout", "mixT"], [pk])
                    self.TT("dve", x[:, oc, 0:NT], ps[:, 0:NT], x[:, oc, 0:NT], ALU.add, [pk, xk], [xk])
                self.DMA(self.scr[wr, :, :, tok0:tok0 + NT].rearrange("c p t -> p c t"), x[:, :, 0:NT], [xk], [])
                if self.dbg:
                    self.DMA(self.O["dbg"][ph, :, :, tok0:tok0 + NT].rearrange("c p t -> p c t"), x[:, :, 0:NT], [xk], [])
                    if ph == 0:
                        self.DMA(self.O["dbg"][4, :, :, tok0:tok0 + NT].rearrange("c p t -> p c t"), mixT[:, :, 0:NT], ["mixT"], [], q="pool")

            def kv_tokmajor(NT, cols, is_v, slots, out_aps):
                nsub = len(cols)
                for s_, (c0, w) in enumerate(cols):
                    ps, pk = self.psA.next()
                    off = 1024 if is_v else 512
                    for kc in range(KC):
                        self.MM(ps[0:w, 0:512], hnT[:, kc, c0:c0 + w], win[:, kc, off:off + 512], kc == 0, kc == KC - 1,
                                ["hnT", "win"], [pk])
                    if is_v and slots[s_] is not None:
                        self.CP("act", Vr[0:w, slots[s_], :, 0:64], ps[0:w, 0:512].rearrange("p (h e) -> p h e", e=64),
                                [pk], ["v%d" % slots[s_]])
                    if out_aps[s_] is not None:
                        stg, sk = kvst.next()
                        self.CP("dve", stg[0:w, :], ps[0:w, 0:512], [pk], [sk])
                        self.DMA(out_aps[s_], stg[0:w, :], [sk], [])

            self.MS("pool", gst[:], 0.0, ["gst"])
            self.MS("pool", upl[:], 0.0, ["upl"])
            nst = S // TS_
            nxt = load_xT(xT[0], 0, TS_, ph == 0, I["xp"])
            self.ckpt("t_load")
            for sti in range(nst):
                x = xT[sti % 2]
                xk = nxt
                tok0 = sti * TS_
                if sti + 1 < nst:
                    nxt = load_xT(xT[(sti + 1) % 2], tok0 + TS_, TS_, ph == 0, I["xp"])
                a0 = tok0 // 128
                slots = [(a0 + s_) % NSLOT for s_ in range(NSUB)]
                for oc, ps, pk in in_proj(xk, x, TS_):
                    if oc < 4:
                        self.CP("act", qT[:, oc, 0:TS_], ps[:, 0:TS_], [pk], ["qT"])
                    elif oc < 8:
                        for s_ in range(NSUB):
                            self.CP("act" if s_ % 2 else "dve", KT[:, oc - 4, slots[s_] * 128:(slots[s_] + 1) * 128],
                                    ps[:, s_ * 128:(s_ + 1) * 128], [pk], ["kt%d" % slots[s_]])
                    elif oc < 14:
                        self.CP("act", usb[:, oc - 12, 0:TS_], ps[:, 0:TS_], [pk], ["usb"])
                        self.CP("dve", usf[:, oc - 12, 0:TS_], ps[:, 0:TS_], [pk], ["usf"])
                    else:
                        self.CP("dve", upl[:, oc - 14, 16:16 + TS_], ps[:, 0:TS_], [pk], ["upl"])
                keep0 = S - self.NKEEP
                cols = [(s_ * 128, 128) for s_ in range(NSUB)]
                outs_v = [O["vp"][l, tok0 + s_ * 128 - keep0: tok0 + s_ * 128 - keep0 + 128, :]
                          if tok0 + s_ * 128 >= keep0 else None for s_ in range(NSUB)]
                outs_k = [O["kp"][l, tok0 + s_ * 128 - keep0: tok0 + s_ * 128 - keep0 + 128, :]
                          if tok0 + s_ * 128 >= keep0 else None for s_ in range(NSUB)]
                self.ckpt("t_inproj")
                kv_tokmajor(TS_, cols, True, slots, outs_v)
                if any(o is not None for o in outs_k):
                    kv_tokmajor(TS_, cols, False, [None] * NSUB, outs_k)
                ktiles = [(a % NSLOT, 128, a0 - a) for a in range(max(0, a0 - 16), a0 + NSUB)]
                sched_tile(attention(0, TS_, ktiles), 8 * ((len(ktiles) + 1) // 2) + 1,
                           [ssm(0, TS_, sti == 0), pool_mix(0, TS_, sti == 0)])
                att_finish(0, TS_)
                if sti + 1 < nst:
                    self.CP("pool", upl[:, :, 1:16], upl[:, :, TS_ + 1:TS_ + 16], ["upl"], ["upl"])
                out_proj(xk, x, TS_, tok0)
            self.ckpt("m_prompt")
            for gp in range(8):
                for pl, nm in ((0, "rep"), (1, "imp")):
                    dst = bass.AP(O[nm].tensor, O[nm][l].offset + gp * 128, [[1, 128], [1, 1]])
                    self.DMA(dst, hl[:, pl, gp:gp + 1], ["hl"], [])
            for ch in range(2):
                for r_ in range(15):
                    dst = bass.AP(O["poolp"].tensor, O["poolp"][l, r_].offset + ch * 128, [[1, 128], [1, 1]])
                    self.DMA(dst, upl[:, ch, TS_ + 1 + r_: TS_ + 2 + r_], ["upl"], [], q="act")

            self.ckpt("m_pout")
            xs_ = xT[nst % 2]
            xk = load_xT(xs_, 0 if ph == 0 else S, NS, ph == 0, I["xs"])
            for oc, ps, pk in in_proj(xk, xs_, NS):
                if oc < 4:
                    self.CP("act", qT[:, oc, 0:NS], ps[:, 0:NS], [pk], ["qT"])
                elif oc < 8:
                    self.CP("act", kts[:, oc - 4, :], ps[:, 0:NS], [pk], ["kts"])
                elif oc < 14:
                    self.CP("act", usb[:, oc - 12, 0:NS], ps[:, 0:NS], [pk], ["usb"])
                    self.CP("dve", usf[:, oc - 12, 0:NS], ps[:, 0:NS], [pk], ["usf"])
                else:
                    self.CP("dve", ups[:, oc - 14, 0:NS], ps[:, 0:NS], [pk], ["ups"])
            for s in range(NSEQ):
                c0 = s * TD
                for m in range(16):
                    kc_, kck = kcs.next()
                    self.DMA(kc_[:], I["ck"][l, s, m * 128:(m + 1) * 128, :], [], [kck])
                    psk, pkk = self.psA.next()
                    for c in range(4):
                        self.TR(psk[:, c * 128:(c + 1) * 128], kc_[:, c * 128:(c + 1) * 128], self.ident_f[:],
                                [kck, "ident"], [pkk])
                    self.CP("act" if m % 2 else "dve", KT[:, :, m * 128:(m + 1) * 128],
                            psk[:, 0:512].rearrange("p (c k) -> p c k", k=128), [pkk], ["kt%d" % m])
                    self.DMA(Vr[:, m, :, 0:64], I["cv"][l, s, m * 128:(m + 1) * 128, :].rearrange("k (h e) -> k h e", e=64),
                             [], ["v%d" % m], q="pool")
                self.CP("dve", KT[:, :, 16 * 128:16 * 128 + TD], kts[:, :, c0:c0 + TD], ["kts"], ["kt16"])
                kv_tokmajor(TD, [(c0, TD)], True, [16], [O["vs"][l, s]])
                kv_tokmajor(TD, [(c0, TD)], False, [None], [O["ks"][l, s]])
                ktiles = [(m, 128, 16 - m) for m in range(16)] + [(16, TD, 0)]
                interleave([attention(c0, TD, ktiles)], [1])
                att_finish(c0, TD)
                for gp in range(8):
                    for pl, nm in ((0, "sre"), (1, "sim")):
                        src = bass.AP(I[nm].tensor, I[nm][l, s].offset + gp * 128, [[1, 128], [1, 1]])
                        self.DMA(h0c[:, pl, gp:gp + 1], src, [], ["h0c"])
                c1_, s1_ = cs[:, :, 1], sn[:, :, 1]
                self.TT("pool", gl_t[:, 0, :], h0c[:, 0, :], c1_, ALU.mult, ["h0c", "cs"], ["glt0"])
                self.TT("pool", gl_t[:, 1, :], h0c[:, 1, :], s1_, ALU.mult, ["h0c", "sn"], ["glt1"])
                self.TT("pool", gl_t[:, 2, :], h0c[:, 1, :], c1_, ALU.mult, ["h0c", "cs"], ["glt2"])
                self.TT("pool", gl_t[:, 3, :], h0c[:, 0, :], s1_, ALU.mult, ["h0c", "sn"], ["glt3"])
                self.TT("pool", gst[:, 0, :], gl_t[:, 0, :], gl_t[:, 1, :], ALU.subtract, ["glt0", "glt1"], ["gst"])
                self.TT("pool", gst[:, 1, :], gl_t[:, 2, :], gl_t[:, 3, :], ALU.add, ["glt2", "glt3"], ["gst"])
                interleave([ssm(c0, TD, False)], [1])
                for gp in range(8):
                    for pl, nm in ((0, "res"), (1, "ims")):
                        dst = bass.AP(O[nm].tensor, O[nm][l, s].offset + gp * 128, [[1, 128], [1, 1]])
                        self.DMA(dst, hl[:, pl, gp:gp + 1], ["hl"], [])
                for ch in range(2):
                    for r_ in range(15):
                        src = bass.AP(I["spool"].tensor, I["spool"][l, s, r_].offset + ch * 128, [[1, 128], [1, 1]])
                        self.DMA(upl[:, ch, 1 + r_:2 + r_], src, [], ["upl"], q="act")
                self.CP("dve", upl[:, :, 16:16 + TD], ups[:, :, c0:c0 + TD], ["ups"], ["upl"])
                for ch in range(2):
                    for r_ in range(15):
                        dst = bass.AP(O["pools"].tensor, O["pools"][l, s, r_].offset + ch * 128, [[1, 128], [1, 1]])
                        self.DMA(dst, upl[:, ch, TD + 1 + r_: TD + 2 + r_], ["upl"], [], q="act")
                interleave([pool_mix(c0, TD, False)], [1])
            out_proj(xk, xs_, NS, S)
            S_.barrier()

    def ffn_phase(self, l, ph, last):
        nc, S_, I, O = self.nc, self.S_, self.I, self.O
        S, TS_ = self.S, self.TSF
        rd, wr = (ph - 1) % 2, ph % 2
        self.set_rings(False)
        with contextlib.ExitStack() as st:
            sb = lambda n, s, d: self.sb(st, n, s, d)
            wup = sb("wup", [128, KC, 2 * DFF], BF16)
            wdn = sb("wdn", [128, AC, D], BF16)
            for c in range(KC):
                for hh in range(4):
                    self.DMA(wup[:, c, hh * 1408:(hh + 1) * 1408], I["w_up"][l, c * 128:(c + 1) * 128, hh * 1408:(hh + 1) * 1408],
                             [], ["wup"], q="pool")
            for c in range(AC):
                self.DMA(wdn[:, c, :], I["w_down"][l, c * 128:(c + 1) * 128, :], [], ["wdn"], q="pool")
            pa = self.load_rows_T(st, [(I["norm2_g"][l].rearrange("(c p) -> c p", p=128), 0),
                                       (I["norm_f_g"].rearrange("(c p) -> c p", p=128), 8),
                                       (I["conv_b"][l].rearrange("(c p) -> c p", p=128), 16),
                                       (I["conv_w"][l, 0].rearrange("(c p) -> c p", p=128), 64)], "pa")
            pb = self.load_rows_T(st, [(I["conv_w"][l, 1].rearrange("(c p) -> c p", p=128), 0),
                                       (I["conv_w"][l, 2].rearrange("(c p) -> c p", p=128), 64)], "pb")
            g2, gf, cb, cw0, cw1, cw2 = pa[:, 0:8], pa[:, 8:16], pa[:, 16:60], pa[:, 64:108], pb[:, 0:44], pb[:, 64:108]
            xT = [sb("fx%d" % i, [128, KC, TS_], F32) for i in range(2)]
            sq = sb("fsq", [128, KC, TS_], BF16)
            sd = sb("fsd", [128, TS_], F32)
            rstd = sb("frstd", [128, TS_], F32)
            hnT = sb("fhn", [128, KC, TS_], BF16)
            actT = sb("factT", [128, AC, TS_], BF16)
            raws = Ring([(sb("raw%d" % i, [128, TS_ + 2 * NSEQ], F32), "raw%d" % i) for i in range(6)])
            cvs = Ring([(sb("cv%d" % i, [128, TS_], F32), "cv%d" % i) for i in range(8)])
            sgs = Ring([(sb("sg%d" % i, [128, TS_], F32), "sg%d" % i) for i in range(3)])
            uptail = sb("uptail", [128, FC, 2], F32)
            stail = sb("stail", [128, FC, NSEQ, 2], F32)
            ynT = sb("ynT", [128, KC, TS_], F32) if last else None
            ytok = Ring([(sb("ytok%d" % i, [128, D], F32), "ytok%d" % i) for i in range(2)]) if last else None
            rmstmp = (sq, sd, rstd)
            if l == 0:
                print("SBUF remaining after ffn alloc:", self.nc.sbuf_bytes_remaining)
            self.MS("pool", uptail[:], 0.0, ["uptail"])
            for c in range(FC):
                src = I["sconv"][l, :, :, c * 128:(c + 1) * 128].rearrange("s r p -> p s r")
                self.DMA(stail[:, c, :, :], src, [], ["stail"], q="act", slow=True)
            S_.barrier()

            def load_x(dst, tok0, NT):
                key = "fx%d" % xT.index(dst)
                self.DMA(dst[:, :, 0:NT], self.scr[rd, :, :, tok0:tok0 + NT].rearrange("c p t -> p c t"), [], [key])
                return key

            def tile(xk, x, tok0, NT, nseq, tl, tail, tailk):
                self.rms_feat(x, KC, NT, g2, hnT, D, [xk], "fhn", rmstmp)
                deferred = None
                for fc in range(AC):
                    st_ = []
                    for cidx in (fc, fc + AC):
                        ps, pk = self.psA.next()
                        for kc in range(KC):
                            self.MM(ps[:, 0:NT], wup[:, kc, cidx * 128:(cidx + 1) * 128], hnT[:, kc, 0:NT], kc == 0,
                                    kc == KC - 1, ["wup", "fhn"], [pk])
                        raw, rk = raws.next()
                        r3 = raw[:, 0:nseq * (tl + 2)].rearrange("p (s t) -> p s t", t=tl + 2)
                        tl_ap = tail[:, cidx, :].unsqueeze(1) if nseq == 1 else tail[:, cidx, :, :]
                        cv, ck = cvs.next()
                        c3 = cv[:, 0:NT].rearrange("p (s t) -> p s t", t=tl)
                        p3 = ps[:, 0:NT].rearrange("p (s t) -> p s t", t=tl)
                        tk_ = "%s%d" % (tailk, cidx)
                        self.CP("pool", r3[:, :, 0:2], tl_ap, [tailk, tk_], [rk])
                        self.CP("act", r3[:, :, 2:2 + tl], p3, [pk], [rk])
                        self.ACT(c3, p3, AF.Identity, [pk, "pacols", "pbcols"], [ck], bias=cb[:, cidx:cidx + 1],
                                 scale=cw2[:, cidx:cidx + 1])
                        self.CP("pool", tl_ap, r3[:, :, tl:tl + 2], [rk], [tk_])
                        st_.append((cidx, r3, c3, rk, ck, cv))
                    for (cidx, r3, c3, rk, ck, cv) in st_:
                        self.STT(c3, r3[:, :, 1:1 + tl], cw1[:, cidx:cidx + 1], c3, ALU.mult, ALU.add, [rk, ck, "pbcols"], [ck])
                    for (cidx, r3, c3, rk, ck, cv) in st_:
                        self.STT(c3, r3[:, :, 0:tl], cw0[:, cidx:cidx + 1], c3, ALU.mult, ALU.add, [rk, ck, "pacols"], [ck])

                    def tail_part(fc=fc, st_=st_):
                        (_, _, _, _, cak, ca), (_, _, _, _, cgk, cg) = st_
                        sg, sgk = sgs.next()
                        self.ACT(sg[:, 0:NT], cg[:, 0:NT], AF.Silu, [cgk], [sgk])
                        self.TT("dve", actT[:, fc, 0:NT], sg[:, 0:NT], ca[:, 0:NT], ALU.mult, [sgk, cak], ["actT"])

                    if deferred is not None:
                        deferred()
                    deferred = tail_part
                deferred()
                for oc in range(KC):
                    ps, pk = self.psA.next()
                    for kc in range(AC):
                        self.MM(ps[:, 0:NT], wdn[:, kc, oc * 128:(oc + 1) * 128], actT[:, kc, 0:NT], kc == 0, kc == AC - 1,
                                ["wdn", "actT"], [pk])
                    self.TT("dve", x[:, oc, 0:NT], ps[:, 0:NT], x[:, oc, 0:NT], ALU.add, [pk, xk], [xk])
                if self.dbg:
                    self.DMA(self.O["dbg"][ph, :, :, tok0:tok0 + NT].rearrange("c p t -> p c t"), x[:, :, 0:NT], [xk], [])
                if not last:
                    self.DMA(self.scr[wr, :, :, tok0:tok0 + NT].rearrange("c p t -> p c t"), x[:, :, 0:NT], [xk], [])
                    return
                self.rms_feat(x, KC, NT, gf, ynT, D, [xk], "ynT", rmstmp)
                nsub = (NT + 127) // 128
                w = min(128, NT)
                ydst = O["yp"] if nseq == 1 else O["ys"]
                t0_ = tok0 if nseq == 1 else 0
                for s_ in range(nsub):
                    yt, ytk = ytok.next()
                    for half in range(2):
                        ps, pk = self.psA.next()
                        for c4 in range(4):
                            c = half * 4 + c4
                            self.TR(ps[0:w, c4 * 128:(c4 + 1) * 128], ynT[:, c, s_ * 128: s_ * 128 + w], self.ident_f[:],
                                    ["ynT", "ident"], [pk])
                        self.CP("act" if half else "dve", yt[0:w, half * 512:(half + 1) * 512], ps[0:w, 0:512], [pk], [ytk])
                    self.DMA(ydst[t0_ + s_ * 128: t0_ + s_ * 128 + w, :], yt[0:w, :], [ytk], [])

            nst = S // TS_
            nxt = load_x(xT[0], 0, TS_)
            for sti in range(nst):
                x, xk = xT[sti % 2], nxt
                if sti + 1 < nst:
                    nxt = load_x(xT[(sti + 1) % 2], (sti + 1) * TS_, TS_)
                else:
                    nxt = load_x(xT[(sti + 1) % 2], S, NS)
                tile(xk, x, sti * TS_, TS_, 1, TS_, uptail, "uptail")
            for c in range(FC):
                for r_ in range(2):
                    dst = bass.AP(O["convp"].tensor, O["convp"][l, r_].offset + c * 128, [[1, 128], [1, 1]])
                    self.DMA(dst, uptail[:, c, r_:r_ + 1], ["uptail", "uptail%d" % c], [], q="act")
            tile(nxt, xT[nst % 2], S, NS, NSEQ, TD, stail, "stail")
            for c in range(FC):
                dst = O["convs"][l, :, :, c * 128:(c + 1) * 128].rearrange("s r p -> p s r")
                self.DMA(dst, stail[:, c, :, :], ["stail", "stail%d" % c], [], q="act", slow=True)
            S_.barrier()

    def build(self):
        self.declare()
        with contextlib.ExitStack() as st:
            self.S_ = Sched(self.nc, st)
            try:
                self.setup(st)
                self.ckpt("setup")
                ph = 0
                for l in range(2):
                    self.mix_phase(l, ph)
                    self.ckpt("mix%d" % l)
                    ph += 1
                    self.ffn_phase(l, ph, last=(l == 1))
                    self.ckpt("ffn%d" % l)
                    ph += 1
            except _Stop:
                self.S_.barrier()
            self.S_.emit()
        return self.nc


class _View:
    def __init__(self, t, c0, col0):
        self.t, self.c0, self.col0 = t, c0, col0

    def __getitem__(self, key):
        p, c, cols = key
        return self.t[p, self.c0 + c, self.col0 + cols.start: self.col0 + cols.stop]


_NC_CACHE = {}


def _get_nc():
    if "nc" not in _NC_CACHE:
        _NC_CACHE["nc"] = Builder().build()
    return _NC_CACHE["nc"]


_WNAMES = ("norm1_g", "w_in", "ssm_log_dt", "ssm_a_re", "ssm_a_im", "ssm_b_re", "ssm_b_im", "ssm_c_re", "ssm_c_im",
           "ssm_d", "ssm_w_glu", "ssm_b_glu", "pool_w", "pool_scale", "out_norm_att", "out_norm_ssm", "out_norm_pool",
           "w_out", "norm2_g", "w_up", "conv_w", "conv_b", "w_down", "norm_f_g")


def make_in_maps(inp, S=8192):
    f = lambda a: np.ascontiguousarray(np.asarray(a, dtype=np.float32))
    maps = []
    zeros_p = np.zeros((S, D), np.float32)
    for c in range(NCORES):
        m = {}
        m["xp"] = f(inp["x_prompt"][c]) if c < inp["x_prompt"].shape[0] else zeros_p
        sl = slice(c * NSEQ, (c + 1) * NSEQ)
        m["xs"] = f(inp["x_sample"][sl]).reshape(NS, D)
        m["ck"] = f(inp["cache_k"][:, sl]).reshape(2, NSEQ, PAST, 512)
        m["cv"] = f(inp["cache_v"][:, sl]).reshape(2, NSEQ, PAST, 512)
        m["sre"] = f(inp["state_ssm_re"][:, sl])
        m["sim"] = f(inp["state_ssm_im"][:, sl])
        m["spool"] = f(inp["state_pool"][:, sl])
        m["sconv"] = f(inp["state_conv"][:, sl])
        for n in _WNAMES:
            m[n] = f(inp[n])
        maps.append(m)
    return maps


def gather(res, S=8192, nkeep=2048, nb=2):
    r = res
    cat = lambda n: np.concatenate([r[c][n] for c in range(NCORES)], axis=1)
    y_prompt = np.stack([r[c]["yp"] for c in range(nb)], 0)
    y_sample = np.concatenate([r[c]["ys"].reshape(NSEQ, TD, D) for c in range(NCORES)], 0)
    pk = lambda n: np.stack([r[c][n] for c in range(nb)], 1)
    new_k_prompt = pk("kp").reshape(2, nb, nkeep, 8, 64)
    new_v_prompt = pk("vp").reshape(2, nb, nkeep, 8, 64)
    outs = (y_prompt, y_sample, new_k_prompt, new_v_prompt, pk("rep"), pk("imp"), pk("poolp"), pk("convp"),
            cat("ks").reshape(2, NCORES * NSEQ, TD, 8, 64), cat("vs").reshape(2, NCORES * NSEQ, TD, 8, 64),
            cat("res"), cat("ims"), cat("pools"), cat("convs"))
    return tuple(np.ascontiguousarray(o, dtype=np.float32) for o in outs)


def kernel(**inputs):
    nc = _get_nc()
    in_maps = make_in_maps(inputs)
    res = run_bass_kernel_spmd(nc, in_maps, core_ids=list(range(NCORES)))
    return gather(res.results)
```

```python
import contextlib
import math
import numpy as np
import concourse.bass as bass
import concourse.mybir as mybir
from concourse.bass_utils import run_bass_kernel_spmd

F32 = mybir.dt.float32
BF16 = mybir.dt.bfloat16
I32 = mybir.dt.int32
ALU = mybir.AluOpType
AF = mybir.ActivationFunctionType

D = 1024
KC = 8
DIN = 2048
DFF = 2816
FC = 44
AC = 22
NSEQ = 4
TD = 8
NS = NSEQ * TD
PAST = 2048
EPS = 1e-6
NCORES = 8
TWO_PI = 2.0 * math.pi


class Sched:
    ENGS = ("pe", "act", "dve", "pool", "sp")

    def __init__(self, nc, stack, n_dma_sems=32):
        self.nc = nc
        self.ops = {e: [] for e in self.ENGS}
        self.count = {e: 0 for e in self.ENGS}
        self.known = {e: {} for e in self.ENGS}
        self.buf = {}
        self.esem = {e: stack.enter_context(nc.semaphore("s_" + e)) for e in self.ENGS}
        self.KD = n_dma_sems
        self.dsem = [stack.enter_context(nc.semaphore("d%d" % i)) for i in range(self.KD)]
        self.ndma = 0
        self.dlast = [0] * self.KD
        self.dead = False

    def _deps(self, reads, writes):
        deps = []
        for k in reads:
            st = self.buf.get(k)
            if st and st["w"] is not None:
                deps.append(st["w"] + ("raw",))
        for k in writes:
            st = self.buf.get(k)
            if st:
                if st["w"] is not None:
                    deps.append(st["w"])
                deps.extend(st["r"])
        return deps

    def _waits(self, eng, deps):
        need = {}
        for t in deps:
            if t[0] == "E":
                if t[1] == eng and (eng == "pe" or len(t) < 4):
                    continue
                key = ("E", t[1])
            else:
                key = ("D", t[1])
            if t[2] > need.get(key, 0):
                need[key] = t[2]
        out = []
        kn = self.known[eng]
        for key, v in need.items():
            if kn.get(key, 0) >= v:
                continue
            kn[key] = v
            sem = self.esem[key[1]] if key[0] == "E" else self.dsem[key[1]]
            out.append((sem, v))
        return out

    def _commit(self, tok, reads, writes):
        for k in reads:
            st = self.buf.setdefault(k, {"w": None, "r": []})
            st["r"].append(tok)
            if len(st["r"]) > 64:
                st["r"] = self._prune(st["r"])
        for k in writes:
            self.buf[k] = {"w": tok, "r": []}

    @staticmethod
    def _prune(lst):
        best = {}
        for t in lst:
            key = (t[0], t[1])
            if key not in best or t[2] > best[key][2]:
                best[key] = t
        return list(best.values())

    def op(self, eng, fn, reads=(), writes=()):
        if self.dead:
            return None
        psr = [k for k in reads if k.startswith("ps") and k[2:].isdigit()]
        if psr:
            reads = [k for k in reads if k not in psr]
            writes = list(writes) + psr
        waits = self._waits(eng, self._deps(reads, writes))
        self.count[eng] += 1
        tok = ("E", eng, self.count[eng])
        self.ops[eng].append((fn, waits, (self.esem[eng], 1)))
        self._commit(tok, reads, writes)
        return tok

    def dma(self, fn, reads=(), writes=(), q="sp"):
        if self.dead:
            return None
        slot = self.ndma % self.KD
        self.ndma += 1
        deps = self._deps(reads, writes)
        if self.dlast[slot] > 0:
            deps.append(("D", slot, self.dlast[slot]))
        waits = self._waits(q, deps)
        self.dlast[slot] += 16
        tok = ("D", slot, self.dlast[slot])
        self.ops[q].append((fn, waits, (self.dsem[slot], 16)))
        self._commit(tok, reads, writes)
        return tok

    def barrier(self):
        if self.dead:
            return
        dd = [("D", s, v) for s, v in enumerate(self.dlast) if v > 0]
        for e in self.ENGS:
            deps = [("E", e2, self.count[e2]) for e2 in self.ENGS if e2 != e and self.count[e2] > 0] + dd
            waits = self._waits(e, deps)
            self.ops[e].append((None, waits, None))
        self.buf = {}

    def emit(self):
        nc = self.nc
        engmap = {"pe": "tensor", "act": "scalar", "dve": "vector", "pool": "gpsimd", "sp": "sync"}
        with nc.Block() as block:
            for e in self.ENGS:
                lst = self.ops[e]

                def body(engine, lst=lst):
                    for fn, waits, inc in lst:
                        for sem, v in waits:
                            engine.wait_ge(sem, v)
                        if fn is not None:
                            fn(engine).then_inc(inc[0], inc[1])

                getattr(block, engmap[e])(body)


class Ring:
    def __init__(self, items):
        self.items = items
        self.i = 0

    def next(self):
        it = self.items[self.i % len(self.items)]
        self.i += 1
        return it


class _Stop(Exception):
    pass


class Builder:
    stop_at = None

    def ckpt(self, name):
        if self.stop_at is not None and name == self.stop_at and not self.S_.dead:
            self.S_.barrier()
            self.S_.dead = True

    def __init__(self, S=8192, TSM=128, TSF=256, dbg=False):
        assert S % TSM == 0 and S % TSF == 0 and TSM % 128 == 0
        self.S, self.TSM, self.TSF, self.dbg = S, TSM, TSF, dbg
        self.NSUB = TSM // 128
        self.JMIN = -(self.NSUB - 1)
        self.NJ = 17 + self.NSUB - 1
        self.NSLOT = 16 + 2 * self.NSUB
        self.NKEEP = min(2048, S)
        self.nc = bass.Bass("TRN2", target_bir_lowering=False)

    def MM(self, out, lhsT, rhs, start, stop, r, w):
        self.S_.op("pe", lambda e: e.matmul(out, lhsT=lhsT, rhs=rhs, start=start, stop=stop,
                                            skip_group_check=True), r, w)

    def TR(self, out, in_, ident, r, w):
        self.S_.op("pe", lambda e: e.transpose(out, in_, ident), r, w)

    def ACT(self, out, in_, func, r, w, bias=None, scale=None, accum=None):
        kw = {}
        if bias is not None:
            kw["bias"] = bias
        if scale is not None:
            kw["scale"] = scale
        if accum is not None:
            kw["accum_out"] = accum
        self.S_.op("act", lambda e: e.activation(out=out, in_=in_, func=func, **kw), r, w)

    def TT(self, eng, out, in0, in1, op, r, w):
        self.S_.op(eng, lambda e: e.tensor_tensor(out=out, in0=in0, in1=in1, op=op), r, w)

    def TS(self, eng, out, in0, s1, s2, op0, op1, r, w):
        if s2 is None:
            self.S_.op(eng, lambda e: e.tensor_scalar(out=out, in0=in0, scalar1=s1, scalar2=None, op0=op0), r, w)
        else:
            self.S_.op(eng, lambda e: e.tensor_scalar(out=out, in0=in0, scalar1=s1, scalar2=s2, op0=op0, op1=op1), r, w)

    def STT(self, out, in0, scalar, in1, op0, op1, r, w):
        self.S_.op("dve", lambda e: e.scalar_tensor_tensor(out=out, in0=in0, scalar=scalar, in1=in1,
                                                          op0=op0, op1=op1), r, w)

    def CP(self, eng, out, in_, r, w):
        if eng == "act":
            self.S_.op("act", lambda e: e.activation(out=out, in_=in_, func=AF.Copy), r, w)
        else:
            self.S_.op(eng, lambda e: e.tensor_copy(out=out, in_=in_), r, w)

    def MS(self, eng, ap, val, w):
        self.S_.op(eng, lambda e: e.memset(ap, val), (), w)

    def DMA(self, out, in_, r, w, q="sp", slow=False):
        if slow:
            self.S_.dma(lambda e: e.dma_start(out=out, in_=in_, allow_slow_non_contiguous=True), r, w, q=q)
        else:
            self.S_.dma(lambda e: e.dma_start(out=out, in_=in_), r, w, q=q)

    def sb(self, stack, name, shape, dt):
        self._uid = getattr(self, "_uid", 0) + 1
        return stack.enter_context(self.nc.sbuf_tensor("%s_%d" % (name, self._uid), shape, dt))

    def declare(self):
        nc, S = self.nc, self.S
        din = lambda n, s: nc.dram_tensor(n, s, F32, kind="ExternalInput").ap()
        dout = lambda n, s: nc.dram_tensor(n, s, F32, kind="ExternalOutput").ap()
        I = {}
        I["xp"] = din("xp", [S, D])
        I["xs"] = din("xs", [NS, D])
        I["ck"] = din("ck", [2, NSEQ, PAST, 512])
        I["cv"] = din("cv", [2, NSEQ, PAST, 512])
        I["sre"] = din("sre", [2, NSEQ, 16, 64])
        I["sim"] = din("sim", [2, NSEQ, 16, 64])
        I["spool"] = din("spool", [2, NSEQ, 15, 256])
        I["sconv"] = din("sconv", [2, NSEQ, 2, 2 * DFF])
        for n, s in (("norm1_g", [2, D]), ("w_in", [2, D, DIN]), ("ssm_log_dt", [2, 16]), ("ssm_a_re", [2, 16, 64]),
                     ("ssm_a_im", [2, 16, 64]), ("ssm_b_re", [2, 16, 64, 16]), ("ssm_b_im", [2, 16, 64, 16]),
                     ("ssm_c_re", [2, 16, 16, 64]), ("ssm_c_im", [2, 16, 16, 64]), ("ssm_d", [2, 256]),
                     ("ssm_w_glu", [2, 256, 256]), ("ssm_b_glu", [2, 256]), ("pool_w", [2, 4, 64, 64]),
                     ("pool_scale", [2, 256]), ("out_norm_att", [2, 512]), ("out_norm_ssm", [2, 256]),
                     ("out_norm_pool", [2, 256]), ("w_out", [2, D, D]), ("norm2_g", [2, D]),
                     ("w_up", [2, D, 2 * DFF]), ("conv_w", [2, 3, 2 * DFF]), ("conv_b", [2, 2 * DFF]),
                     ("w_down", [2, DFF, D]), ("norm_f_g", [D])):
            I[n] = din(n, s)
        O = {}
        O["yp"] = dout("yp", [S, D])
        O["ys"] = dout("ys", [NS, D])
        O["kp"] = dout("kp", [2, self.NKEEP, 512])
        O["vp"] = dout("vp", [2, self.NKEEP, 512])
        O["rep"] = dout("rep", [2, 16, 64])
        O["imp"] = dout("imp", [2, 16, 64])
        O["poolp"] = dout("poolp", [2, 15, 256])
        O["convp"] = dout("convp", [2, 2, 2 * DFF])
        O["ks"] = dout("ks", [2, NSEQ, TD, 512])
        O["vs"] = dout("vs", [2, NSEQ, TD, 512])
        O["res"] = dout("res", [2, NSEQ, 16, 64])
        O["ims"] = dout("ims", [2, NSEQ, 16, 64])
        O["pools"] = dout("pools", [2, NSEQ, 15, 256])
        O["convs"] = dout("convs", [2, NSEQ, 2, 2 * DFF])
        if self.dbg:
            O["dbg"] = dout("dbg", [12, KC, 128, S + NS])
        self.I, self.O = I, O
        self.scr = nc.dram_tensor("scr", [2, KC, 128, S + NS], F32, kind="Internal").ap()

    def sincos(self, st, X, n, tag, want_cos=True):
        ki = self.sb(st, tag + "_ki", [128, n], I32)
        r = self.sb(st, tag + "_r", [128, n], F32)
        sn = self.sb(st, tag + "_sn", [128, n], F32)
        outs = []
        for which, shift, dst in (("s", 0.0, sn),) + ((("c", math.pi / 2, None),) if want_cos else ()):
            if dst is None:
                dst = self.sb(st, tag + "_cs", [128, n], F32)
            src = X
            if shift != 0.0:
                self.TS("dve", r[:], X, shift, None, ALU.add, None, [tag + "X"], [tag + "r"])
                src = r[:]
            self.TS("dve", ki[:], src, 1.0 / TWO_PI, None, ALU.mult, None, [tag + "X", tag + "r"], [tag + "ki"])
            self.STT(r[:], ki[:], -TWO_PI, src, ALU.mult, ALU.add, [tag + "ki", tag + "X", tag + "r"], [tag + "r"])
            self.TS("dve", r[:], r[:], 3.14159, -3.14159, ALU.min, ALU.max, [tag + "r"], [tag + "r"])
            self.ACT(dst[:], r[:], AF.Sin, [tag + "r"], [tag + which])
            outs.append(dst)
        return outs

    def rms_feat(self, src, C, NT, gains, dst, nfeat, keys_r, key_w, tmp):
        sq, sd, rstd = tmp
        self.ACT(sq[:, 0:C, 0:NT], src[:, 0:C, 0:NT], AF.Square, keys_r, ["sq"])
        ps, pk = self.psC.next()
        for c in range(C):
            self.MM(ps[:, 0:NT], self.ones_b[:], sq[:, c, 0:NT], c == 0, c == C - 1, ["sq", "ones"], [pk])
        if self.rms_mode == "lnexp":
            self.ACT(sd[:, 0:NT], ps[:, 0:NT], AF.Ln, [pk], ["sd"], bias=self.eps_col[:, 0:1], scale=1.0 / nfeat)
            self.ACT(rstd[:, 0:NT], sd[:, 0:NT], AF.Exp, ["sd"], ["rstd"], scale=-0.5)
        else:
            self.ACT(sd[:, 0:NT], ps[:, 0:NT], AF.Sqrt, [pk], ["sd"], bias=self.eps_col[:, 0:1], scale=1.0 / nfeat)
            self.S_.op("dve", lambda e: e.reciprocal(out=rstd[:, 0:NT], in_=sd[:, 0:NT]), ["sd"], ["rstd"])
        for c in range(C):
            self.STT(dst[:, c, 0:NT], src[:, c, 0:NT], gains[:, c:c + 1], rstd[:, 0:NT], ALU.mult, ALU.mult,
                     keys_r + ["rstd", "gains"], [key_w])

    def load_rows_T(self, st, rows, tag):
        stage = self.sb(st, tag + "_stg", [128, 128], F32)
        cols = self.sb(st, tag + "_cols", [128, 128], F32)
        self.MS("pool", stage[:], 0.0, [tag + "stg"])
        for ap, base in rows:
            n = ap.shape[0]
            self.DMA(stage[base:base + n, :], ap, [], [tag + "stg"])
        ps, pk = self.psC.next()
        self.TR(ps[:, 0:128], stage[:], self.ident_f[:], [tag + "stg", "ident"], [pk])
        self.CP("dve", cols[:], ps[:, 0:128], [pk], [tag + "cols"])
        return cols

    def set_rings(self, mix):
        rg = lambda ids: Ring([(self.ps[i], "ps%d" % i) for i in ids])
        if mix:
            self.psA, self.psS, self.psB, self.psC = rg((0, 1)), rg((2, 3, 4, 5)), rg((6,)), rg((7,))
        else:
            self.psA, self.psC = rg((0, 1, 2, 3, 4, 5)), rg((6, 7))

    def setup(self, st):
        nc = self.nc
        self.ps = [st.enter_context(nc.psum_tensor("ps%d" % i, [128, 512], F32)) for i in range(8)]
        self.set_rings(True)
        self.ident_f = self.sb(st, "ident_f", [128, 128], F32)
        self.ones_b = self.sb(st, "ones_b", [128, 128], BF16)
        self.eps_col = self.sb(st, "eps_col", [128, 1], F32)
        with contextlib.ExitStack() as t:
            onesf = self.sb(t, "onesf", [128, 128], F32)
            self.MS("pool", onesf[:], 1.0, ["onesf"])
            self.MS("pool", self.ones_b[:], 1.0, ["ones"])
            self.MS("pool", self.eps_col[:], EPS, ["eps"])
            self.S_.op("pool", lambda e: e.affine_select(out=self.ident_f[:], in_=onesf[:], pattern=[[-1, 128]],
                                                         compare_op=ALU.is_equal, fill=0.0, base=0,
                                                         channel_multiplier=1), ["onesf"], ["ident"])
            self.S_.barrier()

    def mix_phase(self, l, ph):
        nc, S_, I, O = self.nc, self.S_, self.I, self.O
        S, TS_, NSUB, NJ, JMIN, NSLOT = self.S, self.TSM, self.NSUB, self.NJ, self.JMIN, self.NSLOT
        TW = TS_ + 1
        rd, wr = (ph - 1) % 2, ph % 2
        self.rms_mode = "lnexp"
        self.set_rings(True)
        with contextlib.ExitStack() as st:
            sb = lambda n, s, d: self.sb(st, n, s, d)
            win = sb("win", [128, KC, DIN], BF16)
            wout = sb("wout", [128, KC, D], BF16)
            wglu = sb("wglu", [128, 2, 256], BF16)
            poolw = sb("poolw", [128, 2, 128], BF16)
            Bre = sb("Bre", [128, 8, 128], BF16)
            Bim = sb("Bim", [128, 8, 128], BF16)
            Cre = sb("Cre", [128, 8, 128], BF16)
            Cim = sb("Cim", [128, 8, 128], BF16)
            Tre = sb("Tre", [128, 8, TW], F32)
            Tim = sb("Tim", [128, 8, TW], F32)
            cs = sb("cs", [128, 8, TW], F32)
            sn = sb("sn", [128, 8, TW], F32)
            rho = sb("rho", [128, 8], F32)
            mask = sb("mask", [128, NJ, TS_], BF16)
            icnt_tab = sb("icnt_tab", [128, 2, 16], F32)
            icnt = sb("icnt", [128, 2], F32)
            for c in range(KC):
                self.DMA(win[:, c, :], I["w_in"][l, c * 128:(c + 1) * 128, :], [], ["win"], q="pool")
            for c in range(KC):
                self.DMA(wout[:, c, :], I["w_out"][l, c * 128:(c + 1) * 128, :], [], ["wout"], q="pool")
            for c in range(2):
                self.DMA(wglu[:, c, :], I["ssm_w_glu"][l, c * 128:(c + 1) * 128, :], [], ["wglu"], q="pool")
            self.MS("pool", poolw[:], 0.0, ["poolw"])
            for gi in range(4):
                h0 = (gi % 2) * 64
                self.DMA(poolw[h0:h0 + 64, gi // 2, h0:h0 + 64], I["pool_w"][l, gi], [], ["poolw"], q="pool")
            rows = [(I["norm1_g"][l].rearrange("(c p) -> c p", p=128), 0),
                    (I["out_norm_att"][l].rearrange("(c p) -> c p", p=128), 8),
                    (I["out_norm_ssm"][l].rearrange("(c p) -> c p", p=128), 12),
                    (I["out_norm_pool"][l].rearrange("(c p) -> c p", p=128), 14),
                    (I["pool_scale"][l].rearrange("(c p) -> c p", p=128), 16),
                    (I["ssm_d"][l].rearrange("(c p) -> c p", p=128), 18),
                    (I["ssm_b_glu"][l].rearrange("(c p) -> c p", p=128), 20)]
            pc = self.load_rows_T(st, rows, "pc")
            self.ckpt("m_pc")
            g1, gatt, gssm, gpool = pc[:, 0:8], pc[:, 8:12], pc[:, 12:14], pc[:, 14:16]
            pscale, dskip, bglu = pc[:, 16:18], pc[:, 18:20], pc[:, 20:22]
            nbglu = sb("nbglu", [128, 2], F32)
            self.TS("dve", nbglu[:], bglu, -1.0, None, ALU.mult, None, ["pccols"], ["nbglu"])
            with contextlib.ExitStack() as t:
                tb = lambda n, s, d: self.sb(t, n, s, d)
                arr = lambda nm: I[nm][l].rearrange("(gp gl) p -> gp (gl p)", gl=2)
                sc = self.load_rows_T(t, [(arr("ssm_a_re"), 0), (arr("ssm_a_im"), 32)], "sc")
                are, aim = sc[:, 0:8], sc[:, 32:40]
                dtr = tb("dtr", [128, 2], F32)
                dtrow = tb("dtrow", [128, 128], F32)
                self.MS("pool", dtr[:], 0.0, ["dtr"])
                self.MS("pool", dtrow[:], 0.0, ["dtrow"])
                self.DMA(dtr[0:8, :], I["ssm_log_dt"][l].rearrange("(gp gl) -> gp gl", gl=2), [], ["dtr"])
                self.ACT(dtr[0:8, :], dtr[0:8, :], AF.Exp, ["dtr"], ["dtr"])
                self.CP("dve", dtrow[0:8, :].rearrange("p (a b) -> p a b", b=64),
                        dtr[0:8, :].unsqueeze(2).to_broadcast([8, 2, 64]), ["dtr"], ["dtrow"])
                psd, pkd = self.psC.next()
                self.TR(psd[:, 0:128], dtrow[:], self.ident_f[:], ["dtrow", "ident"], [pkd])
                dtc = tb("dtc", [128, 8], F32)
                self.CP("dve", dtc[:], psd[:, 0:8], [pkd], ["dtc"])
                th = tb("th", [128, 8], F32)
                lre = tb("lre", [128, 8], F32)
                self.TT("dve", lre[:], are, dtc[:], ALU.mult, ["sccols", "dtc"], ["lre"])
                self.TT("dve", th[:], aim, dtc[:], ALU.mult, ["sccols", "dtc"], ["thX"])
                self.ACT(rho[:], lre[:], AF.Exp, ["lre"], ["rho"])
                s1, c1 = self.sincos(t, th[:], 8, "th")
                abre = tb("abre", [128, 8], F32)
                abim = tb("abim", [128, 8], F32)
                self.TT("dve", abre[:], rho[:], c1[:], ALU.mult, ["rho", "thc"], ["abre"])
                self.TT("dve", abim[:], rho[:], s1[:], ALU.mult, ["rho", "ths"], ["abim"])
                self.TS("dve", abre[:], abre[:], -1.0, None, ALU.add, None, ["abre"], ["abre"])
                den = tb("den", [128, 8], F32)
                t0 = tb("t0", [128, 8], F32)
                core_ = tb("core", [128, 8], F32)
                coim = tb("coim", [128, 8], F32)
                self.TT("dve", den[:], are, are, ALU.mult, ["sccols"], ["den"])
                self.TT("dve", t0[:], aim, aim, ALU.mult, ["sccols"], ["t0"])
                self.TT("dve", den[:], den[:], t0[:], ALU.add, ["den", "t0"], ["den"])
                self.S_.op("dve", lambda e: e.reciprocal(out=den[:], in_=den[:]), ["den"], ["den"])
                self.TT("dve", core_[:], abre[:], are, ALU.mult, ["abre", "sccols"], ["core"])
                self.TT("dve", t0[:], abim[:], aim, ALU.mult, ["abim", "sccols"], ["t0"])
                self.TT("dve", core_[:], core_[:], t0[:], ALU.add, ["core", "t0"], ["core"])
                self.TT("dve", core_[:], core_[:], den[:], ALU.mult, ["core", "den"], ["core"])
                self.TT("dve", coim[:], abim[:], are, ALU.mult, ["abim", "sccols"], ["coim"])
                self.TT("dve", t0[:], abre[:], aim, ALU.mult, ["abre", "sccols"], ["t0"])
                self.TT("dve", coim[:], coim[:], t0[:], ALU.subtract, ["coim", "t0"], ["coim"])
                self.TT("dve", coim[:], coim[:], den[:], ALU.mult, ["coim", "den"], ["coim"])
                self.ckpt("m_co")
                iot = tb("iot", [128, TW], F32)
                self.S_.op("pool", lambda e: e.iota(iot[:], pattern=[[1, TW]], base=0, channel_multiplier=0,
                                                    allow_small_or_imprecise_dtypes=True), [], ["iot"])
                ang = tb("ang", [128, 8, TW], F32)
                self.TT("dve", ang[:], th[:].unsqueeze(2).to_broadcast([128, 8, TW]),
                        iot[:].unsqueeze(1).to_broadcast([128, 8, TW]), ALU.mult, ["thX", "iot"], ["angX"])
                sA, cA = self.sincos(t, ang[:].rearrange("p a b -> p (a b)"), 8 * TW, "ang")
                self.CP("pool", sn[:].rearrange("p a b -> p (a b)"), sA[:], ["angs"], ["sn"])
                self.CP("pool", cs[:].rearrange("p a b -> p (a b)"), cA[:], ["angc"], ["cs"])
                tmpT = tb("tmpT", [128, 8, TW], F32)
                cob = lambda x: x[:].unsqueeze(2).to_broadcast([128, 8, TW])
                self.TT("dve", Tre[:], cs[:], cob(core_), ALU.mult, ["cs", "core"], ["Tre"])
                self.TT("dve", tmpT[:], sn[:], cob(coim), ALU.mult, ["sn", "coim"], ["tmpT"])
                self.TT("dve", Tre[:], Tre[:], tmpT[:], ALU.add, ["Tre", "tmpT"], ["Tre"])
                self.TT("dve", Tim[:], cs[:], cob(coim), ALU.mult, ["cs", "coim"], ["Tim"])
                self.TT("dve", tmpT[:], sn[:], cob(core_), ALU.mult, ["sn", "core"], ["tmpT"])
                self.TT("dve", Tim[:], Tim[:], tmpT[:], ALU.subtract, ["Tim", "tmpT"], ["Tim"])
                self.ckpt("m_tab")
                X4 = tb("X4", [128, 8, 128], F32)
                for nm, dst, isB in (("ssm_b_re", Bre, True), ("ssm_b_im", Bim, True),
                                     ("ssm_c_re", Cre, False), ("ssm_c_im", Cim, False)):
                    self.MS("pool", X4[:], 0.0, ["X4"])
                    for g in range(16):
                        gp, gl = g // 2, g % 2
                        r0 = (gp % 4) * 32
                        if isB:
                            self.DMA(X4[gl * 64:(gl + 1) * 64, gp, r0 + gl * 16:r0 + gl * 16 + 16], I[nm][l, g],
                                     [], ["X4"])
                        else:
                            self.DMA(X4[r0 + gl * 16:r0 + gl * 16 + 16, gp, gl * 64:(gl + 1) * 64], I[nm][l, g],
                                     [], ["X4"])
                    for gp in range(8):
                        psx, pkx = self.psA.next()
                        self.TR(psx[:, 0:128], X4[:, gp, :], self.ident_f[:], ["X4", "ident"], [pkx])
                        self.CP("act", dst[:, gp, :], psx[:, 0:128], [pkx], [nm])
                S_.barrier()
                self.ckpt("m_bc")
            with contextlib.ExitStack() as t:
                tb = lambda n, s, d: self.sb(t, n, s, d)
                NM = NJ * TS_
                v = tb("mv", [128, NM], F32)
                m0 = tb("m0", [128, NM], F32)
                cc = tb("mc", [128, NM], F32)
                qi = tb("mq", [128, NM], I32)
                rr = tb("mr", [128, NM], F32)
                self.S_.op("pool", lambda e: e.iota(v[:], pattern=[[128, NJ], [1, TS_]], base=128 * JMIN,
                                                    channel_multiplier=-1, allow_small_or_imprecise_dtypes=True),
                           [], ["mv"])
                self.TS("dve", m0[:], v[:], 0.0, None, ALU.is_ge, None, ["mv"], ["m0"])
                self.TS("dve", cc[:], v[:], 128.0, None, ALU.is_le, None, ["mv"], ["mc"])
                self.TT("dve", cc[:], cc[:], m0[:], ALU.mult, ["mc", "m0"], ["mc"])
                for dil, lim in ((4, 512.0), (16, 2048.0)):
                    self.TS("dve", qi[:], v[:], 1.0 / dil, None, ALU.mult, None, ["mv"], ["mq"])
                    self.STT(rr[:], qi[:], -float(dil), v[:], ALU.mult, ALU.add, ["mq", "mv"], ["mr"])
                    self.TS("dve", rr[:], rr[:], 0.0, None, ALU.is_equal, None, ["mr"], ["mr"])
                    self.TT("dve", rr[:], rr[:], m0[:], ALU.mult, ["mr", "m0"], ["mr"])
                    self.TS("dve", qi[:], v[:], lim, None, ALU.is_le, None, ["mv"], ["mq"])
                    self.TT("dve", rr[:], rr[:], qi[:], ALU.mult, ["mr", "mq"], ["mr"])
                    self.TT("dve", cc[:], cc[:], rr[:], ALU.add, ["mc", "mr"], ["mc"])
                self.CP("dve", mask[:].rearrange("p a b -> p (a b)"), cc[:], ["mc"], ["mask"])
                self.ckpt("m_mask")
                wcol = tb("wcol", [128, 2], F32)
                for ch, (wa, wb) in enumerate(((2.0, 4.0), (8.0, 16.0))):
                    self.MS("pool", wcol[0:64, ch:ch + 1], wa, ["wcol"])
                    self.MS("pool", wcol[64:128, ch:ch + 1], wb, ["wcol"])
                self.S_.op("dve", lambda e: e.reciprocal(out=icnt[:], in_=wcol[:]), ["wcol"], ["icnt"])
                i16 = tb("i16", [128, 16], F32)
                self.S_.op("pool", lambda e: e.iota(i16[:], pattern=[[1, 16]], base=1, channel_multiplier=0,
                                                    allow_small_or_imprecise_dtypes=True), [], ["i16"])
                self.TT("dve", icnt_tab[:], i16[:].unsqueeze(1).to_broadcast([128, 2, 16]),
                        wcol[:].unsqueeze(2).to_broadcast([128, 2, 16]), ALU.min, ["i16", "wcol"], ["icnt_tab"])
                self.S_.op("dve", lambda e: e.reciprocal(out=icnt_tab[:], in_=icnt_tab[:]), ["icnt_tab"], ["icnt_tab"])
                S_.barrier()
            xT = [sb("xT%d" % i, [128, KC, TS_], F32) for i in range(3)]
            xtoks = [sb("xtok%d" % i, [128, NSUB, D], F32) for i in range(2)]
            sq1 = sb("sq1", [128, KC, TS_], BF16)
            sd1 = sb("sd1", [128, TS_], F32)
            rstd1 = sb("rstd1", [128, TS_], F32)
            sq = sb("sq", [128, KC, TS_], BF16)
            sd = sb("sd", [128, TS_], F32)
            rstd = sb("rstd", [128, TS_], F32)
            hnT = sb("hnT", [128, KC, TS_], BF16)
            qT = sb("qT", [128, 4, TS_], BF16)
            qT_b = sb("qTb", [128, 4, TS_], BF16)
            KT = sb("KT", [128, 4, NSLOT * 128], BF16)
            Vr = sb("Vr", [128, NSLOT, 8, 65], BF16)
            usb = sb("usb", [128, 2, TS_], BF16)
            usf = sb("usf", [128, 2, TS_], F32)
            upl = sb("upl", [128, 2, 16 + TS_], F32)
            usb_b = sb("usbb", [128, 2, TS_], BF16)
            usf_b = sb("usfb", [128, 2, TS_], F32)
            upl_b = sb("uplb", [128, 2, 16 + TS_], F32)
            Pr = Ring([(sb("P%d" % i, [128, 2 * TS_], BF16), "P%d" % i) for i in range(6)])
            rec = sb("rec", [128, NSUB], F32)
            accs = sb("accs", [128, 8, NSUB * 65], F32)
            rec8 = sb("rec8", [128, 8, NSUB], F32)
            att = sb("att", [128, NSUB, 512], F32)
            att_b = sb("attb", [128, NSUB, 512], F32)
            ssa = sb("ssa", [128, NSUB], F32)
            junk = sb("junk", [128, 512], BF16)
            mixT = sb("mixT", [128, KC, TS_], BF16)
            mixT_b = sb("mixTb", [128, KC, TS_], BF16)

            class _B:
                pass
            bs = []
            for i_, (q_, ub_, uf_, up_, at_, mx_) in enumerate(((qT, usb, usf, upl, att, mixT),
                                                                 (qT_b, usb_b, usf_b, upl_b, att_b, mixT_b))):
                B_ = _B()
                B_.qT, B_.usb, B_.usf, B_.upl, B_.att, B_.mixT = q_, ub_, uf_, up_, at_, mx_
                B_.qk, B_.ubk, B_.ufk, B_.upk, B_.ak, B_.mk = ("qT%d" % i_, "usb%d" % i_, "usf%d" % i_, "upl%d" % i_,
                                                              "att%d" % i_, "mixT%d" % i_)
                bs.append(B_)
            Gre = sb("Gre", [128, 8, TS_], F32)
            Gim = sb("Gim", [128, 8, TS_], F32)
            ta = sb("ta", [128, 8, TS_], F32)
            tb2 = sb("tb2", [128, 8, TS_], F32)
            hre = sb("hre", [128, 8, TS_], BF16)
            himn = sb("himn", [128, 8, TS_], BF16)
            wtmp = Ring([(sb("wt%d" % i, [128, TS_], F32), "wt%d" % i) for i in range(6)])
            gst = sb("gst", [128, 2, 8], F32)
            gl_t = sb("gl_t", [128, 4, 8], F32)
            hl = sb("hl", [128, 2, 8], F32)
            yss = sb("yss", [128, 2, TS_], F32)
            gact = sb("gact", [128, 2, TS_], F32)
            gactb = sb("gactb", [128, 2, TS_], BF16)
            sgl = sb("sgl", [128, 2, TS_], F32)
            pl1 = sb("pl1", [128, 2, 16 + TS_], F32)
            pl2 = sb("pl2", [128, 2, 16 + TS_], F32)
            wsel = sb("wsel", [128, 2, TS_], F32)
            pooled = sb("pooled", [128, 2, TS_], BF16)
            pout = sb("pout", [128, 2, TS_], F32)
            t16 = sb("t16", [128, 2, 16], F32)
            kvst = Ring([(sb("kvst%d" % i, [128, 512], F32), "kvst%d" % i) for i in range(2)])
            kts = sb("kts", [128, 4, NS], BF16)
            h0c = sb("h0c", [128, 2, 8], F32)
            ups = sb("ups", [128, 2, NS], F32)
            kcs = Ring([(sb("kcs%d" % i, [128, 512], F32), "kcs%d" % i) for i in range(2)])
            rmstmp = (sq, sd, rstd)
            self.ckpt("m_icnt")
            if l == 0:
                print("SBUF remaining after mix alloc:", self.nc.sbuf_bytes_remaining)
            self.MS("pool", Vr[:], 1.0, ["vall"])
            S_.barrier()
            self.ckpt("m_work")

            def load_dma(dst, tok0, NT, from_tok, src_tok=None, xi=0):
                key = "xT%d" % xT.index(dst)
                if from_tok:
                    xtok = xtoks[xi % 2]
                    nsub = (NT + 127) // 128
                    w = min(128, NT)
                    for sub in range(nsub):
                        self.DMA(xtok[0:w, sub, :], src_tok[tok0 + sub * 128: tok0 + sub * 128 + w, :], [], ["xtok%d" % (xi % 2)])
                else:
                    self.DMA(dst[:, :, 0:NT], self.scr[rd, :, :, tok0:tok0 + NT].rearrange("c p t -> p c t"),
                             [], [key])
                return key

            def load_tr(dst, NT, from_tok, xi=0):
                key = "xT%d" % xT.index(dst)
                if from_tok:
                    xtok = xtoks[xi % 2]
                    nsub = (NT + 127) // 128
                    w = min(128, NT)
                    for c in range(KC):
                        ps, pk = self.psA.next()
                        for sub in range(nsub):
                            self.TR(ps[:, sub * 128: sub * 128 + w], xtok[0:w, sub, c * 128:(c + 1) * 128],
                                    self.ident_f[0:w, 0:w], ["xtok%d" % (xi % 2), "ident"], [pk])
                        self.CP("act" if c % 2 else "dve", dst[:, c, 0:NT], ps[:, 0:NT], [pk], [key])
                return key

            def load_xT(dst, tok0, NT, from_tok, src_tok=None):
                load_dma(dst, tok0, NT, from_tok, src_tok)
                return load_tr(dst, NT, from_tok)

            def in_proj(xk, x, NT, tmp=None):
                self.rms_feat(x, KC, NT, g1, hnT, D, [xk], "hnT", tmp or rmstmp)
                self.ckpt("t_rms")
                outs = {}
                for oc in (0, 1, 2, 3, 4, 5, 6, 7, 12, 13, 14, 15):
                    ps, pk = self.psA.next()
                    for kc in range(KC):
                        self.MM(ps[:, 0:NT], win[:, kc, oc * 128:(oc + 1) * 128], hnT[:, kc, 0:NT], kc == 0,
                                kc == KC - 1, ["win", "hnT"], [pk])
                    outs[oc] = (ps, pk)
                    yield oc, ps, pk

            def attention(qcol0, NQ, ktiles, B, LA=4):
                nsub = (NQ + 127) // 128
                qw = min(128, NQ)
                pending = []

                def flush_one():
                    item = pending.pop(0)
                    if item[0] == "pv":
                        _, acc, acck, P, Pk, nk, slot, h, subs, first, last, pc0 = item
                        for i_, s_ in enumerate(subs):
                            self.MM(acc[0:qw, s_ * 65:(s_ + 1) * 65], P[0:nk, pc0 + s_ * 128: pc0 + s_ * 128 + qw],
                                    Vr[0:nk, slot, h, :], first and i_ == 0, last and i_ == len(subs) - 1,
                                    [Pk, "v%d" % slot, "vall"], [acck])
                    else:
                        _, acc, acck, h = item
                        self.CP("act", accs[0:qw, h, 0:nsub * 65], acc[0:qw, 0:nsub * 65], [acck], ["accs"])

                mm_i = [0]
                gq = []
                for h in range(8):
                    hp, hc = (h % 2) * 64, h // 2
                    acc, acck = self.psB.next()
                    plan = []
                    for (slot, nk, jj) in ktiles:
                        subs = [s_ for s_ in range(nsub) if 0 <= jj + s_ <= 16]
                        if subs:
                            plan.append((slot, nk, jj, subs))
                    groups = []
                    i_ = 0
                    while i_ < len(plan):
                        if (nsub == 1 and i_ + 1 < len(plan) and plan[i_][1] == 128 and plan[i_ + 1][1] == 128
                                and plan[i_ + 1][2] == plan[i_][2] - 1):
                            groups.append([plan[i_ + 1], plan[i_]])
                            i_ += 2
                        else:
                            groups.append([plan[i_]])
                            i_ += 1
                    npv = len(plan)
                    ipv = 0
                    for grp in groups:
                        ng = len(grp)
                        nk = grp[0][1]
                        scp, sck = self.psS.next()
                        P, Pk = Pr.next()
                        for gi_, (slot, nk_, jj, subs) in enumerate(grp):
                            self.MM(scp[0:nk, gi_ * NQ:(gi_ + 1) * NQ], KT[hp:hp + 64, hc, slot * 128: slot * 128 + nk],
                                    B.qT[hp:hp + 64, hc, qcol0:qcol0 + NQ], gi_ == 0, gi_ == ng - 1,
                                    ["kt%d" % slot, B.qk], [sck])
                        self.ACT(P[0:nk, 0:ng * NQ], scp[0:nk, 0:ng * NQ], AF.Exp, [sck], [Pk], scale=0.125)
                        m0_ = grp[0][2] - JMIN
                        mm_i[0] += 1
                        self.TT("pool",
                                P[0:nk, 0:ng * NQ].rearrange("p (a q) -> p a q", a=ng),
                                P[0:nk, 0:ng * NQ].rearrange("p (a q) -> p a q", a=ng),
                                mask[0:nk, m0_:m0_ + ng, 0:NQ], ALU.mult, [Pk, "mask"], [Pk])
                        for gi_, (slot, nk_, jj, subs) in enumerate(grp):
                            pending.append(("pv", acc, acck, P, Pk, nk, slot, h, subs, ipv == 0, ipv == npv - 1, gi_ * NQ))
                            ipv += 1
                        if ipv == npv:
                            pending.append(("fin", acc, acck, h))
                        gq.append(ng)
                        while len(gq) > LA:
                            for _ in range(gq.pop(0)):
                                while pending[0][0] != "pv":
                                    flush_one()
                                flush_one()
                        yield
                while pending:
                    flush_one()
                a4 = accs[0:qw, :, 0:nsub * 65].rearrange("p h (s e) -> p h s e", e=65)
                for s_ in range(nsub):
                    self.S_.op("dve", lambda e, s_=s_: e.reciprocal(out=rec8[0:qw, :, s_:s_ + 1], in_=a4[:, :, s_, 64:65]),
                               ["accs"], ["rec8"])
                    self.TT("dve", B.att[0:qw, s_, :].rearrange("p (h e) -> p h e", e=64), a4[:, :, s_, 0:64],
                            rec8[0:qw, :, s_:s_ + 1].to_broadcast([qw, 8, 64]), ALU.mult, ["accs", "rec8"], [B.ak])
                yield

            def interleave(gens, weights):
                alive = list(gens)
                wts = list(weights)
                while alive:
                    for i in range(len(alive) - 1, -1, -1):
                        pass
                    nxt_alive, nxt_w = [], []
                    for g_, w_ in zip(alive, wts):
                        ok = True
                        for _ in range(w_):
                            try:
                                next(g_)
                            except StopIteration:
                                ok = False
                                break
                        if ok:
                            nxt_alive.append(g_)
                            nxt_w.append(w_)
                    alive, wts = nxt_alive, nxt_w

            def sched_tile(att_g, n_att, others):
                late_start = int(0.5 * n_att)
                held = [False] * len(others)
                alive = [True] * len(others)
                i = 0

                def adv(j):
                    if not alive[j]:
                        return
                    try:
                        tag = next(others[j])
                        if tag == "late":
                            held[j] = True
                    except StopIteration:
                        alive[j] = False

                for _ in att_g:
                    i += 1
                    for j in range(len(others)):
                        if i % (3 + 2 * j) == 0 and alive[j] and not (held[j] and i < late_start):
                            adv(j)
                for j in range(len(others)):
                    while alive[j]:
                        adv(j)

            def att_finish(col0, NQ, B):
                nsub = (NQ + 127) // 128
                qw = min(128, NQ)
                for s_ in range(nsub):
                    self.ACT(junk[0:qw, :], B.att[0:qw, s_, :], AF.Square, [B.ak], ["junk", "ssa"], accum=ssa[0:qw, s_:s_ + 1])
                self.ACT(ssa[0:qw, 0:nsub], ssa[0:qw, 0:nsub], AF.Ln, ["junk", "ssa"], ["ssa"], bias=self.eps_col[0:qw, 0:1],
                         scale=1.0 / 512)
                self.ACT(ssa[0:qw, 0:nsub], ssa[0:qw, 0:nsub], AF.Exp, ["ssa"], ["ssa"], scale=-0.5)
                self.TT("dve", B.att[0:qw, 0:nsub, :], B.att[0:qw, 0:nsub, :],
                        ssa[0:qw, 0:nsub].unsqueeze(2).to_broadcast([qw, nsub, 512]), ALU.mult, [B.ak, "ssa"], [B.ak])
                for c in range(4):
                    ps, pk = self.psA.next()
                    for s_ in range(nsub):
                        self.TR(ps[:, s_ * 128: s_ * 128 + qw], B.att[0:qw, s_, c * 128:(c + 1) * 128],
                                self.ident_f[0:qw, 0:qw], [B.ak, "ident"], [pk])
                    self.TS("dve", B.mixT[:, c, col0:col0 + NQ], ps[:, 0:NQ], gatt[:, c:c + 1], None, ALU.mult, None,
                            [pk, "pccols"], [B.mk])

            def ssm(col0, NT, first_zero, B):
                for gp in range(8):
                    ch = gp // 4
                    pre, prk = self.psA.next()
                    self.MM(pre[:, 0:NT], Bre[:, gp, :], B.usb[:, ch, col0:col0 + NT], True, True, [B.ubk, "ssm_b_re"], [prk])
                    pim, pik = self.psA.next()
                    self.MM(pim[:, 0:NT], Bim[:, gp, :], B.usb[:, ch, col0:col0 + NT], True, True, [B.ubk, "ssm_b_im"], [pik])
                    wre, wrk = wtmp.next()
                    wim, wik = wtmp.next()
                    t1, t1k = wtmp.next()
                    self.TT("dve", wre[:, 0:NT], pre[:, 0:NT], Tre[:, gp, 0:NT], ALU.mult, [prk, "Tre"], [wrk])
                    self.TT("dve", t1[:, 0:NT], pim[:, 0:NT], Tim[:, gp, 0:NT], ALU.mult, [pik, "Tim"], [t1k])
                    self.TT("dve", wre[:, 0:NT], wre[:, 0:NT], t1[:, 0:NT], ALU.subtract, [wrk, t1k], [wrk])
                    t2, t2k = wtmp.next()
                    self.TT("dve", wim[:, 0:NT], pim[:, 0:NT], Tre[:, gp, 0:NT], ALU.mult, [pik, "Tre"], [wik])
                    self.TT("dve", t2[:, 0:NT], pre[:, 0:NT], Tim[:, gp, 0:NT], ALU.mult, [prk, "Tim"], [t2k])
                    self.TT("dve", wim[:, 0:NT], wim[:, 0:NT], t2[:, 0:NT], ALU.add, [wik, t2k], [wik])
                    for pl, (G, wv, wk) in enumerate(((Gre, wre, wrk), (Gim, wim, wik))):
                        self.S_.op("dve", lambda e, G=G, wv=wv, pl=pl, gp=gp: e.tensor_tensor_scan(
                            out=G[:, gp, 0:NT], data0=rho[:, gp:gp + 1].to_broadcast([128, NT]), data1=wv[:, 0:NT],
                            initial=gst[:, pl, gp:gp + 1], op0=ALU.mult, op1=ALU.add),
                            [wk, "rho", "gst"], ["G%d" % pl])
                    yield
                csn, snn = cs[:, :, 0:NT], sn[:, :, 0:NT]
                self.TT("dve", ta[:, :, 0:NT], Gre[:, :, 0:NT], csn, ALU.mult, ["G0", "cs"], ["ta"])
                self.TT("dve", tb2[:, :, 0:NT], Gim[:, :, 0:NT], snn, ALU.mult, ["G1", "sn"], ["tb2"])
                self.TT("dve", hre[:, :, 0:NT], ta[:, :, 0:NT], tb2[:, :, 0:NT], ALU.subtract, ["ta", "tb2"], ["hre"])
                yield
                self.TT("dve", ta[:, :, 0:NT], Gim[:, :, 0:NT], csn, ALU.mult, ["G1", "cs"], ["ta"])
                self.TT("dve", tb2[:, :, 0:NT], Gre[:, :, 0:NT], snn, ALU.mult, ["G0", "sn"], ["tb2"])
                self.STT(himn[:, :, 0:NT], ta[:, :, 0:NT], -1.0, tb2[:, :, 0:NT], ALU.mult, ALU.subtract,
                         ["ta", "tb2"], ["himn"])
                yield
                glr, gli = Gre[:, :, NT - 1], Gim[:, :, NT - 1]
                for dst, col in ((gst, NT), (hl, NT - 1)):
                    cN, sN = cs[:, :, col], sn[:, :, col]
                    self.TT("dve", gl_t[:, 0, :], glr, cN, ALU.mult, ["G0", "cs"], ["glt0"])
                    self.TT("dve", gl_t[:, 1, :], gli, sN, ALU.mult, ["G1", "sn"], ["glt1"])
                    self.TT("dve", gl_t[:, 2, :], gli, cN, ALU.mult, ["G1", "cs"], ["glt2"])
                    self.TT("dve", gl_t[:, 3, :], glr, sN, ALU.mult, ["G0", "sn"], ["glt3"])
                    dk = "gst" if dst is gst else "hl"
                    self.TT("dve", dst[:, 0, :], gl_t[:, 0, :], gl_t[:, 1, :], ALU.subtract, ["glt0", "glt1"], [dk])
                    self.TT("dve", dst[:, 1, :], gl_t[:, 2, :], gl_t[:, 3, :], ALU.add, ["glt2", "glt3"], [dk])
                if self.dbg and ph == 0 and col0 == 0 and not getattr(self, "_dumped", False):
                    self._dumped = True
                    dd = self.O["dbg"]
                    for i, (tt, kk) in enumerate(((Tre, "Tre"), (Tim, "Tim"), (cs, "cs"), (sn, "sn"))):
                        self.DMA(dd[5 + i, :, :, 0:TW].rearrange("c p t -> p c t"), tt[:, :, :], [kk], [])
                    self.DMA(dd[9, :, :, 0:NT].rearrange("c p t -> p c t"), Gre[:, :, 0:NT], ["G0"], [])
                    self.DMA(dd[10, :, :, 0:NT].rearrange("c p t -> p c t"), Gim[:, :, 0:NT], ["G1"], [])
                    self.DMA(dd[11, 0, :, 0:8], rho[:, :], ["rho"], [])
                    self.DMA(dd[11, 1, :, 0:NT], B.usf[:, 0, 0:NT], [B.ufk], [])
                    self.DMA(dd[11, 2, :, 0:NT], B.usf[:, 1, 0:NT], [B.ufk], [])
                yield "late"
                for ch in range(2):
                    yp_, ypk = self.psC.next()
                    for i, gp in enumerate(range(ch * 4, ch * 4 + 4)):
                        self.MM(yp_[:, 0:NT], Cre[:, gp, :], hre[:, gp, 0:NT], i == 0, False, ["hre", "ssm_c_re"], [ypk])
                        self.MM(yp_[:, 0:NT], Cim[:, gp, :], himn[:, gp, 0:NT], False, i == 3, ["himn", "ssm_c_im"], [ypk])
                    self.STT(yss[:, ch, 0:NT], B.usf[:, ch, col0:col0 + NT], dskip[:, ch:ch + 1], yp_[:, 0:NT], ALU.mult,
                             ALU.add, [ypk, B.ufk, "pccols"], ["yss"])
                    yield
                yv, gv, sv = yss[:, :, 0:NT], gact[:, :, 0:NT], sgl[:, :, 0:NT]
                self.TT("dve", gv, yv, yv, ALU.mult, ["yss"], ["gact"])
                self.TS("dve", gv, gv, 0.044715, 1.0, ALU.mult, ALU.add, ["gact"], ["gact"])
                self.TT("dve", gv, gv, yv, ALU.mult, ["gact", "yss"], ["gact"])
                self.ACT(sv, gv, AF.Exp, ["gact"], ["sgl"], scale=-1.5957691216057308)
                self.TS("dve", sv, sv, 1.0, None, ALU.add, None, ["sgl"], ["sgl"])
                self.S_.op("dve", lambda e, sv=sv: e.reciprocal(out=sv, in_=sv), ["sgl"], ["sgl"])
                self.TT("dve", gv, yv, sv, ALU.mult, ["yss", "sgl"], ["gact"])
                self.CP("dve", gactb[:, :, 0:NT], gv, ["gact"], ["gactb"])
                yield
                for oc in range(2):
                    zp, zk = self.psC.next()
                    for kc in range(2):
                        self.MM(zp[:, 0:NT], wglu[:, kc, oc * 128:(oc + 1) * 128], gactb[:, kc, 0:NT], kc == 0, kc == 1,
                                ["wglu", "gactb"], [zk])
                    self.ACT(sgl[:, oc, 0:NT], zp[:, 0:NT], AF.Exp, [zk], ["sgl"], bias=nbglu[:, oc:oc + 1], scale=-1.0)
                self.TS("dve", sv, sv, 1.0, None, ALU.add, None, ["sgl"], ["sgl"])
                self.S_.op("dve", lambda e, sv=sv: e.reciprocal(out=sv, in_=sv), ["sgl"], ["sgl"])
                self.TT("dve", yss[:, :, 0:NT], gv, sv, ALU.mult, ["gact", "sgl"], ["yss"])
                yield
                self.rms_feat(yss, 2, NT, gssm, _View(B.mixT, 4, col0), 256, ["yss"], B.mk, rmstmp)
                yield

            def pool_mix(col0, NT, first, B):
                W = 16 + NT
                self.TT("dve", pl1[:, :, 1:W], B.upl[:, :, 1:W], B.upl[:, :, 0:W - 1], ALU.add, [B.upk], ["pl1"])
                self.TT("dve", pl2[:, :, 3:W], pl1[:, :, 3:W], pl1[:, :, 1:W - 2], ALU.add, ["pl1"], ["pl2"])
                self.CP("dve", wsel[0:64, 0, 0:NT], pl1[0:64, 0, 16:W], ["pl1"], ["wsel"])
                self.CP("dve", wsel[64:128, 0, 0:NT], pl2[64:128, 0, 16:W], ["pl2"], ["wsel"])
                yield
                self.TT("dve", pl1[:, :, 7:W], pl2[:, :, 7:W], pl2[:, :, 3:W - 4], ALU.add, ["pl2", "wsel"], ["pl1"])
                self.TT("dve", pl2[:, :, 15:W], pl1[:, :, 15:W], pl1[:, :, 7:W - 8], ALU.add, ["pl1", "wsel"], ["pl2"])
                self.CP("dve", wsel[0:64, 1, 0:NT], pl1[0:64, 1, 16:W], ["pl1"], ["wsel"])
                self.CP("dve", wsel[64:128, 1, 0:NT], pl2[64:128, 1, 16:W], ["pl2"], ["wsel"])
                yield
                for ch in range(2):
                    self.STT(pooled[:, ch, 0:NT], wsel[:, ch, 0:NT], icnt[:, ch:ch + 1], B.upl[:, ch, 16:W], ALU.mult,
                             ALU.subtract, ["wsel", "icnt", B.upk], ["pooled"])
                if first:
                    self.TT("dve", t16[:], wsel[:, :, 0:16], icnt_tab[:], ALU.mult, ["wsel", "icnt_tab"], ["t16"])
                    self.TT("dve", pooled[:, :, 0:16], t16[:], B.upl[:, :, 16:32], ALU.subtract, ["t16", B.upk], ["pooled"])
                yield "late"
                for ch in range(2):
                    pp, ppk = self.psC.next()
                    self.MM(pp[:, 0:NT], poolw[:, ch, :], pooled[:, ch, 0:NT], True, True, ["poolw", "pooled"], [ppk])
                    self.TS("dve", pout[:, ch, 0:NT], pp[:, 0:NT], pscale[:, ch:ch + 1], None, ALU.mult, None,
                            [ppk, "pccols"], ["pout"])
                yield
                self.rms_feat(pout, 2, NT, gpool, _View(B.mixT, 6, col0), 256, ["pout"], B.mk, rmstmp)
                yield

            def out_proj(xk, x, NT, tok0, B):
                for oc in range(KC):
                    ps, pk = self.psA.next()
                    for kc in range(KC):
                        self.MM(ps[:, 0:NT], wout[:, kc, oc * 128:(oc + 1) * 128], B.mixT[:, kc, 0:NT], kc == 0,
                                kc == KC - 1, ["wout", B.mk], [pk])
                    self.TT("dve", x[:, oc, 0:NT], ps[:, 0:NT], x[:, oc, 0:NT], ALU.add, [pk, xk], [xk])
                    yield
                self.DMA(self.scr[wr, :, :, tok0:tok0 + NT].rearrange("c p t -> p c t"), x[:, :, 0:NT], [xk], [])
                if self.dbg:
                    self.DMA(self.O["dbg"][ph, :, :, tok0:tok0 + NT].rearrange("c p t -> p c t"), x[:, :, 0:NT], [xk], [])
                    if ph == 0:
                        self.DMA(self.O["dbg"][4, :, :, tok0:tok0 + NT].rearrange("c p t -> p c t"), B.mixT[:, :, 0:NT], [B.mk], [], q="pool")

            def kv_tokmajor(NT, cols, is_v, slots, out_aps):
                nsub = len(cols)
                for s_, (c0, w) in enumerate(cols):
                    ps, pk = self.psA.next()
                    off = 1024 if is_v else 512
                    for kc in range(KC):
                        self.MM(ps[0:w, 0:512], hnT[:, kc, c0:c0 + w], win[:, kc, off:off + 512], kc == 0, kc == KC - 1,
                                ["hnT", "win"], [pk])
                    if is_v and slots[s_] is not None:
                        self.CP("act", Vr[0:w, slots[s_], :, 0:64], ps[0:w, 0:512].rearrange("p (h e) -> p h e", e=64),
                                [pk], ["v%d" % slots[s_]])
                    if out_aps[s_] is not None:
                        stg, sk = kvst.next()
                        self.CP("dve", stg[0:w, :], ps[0:w, 0:512], [pk], [sk])
                        self.DMA(out_aps[s_], stg[0:w, :], [sk], [])

            self.MS("pool", gst[:], 0.0, ["gst"])
            self.MS("pool", upl[:], 0.0, [bs[0].upk])
            self.MS("pool", upl_b[:], 0.0, [bs[1].upk])
            nst = S // TS_
            keep0 = S - self.NKEEP
            rmstmp1 = (sq1, sd1, rstd1)
            xkeys = {}

            def P1(sti):
                x, B = xT[sti % 3], bs[sti % 2]
                tok0 = sti * TS_
                xk = load_tr(x, TS_, ph == 0, sti)
                xkeys[sti] = xk
                if sti > 0:
                    Bp = bs[(sti - 1) % 2]
                    self.CP("dve", B.upl[:, :, 1:16], Bp.upl[:, :, TS_ + 1:TS_ + 16], [Bp.upk], [B.upk])
                yield
                a0 = tok0 // 128
                slots = [(a0 + s_) % NSLOT for s_ in range(NSUB)]
                for oc, ps, pk in in_proj(xk, x, TS_, rmstmp1):
                    if oc < 4:
                        self.CP("act", B.qT[:, oc, 0:TS_], ps[:, 0:TS_], [pk], [B.qk])
                    elif oc < 8:
                        for s_ in range(NSUB):
                            self.CP("act" if s_ % 2 else "dve", KT[:, oc - 4, slots[s_] * 128:(slots[s_] + 1) * 128],
                                    ps[:, s_ * 128:(s_ + 1) * 128], [pk], ["kt%d" % slots[s_]])
                    elif oc < 14:
                        self.CP("act", B.usb[:, oc - 12, 0:TS_], ps[:, 0:TS_], [pk], [B.ubk])
                        self.CP("dve", B.usf[:, oc - 12, 0:TS_], ps[:, 0:TS_], [pk], [B.ufk])
                    else:
                        self.CP("dve", B.upl[:, oc - 14, 16:16 + TS_], ps[:, 0:TS_], [pk], [B.upk])
                    yield
                cols = [(s_ * 128, 128) for s_ in range(NSUB)]
                outs_v = [O["vp"][l, tok0 + s_ * 128 - keep0: tok0 + s_ * 128 - keep0 + 128, :]
                          if tok0 + s_ * 128 >= keep0 else None for s_ in range(NSUB)]
                outs_k = [O["kp"][l, tok0 + s_ * 128 - keep0: tok0 + s_ * 128 - keep0 + 128, :]
                          if tok0 + s_ * 128 >= keep0 else None for s_ in range(NSUB)]
                kv_tokmajor(TS_, cols, True, slots, outs_v)
                yield
                if any(o is not None for o in outs_k):
                    kv_tokmajor(TS_, cols, False, [None] * NSUB, outs_k)
                yield

            def P3(sti):
                x, B = xT[sti % 3], bs[sti % 2]
                att_finish(0, TS_, B)
                yield
                yield from out_proj(xkeys[sti], x, TS_, sti * TS_, B)

            def drain(g):
                for _ in g:
                    pass

            def sched2(att_g, n_att, sides):
                late_start = int(0.5 * n_att)
                st_ = [{"g": g, "p": p, "s": s0, "h": h, "held": False, "alive": True} for (g, p, s0, h) in sides]

                def adv(d):
                    try:
                        if next(d["g"]) == "late":
                            d["held"] = True
                    except StopIteration:
                        d["alive"] = False

                i = 0
                for _ in att_g:
                    i += 1
                    for d in st_:
                        if d["alive"] and i >= d["s"] and (i - d["s"]) % d["p"] == 0 and not (d["h"] and d["held"] and i < late_start):
                            adv(d)
                for d in st_:
                    while d["alive"]:
                        adv(d)

            for i_ in range(min(2, nst)):
                load_dma(xT[i_ % 3], i_ * TS_, TS_, ph == 0, I["xp"], i_)
            drain(P1(0))
            for sti in range(nst):
                B = bs[sti % 2]
                a0 = sti * TS_ // 128
                ktiles = [(a % NSLOT, 128, a0 - a) for a in range(max(0, a0 - 16), a0 + NSUB)]
                n_att = 8 * ((len(ktiles) + 1) // 2) + 1
                sides = []
                if sti > 0:
                    sides.append((P3(sti - 1), 2, 1, False))
                sides.append((ssm(0, TS_, sti == 0, B), 3, 2, True))
                sides.append((pool_mix(0, TS_, sti == 0, B), 6, 4, True))
                if sti + 1 < nst:
                    sides.append((P1(sti + 1), 3, 12, False))
                sched2(attention(0, TS_, ktiles, B), n_att, sides)
                if sti + 2 < nst:
                    load_dma(xT[(sti + 2) % 3], (sti + 2) * TS_, TS_, ph == 0, I["xp"], sti + 2)
            drain(P3(nst - 1))
            Bl = bs[(nst - 1) % 2]
            self.ckpt("m_prompt")
            for gp in range(8):
                for pl, nm in ((0, "rep"), (1, "imp")):
                    dst = bass.AP(O[nm].tensor, O[nm][l].offset + gp * 128, [[1, 128], [1, 1]])
                    self.DMA(dst, hl[:, pl, gp:gp + 1], ["hl"], [])
            for ch in range(2):
                for r_ in range(15):
                    dst = bass.AP(O["poolp"].tensor, O["poolp"][l, r_].offset + ch * 128, [[1, 128], [1, 1]])
                    self.DMA(dst, Bl.upl[:, ch, TS_ + 1 + r_: TS_ + 2 + r_], [Bl.upk], [], q="act")

            self.ckpt("m_pout")
            xs_ = xT[nst % 3]
            B0 = bs[0]
            xk = load_xT(xs_, 0 if ph == 0 else S, NS, ph == 0, I["xs"])
            for oc, ps, pk in in_proj(xk, xs_, NS):
                if oc < 4:
                    self.CP("act", B0.qT[:, oc, 0:NS], ps[:, 0:NS], [pk], [B0.qk])
                elif oc < 8:
                    self.CP("act", kts[:, oc - 4, :], ps[:, 0:NS], [pk], ["kts"])
                elif oc < 14:
                    self.CP("act", B0.usb[:, oc - 12, 0:NS], ps[:, 0:NS], [pk], [B0.ubk])
                    self.CP("dve", B0.usf[:, oc - 12, 0:NS], ps[:, 0:NS], [pk], [B0.ufk])
                else:
                    self.CP("dve", ups[:, oc - 14, 0:NS], ps[:, 0:NS], [pk], ["ups"])
            for s in range(NSEQ):
                c0 = s * TD
                for m in range(16):
                    kc_, kck = kcs.next()
                    self.DMA(kc_[:], I["ck"][l, s, m * 128:(m + 1) * 128, :], [], [kck])
                    psk, pkk = self.psA.next()
                    for c in range(4):
                        self.TR(psk[:, c * 128:(c + 1) * 128], kc_[:, c * 128:(c + 1) * 128], self.ident_f[:],
                                [kck, "ident"], [pkk])
                    self.CP("act" if m % 2 else "dve", KT[:, :, m * 128:(m + 1) * 128],
                            psk[:, 0:512].rearrange("p (c k) -> p c k", k=128), [pkk], ["kt%d" % m])
                    self.DMA(Vr[:, m, :, 0:64], I["cv"][l, s, m * 128:(m + 1) * 128, :].rearrange("k (h e) -> k h e", e=64),
                             [], ["v%d" % m], q="pool")
                self.CP("dve", KT[:, :, 16 * 128:16 * 128 + TD], kts[:, :, c0:c0 + TD], ["kts"], ["kt16"])
                kv_tokmajor(TD, [(c0, TD)], True, [16], [O["vs"][l, s]])
                kv_tokmajor(TD, [(c0, TD)], False, [None], [O["ks"][l, s]])
                ktiles = [(m, 128, 16 - m) for m in range(16)] + [(16, TD, 0)]
                interleave([attention(c0, TD, ktiles, B0)], [1])
                att_finish(c0, TD, B0)
                for gp in range(8):
                    for pl, nm in ((0, "sre"), (1, "sim")):
                        src = bass.AP(I[nm].tensor, I[nm][l, s].offset + gp * 128, [[1, 128], [1, 1]])
                        self.DMA(h0c[:, pl, gp:gp + 1], src, [], ["h0c"])
                c1_, s1_ = cs[:, :, 1], sn[:, :, 1]
                self.TT("pool", gl_t[:, 0, :], h0c[:, 0, :], c1_, ALU.mult, ["h0c", "cs"], ["glt0"])
                self.TT("pool", gl_t[:, 1, :], h0c[:, 1, :], s1_, ALU.mult, ["h0c", "sn"], ["glt1"])
                self.TT("pool", gl_t[:, 2, :], h0c[:, 1, :], c1_, ALU.mult, ["h0c", "cs"], ["glt2"])
                self.TT("pool", gl_t[:, 3, :], h0c[:, 0, :], s1_, ALU.mult, ["h0c", "sn"], ["glt3"])
                self.TT("pool", gst[:, 0, :], gl_t[:, 0, :], gl_t[:, 1, :], ALU.subtract, ["glt0", "glt1"], ["gst"])
                self.TT("pool", gst[:, 1, :], gl_t[:, 2, :], gl_t[:, 3, :], ALU.add, ["glt2", "glt3"], ["gst"])
                interleave([ssm(c0, TD, False, B0)], [1])
                for gp in range(8):
                    for pl, nm in ((0, "res"), (1, "ims")):
                        dst = bass.AP(O[nm].tensor, O[nm][l, s].offset + gp * 128, [[1, 128], [1, 1]])
                        self.DMA(dst, hl[:, pl, gp:gp + 1], ["hl"], [])
                for ch in range(2):
                    for r_ in range(15):
                        src = bass.AP(I["spool"].tensor, I["spool"][l, s, r_].offset + ch * 128, [[1, 128], [1, 1]])
                        self.DMA(B0.upl[:, ch, 1 + r_:2 + r_], src, [], [B0.upk], q="act")
                self.CP("dve", B0.upl[:, :, 16:16 + TD], ups[:, :, c0:c0 + TD], ["ups"], [B0.upk])
                for ch in range(2):
                    for r_ in range(15):
                        dst = bass.AP(O["pools"].tensor, O["pools"][l, s, r_].offset + ch * 128, [[1, 128], [1, 1]])
                        self.DMA(dst, B0.upl[:, ch, TD + 1 + r_: TD + 2 + r_], [B0.upk], [], q="act")
                interleave([pool_mix(c0, TD, False, B0)], [1])
            drain(out_proj(xk, xs_, NS, S, B0))
            S_.barrier()

    def ffn_phase(self, l, ph, last):
        nc, S_, I, O = self.nc, self.S_, self.I, self.O
        S, TS_ = self.S, self.TSF
        rd, wr = (ph - 1) % 2, ph % 2
        self.rms_mode = "pow"
        self.set_rings(False)
        with contextlib.ExitStack() as st:
            sb = lambda n, s, d: self.sb(st, n, s, d)
            wup = sb("wup", [128, KC, 2 * DFF], BF16)
            wdn = sb("wdn", [128, AC, D], BF16)
            for c in range(KC):
                for hh in range(4):
                    self.DMA(wup[:, c, hh * 1408:(hh + 1) * 1408], I["w_up"][l, c * 128:(c + 1) * 128, hh * 1408:(hh + 1) * 1408],
                             [], ["wup"], q="pool")
            for c in range(AC):
                self.DMA(wdn[:, c, :], I["w_down"][l, c * 128:(c + 1) * 128, :], [], ["wdn"], q="pool")
            pa = self.load_rows_T(st, [(I["norm2_g"][l].rearrange("(c p) -> c p", p=128), 0),
                                       (I["norm_f_g"].rearrange("(c p) -> c p", p=128), 8),
                                       (I["conv_b"][l].rearrange("(c p) -> c p", p=128), 16),
                                       (I["conv_w"][l, 0].rearrange("(c p) -> c p", p=128), 64)], "pa")
            pb = self.load_rows_T(st, [(I["conv_w"][l, 1].rearrange("(c p) -> c p", p=128), 0),
                                       (I["conv_w"][l, 2].rearrange("(c p) -> c p", p=128), 64)], "pb")
            g2, gf, cb, cw0, cw1, cw2 = pa[:, 0:8], pa[:, 8:16], pa[:, 16:60], pa[:, 64:108], pb[:, 0:44], pb[:, 64:108]
            xT = [sb("fx%d" % i, [128, KC, TS_], F32) for i in range(2)]
            sq = sb("fsq", [128, KC, TS_], BF16)
            sd = sb("fsd", [128, TS_], F32)
            rstd = sb("frstd", [128, TS_], F32)
            hnT = sb("fhn", [128, KC, TS_], BF16)
            actT = sb("factT", [128, AC, TS_], BF16)
            raws = Ring([(sb("raw%d" % i, [128, TS_ + 2 * NSEQ], F32), "raw%d" % i) for i in range(6)])
            cvs = Ring([(sb("cv%d" % i, [128, TS_], F32), "cv%d" % i) for i in range(8)])
            sgs = Ring([(sb("sg%d" % i, [128, TS_], F32), "sg%d" % i) for i in range(3)])
            uptail = sb("uptail", [128, FC, 2], F32)
            stail = sb("stail", [128, FC, NSEQ, 2], F32)
            ynT = sb("ynT", [128, KC, TS_], F32) if last else None
            ytok = Ring([(sb("ytok%d" % i, [128, D], F32), "ytok%d" % i) for i in range(2)]) if last else None
            rmstmp = (sq, sd, rstd)
            if l == 0:
                print("SBUF remaining after ffn alloc:", self.nc.sbuf_bytes_remaining)
            self.MS("pool", uptail[:], 0.0, ["uptail"])
            for c in range(FC):
                src = I["sconv"][l, :, :, c * 128:(c + 1) * 128].rearrange("s r p -> p s r")
                self.DMA(stail[:, c, :, :], src, [], ["stail"], q="act", slow=True)
            S_.barrier()

            def load_x(dst, tok0, NT):
                key = "fx%d" % xT.index(dst)
                self.DMA(dst[:, :, 0:NT], self.scr[rd, :, :, tok0:tok0 + NT].rearrange("c p t -> p c t"), [], [key])
                return key

            def tile(xk, x, tok0, NT, nseq, tl, tail, tailk):
                self.rms_feat(x, KC, NT, g2, hnT, D, [xk], "fhn", rmstmp)
                deferred = None
                for fc in range(AC):
                    st_ = []
                    for cidx in (fc, fc + AC):
                        ps, pk = self.psA.next()
                        for kc in range(KC):
                            self.MM(ps[:, 0:NT], wup[:, kc, cidx * 128:(cidx + 1) * 128], hnT[:, kc, 0:NT], kc == 0,
                                    kc == KC - 1, ["wup", "fhn"], [pk])
                        raw, rk = raws.next()
                        r3 = raw[:, 0:nseq * (tl + 2)].rearrange("p (s t) -> p s t", t=tl + 2)
                        tl_ap = tail[:, cidx, :].unsqueeze(1) if nseq == 1 else tail[:, cidx, :, :]
                        cv, ck = cvs.next()
                        c3 = cv[:, 0:NT].rearrange("p (s t) -> p s t", t=tl)
                        p3 = ps[:, 0:NT].rearrange("p (s t) -> p s t", t=tl)
                        tk_ = "%s%d" % (tailk, cidx)
                        self.CP("pool", r3[:, :, 0:2], tl_ap, [tailk, tk_], [rk])
                        self.CP("act", r3[:, :, 2:2 + tl], p3, [pk], [rk])
                        self.ACT(c3, p3, AF.Identity, [pk, "pacols", "pbcols"], [ck], bias=cb[:, cidx:cidx + 1],
                                 scale=cw2[:, cidx:cidx + 1])
                        self.CP("pool", tl_ap, r3[:, :, tl:tl + 2], [rk], [tk_])
                        st_.append((cidx, r3, c3, rk, ck, cv))
                    for (cidx, r3, c3, rk, ck, cv) in st_:
                        self.STT(c3, r3[:, :, 1:1 + tl], cw1[:, cidx:cidx + 1], c3, ALU.mult, ALU.add, [rk, ck, "pbcols"], [ck])
                    for (cidx, r3, c3, rk, ck, cv) in st_:
                        self.STT(c3, r3[:, :, 0:tl], cw0[:, cidx:cidx + 1], c3, ALU.mult, ALU.add, [rk, ck, "pacols"], [ck])

                    def tail_part(fc=fc, st_=st_):
                        (_, _, _, _, cak, ca), (_, _, _, _, cgk, cg) = st_
                        sg, sgk = sgs.next()
                        self.ACT(sg[:, 0:NT], cg[:, 0:NT], AF.Silu, [cgk], [sgk])
                        self.TT("dve", actT[:, fc, 0:NT], sg[:, 0:NT], ca[:, 0:NT], ALU.mult, [sgk, cak], ["actT"])

                    if deferred is not None:
                        deferred()
                    deferred = tail_part
                deferred()
                for oc in range(KC):
                    ps, pk = self.psA.next()
                    for kc in range(AC):
                        self.MM(ps[:, 0:NT], wdn[:, kc, oc * 128:(oc + 1) * 128], actT[:, kc, 0:NT], kc == 0, kc == AC - 1,
                                ["wdn", "actT"], [pk])
                    self.TT("dve", x[:, oc, 0:NT], ps[:, 0:NT], x[:, oc, 0:NT], ALU.add, [pk, xk], [xk])
                if self.dbg:
                    self.DMA(self.O["dbg"][ph, :, :, tok0:tok0 + NT].rearrange("c p t -> p c t"), x[:, :, 0:NT], [xk], [])
                if not last:
                    self.DMA(self.scr[wr, :, :, tok0:tok0 + NT].rearrange("c p t -> p c t"), x[:, :, 0:NT], [xk], [])
                    return
                self.rms_feat(x, KC, NT, gf, ynT, D, [xk], "ynT", rmstmp)
                nsub = (NT + 127) // 128
                w = min(128, NT)
                ydst = O["yp"] if nseq == 1 else O["ys"]
                t0_ = tok0 if nseq == 1 else 0
                for s_ in range(nsub):
                    yt, ytk = ytok.next()
                    for half in range(2):
                        ps, pk = self.psA.next()
                        for c4 in range(4):
                            c = half * 4 + c4
                            self.TR(ps[0:w, c4 * 128:(c4 + 1) * 128], ynT[:, c, s_ * 128: s_ * 128 + w], self.ident_f[:],
                                    ["ynT", "ident"], [pk])
                        self.CP("act" if half else "dve", yt[0:w, half * 512:(half + 1) * 512], ps[0:w, 0:512], [pk], [ytk])
                    self.DMA(ydst[t0_ + s_ * 128: t0_ + s_ * 128 + w, :], yt[0:w, :], [ytk], [])

            nst = S // TS_
            nxt = load_x(xT[0], 0, TS_)
            for sti in range(nst):
                x, xk = xT[sti % 2], nxt
                if sti + 1 < nst:
                    nxt = load_x(xT[(sti + 1) % 2], (sti + 1) * TS_, TS_)
                else:
                    nxt = load_x(xT[(sti + 1) % 2], S, NS)
                tile(xk, x, sti * TS_, TS_, 1, TS_, uptail, "uptail")
            for c in range(FC):
                for r_ in range(2):
                    dst = bass.AP(O["convp"].tensor, O["convp"][l, r_].offset + c * 128, [[1, 128], [1, 1]])
                    self.DMA(dst, uptail[:, c, r_:r_ + 1], ["uptail", "uptail%d" % c], [], q="act")
            tile(nxt, xT[nst % 2], S, NS, NSEQ, TD, stail, "stail")
            for c in range(FC):
                dst = O["convs"][l, :, :, c * 128:(c + 1) * 128].rearrange("s r p -> p s r")
                self.DMA(dst, stail[:, c, :, :], ["stail", "stail%d" % c], [], q="act", slow=True)
            S_.barrier()

    def build(self):
        self.declare()
        with contextlib.ExitStack() as st:
            self.S_ = Sched(self.nc, st)
            try:
                self.setup(st)
                self.ckpt("setup")
                ph = 0
                for l in range(2):
                    self.mix_phase(l, ph)
                    self.ckpt("mix%d" % l)
                    ph += 1
                    self.ffn_phase(l, ph, last=(l == 1))
                    self.ckpt("ffn%d" % l)
                    ph += 1
            except _Stop:
                self.S_.barrier()
            self.S_.emit()
        return self.nc


class _View:
    def __init__(self, t, c0, col0):
        self.t, self.c0, self.col0 = t, c0, col0

    def __getitem__(self, key):
        p, c, cols = key
        return self.t[p, self.c0 + c, self.col0 + cols.start: self.col0 + cols.stop]


_NC_CACHE = {}


def _get_nc():
    if "nc" not in _NC_CACHE:
        _NC_CACHE["nc"] = Builder().build()
    return _NC_CACHE["nc"]


_WNAMES = ("norm1_g", "w_in", "ssm_log_dt", "ssm_a_re", "ssm_a_im", "ssm_b_re", "ssm_b_im", "ssm_c_re", "ssm_c_im",
           "ssm_d", "ssm_w_glu", "ssm_b_glu", "pool_w", "pool_scale", "out_norm_att", "out_norm_ssm", "out_norm_pool",
           "w_out", "norm2_g", "w_up", "conv_w", "conv_b", "w_down", "norm_f_g")


def make_in_maps(inp, S=8192):
    f = lambda a: np.ascontiguousarray(np.asarray(a, dtype=np.float32))
    maps = []
    zeros_p = np.zeros((S, D), np.float32)
    for c in range(NCORES):
        m = {}
        m["xp"] = f(inp["x_prompt"][c]) if c < inp["x_prompt"].shape[0] else zeros_p
        sl = slice(c * NSEQ, (c + 1) * NSEQ)
        m["xs"] = f(inp["x_sample"][sl]).reshape(NS, D)
        m["ck"] = f(inp["cache_k"][:, sl]).reshape(2, NSEQ, PAST, 512)
        m["cv"] = f(inp["cache_v"][:, sl]).reshape(2, NSEQ, PAST, 512)
        m["sre"] = f(inp["state_ssm_re"][:, sl])
        m["sim"] = f(inp["state_ssm_im"][:, sl])
        m["spool"] = f(inp["state_pool"][:, sl])
        m["sconv"] = f(inp["state_conv"][:, sl])
        for n in _WNAMES:
            m[n] = f(inp[n])
        maps.append(m)
    return maps


def gather(res, S=8192, nkeep=2048, nb=2):
    r = res
    cat = lambda n: np.concatenate([r[c][n] for c in range(NCORES)], axis=1)
    y_prompt = np.stack([r[c]["yp"] for c in range(nb)], 0)
    y_sample = np.concatenate([r[c]["ys"].reshape(NSEQ, TD, D) for c in range(NCORES)], 0)
    pk = lambda n: np.stack([r[c][n] for c in range(nb)], 1)
    new_k_prompt = pk("kp").reshape(2, nb, nkeep, 8, 64)
    new_v_prompt = pk("vp").reshape(2, nb, nkeep, 8, 64)
    outs = (y_prompt, y_sample, new_k_prompt, new_v_prompt, pk("rep"), pk("imp"), pk("poolp"), pk("convp"),
            cat("ks").reshape(2, NCORES * NSEQ, TD, 8, 64), cat("vs").reshape(2, NCORES * NSEQ, TD, 8, 64),
            cat("res"), cat("ims"), cat("pools"), cat("convs"))
    return tuple(np.ascontiguousarray(o, dtype=np.float32) for o in outs)


def kernel(**inputs):
    nc = _get_nc()
    in_maps = make_in_maps(inputs)
    res = run_bass_kernel_spmd(nc, in_maps, core_ids=list(range(NCORES)))
    return gather(res.results)
```

```python
import contextlib
import math
import numpy as np
import concourse.bass as bass
import concourse.mybir as mybir
from concourse.bass_utils import run_bass_kernel_spmd

F32 = mybir.dt.float32
BF16 = mybir.dt.bfloat16
I32 = mybir.dt.int32
ALU = mybir.AluOpType
AF = mybir.ActivationFunctionType

D = 1024
KC = 8
DIN = 2048
DFF = 2816
FC = 44
AC = 22
NSEQ = 4
TD = 8
NS = NSEQ * TD
PAST = 2048
EPS = 1e-6
NCORES = 8
TWO_PI = 2.0 * math.pi


class Sched:
    ENGS = ("pe", "act", "dve", "pool", "sp")

    def __init__(self, nc, stack, n_dma_sems=32):
        self.nc = nc
        self.ops = {e: [] for e in self.ENGS}
        self.count = {e: 0 for e in self.ENGS}
        self.known = {e: {} for e in self.ENGS}
        self.buf = {}
        self.esem = {e: stack.enter_context(nc.semaphore("s_" + e)) for e in self.ENGS}
        self.KD = n_dma_sems
        self.dsem = [stack.enter_context(nc.semaphore("d%d" % i)) for i in range(self.KD)]
        self.ndma = 0
        self.dlast = [0] * self.KD
        self.dead = False

    def _deps(self, reads, writes):
        deps = []
        for k in reads:
            st = self.buf.get(k)
            if st and st["w"] is not None:
                deps.append(st["w"] + ("raw",))
        for k in writes:
            st = self.buf.get(k)
            if st:
                if st["w"] is not None:
                    deps.append(st["w"])
                deps.extend(st["r"])
        return deps

    def _waits(self, eng, deps):
        need = {}
        for t in deps:
            if t[0] == "E":
                if t[1] == eng and (eng == "pe" or len(t) < 4):
                    continue
                key = ("E", t[1])
            else:
                key = ("D", t[1])
            if t[2] > need.get(key, 0):
                need[key] = t[2]
        out = []
        kn = self.known[eng]
        for key, v in need.items():
            if kn.get(key, 0) >= v:
                continue
            kn[key] = v
            sem = self.esem[key[1]] if key[0] == "E" else self.dsem[key[1]]
            out.append((sem, v))
        return out

    def _commit(self, tok, reads, writes):
        for k in reads:
            st = self.buf.setdefault(k, {"w": None, "r": []})
            st["r"].append(tok)
            if len(st["r"]) > 64:
                st["r"] = self._prune(st["r"])
        for k in writes:
            self.buf[k] = {"w": tok, "r": []}

    @staticmethod
    def _prune(lst):
        best = {}
        for t in lst:
            key = (t[0], t[1])
            if key not in best or t[2] > best[key][2]:
                best[key] = t
        return list(best.values())

    def op(self, eng, fn, reads=(), writes=()):
        if self.dead:
            return None
        psr = [k for k in reads if k.startswith("ps") and k[2:].isdigit()]
        if psr:
            reads = [k for k in reads if k not in psr]
            writes = list(writes) + psr
        waits = self._waits(eng, self._deps(reads, writes))
        self.count[eng] += 1
        tok = ("E", eng, self.count[eng])
        self.ops[eng].append((fn, waits, (self.esem[eng], 1)))
        self._commit(tok, reads, writes)
        return tok

    def dma(self, fn, reads=(), writes=(), q="sp"):
        if self.dead:
            return None
        slot = self.ndma % self.KD
        self.ndma += 1
        deps = self._deps(reads, writes)
        if self.dlast[slot] > 0:
            deps.append(("D", slot, self.dlast[slot]))
        waits = self._waits(q, deps)
        self.dlast[slot] += 16
        tok = ("D", slot, self.dlast[slot])
        self.ops[q].append((fn, waits, (self.dsem[slot], 16)))
        self._commit(tok, reads, writes)
        return tok

    def barrier(self):
        if self.dead:
            return
        dd = [("D", s, v) for s, v in enumerate(self.dlast) if v > 0]
        for e in self.ENGS:
            deps = [("E", e2, self.count[e2]) for e2 in self.ENGS if e2 != e and self.count[e2] > 0] + dd
            waits = self._waits(e, deps)
            self.ops[e].append((None, waits, None))
        self.buf = {}

    def emit(self):
        nc = self.nc
        engmap = {"pe": "tensor", "act": "scalar", "dve": "vector", "pool": "gpsimd", "sp": "sync"}
        with nc.Block() as block:
            for e in self.ENGS:
                lst = self.ops[e]

                def body(engine, lst=lst):
                    for fn, waits, inc in lst:
                        for sem, v in waits:
                            engine.wait_ge(sem, v)
                        if fn is not None:
                            fn(engine).then_inc(inc[0], inc[1])

                getattr(block, engmap[e])(body)


class Ring:
    def __init__(self, items):
        self.items = items
        self.i = 0

    def next(self):
        it = self.items[self.i % len(self.items)]
        self.i += 1
        return it


class _Stop(Exception):
    pass


class Builder:
    stop_at = None

    def ckpt(self, name):
        if self.stop_at is not None and name == self.stop_at and not self.S_.dead:
            self.S_.barrier()
            self.S_.dead = True

    def __init__(self, S=8192, TSM=128, TSF=256, dbg=False):
        assert S % TSM == 0 and S % TSF == 0 and TSM % 128 == 0
        self.S, self.TSM, self.TSF, self.dbg = S, TSM, TSF, dbg
        self.NSUB = TSM // 128
        self.JMIN = -(self.NSUB - 1)
        self.NJ = 17 + self.NSUB - 1
        self.NSLOT = 16 + 2 * self.NSUB
        self.NKEEP = min(2048, S)
        self.nc = bass.Bass("TRN2", target_bir_lowering=False)

    def MM(self, out, lhsT, rhs, start, stop, r, w):
        self.S_.op("pe", lambda e: e.matmul(out, lhsT=lhsT, rhs=rhs, start=start, stop=stop,
                                            skip_group_check=True), r, w)

    def TR(self, out, in_, ident, r, w):
        self.S_.op("pe", lambda e: e.transpose(out, in_, ident), r, w)

    def ACT(self, out, in_, func, r, w, bias=None, scale=None, accum=None):
        kw = {}
        if bias is not None:
            kw["bias"] = bias
        if scale is not None:
            kw["scale"] = scale
        if accum is not None:
            kw["accum_out"] = accum
        self.S_.op("act", lambda e: e.activation(out=out, in_=in_, func=func, **kw), r, w)

    def TT(self, eng, out, in0, in1, op, r, w):
        self.S_.op(eng, lambda e: e.tensor_tensor(out=out, in0=in0, in1=in1, op=op), r, w)

    def TS(self, eng, out, in0, s1, s2, op0, op1, r, w):
        if s2 is None:
            self.S_.op(eng, lambda e: e.tensor_scalar(out=out, in0=in0, scalar1=s1, scalar2=None, op0=op0), r, w)
        else:
            self.S_.op(eng, lambda e: e.tensor_scalar(out=out, in0=in0, scalar1=s1, scalar2=s2, op0=op0, op1=op1), r, w)

    def STT(self, out, in0, scalar, in1, op0, op1, r, w):
        self.S_.op("dve", lambda e: e.scalar_tensor_tensor(out=out, in0=in0, scalar=scalar, in1=in1,
                                                          op0=op0, op1=op1), r, w)

    def CP(self, eng, out, in_, r, w):
        if eng == "act":
            self.S_.op("act", lambda e: e.activation(out=out, in_=in_, func=AF.Copy), r, w)
        else:
            self.S_.op(eng, lambda e: e.tensor_copy(out=out, in_=in_), r, w)

    def MS(self, eng, ap, val, w):
        self.S_.op(eng, lambda e: e.memset(ap, val), (), w)

    def DMA(self, out, in_, r, w, q="sp", slow=False):
        if slow:
            self.S_.dma(lambda e: e.dma_start(out=out, in_=in_, allow_slow_non_contiguous=True), r, w, q=q)
        else:
            self.S_.dma(lambda e: e.dma_start(out=out, in_=in_), r, w, q=q)

    def sb(self, stack, name, shape, dt):
        self._uid = getattr(self, "_uid", 0) + 1
        return stack.enter_context(self.nc.sbuf_tensor("%s_%d" % (name, self._uid), shape, dt))

    def declare(self):
        nc, S = self.nc, self.S
        din = lambda n, s: nc.dram_tensor(n, s, F32, kind="ExternalInput").ap()
        dout = lambda n, s: nc.dram_tensor(n, s, F32, kind="ExternalOutput").ap()
        I = {}
        I["xp"] = din("xp", [S, D])
        I["xs"] = din("xs", [NS, D])
        I["ck"] = din("ck", [2, NSEQ, PAST, 512])
        I["cv"] = din("cv", [2, NSEQ, PAST, 512])
        I["sre"] = din("sre", [2, NSEQ, 16, 64])
        I["sim"] = din("sim", [2, NSEQ, 16, 64])
        I["spool"] = din("spool", [2, NSEQ, 15, 256])
        I["sconv"] = din("sconv", [2, NSEQ, 2, 2 * DFF])
        for n, s in (("norm1_g", [2, D]), ("w_in", [2, D, DIN]), ("ssm_log_dt", [2, 16]), ("ssm_a_re", [2, 16, 64]),
                     ("ssm_a_im", [2, 16, 64]), ("ssm_b_re", [2, 16, 64, 16]), ("ssm_b_im", [2, 16, 64, 16]),
                     ("ssm_c_re", [2, 16, 16, 64]), ("ssm_c_im", [2, 16, 16, 64]), ("ssm_d", [2, 256]),
                     ("ssm_w_glu", [2, 256, 256]), ("ssm_b_glu", [2, 256]), ("pool_w", [2, 4, 64, 64]),
                     ("pool_scale", [2, 256]), ("out_norm_att", [2, 512]), ("out_norm_ssm", [2, 256]),
                     ("out_norm_pool", [2, 256]), ("w_out", [2, D, D]), ("norm2_g", [2, D]),
                     ("w_up", [2, D, 2 * DFF]), ("conv_w", [2, 3, 2 * DFF]), ("conv_b", [2, 2 * DFF]),
                     ("w_down", [2, DFF, D]), ("norm_f_g", [D])):
            I[n] = din(n, s)
        O = {}
        O["yp"] = dout("yp", [S, D])
        O["ys"] = dout("ys", [NS, D])
        O["kp"] = dout("kp", [2, self.NKEEP, 512])
        O["vp"] = dout("vp", [2, self.NKEEP, 512])
        O["rep"] = dout("rep", [2, 16, 64])
        O["imp"] = dout("imp", [2, 16, 64])
        O["poolp"] = dout("poolp", [2, 15, 256])
        O["convp"] = dout("convp", [2, 2, 2 * DFF])
        O["ks"] = dout("ks", [2, NSEQ, TD, 512])
        O["vs"] = dout("vs", [2, NSEQ, TD, 512])
        O["res"] = dout("res", [2, NSEQ, 16, 64])
        O["ims"] = dout("ims", [2, NSEQ, 16, 64])
        O["pools"] = dout("pools", [2, NSEQ, 15, 256])
        O["convs"] = dout("convs", [2, NSEQ, 2, 2 * DFF])
        if self.dbg:
            O["dbg"] = dout("dbg", [12, KC, 128, S + NS])
        self.I, self.O = I, O
        self.scr = nc.dram_tensor("scr", [2, KC, 128, S + NS], F32, kind="Internal").ap()

    def sincos(self, st, X, n, tag, want_cos=True):
        ki = self.sb(st, tag + "_ki", [128, n], I32)
        r = self.sb(st, tag + "_r", [128, n], F32)
        sn = self.sb(st, tag + "_sn", [128, n], F32)
        outs = []
        for which, shift, dst in (("s", 0.0, sn),) + ((("c", math.pi / 2, None),) if want_cos else ()):
            if dst is None:
                dst = self.sb(st, tag + "_cs", [128, n], F32)
            src = X
            if shift != 0.0:
                self.TS("dve", r[:], X, shift, None, ALU.add, None, [tag + "X"], [tag + "r"])
                src = r[:]
            self.TS("dve", ki[:], src, 1.0 / TWO_PI, None, ALU.mult, None, [tag + "X", tag + "r"], [tag + "ki"])
            self.STT(r[:], ki[:], -TWO_PI, src, ALU.mult, ALU.add, [tag + "ki", tag + "X", tag + "r"], [tag + "r"])
            self.TS("dve", r[:], r[:], 3.14159, -3.14159, ALU.min, ALU.max, [tag + "r"], [tag + "r"])
            self.ACT(dst[:], r[:], AF.Sin, [tag + "r"], [tag + which])
            outs.append(dst)
        return outs

    def rms_feat(self, src, C, NT, gains, dst, nfeat, keys_r, key_w, tmp):
        sq, sd, rstd = tmp
        self.ACT(sq[:, 0:C, 0:NT], src[:, 0:C, 0:NT], AF.Square, keys_r, ["sq"])
        ps, pk = self.psC.next()
        for c in range(C):
            self.MM(ps[:, 0:NT], self.ones_b[:], sq[:, c, 0:NT], c == 0, c == C - 1, ["sq", "ones"], [pk])
        if self.rms_mode == "lnexp":
            self.ACT(sd[:, 0:NT], ps[:, 0:NT], AF.Ln, [pk], ["sd"], bias=self.eps_col[:, 0:1], scale=1.0 / nfeat)
            self.ACT(rstd[:, 0:NT], sd[:, 0:NT], AF.Exp, ["sd"], ["rstd"], scale=-0.5)
        else:
            self.ACT(sd[:, 0:NT], ps[:, 0:NT], AF.Sqrt, [pk], ["sd"], bias=self.eps_col[:, 0:1], scale=1.0 / nfeat)
            self.S_.op("dve", lambda e: e.reciprocal(out=rstd[:, 0:NT], in_=sd[:, 0:NT]), ["sd"], ["rstd"])
        for c in range(C):
            self.STT(dst[:, c, 0:NT], src[:, c, 0:NT], gains[:, c:c + 1], rstd[:, 0:NT], ALU.mult, ALU.mult,
                     keys_r + ["rstd", "gains"], [key_w])

    def cols_to_rows(self, src, dram_rows, R, rkeys, q="sp"):
        ps, pk = self.psC.next()
        stg, sk = self.tstage.next()
        self.TR(ps[0:R, 0:128], src, self.ident_f[:], list(rkeys) + ["ident"], [pk])
        self.CP("dve", stg[0:R, :], ps[0:R, 0:128], [pk], [sk])
        self.DMA(dram_rows, stg[0:R, :], [sk], [], q=q)

    def rows_to_cols(self, dram_rows, dst, R, wkeys, q="sp"):
        ps, pk = self.psC.next()
        stg, sk = self.tstage.next()
        self.DMA(stg[0:R, :], dram_rows, [], [sk], q=q)
        self.TR(ps[:, 0:R], stg[0:R, :], self.ident_f[0:R, 0:R], [sk, "ident"], [pk])
        self.CP("dve", dst, ps[:, 0:R], [pk], list(wkeys))

    def load_rows_T(self, st, rows, tag):
        stage = self.sb(st, tag + "_stg", [128, 128], F32)
        cols = self.sb(st, tag + "_cols", [128, 128], F32)
        self.MS("pool", stage[:], 0.0, [tag + "stg"])
        for ap, base in rows:
            n = ap.shape[0]
            self.DMA(stage[base:base + n, :], ap, [], [tag + "stg"])
        ps, pk = self.psC.next()
        self.TR(ps[:, 0:128], stage[:], self.ident_f[:], [tag + "stg", "ident"], [pk])
        self.CP("dve", cols[:], ps[:, 0:128], [pk], [tag + "cols"])
        return cols

    def set_rings(self, mix):
        rg = lambda ids: Ring([(self.ps[i], "ps%d" % i) for i in ids])
        if mix:
            self.psA, self.psS, self.psB, self.psC = rg((0, 1, 2)), rg((3, 4)), rg((5,)), rg((6, 7))
        else:
            self.psA, self.psC = rg((0, 1, 2, 3, 4, 5)), rg((6, 7))

    def setup(self, st):
        nc = self.nc
        self.ps = [st.enter_context(nc.psum_tensor("ps%d" % i, [128, 512], F32)) for i in range(8)]
        self.set_rings(True)
        self.ident_f = self.sb(st, "ident_f", [128, 128], F32)
        self.ones_b = self.sb(st, "ones_b", [128, 128], BF16)
        self.eps_col = self.sb(st, "eps_col", [128, 1], F32)
        self.tstage = Ring([(self.sb(st, "tstage%d" % i, [128, 128], F32), "tstage%d" % i) for i in range(2)])
        with contextlib.ExitStack() as t:
            onesf = self.sb(t, "onesf", [128, 128], F32)
            self.MS("pool", onesf[:], 1.0, ["onesf"])
            self.MS("pool", self.ones_b[:], 1.0, ["ones"])
            self.MS("pool", self.eps_col[:], EPS, ["eps"])
            self.S_.op("pool", lambda e: e.affine_select(out=self.ident_f[:], in_=onesf[:], pattern=[[-1, 128]],
                                                         compare_op=ALU.is_equal, fill=0.0, base=0,
                                                         channel_multiplier=1), ["onesf"], ["ident"])
            self.S_.barrier()

    def mix_phase(self, l, ph):
        nc, S_, I, O = self.nc, self.S_, self.I, self.O
        S, TS_, NSUB, NJ, JMIN, NSLOT = self.S, self.TSM, self.NSUB, self.NJ, self.JMIN, self.NSLOT
        TW = TS_ + 1
        rd, wr = (ph - 1) % 2, ph % 2
        self.rms_mode = "lnexp"
        self.set_rings(True)
        with contextlib.ExitStack() as st:
            sb = lambda n, s, d: self.sb(st, n, s, d)
            win = sb("win", [128, KC, DIN], BF16)
            wout = sb("wout", [128, KC, D], BF16)
            wglu = sb("wglu", [128, 2, 256], BF16)
            poolw = sb("poolw", [128, 2, 128], BF16)
            Bre = sb("Bre", [128, 8, 128], BF16)
            Bim = sb("Bim", [128, 8, 128], BF16)
            Cre = sb("Cre", [128, 8, 128], BF16)
            Cim = sb("Cim", [128, 8, 128], BF16)
            Tre = sb("Tre", [128, 8, TW], F32)
            Tim = sb("Tim", [128, 8, TW], F32)
            cs = sb("cs", [128, 8, TW], F32)
            sn = sb("sn", [128, 8, TW], F32)
            rho = sb("rho", [128, 8], F32)
            mask = sb("mask", [128, NJ, TS_], BF16)
            icnt_tab = sb("icnt_tab", [128, 2, 16], F32)
            icnt = sb("icnt", [128, 2], F32)
            for c in range(KC):
                self.DMA(win[:, c, :], I["w_in"][l, c * 128:(c + 1) * 128, :], [], ["win"], q="pool")
            for c in range(KC):
                self.DMA(wout[:, c, :], I["w_out"][l, c * 128:(c + 1) * 128, :], [], ["wout"], q="pool")
            for c in range(2):
                self.DMA(wglu[:, c, :], I["ssm_w_glu"][l, c * 128:(c + 1) * 128, :], [], ["wglu"], q="pool")
            self.MS("pool", poolw[:], 0.0, ["poolw"])
            for gi in range(4):
                h0 = (gi % 2) * 64
                self.DMA(poolw[h0:h0 + 64, gi // 2, h0:h0 + 64], I["pool_w"][l, gi], [], ["poolw"], q="pool")
            rows = [(I["norm1_g"][l].rearrange("(c p) -> c p", p=128), 0),
                    (I["out_norm_att"][l].rearrange("(c p) -> c p", p=128), 8),
                    (I["out_norm_ssm"][l].rearrange("(c p) -> c p", p=128), 12),
                    (I["out_norm_pool"][l].rearrange("(c p) -> c p", p=128), 14),
                    (I["pool_scale"][l].rearrange("(c p) -> c p", p=128), 16),
                    (I["ssm_d"][l].rearrange("(c p) -> c p", p=128), 18),
                    (I["ssm_b_glu"][l].rearrange("(c p) -> c p", p=128), 20)]
            pc = self.load_rows_T(st, rows, "pc")
            self.ckpt("m_pc")
            g1, gatt, gssm, gpool = pc[:, 0:8], pc[:, 8:12], pc[:, 12:14], pc[:, 14:16]
            pscale, dskip, bglu = pc[:, 16:18], pc[:, 18:20], pc[:, 20:22]
            nbglu = sb("nbglu", [128, 2], F32)
            self.TS("dve", nbglu[:], bglu, -1.0, None, ALU.mult, None, ["pccols"], ["nbglu"])
            with contextlib.ExitStack() as t:
                tb = lambda n, s, d: self.sb(t, n, s, d)
                arr = lambda nm: I[nm][l].rearrange("(gp gl) p -> gp (gl p)", gl=2)
                sc = self.load_rows_T(t, [(arr("ssm_a_re"), 0), (arr("ssm_a_im"), 32)], "sc")
                are, aim = sc[:, 0:8], sc[:, 32:40]
                dtr = tb("dtr", [128, 2], F32)
                dtrow = tb("dtrow", [128, 128], F32)
                self.MS("pool", dtr[:], 0.0, ["dtr"])
                self.MS("pool", dtrow[:], 0.0, ["dtrow"])
                self.DMA(dtr[0:8, :], I["ssm_log_dt"][l].rearrange("(gp gl) -> gp gl", gl=2), [], ["dtr"])
                self.ACT(dtr[0:8, :], dtr[0:8, :], AF.Exp, ["dtr"], ["dtr"])
                self.CP("dve", dtrow[0:8, :].rearrange("p (a b) -> p a b", b=64),
                        dtr[0:8, :].unsqueeze(2).to_broadcast([8, 2, 64]), ["dtr"], ["dtrow"])
                psd, pkd = self.psC.next()
                self.TR(psd[:, 0:128], dtrow[:], self.ident_f[:], ["dtrow", "ident"], [pkd])
                dtc = tb("dtc", [128, 8], F32)
                self.CP("dve", dtc[:], psd[:, 0:8], [pkd], ["dtc"])
                th = tb("th", [128, 8], F32)
                lre = tb("lre", [128, 8], F32)
                self.TT("dve", lre[:], are, dtc[:], ALU.mult, ["sccols", "dtc"], ["lre"])
                self.TT("dve", th[:], aim, dtc[:], ALU.mult, ["sccols", "dtc"], ["thX"])
                self.ACT(rho[:], lre[:], AF.Exp, ["lre"], ["rho"])
                s1, c1 = self.sincos(t, th[:], 8, "th")
                abre = tb("abre", [128, 8], F32)
                abim = tb("abim", [128, 8], F32)
                self.TT("dve", abre[:], rho[:], c1[:], ALU.mult, ["rho", "thc"], ["abre"])
                self.TT("dve", abim[:], rho[:], s1[:], ALU.mult, ["rho", "ths"], ["abim"])
                self.TS("dve", abre[:], abre[:], -1.0, None, ALU.add, None, ["abre"], ["abre"])
                den = tb("den", [128, 8], F32)
                t0 = tb("t0", [128, 8], F32)
                core_ = tb("core", [128, 8], F32)
                coim = tb("coim", [128, 8], F32)
                self.TT("dve", den[:], are, are, ALU.mult, ["sccols"], ["den"])
                self.TT("dve", t0[:], aim, aim, ALU.mult, ["sccols"], ["t0"])
                self.TT("dve", den[:], den[:], t0[:], ALU.add, ["den", "t0"], ["den"])
                self.S_.op("dve", lambda e: e.reciprocal(out=den[:], in_=den[:]), ["den"], ["den"])
                self.TT("dve", core_[:], abre[:], are, ALU.mult, ["abre", "sccols"], ["core"])
                self.TT("dve", t0[:], abim[:], aim, ALU.mult, ["abim", "sccols"], ["t0"])
                self.TT("dve", core_[:], core_[:], t0[:], ALU.add, ["core", "t0"], ["core"])
                self.TT("dve", core_[:], core_[:], den[:], ALU.mult, ["core", "den"], ["core"])
                self.TT("dve", coim[:], abim[:], are, ALU.mult, ["abim", "sccols"], ["coim"])
                self.TT("dve", t0[:], abre[:], aim, ALU.mult, ["abre", "sccols"], ["t0"])
                self.TT("dve", coim[:], coim[:], t0[:], ALU.subtract, ["coim", "t0"], ["coim"])
                self.TT("dve", coim[:], coim[:], den[:], ALU.mult, ["coim", "den"], ["coim"])
                self.ckpt("m_co")
                iot = tb("iot", [128, TW], F32)
                self.S_.op("pool", lambda e: e.iota(iot[:], pattern=[[1, TW]], base=0, channel_multiplier=0,
                                                    allow_small_or_imprecise_dtypes=True), [], ["iot"])
                ang = tb("ang", [128, 8, TW], F32)
                self.TT("dve", ang[:], th[:].unsqueeze(2).to_broadcast([128, 8, TW]),
                        iot[:].unsqueeze(1).to_broadcast([128, 8, TW]), ALU.mult, ["thX", "iot"], ["angX"])
                sA, cA = self.sincos(t, ang[:].rearrange("p a b -> p (a b)"), 8 * TW, "ang")
                self.CP("pool", sn[:].rearrange("p a b -> p (a b)"), sA[:], ["angs"], ["sn"])
                self.CP("pool", cs[:].rearrange("p a b -> p (a b)"), cA[:], ["angc"], ["cs"])
                tmpT = tb("tmpT", [128, 8, TW], F32)
                cob = lambda x: x[:].unsqueeze(2).to_broadcast([128, 8, TW])
                self.TT("dve", Tre[:], cs[:], cob(core_), ALU.mult, ["cs", "core"], ["Tre"])
                self.TT("dve", tmpT[:], sn[:], cob(coim), ALU.mult, ["sn", "coim"], ["tmpT"])
                self.TT("dve", Tre[:], Tre[:], tmpT[:], ALU.add, ["Tre", "tmpT"], ["Tre"])
                self.TT("dve", Tim[:], cs[:], cob(coim), ALU.mult, ["cs", "coim"], ["Tim"])
                self.TT("dve", tmpT[:], sn[:], cob(core_), ALU.mult, ["sn", "core"], ["tmpT"])
                self.TT("dve", Tim[:], Tim[:], tmpT[:], ALU.subtract, ["Tim", "tmpT"], ["Tim"])
                self.ckpt("m_tab")
                X4 = tb("X4", [128, 8, 128], F32)
                for nm, dst, isB in (("ssm_b_re", Bre, True), ("ssm_b_im", Bim, True),
                                     ("ssm_c_re", Cre, False), ("ssm_c_im", Cim, False)):
                    self.MS("pool", X4[:], 0.0, ["X4"])
                    for g in range(16):
                        gp, gl = g // 2, g % 2
                        r0 = (gp % 4) * 32
                        if isB:
                            self.DMA(X4[gl * 64:(gl + 1) * 64, gp, r0 + gl * 16:r0 + gl * 16 + 16], I[nm][l, g],
                                     [], ["X4"])
                        else:
                            self.DMA(X4[r0 + gl * 16:r0 + gl * 16 + 16, gp, gl * 64:(gl + 1) * 64], I[nm][l, g],
                                     [], ["X4"])
                    for gp in range(8):
                        psx, pkx = self.psA.next()
                        self.TR(psx[:, 0:128], X4[:, gp, :], self.ident_f[:], ["X4", "ident"], [pkx])
                        self.CP("act", dst[:, gp, :], psx[:, 0:128], [pkx], [nm])
                S_.barrier()
                self.ckpt("m_bc")
            with contextlib.ExitStack() as t:
                tb = lambda n, s, d: self.sb(t, n, s, d)
                NM = NJ * TS_
                v = tb("mv", [128, NM], F32)
                m0 = tb("m0", [128, NM], F32)
                cc = tb("mc", [128, NM], F32)
                qi = tb("mq", [128, NM], I32)
                rr = tb("mr", [128, NM], F32)
                self.S_.op("pool", lambda e: e.iota(v[:], pattern=[[128, NJ], [1, TS_]], base=128 * JMIN,
                                                    channel_multiplier=-1, allow_small_or_imprecise_dtypes=True),
                           [], ["mv"])
                self.TS("dve", m0[:], v[:], 0.0, None, ALU.is_ge, None, ["mv"], ["m0"])
                self.TS("dve", cc[:], v[:], 128.0, None, ALU.is_le, None, ["mv"], ["mc"])
                self.TT("dve", cc[:], cc[:], m0[:], ALU.mult, ["mc", "m0"], ["mc"])
                for dil, lim in ((4, 512.0), (16, 2048.0)):
                    self.TS("dve", qi[:], v[:], 1.0 / dil, None, ALU.mult, None, ["mv"], ["mq"])
                    self.STT(rr[:], qi[:], -float(dil), v[:], ALU.mult, ALU.add, ["mq", "mv"], ["mr"])
                    self.TS("dve", rr[:], rr[:], 0.0, None, ALU.is_equal, None, ["mr"], ["mr"])
                    self.TT("dve", rr[:], rr[:], m0[:], ALU.mult, ["mr", "m0"], ["mr"])
                    self.TS("dve", qi[:], v[:], lim, None, ALU.is_le, None, ["mv"], ["mq"])
                    self.TT("dve", rr[:], rr[:], qi[:], ALU.mult, ["mr", "mq"], ["mr"])
                    self.TT("dve", cc[:], cc[:], rr[:], ALU.add, ["mc", "mr"], ["mc"])
                self.CP("dve", mask[:].rearrange("p a b -> p (a b)"), cc[:], ["mc"], ["mask"])
                self.ckpt("m_mask")
                wcol = tb("wcol", [128, 2], F32)
                for ch, (wa, wb) in enumerate(((2.0, 4.0), (8.0, 16.0))):
                    self.MS("pool", wcol[0:64, ch:ch + 1], wa, ["wcol"])
                    self.MS("pool", wcol[64:128, ch:ch + 1], wb, ["wcol"])
                self.S_.op("dve", lambda e: e.reciprocal(out=icnt[:], in_=wcol[:]), ["wcol"], ["icnt"])
                i16 = tb("i16", [128, 16], F32)
                self.S_.op("pool", lambda e: e.iota(i16[:], pattern=[[1, 16]], base=1, channel_multiplier=0,
                                                    allow_small_or_imprecise_dtypes=True), [], ["i16"])
                self.TT("dve", icnt_tab[:], i16[:].unsqueeze(1).to_broadcast([128, 2, 16]),
                        wcol[:].unsqueeze(2).to_broadcast([128, 2, 16]), ALU.min, ["i16", "wcol"], ["icnt_tab"])
                self.S_.op("dve", lambda e: e.reciprocal(out=icnt_tab[:], in_=icnt_tab[:]), ["icnt_tab"], ["icnt_tab"])
                S_.barrier()
            xT = [sb("xT%d" % i, [128, KC, TS_], F32) for i in range(3)]
            xtoks = [sb("xtok%d" % i, [128, NSUB, D], F32) for i in range(2)]
            sq1 = sb("sq1", [128, KC, TS_], BF16)
            sd1 = sb("sd1", [128, TS_], F32)
            rstd1 = sb("rstd1", [128, TS_], F32)
            sq = sb("sq", [128, KC, TS_], BF16)
            sd = sb("sd", [128, TS_], F32)
            rstd = sb("rstd", [128, TS_], F32)
            hnT = sb("hnT", [128, KC, TS_], BF16)
            qT = sb("qT", [128, 4, TS_], BF16)
            qT_b = sb("qTb", [128, 4, TS_], BF16)
            KT = sb("KT", [128, 4, NSLOT * 128], BF16)
            Vr = sb("Vr", [128, NSLOT, 8, 65], BF16)
            usb = sb("usb", [128, 2, TS_], BF16)
            usf = sb("usf", [128, 2, TS_], F32)
            upl = sb("upl", [128, 2, 16 + TS_], F32)
            usb_b = sb("usbb", [128, 2, TS_], BF16)
            usf_b = sb("usfb", [128, 2, TS_], F32)
            upl_b = sb("uplb", [128, 2, 16 + TS_], F32)
            Pr = Ring([(sb("P%d" % i, [128, 2 * TS_], BF16), "P%d" % i) for i in range(6)])
            rec = sb("rec", [128, NSUB], F32)
            accs = sb("accs", [128, 8, NSUB * 65], F32)
            rec8 = sb("rec8", [128, 8, NSUB], F32)
            att = sb("att", [128, NSUB, 512], F32)
            att_b = sb("attb", [128, NSUB, 512], F32)
            ssa = sb("ssa", [128, NSUB], F32)
            junk = sb("junk", [128, 512], BF16)
            mixT = sb("mixT", [128, KC, TS_], BF16)
            mixT_b = sb("mixTb", [128, KC, TS_], BF16)

            class _B:
                pass
            bs = []
            for i_, (q_, ub_, uf_, up_, at_, mx_) in enumerate(((qT, usb, usf, upl, att, mixT),
                                                                 (qT_b, usb_b, usf_b, upl_b, att_b, mixT_b))):
                B_ = _B()
                B_.qT, B_.usb, B_.usf, B_.upl, B_.att, B_.mixT = q_, ub_, uf_, up_, at_, mx_
                B_.qk, B_.ubk, B_.ufk, B_.upk, B_.ak, B_.mk = ("qT%d" % i_, "usb%d" % i_, "usf%d" % i_, "upl%d" % i_,
                                                              "att%d" % i_, "mixT%d" % i_)
                bs.append(B_)
            Gre = sb("Gre", [128, 8, TS_], F32)
            Gim = sb("Gim", [128, 8, TS_], F32)
            ta = sb("ta", [128, 8, TS_], F32)
            tb2 = sb("tb2", [128, 8, TS_], F32)
            hre = sb("hre", [128, 8, TS_], BF16)
            himn = sb("himn", [128, 8, TS_], BF16)
            wtmp = Ring([(sb("wt%d" % i, [128, TS_], F32), "wt%d" % i) for i in range(6)])
            gst = sb("gst", [128, 2, 8], F32)
            gl_t = sb("gl_t", [128, 4, 8], F32)
            hl = sb("hl", [128, 2, 8], F32)
            yss = sb("yss", [128, 2, TS_], F32)
            gact = sb("gact", [128, 2, TS_], F32)
            gactb = sb("gactb", [128, 2, TS_], BF16)
            sgl = sb("sgl", [128, 2, TS_], F32)
            pl1 = sb("pl1", [128, 2, 16 + TS_], F32)
            pl2 = sb("pl2", [128, 2, 16 + TS_], F32)
            wsel = sb("wsel", [128, 2, TS_], F32)
            pooled = sb("pooled", [128, 2, TS_], BF16)
            pout = sb("pout", [128, 2, TS_], F32)
            t16 = sb("t16", [128, 2, 16], F32)
            kvst = Ring([(sb("kvst%d" % i, [128, 512], F32), "kvst%d" % i) for i in range(2)])
            kts = sb("kts", [128, 4, NS], BF16)
            h0c = sb("h0c", [128, 2, 8], F32)
            ups = sb("ups", [128, 2, NS], F32)
            kcs = Ring([(sb("kcs%d" % i, [128, 512], F32), "kcs%d" % i) for i in range(2)])
            rmstmp = (sq, sd, rstd)
            self.ckpt("m_icnt")
            if l == 0:
                print("SBUF remaining after mix alloc:", self.nc.sbuf_bytes_remaining)
            self.MS("pool", Vr[:], 1.0, ["vall"])
            S_.barrier()
            self.ckpt("m_work")

            def load_dma(dst, tok0, NT, from_tok, src_tok=None, xi=0):
                key = "xT%d" % xT.index(dst)
                if from_tok:
                    xtok = xtoks[xi % 2]
                    nsub = (NT + 127) // 128
                    w = min(128, NT)
                    for sub in range(nsub):
                        self.DMA(xtok[0:w, sub, :], src_tok[tok0 + sub * 128: tok0 + sub * 128 + w, :], [], ["xtok%d" % (xi % 2)])
                else:
                    self.DMA(dst[:, :, 0:NT], self.scr[rd, :, :, tok0:tok0 + NT].rearrange("c p t -> p c t"),
                             [], [key])
                return key

            def load_tr(dst, NT, from_tok, xi=0):
                key = "xT%d" % xT.index(dst)
                if from_tok:
                    xtok = xtoks[xi % 2]
                    nsub = (NT + 127) // 128
                    w = min(128, NT)
                    for c in range(KC):
                        ps, pk = self.psA.next()
                        for sub in range(nsub):
                            self.TR(ps[:, sub * 128: sub * 128 + w], xtok[0:w, sub, c * 128:(c + 1) * 128],
                                    self.ident_f[0:w, 0:w], ["xtok%d" % (xi % 2), "ident"], [pk])
                        self.CP("act", dst[:, c, 0:NT], ps[:, 0:NT], [pk], [key])
                return key

            def load_xT(dst, tok0, NT, from_tok, src_tok=None):
                load_dma(dst, tok0, NT, from_tok, src_tok)
                return load_tr(dst, NT, from_tok)

            def in_proj(xk, x, NT, tmp=None):
                self.rms_feat(x, KC, NT, g1, hnT, D, [xk], "hnT", tmp or rmstmp)
                self.ckpt("t_rms")
                outs = {}
                for oc in (0, 1, 2, 3, 4, 5, 6, 7, 12, 13, 14, 15):
                    ps, pk = self.psA.next()
                    for kc in range(KC):
                        self.MM(ps[:, 0:NT], win[:, kc, oc * 128:(oc + 1) * 128], hnT[:, kc, 0:NT], kc == 0,
                                kc == KC - 1, ["win", "hnT"], [pk])
                    outs[oc] = (ps, pk)
                    yield oc, ps, pk

            def attention(qcol0, NQ, ktiles, B, LA=4):
                nsub = (NQ + 127) // 128
                qw = min(128, NQ)
                pending = []

                def flush_one():
                    item = pending.pop(0)
                    if item[0] == "pv":
                        _, acc, acck, P, Pk, nk, slot, h, subs, first, last, pc0 = item
                        for i_, s_ in enumerate(subs):
                            self.MM(acc[0:qw, s_ * 65:(s_ + 1) * 65], P[0:nk, pc0 + s_ * 128: pc0 + s_ * 128 + qw],
                                    Vr[0:nk, slot, h, :], first and i_ == 0, last and i_ == len(subs) - 1,
                                    [Pk, "v%d" % slot, "vall"], [acck])
                    else:
                        _, acc, acck, h = item
                        self.CP("act", accs[0:qw, h, 0:nsub * 65], acc[0:qw, 0:nsub * 65], [acck], ["accs"])

                mm_i = [0]
                gq = []
                for h in range(8):
                    hp, hc = (h % 2) * 64, h // 2
                    acc, acck = self.psB.next()
                    plan = []
                    for (slot, nk, jj) in ktiles:
                        subs = [s_ for s_ in range(nsub) if 0 <= jj + s_ <= 16]
                        if subs:
                            plan.append((slot, nk, jj, subs))
                    groups = []
                    i_ = 0
                    while i_ < len(plan):
                        if (nsub == 1 and i_ + 1 < len(plan) and plan[i_][1] == 128 and plan[i_ + 1][1] == 128
                                and plan[i_ + 1][2] == plan[i_][2] - 1):
                            groups.append([plan[i_ + 1], plan[i_]])
                            i_ += 2
                        else:
                            groups.append([plan[i_]])
                            i_ += 1
                    npv = len(plan)
                    ipv = 0
                    for grp in groups:
                        ng = len(grp)
                        nk = grp[0][1]
                        scp, sck = self.psS.next()
                        P, Pk = Pr.next()
                        for gi_, (slot, nk_, jj, subs) in enumerate(grp):
                            self.MM(scp[0:nk, gi_ * NQ:(gi_ + 1) * NQ], KT[hp:hp + 64, hc, slot * 128: slot * 128 + nk],
                                    B.qT[hp:hp + 64, hc, qcol0:qcol0 + NQ], gi_ == 0, gi_ == ng - 1,
                                    ["kt%d" % slot, B.qk], [sck])
                        self.ACT(P[0:nk, 0:ng * NQ], scp[0:nk, 0:ng * NQ], AF.Exp, [sck], [Pk], scale=0.125)
                        m0_ = grp[0][2] - JMIN
                        mm_i[0] += 1
                        self.TT("pool",
                                P[0:nk, 0:ng * NQ].rearrange("p (a q) -> p a q", a=ng),
                                P[0:nk, 0:ng * NQ].rearrange("p (a q) -> p a q", a=ng),
                                mask[0:nk, m0_:m0_ + ng, 0:NQ], ALU.mult, [Pk, "mask"], [Pk])
                        for gi_, (slot, nk_, jj, subs) in enumerate(grp):
                            pending.append(("pv", acc, acck, P, Pk, nk, slot, h, subs, ipv == 0, ipv == npv - 1, gi_ * NQ))
                            ipv += 1
                        if ipv == npv:
                            pending.append(("fin", acc, acck, h))
                        gq.append(ng)
                        while len(gq) > LA:
                            for _ in range(gq.pop(0)):
                                while pending[0][0] != "pv":
                                    flush_one()
                                flush_one()
                        yield
                while pending:
                    flush_one()
                a4 = accs[0:qw, :, 0:nsub * 65].rearrange("p h (s e) -> p h s e", e=65)
                for s_ in range(nsub):
                    self.S_.op("dve", lambda e, s_=s_: e.reciprocal(out=rec8[0:qw, :, s_:s_ + 1], in_=a4[:, :, s_, 64:65]),
                               ["accs"], ["rec8"])
                    self.TT("dve", B.att[0:qw, s_, :].rearrange("p (h e) -> p h e", e=64), a4[:, :, s_, 0:64],
                            rec8[0:qw, :, s_:s_ + 1].to_broadcast([qw, 8, 64]), ALU.mult, ["accs", "rec8"], [B.ak])
                yield

            def interleave(gens, weights):
                alive = list(gens)
                wts = list(weights)
                while alive:
                    for i in range(len(alive) - 1, -1, -1):
                        pass
                    nxt_alive, nxt_w = [], []
                    for g_, w_ in zip(alive, wts):
                        ok = True
                        for _ in range(w_):
                            try:
                                next(g_)
                            except StopIteration:
                                ok = False
                                break
                        if ok:
                            nxt_alive.append(g_)
                            nxt_w.append(w_)
                    alive, wts = nxt_alive, nxt_w

            def sched_tile(att_g, n_att, others):
                late_start = int(0.5 * n_att)
                held = [False] * len(others)
                alive = [True] * len(others)
                i = 0

                def adv(j):
                    if not alive[j]:
                        return
                    try:
                        tag = next(others[j])
                        if tag == "late":
                            held[j] = True
                    except StopIteration:
                        alive[j] = False

                for _ in att_g:
                    i += 1
                    for j in range(len(others)):
                        if i % (3 + 2 * j) == 0 and alive[j] and not (held[j] and i < late_start):
                            adv(j)
                for j in range(len(others)):
                    while alive[j]:
                        adv(j)

            def att_finish(col0, NQ, B):
                nsub = (NQ + 127) // 128
                qw = min(128, NQ)
                for s_ in range(nsub):
                    self.ACT(junk[0:qw, :], B.att[0:qw, s_, :], AF.Square, [B.ak], ["junk", "ssa"], accum=ssa[0:qw, s_:s_ + 1])
                self.ACT(ssa[0:qw, 0:nsub], ssa[0:qw, 0:nsub], AF.Ln, ["junk", "ssa"], ["ssa"], bias=self.eps_col[0:qw, 0:1],
                         scale=1.0 / 512)
                self.ACT(ssa[0:qw, 0:nsub], ssa[0:qw, 0:nsub], AF.Exp, ["ssa"], ["ssa"], scale=-0.5)
                self.TT("dve", B.att[0:qw, 0:nsub, :], B.att[0:qw, 0:nsub, :],
                        ssa[0:qw, 0:nsub].unsqueeze(2).to_broadcast([qw, nsub, 512]), ALU.mult, [B.ak, "ssa"], [B.ak])
                for c in range(4):
                    ps, pk = self.psA.next()
                    for s_ in range(nsub):
                        self.TR(ps[:, s_ * 128: s_ * 128 + qw], B.att[0:qw, s_, c * 128:(c + 1) * 128],
                                self.ident_f[0:qw, 0:qw], [B.ak, "ident"], [pk])
                    self.ACT(B.mixT[:, c, col0:col0 + NQ], ps[:, 0:NQ], AF.Copy, [pk, "pccols"], [B.mk], scale=gatt[:, c:c + 1])

            def ssm(col0, NT, first_zero, B):
                for gp in range(8):
                    ch = gp // 4
                    pre, prk = self.psA.next()
                    self.MM(pre[:, 0:NT], Bre[:, gp, :], B.usb[:, ch, col0:col0 + NT], True, True, [B.ubk, "ssm_b_re"], [prk])
                    pim, pik = self.psA.next()
                    self.MM(pim[:, 0:NT], Bim[:, gp, :], B.usb[:, ch, col0:col0 + NT], True, True, [B.ubk, "ssm_b_im"], [pik])
                    wre, wrk = wtmp.next()
                    wim, wik = wtmp.next()
                    t1, t1k = wtmp.next()
                    self.TT("dve", wre[:, 0:NT], pre[:, 0:NT], Tre[:, gp, 0:NT], ALU.mult, [prk, "Tre"], [wrk])
                    self.TT("dve", t1[:, 0:NT], pim[:, 0:NT], Tim[:, gp, 0:NT], ALU.mult, [pik, "Tim"], [t1k])
                    self.TT("pool", wre[:, 0:NT], wre[:, 0:NT], t1[:, 0:NT], ALU.subtract, [wrk, t1k], [wrk])
                    t2, t2k = wtmp.next()
                    self.TT("dve", wim[:, 0:NT], pim[:, 0:NT], Tre[:, gp, 0:NT], ALU.mult, [pik, "Tre"], [wik])
                    self.TT("dve", t2[:, 0:NT], pre[:, 0:NT], Tim[:, gp, 0:NT], ALU.mult, [prk, "Tim"], [t2k])
                    self.TT("pool", wim[:, 0:NT], wim[:, 0:NT], t2[:, 0:NT], ALU.add, [wik, t2k], [wik])
                    for pl, (G, wv, wk) in enumerate(((Gre, wre, wrk), (Gim, wim, wik))):
                        self.S_.op("dve", lambda e, G=G, wv=wv, pl=pl, gp=gp: e.tensor_tensor_scan(
                            out=G[:, gp, 0:NT], data0=rho[:, gp:gp + 1].to_broadcast([128, NT]), data1=wv[:, 0:NT],
                            initial=gst[:, pl, gp:gp + 1], op0=ALU.mult, op1=ALU.add),
                            [wk, "rho", "gst"], ["G%d" % pl])
                    yield
                csn, snn = cs[:, :, 0:NT], sn[:, :, 0:NT]
                self.TT("dve", ta[:, :, 0:NT], Gre[:, :, 0:NT], csn, ALU.mult, ["G0", "cs"], ["ta"])
                self.TT("dve", tb2[:, :, 0:NT], Gim[:, :, 0:NT], snn, ALU.mult, ["G1", "sn"], ["tb2"])
                self.TT("dve", hre[:, :, 0:NT], ta[:, :, 0:NT], tb2[:, :, 0:NT], ALU.subtract, ["ta", "tb2"], ["hre"])
                yield
                self.TT("dve", ta[:, :, 0:NT], Gim[:, :, 0:NT], csn, ALU.mult, ["G1", "cs"], ["ta"])
                self.TT("dve", tb2[:, :, 0:NT], Gre[:, :, 0:NT], snn, ALU.mult, ["G0", "sn"], ["tb2"])
                self.STT(himn[:, :, 0:NT], ta[:, :, 0:NT], -1.0, tb2[:, :, 0:NT], ALU.mult, ALU.subtract,
                         ["ta", "tb2"], ["himn"])
                yield
                glr, gli = Gre[:, :, NT - 1], Gim[:, :, NT - 1]
                for dst, col in ((gst, NT), (hl, NT - 1)):
                    cN, sN = cs[:, :, col], sn[:, :, col]
                    self.TT("dve", gl_t[:, 0, :], glr, cN, ALU.mult, ["G0", "cs"], ["glt0"])
                    self.TT("dve", gl_t[:, 1, :], gli, sN, ALU.mult, ["G1", "sn"], ["glt1"])
                    self.TT("dve", gl_t[:, 2, :], gli, cN, ALU.mult, ["G1", "cs"], ["glt2"])
                    self.TT("dve", gl_t[:, 3, :], glr, sN, ALU.mult, ["G0", "sn"], ["glt3"])
                    dk = "gst" if dst is gst else "hl"
                    self.TT("dve", dst[:, 0, :], gl_t[:, 0, :], gl_t[:, 1, :], ALU.subtract, ["glt0", "glt1"], [dk])
                    self.TT("dve", dst[:, 1, :], gl_t[:, 2, :], gl_t[:, 3, :], ALU.add, ["glt2", "glt3"], [dk])
                if self.dbg and ph == 0 and col0 == 0 and not getattr(self, "_dumped", False):
                    self._dumped = True
                    dd = self.O["dbg"]
                    for i, (tt, kk) in enumerate(((Tre, "Tre"), (Tim, "Tim"), (cs, "cs"), (sn, "sn"))):
                        self.DMA(dd[5 + i, :, :, 0:TW].rearrange("c p t -> p c t"), tt[:, :, :], [kk], [])
                    self.DMA(dd[9, :, :, 0:NT].rearrange("c p t -> p c t"), Gre[:, :, 0:NT], ["G0"], [])
                    self.DMA(dd[10, :, :, 0:NT].rearrange("c p t -> p c t"), Gim[:, :, 0:NT], ["G1"], [])
                    self.DMA(dd[11, 0, :, 0:8], rho[:, :], ["rho"], [])
                    self.DMA(dd[11, 1, :, 0:NT], B.usf[:, 0, 0:NT], [B.ufk], [])
                    self.DMA(dd[11, 2, :, 0:NT], B.usf[:, 1, 0:NT], [B.ufk], [])
                yield "late"
                for ch in range(2):
                    yp_, ypk = self.psC.next()
                    for i, gp in enumerate(range(ch * 4, ch * 4 + 4)):
                        self.MM(yp_[:, 0:NT], Cre[:, gp, :], hre[:, gp, 0:NT], i == 0, False, ["hre", "ssm_c_re"], [ypk])
                        self.MM(yp_[:, 0:NT], Cim[:, gp, :], himn[:, gp, 0:NT], False, i == 3, ["himn", "ssm_c_im"], [ypk])
                    self.STT(yss[:, ch, 0:NT], B.usf[:, ch, col0:col0 + NT], dskip[:, ch:ch + 1], yp_[:, 0:NT], ALU.mult,
                             ALU.add, [ypk, B.ufk, "pccols"], ["yss"])
                    yield
                yv, gv, sv = yss[:, :, 0:NT], gact[:, :, 0:NT], sgl[:, :, 0:NT]
                self.TT("dve", gv, yv, yv, ALU.mult, ["yss"], ["gact"])
                self.TS("dve", gv, gv, 0.044715, 1.0, ALU.mult, ALU.add, ["gact"], ["gact"])
                self.TT("dve", gv, gv, yv, ALU.mult, ["gact", "yss"], ["gact"])
                self.ACT(sv, gv, AF.Exp, ["gact"], ["sgl"], scale=-1.5957691216057308)
                self.TS("dve", sv, sv, 1.0, None, ALU.add, None, ["sgl"], ["sgl"])
                self.S_.op("dve", lambda e, sv=sv: e.reciprocal(out=sv, in_=sv), ["sgl"], ["sgl"])
                self.TT("dve", gv, yv, sv, ALU.mult, ["yss", "sgl"], ["gact"])
                self.CP("dve", gactb[:, :, 0:NT], gv, ["gact"], ["gactb"])
                yield
                for oc in range(2):
                    zp, zk = self.psC.next()
                    for kc in range(2):
                        self.MM(zp[:, 0:NT], wglu[:, kc, oc * 128:(oc + 1) * 128], gactb[:, kc, 0:NT], kc == 0, kc == 1,
                                ["wglu", "gactb"], [zk])
                    self.ACT(sgl[:, oc, 0:NT], zp[:, 0:NT], AF.Exp, [zk], ["sgl"], bias=nbglu[:, oc:oc + 1], scale=-1.0)
                self.TS("dve", sv, sv, 1.0, None, ALU.add, None, ["sgl"], ["sgl"])
                self.S_.op("dve", lambda e, sv=sv: e.reciprocal(out=sv, in_=sv), ["sgl"], ["sgl"])
                self.TT("dve", yss[:, :, 0:NT], gv, sv, ALU.mult, ["gact", "sgl"], ["yss"])
                yield
                self.rms_feat(yss, 2, NT, gssm, _View(B.mixT, 4, col0), 256, ["yss"], B.mk, rmstmp)
                yield

            def pool_mix(col0, NT, first, B):
                W = 16 + NT
                self.TT("dve", pl1[:, :, 1:W], B.upl[:, :, 1:W], B.upl[:, :, 0:W - 1], ALU.add, [B.upk], ["pl1"])
                self.TT("dve", pl2[:, :, 3:W], pl1[:, :, 3:W], pl1[:, :, 1:W - 2], ALU.add, ["pl1"], ["pl2"])
                self.CP("dve", wsel[0:64, 0, 0:NT], pl1[0:64, 0, 16:W], ["pl1"], ["wsel"])
                self.CP("dve", wsel[64:128, 0, 0:NT], pl2[64:128, 0, 16:W], ["pl2"], ["wsel"])
                yield
                self.TT("dve", pl1[:, :, 7:W], pl2[:, :, 7:W], pl2[:, :, 3:W - 4], ALU.add, ["pl2", "wsel"], ["pl1"])
                self.TT("dve", pl2[:, :, 15:W], pl1[:, :, 15:W], pl1[:, :, 7:W - 8], ALU.add, ["pl1", "wsel"], ["pl2"])
                self.CP("dve", wsel[0:64, 1, 0:NT], pl1[0:64, 1, 16:W], ["pl1"], ["wsel"])
                self.CP("dve", wsel[64:128, 1, 0:NT], pl2[64:128, 1, 16:W], ["pl2"], ["wsel"])
                yield
                for ch in range(2):
                    self.STT(pooled[:, ch, 0:NT], wsel[:, ch, 0:NT], icnt[:, ch:ch + 1], B.upl[:, ch, 16:W], ALU.mult,
                             ALU.subtract, ["wsel", "icnt", B.upk], ["pooled"])
                if first:
                    self.TT("dve", t16[:], wsel[:, :, 0:16], icnt_tab[:], ALU.mult, ["wsel", "icnt_tab"], ["t16"])
                    self.TT("dve", pooled[:, :, 0:16], t16[:], B.upl[:, :, 16:32], ALU.subtract, ["t16", B.upk], ["pooled"])
                yield "late"
                for ch in range(2):
                    pp, ppk = self.psC.next()
                    self.MM(pp[:, 0:NT], poolw[:, ch, :], pooled[:, ch, 0:NT], True, True, ["poolw", "pooled"], [ppk])
                    self.ACT(pout[:, ch, 0:NT], pp[:, 0:NT], AF.Copy, [ppk, "pccols"], ["pout"], scale=pscale[:, ch:ch + 1])
                yield
                self.rms_feat(pout, 2, NT, gpool, _View(B.mixT, 6, col0), 256, ["pout"], B.mk, rmstmp)
                yield

            def out_proj(xk, x, NT, tok0, B):
                for oc in range(KC):
                    ps, pk = self.psA.next()
                    for kc in range(KC):
                        self.MM(ps[:, 0:NT], wout[:, kc, oc * 128:(oc + 1) * 128], B.mixT[:, kc, 0:NT], kc == 0,
                                kc == KC - 1, ["wout", B.mk], [pk])
                    self.TT("dve", x[:, oc, 0:NT], ps[:, 0:NT], x[:, oc, 0:NT], ALU.add, [pk, xk], [xk])
                    yield
                self.DMA(self.scr[wr, :, :, tok0:tok0 + NT].rearrange("c p t -> p c t"), x[:, :, 0:NT], [xk], [])
                if self.dbg:
                    self.DMA(self.O["dbg"][ph, :, :, tok0:tok0 + NT].rearrange("c p t -> p c t"), x[:, :, 0:NT], [xk], [])
                    if ph == 0:
                        self.DMA(self.O["dbg"][4, :, :, tok0:tok0 + NT].rearrange("c p t -> p c t"), B.mixT[:, :, 0:NT], [B.mk], [], q="pool")

            def kv_tokmajor(NT, cols, is_v, slots, out_aps):
                nsub = len(cols)
                for s_, (c0, w) in enumerate(cols):
                    ps, pk = self.psA.next()
                    off = 1024 if is_v else 512
                    for kc in range(KC):
                        self.MM(ps[0:w, 0:512], hnT[:, kc, c0:c0 + w], win[:, kc, off:off + 512], kc == 0, kc == KC - 1,
                                ["hnT", "win"], [pk])
                    if is_v and slots[s_] is not None:
                        self.CP("act", Vr[0:w, slots[s_], :, 0:64], ps[0:w, 0:512].rearrange("p (h e) -> p h e", e=64),
                                [pk], ["v%d" % slots[s_]])
                    if out_aps[s_] is not None:
                        stg, sk = kvst.next()
                        self.CP("act", stg[0:w, :], ps[0:w, 0:512], [pk], [sk])
                        self.DMA(out_aps[s_], stg[0:w, :], [sk], [])

            self.MS("pool", gst[:], 0.0, ["gst"])
            self.MS("pool", upl[:], 0.0, [bs[0].upk])
            self.MS("pool", upl_b[:], 0.0, [bs[1].upk])
            nst = S // TS_
            keep0 = S - self.NKEEP
            rmstmp1 = (sq1, sd1, rstd1)
            xkeys = {}

            def P1(sti):
                x, B = xT[sti % 3], bs[sti % 2]
                tok0 = sti * TS_
                xk = load_tr(x, TS_, ph == 0, sti)
                xkeys[sti] = xk
                if sti > 0:
                    Bp = bs[(sti - 1) % 2]
                    self.CP("dve", B.upl[:, :, 1:16], Bp.upl[:, :, TS_ + 1:TS_ + 16], [Bp.upk], [B.upk])
                yield
                a0 = tok0 // 128
                slots = [(a0 + s_) % NSLOT for s_ in range(NSUB)]
                for oc, ps, pk in in_proj(xk, x, TS_, rmstmp1):
                    if oc < 4:
                        self.CP("act", B.qT[:, oc, 0:TS_], ps[:, 0:TS_], [pk], [B.qk])
                    elif oc < 8:
                        for s_ in range(NSUB):
                            self.CP("act", KT[:, oc - 4, slots[s_] * 128:(slots[s_] + 1) * 128],
                                    ps[:, s_ * 128:(s_ + 1) * 128], [pk], ["kt%d" % slots[s_]])
                    elif oc < 14:
                        self.CP("act", B.usb[:, oc - 12, 0:TS_], ps[:, 0:TS_], [pk], [B.ubk])
                        self.CP("act", B.usf[:, oc - 12, 0:TS_], ps[:, 0:TS_], [pk], [B.ufk])
                    else:
                        self.CP("act", B.upl[:, oc - 14, 16:16 + TS_], ps[:, 0:TS_], [pk], [B.upk])
                    yield
                cols = [(s_ * 128, 128) for s_ in range(NSUB)]
                outs_v = [O["vp"][l, tok0 + s_ * 128 - keep0: tok0 + s_ * 128 - keep0 + 128, :]
                          if tok0 + s_ * 128 >= keep0 else None for s_ in range(NSUB)]
                outs_k = [O["kp"][l, tok0 + s_ * 128 - keep0: tok0 + s_ * 128 - keep0 + 128, :]
                          if tok0 + s_ * 128 >= keep0 else None for s_ in range(NSUB)]
                kv_tokmajor(TS_, cols, True, slots, outs_v)
                yield
                if any(o is not None for o in outs_k):
                    kv_tokmajor(TS_, cols, False, [None] * NSUB, outs_k)
                yield

            def P3(sti):
                x, B = xT[sti % 3], bs[sti % 2]
                att_finish(0, TS_, B)
                yield
                yield from out_proj(xkeys[sti], x, TS_, sti * TS_, B)

            def drain(g):
                for _ in g:
                    pass

            def sched2(att_g, n_att, sides):
                late_start = int(0.3 * n_att)
                st_ = [{"g": g, "p": p, "s": s0, "h": h, "held": False, "alive": True} for (g, p, s0, h) in sides]

                def adv(d):
                    try:
                        if next(d["g"]) == "late":
                            d["held"] = True
                    except StopIteration:
                        d["alive"] = False

                i = 0
                for _ in att_g:
                    i += 1
                    for d in st_:
                        if d["alive"] and i >= d["s"] and (i - d["s"]) % d["p"] == 0 and not (d["h"] and d["held"] and i < late_start):
                            adv(d)
                for d in st_:
                    while d["alive"]:
                        adv(d)

            for i_ in range(min(2, nst)):
                load_dma(xT[i_ % 3], i_ * TS_, TS_, ph == 0, I["xp"], i_)
            drain(P1(0))
            for sti in range(nst):
                B = bs[sti % 2]
                a0 = sti * TS_ // 128
                ktiles = [(a % NSLOT, 128, a0 - a) for a in range(max(0, a0 - 16), a0 + NSUB)]
                n_att = 8 * ((len(ktiles) + 1) // 2) + 1
                sides = []
                if sti > 0:
                    sides.append((P3(sti - 1), 2, 1, False))
                sides.append((ssm(0, TS_, sti == 0, B), 3, 2, True))
                sides.append((pool_mix(0, TS_, sti == 0, B), 6, 4, True))
                if sti + 1 < nst:
                    sides.append((P1(sti + 1), 2, 3, False))
                sched2(attention(0, TS_, ktiles, B), n_att, sides)
                if sti + 2 < nst:
                    load_dma(xT[(sti + 2) % 3], (sti + 2) * TS_, TS_, ph == 0, I["xp"], sti + 2)
            drain(P3(nst - 1))
            Bl = bs[(nst - 1) % 2]
            self.ckpt("m_prompt")
            for pl, nm in ((0, "rep"), (1, "imp")):
                self.cols_to_rows(hl[:, pl, :], O[nm][l].rearrange("(gp gl) p -> gp (gl p)", gl=2), 8, ["hl"])
            for ch in range(2):
                self.cols_to_rows(Bl.upl[:, ch, TS_ + 1:TS_ + 16], O["poolp"][l, :, ch * 128:(ch + 1) * 128], 15, [Bl.upk])

            self.ckpt("m_pout")
            xs_ = xT[nst % 3]
            B0 = bs[0]
            xk = load_xT(xs_, 0 if ph == 0 else S, NS, ph == 0, I["xs"])
            for oc, ps, pk in in_proj(xk, xs_, NS):
                if oc < 4:
                    self.CP("act", B0.qT[:, oc, 0:NS], ps[:, 0:NS], [pk], [B0.qk])
                elif oc < 8:
                    self.CP("act", kts[:, oc - 4, :], ps[:, 0:NS], [pk], ["kts"])
                elif oc < 14:
                    self.CP("act", B0.usb[:, oc - 12, 0:NS], ps[:, 0:NS], [pk], [B0.ubk])
                    self.CP("dve", B0.usf[:, oc - 12, 0:NS], ps[:, 0:NS], [pk], [B0.ufk])
                else:
                    self.CP("dve", ups[:, oc - 14, 0:NS], ps[:, 0:NS], [pk], ["ups"])
            for s in range(NSEQ):
                c0 = s * TD
                for m in range(16):
                    kc_, kck = kcs.next()
                    self.DMA(kc_[:], I["ck"][l, s, m * 128:(m + 1) * 128, :], [], [kck])
                    psk, pkk = self.psA.next()
                    for c in range(4):
                        self.TR(psk[:, c * 128:(c + 1) * 128], kc_[:, c * 128:(c + 1) * 128], self.ident_f[:],
                                [kck, "ident"], [pkk])
                    self.CP("act" if m % 2 else "dve", KT[:, :, m * 128:(m + 1) * 128],
                            psk[:, 0:512].rearrange("p (c k) -> p c k", k=128), [pkk], ["kt%d" % m])
                    vst, vsk = xtoks[m % 2], "xtok%d" % (m % 2)
                    self.DMA(vst[:, 0, 0:512], I["cv"][l, s, m * 128:(m + 1) * 128, :], [], [vsk], q="act")
                    self.CP("pool" if m % 2 else "dve", Vr[:, m, :, 0:64],
                            vst[:, 0, 0:512].rearrange("k (h e) -> k h e", e=64), [vsk], ["v%d" % m])
                self.CP("dve", KT[:, :, 16 * 128:16 * 128 + TD], kts[:, :, c0:c0 + TD], ["kts"], ["kt16"])
                kv_tokmajor(TD, [(c0, TD)], True, [16], [O["vs"][l, s]])
                kv_tokmajor(TD, [(c0, TD)], False, [None], [O["ks"][l, s]])
                ktiles = [(m, 128, 16 - m) for m in range(16)] + [(16, TD, 0)]
                interleave([attention(c0, TD, ktiles, B0)], [1])
                att_finish(c0, TD, B0)
                for pl, nm in ((0, "sre"), (1, "sim")):
                    self.rows_to_cols(I[nm][l, s].rearrange("(gp gl) p -> gp (gl p)", gl=2), h0c[:, pl, :], 8, ["h0c"])
                c1_, s1_ = cs[:, :, 1], sn[:, :, 1]
                self.TT("pool", gl_t[:, 0, :], h0c[:, 0, :], c1_, ALU.mult, ["h0c", "cs"], ["glt0"])
                self.TT("pool", gl_t[:, 1, :], h0c[:, 1, :], s1_, ALU.mult, ["h0c", "sn"], ["glt1"])
                self.TT("pool", gl_t[:, 2, :], h0c[:, 1, :], c1_, ALU.mult, ["h0c", "cs"], ["glt2"])
                self.TT("pool", gl_t[:, 3, :], h0c[:, 0, :], s1_, ALU.mult, ["h0c", "sn"], ["glt3"])
                self.TT("pool", gst[:, 0, :], gl_t[:, 0, :], gl_t[:, 1, :], ALU.subtract, ["glt0", "glt1"], ["gst"])
                self.TT("pool", gst[:, 1, :], gl_t[:, 2, :], gl_t[:, 3, :], ALU.add, ["glt2", "glt3"], ["gst"])
                interleave([ssm(c0, TD, False, B0)], [1])
                for pl, nm in ((0, "res"), (1, "ims")):
                    self.cols_to_rows(hl[:, pl, :], O[nm][l, s].rearrange("(gp gl) p -> gp (gl p)", gl=2), 8, ["hl"])
                for ch in range(2):
                    self.rows_to_cols(I["spool"][l, s, :, ch * 128:(ch + 1) * 128], B0.upl[:, ch, 1:16], 15, [B0.upk], q="act")
                self.CP("dve", B0.upl[:, :, 16:16 + TD], ups[:, :, c0:c0 + TD], ["ups"], [B0.upk])
                for ch in range(2):
                    self.cols_to_rows(B0.upl[:, ch, TD + 1:TD + 16], O["pools"][l, s, :, ch * 128:(ch + 1) * 128], 15, [B0.upk], q="act")
                interleave([pool_mix(c0, TD, False, B0)], [1])
            drain(out_proj(xk, xs_, NS, S, B0))
            S_.barrier()

    def ffn_phase(self, l, ph, last):
        nc, S_, I, O = self.nc, self.S_, self.I, self.O
        S, TS_ = self.S, self.TSF
        rd, wr = (ph - 1) % 2, ph % 2
        self.rms_mode = "pow"
        self.set_rings(False)
        with contextlib.ExitStack() as st:
            sb = lambda n, s, d: self.sb(st, n, s, d)
            wup = sb("wup", [128, KC, 2 * DFF], BF16)
            wdn = sb("wdn", [128, AC, D], BF16)
            for c in range(KC):
                for hh in range(4):
                    self.DMA(wup[:, c, hh * 1408:(hh + 1) * 1408], I["w_up"][l, c * 128:(c + 1) * 128, hh * 1408:(hh + 1) * 1408],
                             [], ["wup"], q="pool")
            for c in range(AC):
                self.DMA(wdn[:, c, :], I["w_down"][l, c * 128:(c + 1) * 128, :], [], ["wdn"], q="pool")
            pa = self.load_rows_T(st, [(I["norm2_g"][l].rearrange("(c p) -> c p", p=128), 0),
                                       (I["norm_f_g"].rearrange("(c p) -> c p", p=128), 8),
                                       (I["conv_b"][l].rearrange("(c p) -> c p", p=128), 16),
                                       (I["conv_w"][l, 0].rearrange("(c p) -> c p", p=128), 64)], "pa")
            pb = self.load_rows_T(st, [(I["conv_w"][l, 1].rearrange("(c p) -> c p", p=128), 0),
                                       (I["conv_w"][l, 2].rearrange("(c p) -> c p", p=128), 64)], "pb")
            g2, gf, cb, cw0, cw1, cw2 = pa[:, 0:8], pa[:, 8:16], pa[:, 16:60], pa[:, 64:108], pb[:, 0:44], pb[:, 64:108]
            xT = [sb("fx%d" % i, [128, KC, TS_], F32) for i in range(2)]
            sq = sb("fsq", [128, KC, TS_], BF16)
            sd = sb("fsd", [128, TS_], F32)
            rstd = sb("frstd", [128, TS_], F32)
            hnT = sb("fhn", [128, KC, TS_], BF16)
            actT = sb("factT", [128, AC, TS_], BF16)
            raws = Ring([(sb("raw%d" % i, [128, TS_ + 2 * NSEQ], F32), "raw%d" % i) for i in range(6)])
            cvs = Ring([(sb("cv%d" % i, [128, TS_], F32), "cv%d" % i) for i in range(6)])
            sgs = Ring([(sb("sg%d" % i, [128, TS_], F32), "sg%d" % i) for i in range(2)])
            uptail = sb("uptail", [128, FC, 2], F32)
            stail = sb("stail", [128, FC, NSEQ, 2], F32)
            ytok = Ring([(sb("ytok%d" % i, [128, D], F32), "ytok%d" % i) for i in range(1)]) if last else None
            rmstmp = (sq, sd, rstd)
            if l == 0:
                print("SBUF remaining after ffn alloc:", self.nc.sbuf_bytes_remaining)
            self.MS("pool", uptail[:], 0.0, ["uptail"])
            for s_ in range(NSEQ):
                for r_ in range(2):
                    self.rows_to_cols(I["sconv"][l, s_, r_].rearrange("(c p) -> c p", p=128), stail[:, :, s_, r_], FC, ["stail"],
                                      q="act")
            S_.barrier()

            def load_x(dst, tok0, NT):
                key = "fx%d" % xT.index(dst)
                self.DMA(dst[:, :, 0:NT], self.scr[rd, :, :, tok0:tok0 + NT].rearrange("c p t -> p c t"), [], [key])
                return key

            hn2 = [hnT, sb("fhnb", [128, KC, TS_], BF16)]
            act2 = [actT, sb("factTb", [128, AC, TS_], BF16)]
            dstate = {"d": None}

            def stA(xk, x, NT, bi):
                self.rms_feat(x, KC, NT, g2, hn2[bi], D, [xk], "fhn%d" % bi, rmstmp)

            def stB(NT, nseq, tl, tail, tailk, bi, lo, hi):
                hn_, hk_, at_, ak_ = hn2[bi], "fhn%d" % bi, act2[bi], "actT%d" % bi
                for fc in range(lo, hi):
                    st_ = []
                    for cidx in (fc, fc + AC):
                        ps, pk = self.psA.next()
                        for kc in range(KC):
                            self.MM(ps[:, 0:NT], wup[:, kc, cidx * 128:(cidx + 1) * 128], hn_[:, kc, 0:NT], kc == 0,
                                    kc == KC - 1, ["wup", hk_], [pk])
                        raw, rk = raws.next()
                        r3 = raw[:, 0:nseq * (tl + 2)].rearrange("p (s t) -> p s t", t=tl + 2)
                        tl_ap = tail[:, cidx, :].unsqueeze(1) if nseq == 1 else tail[:, cidx, :, :]
                        cv, ck = cvs.next()
                        c3 = cv[:, 0:NT].rearrange("p (s t) -> p s t", t=tl)
                        p3 = ps[:, 0:NT].rearrange("p (s t) -> p s t", t=tl)
                        tk_ = "%s%d" % (tailk, cidx)
                        self.CP("pool", r3[:, :, 0:2], tl_ap, [tailk, tk_], [rk])
                        self.CP("act", r3[:, :, 2:2 + tl], p3, [pk], [rk])
                        self.ACT(c3, p3, AF.Identity, [pk, "pacols", "pbcols"], [ck], bias=cb[:, cidx:cidx + 1],
                                 scale=cw2[:, cidx:cidx + 1])
                        self.CP("pool", tl_ap, r3[:, :, tl:tl + 2], [rk], [tk_])
                        st_.append((cidx, r3, c3, rk, ck, cv))
                    for (cidx, r3, c3, rk, ck, cv) in st_:
                        self.STT(c3, r3[:, :, 1:1 + tl], cw1[:, cidx:cidx + 1], c3, ALU.mult, ALU.add, [rk, ck, "pbcols"], [ck])
                    for (cidx, r3, c3, rk, ck, cv) in st_:
                        self.STT(c3, r3[:, :, 0:tl], cw0[:, cidx:cidx + 1], c3, ALU.mult, ALU.add, [rk, ck, "pacols"], [ck])

                    def tail_part(fc=fc, st_=st_, at_=at_, ak_=ak_, NT=NT):
                        (_, _, _, _, cak, ca), (_, _, _, _, cgk, cg) = st_
                        sg, sgk = sgs.next()
                        self.ACT(sg[:, 0:NT], cg[:, 0:NT], AF.Silu, [cgk], [sgk])
                        self.TT("dve", at_[:, fc, 0:NT], sg[:, 0:NT], ca[:, 0:NT], ALU.mult, [sgk, cak], [ak_])

                    if dstate["d"] is not None:
                        dstate["d"]()
                    dstate["d"] = tail_part

            def flushB():
                if dstate["d"] is not None:
                    dstate["d"]()
                    dstate["d"] = None

            def stC(xk, x, tok0, NT, nseq, bi):
                at_, ak_ = act2[bi], "actT%d" % bi
                for oc in range(KC):
                    ps, pk = self.psA.next()
                    for kc in range(AC):
                        self.MM(ps[:, 0:NT], wdn[:, kc, oc * 128:(oc + 1) * 128], at_[:, kc, 0:NT], kc == 0, kc == AC - 1,
                                ["wdn", ak_], [pk])
                    self.TT("dve", x[:, oc, 0:NT], ps[:, 0:NT], x[:, oc, 0:NT], ALU.add, [pk, xk], [xk])
                if self.dbg:
                    self.DMA(self.O["dbg"][ph, :, :, tok0:tok0 + NT].rearrange("c p t -> p c t"), x[:, :, 0:NT], [xk], [])
                if not last:
                    self.DMA(self.scr[wr, :, :, tok0:tok0 + NT].rearrange("c p t -> p c t"), x[:, :, 0:NT], [xk], [])
                    return
                ynT = at_[:, :, :].rearrange("p a b -> p (a b)").bitcast(F32)[:, 0:KC * TS_].rearrange("p (c t) -> p c t", t=TS_)
                self.rms_feat(x, KC, NT, gf, ynT, D, [xk], ak_, rmstmp)
                nsub = (NT + 127) // 128
                w = min(128, NT)
                ydst = O["yp"] if nseq == 1 else O["ys"]
                t0_ = tok0 if nseq == 1 else 0
                for s_ in range(nsub):
                    yt, ytk = ytok.next()
                    for half in range(2):
                        ps, pk = self.psA.next()
                        for c4 in range(4):
                            c = half * 4 + c4
                            self.TR(ps[0:w, c4 * 128:(c4 + 1) * 128], ynT[:, c, s_ * 128: s_ * 128 + w], self.ident_f[:],
                                    [ak_, "ident"], [pk])
                        self.CP("act" if half else "dve", yt[0:w, half * 512:(half + 1) * 512], ps[0:w, 0:512], [pk], [ytk])
                    self.DMA(ydst[t0_ + s_ * 128: t0_ + s_ * 128 + w, :], yt[0:w, :], [ytk], [])

            nst = S // TS_
            NPRE = 4
            xk_of = {0: load_x(xT[0], 0, TS_)}
            stA(xk_of[0], xT[0], TS_, 0)
            stB(TS_, 1, TS_, uptail, "uptail", 0, 0, NPRE)
            for sti in range(nst):
                bi = sti % 2
                x, xk = xT[bi], xk_of[sti]
                if sti + 1 < nst:
                    xk_of[sti + 1] = load_x(xT[(sti + 1) % 2], (sti + 1) * TS_, TS_)
                stB(TS_, 1, TS_, uptail, "uptail", bi, NPRE, 10)
                if sti + 1 < nst:
                    stA(xk_of[sti + 1], xT[(sti + 1) % 2], TS_, 1 - bi)
                stB(TS_, 1, TS_, uptail, "uptail", bi, 10, AC)
                if sti + 1 < nst:
                    stB(TS_, 1, TS_, uptail, "uptail", 1 - bi, 0, NPRE)
                else:
                    flushB()
                stC(xk, x, sti * TS_, TS_, 1, bi)
            allup = ["uptail"] + ["uptail%d" % c for c in range(FC)]
            for r_ in range(2):
                self.cols_to_rows(uptail[:, :, r_], O["convp"][l, r_].rearrange("(c p) -> c p", p=128), FC, allup, q="act")
            xs_ = xT[nst % 2]
            xks = load_x(xs_, S, NS)
            stA(xks, xs_, NS, 0)
            stB(NS, NSEQ, TD, stail, "stail", 0, 0, AC)
            flushB()
            stC(xks, xs_, S, NS, NSEQ, 0)
            allst = ["stail"] + ["stail%d" % c for c in range(FC)]
            for s_ in range(NSEQ):
                for r_ in range(2):
                    self.cols_to_rows(stail[:, :, s_, r_], O["convs"][l, s_, r_].rearrange("(c p) -> c p", p=128), FC, allst,
                                      q="act")
            S_.barrier()

    def build(self):
        self.declare()
        with contextlib.ExitStack() as st:
            self.S_ = Sched(self.nc, st)
            try:
                self.setup(st)
                self.ckpt("setup")
                ph = 0
                for l in range(2):
                    self.mix_phase(l, ph)
                    self.ckpt("mix%d" % l)
                    ph += 1
                    self.ffn_phase(l, ph, last=(l == 1))
                    self.ckpt("ffn%d" % l)
                    ph += 1
            except _Stop:
                self.S_.barrier()
            self.S_.emit()
        return self.nc


class _View:
    def __init__(self, t, c0, col0):
        self.t, self.c0, self.col0 = t, c0, col0

    def __getitem__(self, key):
        p, c, cols = key
        return self.t[p, self.c0 + c, self.col0 + cols.start: self.col0 + cols.stop]


_NC_CACHE = {}


def _get_nc():
    if "nc" not in _NC_CACHE:
        _NC_CACHE["nc"] = Builder().build()
    return _NC_CACHE["nc"]


_WNAMES = ("norm1_g", "w_in", "ssm_log_dt", "ssm_a_re", "ssm_a_im", "ssm_b_re", "ssm_b_im", "ssm_c_re", "ssm_c_im",
           "ssm_d", "ssm_w_glu", "ssm_b_glu", "pool_w", "pool_scale", "out_norm_att", "out_norm_ssm", "out_norm_pool",
           "w_out", "norm2_g", "w_up", "conv_w", "conv_b", "w_down", "norm_f_g")


def make_in_maps(inp, S=8192):
    f = lambda a: np.ascontiguousarray(np.asarray(a, dtype=np.float32))
    maps = []
    zeros_p = np.zeros((S, D), np.float32)
    for c in range(NCORES):
        m = {}
        m["xp"] = f(inp["x_prompt"][c]) if c < inp["x_prompt"].shape[0] else zeros_p
        sl = slice(c * NSEQ, (c + 1) * NSEQ)
        m["xs"] = f(inp["x_sample"][sl]).reshape(NS, D)
        m["ck"] = f(inp["cache_k"][:, sl]).reshape(2, NSEQ, PAST, 512)
        m["cv"] = f(inp["cache_v"][:, sl]).reshape(2, NSEQ, PAST, 512)
        m["sre"] = f(inp["state_ssm_re"][:, sl])
        m["sim"] = f(inp["state_ssm_im"][:, sl])
        m["spool"] = f(inp["state_pool"][:, sl])
        m["sconv"] = f(inp["state_conv"][:, sl])
        for n in _WNAMES:
            m[n] = f(inp[n])
        maps.append(m)
    return maps


def gather(res, S=8192, nkeep=2048, nb=2):
    r = res
    cat = lambda n: np.concatenate([r[c][n] for c in range(NCORES)], axis=1)
    y_prompt = np.stack([r[c]["yp"] for c in range(nb)], 0)
    y_sample = np.concatenate([r[c]["ys"].reshape(NSEQ, TD, D) for c in range(NCORES)], 0)
    pk = lambda n: np.stack([r[c][n] for c in range(nb)], 1)
    new_k_prompt = pk("kp").reshape(2, nb, nkeep, 8, 64)
    new_v_prompt = pk("vp").reshape(2, nb, nkeep, 8, 64)
    outs = (y_prompt, y_sample, new_k_prompt, new_v_prompt, pk("rep"), pk("imp"), pk("poolp"), pk("convp"),
            cat("ks").reshape(2, NCORES * NSEQ, TD, 8, 64), cat("vs").reshape(2, NCORES * NSEQ, TD, 8, 64),
            cat("res"), cat("ims"), cat("pools"), cat("convs"))
    return tuple(np.ascontiguousarray(o, dtype=np.float32) for o in outs)


def kernel(**inputs):
    nc = _get_nc()
    in_maps = make_in_maps(inputs)
    res = run_bass_kernel_spmd(nc, in_maps, core_ids=list(range(NCORES)))
    return gather(res.results)
```

```python
import contextlib
import math
import numpy as np
import concourse.bass as bass
import concourse.mybir as mybir
from concourse.bass_utils import run_bass_kernel_spmd

F32 = mybir.dt.float32
BF16 = mybir.dt.bfloat16
I32 = mybir.dt.int32
ALU = mybir.AluOpType
AF = mybir.ActivationFunctionType

D = 1024
KC = 8
DIN = 2048
DFF = 2816
FC = 44
AC = 22
NSEQ = 4
TD = 8
NS = NSEQ * TD
PAST = 2048
EPS = 1e-6
NCORES = 8
TWO_PI = 2.0 * math.pi


class Sched:
    ENGS = ("pe", "act", "dve", "pool", "sp")
    STRICT = False

    def __init__(self, nc, stack, n_dma_sems=32):
        self.nc = nc
        self.ops = {e: [] for e in self.ENGS}
        self.count = {e: 0 for e in self.ENGS}
        self.known = {e: {} for e in self.ENGS}
        self.buf = {}
        self.esem = {e: stack.enter_context(nc.semaphore("s_" + e)) for e in self.ENGS}
        self.KDQ = {"sp": 16, "act": 8, "pool": 8}
        self.dsem = {}
        self.dlast = {}
        self.ndma = {}
        for q, k in self.KDQ.items():
            self.ndma[q] = 0
            for i in range(k):
                self.dsem[(q, i)] = stack.enter_context(nc.semaphore("d_%s%d" % (q, i)))
                self.dlast[(q, i)] = 0
        self.dead = False

    def _deps(self, reads, writes):
        deps = []
        for k in reads:
            st = self.buf.get(k)
            if st and st["w"] is not None:
                deps.append(st["w"] + ("raw",))
        for k in writes:
            st = self.buf.get(k)
            if st:
                if st["w"] is not None:
                    deps.append(st["w"] + (("raw",) if self.STRICT else ()))
                deps.extend([t + ("raw",) for t in st["r"]] if self.STRICT else st["r"])
        return deps

    def _waits(self, eng, deps, is_dma=False):
        need = {}
        for t in deps:
            if t[0] == "E" and is_dma and t[1] == eng:
                key = ("E", t[1])
                if t[2] > need.get(key, 0):
                    need[key] = t[2]
                continue
            if t[0] == "E":
                if t[1] == eng and (eng == "pe" or len(t) < 4):
                    continue
                key = ("E", t[1])
            else:
                key = ("D", t[1])
            if t[2] > need.get(key, 0):
                need[key] = t[2]
        out = []
        kn = self.known[eng]
        for key, v in need.items():
            if kn.get(key, 0) >= v:
                continue
            kn[key] = v
            sem = self.esem[key[1]] if key[0] == "E" else self.dsem[key[1]]
            out.append((sem, v))
        return out

    def _commit(self, tok, reads, writes):
        for k in reads:
            st = self.buf.setdefault(k, {"w": None, "r": []})
            st["r"].append(tok)
            if len(st["r"]) > 64:
                st["r"] = self._prune(st["r"])
        for k in writes:
            self.buf[k] = {"w": tok, "r": []}

    @staticmethod
    def _prune(lst):
        best = {}
        for t in lst:
            key = (t[0], t[1])
            if key not in best or t[2] > best[key][2]:
                best[key] = t
        return list(best.values())

    def op(self, eng, fn, reads=(), writes=()):
        if self.dead:
            return None
        psr = [k for k in reads if k.startswith("ps") and k[2:].isdigit()]
        if psr:
            reads = [k for k in reads if k not in psr]
            writes = list(writes) + psr
        waits = self._waits(eng, self._deps(reads, writes))
        self.count[eng] += 1
        tok = ("E", eng, self.count[eng])
        self.ops[eng].append((fn, waits, (self.esem[eng], 1)))
        self._commit(tok, reads, writes)
        return tok

    def dma(self, fn, reads=(), writes=(), q="sp"):
        if self.dead:
            return None
        slot = (q, self.ndma[q] % self.KDQ[q])
        self.ndma[q] += 1
        deps = self._deps(reads, writes)
        if self.dlast[slot] > 0:
            deps.append(("D", slot, self.dlast[slot]))
        waits = self._waits(q, deps, is_dma=True)
        self.dlast[slot] += 16
        tok = ("D", slot, self.dlast[slot])
        self.ops[q].append((fn, waits, (self.dsem[slot], 16)))
        self._commit(tok, reads, writes)
        return tok

    def barrier(self):
        if self.dead:
            return
        dd = [("D", s, v) for s, v in self.dlast.items() if v > 0]
        for e in self.ENGS:
            deps = [("E", e2, self.count[e2]) for e2 in self.ENGS if self.count[e2] > 0] + dd
            waits = self._waits(e, deps, is_dma=True)
            self.ops[e].append((None, waits, None))
        self.buf = {}

    def emit(self):
        nc = self.nc
        engmap = {"pe": "tensor", "act": "scalar", "dve": "vector", "pool": "gpsimd", "sp": "sync"}
        with nc.Block() as block:
            for e in self.ENGS:
                lst = self.ops[e]

                def body(engine, lst=lst):
                    for fn, waits, inc in lst:
                        for sem, v in waits:
                            engine.wait_ge(sem, v)
                        if fn is not None:
                            fn(engine).then_inc(inc[0], inc[1])

                getattr(block, engmap[e])(body)


class Ring:
    def __init__(self, items):
        self.items = items
        self.i = 0

    def next(self):
        it = self.items[self.i % len(self.items)]
        self.i += 1
        return it


class _Stop(Exception):
    pass


class Builder:
    stop_at = None

    def ckpt(self, name):
        if self.stop_at is not None and name == self.stop_at and not self.S_.dead:
            self.S_.barrier()
            self.S_.dead = True

    def __init__(self, S=8192, TSM=128, TSF=256, dbg=False):
        assert S % TSM == 0 and S % TSF == 0 and TSM % 128 == 0
        self.S, self.TSM, self.TSF, self.dbg = S, TSM, TSF, dbg
        self.NSUB = TSM // 128
        self.JMIN = -(self.NSUB - 1)
        self.NJ = 17 + self.NSUB - 1
        self.NSLOT = 16 + 2 * self.NSUB
        self.NKEEP = min(2048, S)
        self.nc = bass.Bass("TRN2", target_bir_lowering=False)

    def MM(self, out, lhsT, rhs, start, stop, r, w):
        self.S_.op("pe", lambda e: e.matmul(out, lhsT=lhsT, rhs=rhs, start=start, stop=stop,
                                            skip_group_check=True), r, w)

    def TR(self, out, in_, ident, r, w):
        self.S_.op("pe", lambda e: e.transpose(out, in_, ident), r, w)

    def ACT(self, out, in_, func, r, w, bias=None, scale=None, accum=None):
        kw = {}
        if bias is not None:
            kw["bias"] = bias
        if scale is not None:
            kw["scale"] = scale
        if accum is not None:
            kw["accum_out"] = accum
        self.S_.op("act", lambda e: e.activation(out=out, in_=in_, func=func, **kw), r, w)

    def TT(self, eng, out, in0, in1, op, r, w):
        self.S_.op(eng, lambda e: e.tensor_tensor(out=out, in0=in0, in1=in1, op=op), r, w)

    def TS(self, eng, out, in0, s1, s2, op0, op1, r, w):
        if s2 is None:
            self.S_.op(eng, lambda e: e.tensor_scalar(out=out, in0=in0, scalar1=s1, scalar2=None, op0=op0), r, w)
        else:
            self.S_.op(eng, lambda e: e.tensor_scalar(out=out, in0=in0, scalar1=s1, scalar2=s2, op0=op0, op1=op1), r, w)

    def STT(self, out, in0, scalar, in1, op0, op1, r, w):
        self.S_.op("dve", lambda e: e.scalar_tensor_tensor(out=out, in0=in0, scalar=scalar, in1=in1,
                                                          op0=op0, op1=op1), r, w)

    def CP(self, eng, out, in_, r, w):
        if eng == "act":
            self.S_.op("act", lambda e: e.activation(out=out, in_=in_, func=AF.Copy), r, w)
        else:
            self.S_.op(eng, lambda e: e.tensor_copy(out=out, in_=in_), r, w)

    def MS(self, eng, ap, val, w):
        self.S_.op(eng, lambda e: e.memset(ap, val), (), w)

    def DMA(self, out, in_, r, w, q="sp", slow=False):
        if slow:
            self.S_.dma(lambda e: e.dma_start(out=out, in_=in_, allow_slow_non_contiguous=True), r, w, q=q)
        else:
            self.S_.dma(lambda e: e.dma_start(out=out, in_=in_), r, w, q=q)

    def sb(self, stack, name, shape, dt):
        self._uid = getattr(self, "_uid", 0) + 1
        return stack.enter_context(self.nc.sbuf_tensor("%s_%d" % (name, self._uid), shape, dt))

    def declare(self):
        nc, S = self.nc, self.S
        din = lambda n, s: nc.dram_tensor(n, s, F32, kind="ExternalInput").ap()
        dout = lambda n, s: nc.dram_tensor(n, s, F32, kind="ExternalOutput").ap()
        I = {}
        I["xp"] = din("xp", [S, D])
        I["xs"] = din("xs", [NS, D])
        I["ck"] = din("ck", [2, NSEQ, PAST, 512])
        I["cv"] = din("cv", [2, NSEQ, PAST, 512])
        I["sre"] = din("sre", [2, NSEQ, 16, 64])
        I["sim"] = din("sim", [2, NSEQ, 16, 64])
        I["spool"] = din("spool", [2, NSEQ, 15, 256])
        I["sconv"] = din("sconv", [2, NSEQ, 2, 2 * DFF])
        for n, s in (("norm1_g", [2, D]), ("w_in", [2, D, DIN]), ("ssm_log_dt", [2, 16]), ("ssm_a_re", [2, 16, 64]),
                     ("ssm_a_im", [2, 16, 64]), ("ssm_b_re", [2, 16, 64, 16]), ("ssm_b_im", [2, 16, 64, 16]),
                     ("ssm_c_re", [2, 16, 16, 64]), ("ssm_c_im", [2, 16, 16, 64]), ("ssm_d", [2, 256]),
                     ("ssm_w_glu", [2, 256, 256]), ("ssm_b_glu", [2, 256]), ("pool_w", [2, 4, 64, 64]),
                     ("pool_scale", [2, 256]), ("out_norm_att", [2, 512]), ("out_norm_ssm", [2, 256]),
                     ("out_norm_pool", [2, 256]), ("w_out", [2, D, D]), ("norm2_g", [2, D]),
                     ("w_up", [2, D, 2 * DFF]), ("conv_w", [2, 3, 2 * DFF]), ("conv_b", [2, 2 * DFF]),
                     ("w_down", [2, DFF, D]), ("norm_f_g", [D])):
            I[n] = din(n, s)
        O = {}
        O["yp"] = dout("yp", [S, D])
        O["ys"] = dout("ys", [NS, D])
        O["kp"] = dout("kp", [2, self.NKEEP, 512])
        O["vp"] = dout("vp", [2, self.NKEEP, 512])
        O["rep"] = dout("rep", [2, 16, 64])
        O["imp"] = dout("imp", [2, 16, 64])
        O["poolp"] = dout("poolp", [2, 15, 256])
        O["convp"] = dout("convp", [2, 2, 2 * DFF])
        O["ks"] = dout("ks", [2, NSEQ, TD, 512])
        O["vs"] = dout("vs", [2, NSEQ, TD, 512])
        O["res"] = dout("res", [2, NSEQ, 16, 64])
        O["ims"] = dout("ims", [2, NSEQ, 16, 64])
        O["pools"] = dout("pools", [2, NSEQ, 15, 256])
        O["convs"] = dout("convs", [2, NSEQ, 2, 2 * DFF])
        if self.dbg:
            O["dbg"] = dout("dbg", [12, KC, 128, S + NS])
        self.I, self.O = I, O
        self.scr = nc.dram_tensor("scr", [2, KC, 128, S + NS], F32, kind="Internal").ap()

    def sincos(self, st, X, n, tag, want_cos=True):
        ki = self.sb(st, tag + "_ki", [128, n], I32)
        r = self.sb(st, tag + "_r", [128, n], F32)
        sn = self.sb(st, tag + "_sn", [128, n], F32)
        outs = []
        for which, shift, dst in (("s", 0.0, sn),) + ((("c", math.pi / 2, None),) if want_cos else ()):
            if dst is None:
                dst = self.sb(st, tag + "_cs", [128, n], F32)
            src = X
            if shift != 0.0:
                self.TS("dve", r[:], X, shift, None, ALU.add, None, [tag + "X"], [tag + "r"])
                src = r[:]
            self.TS("dve", ki[:], src, 1.0 / TWO_PI, None, ALU.mult, None, [tag + "X", tag + "r"], [tag + "ki"])
            self.STT(r[:], ki[:], -TWO_PI, src, ALU.mult, ALU.add, [tag + "ki", tag + "X", tag + "r"], [tag + "r"])
            self.TS("dve", r[:], r[:], 3.14159, -3.14159, ALU.min, ALU.max, [tag + "r"], [tag + "r"])
            self.ACT(dst[:], r[:], AF.Sin, [tag + "r"], [tag + which])
            outs.append(dst)
        return outs

    def rms_feat(self, src, C, NT, gains, dst, nfeat, keys_r, key_w, tmp):
        sq, sd, rstd = tmp
        self.ACT(sq[:, 0:C, 0:NT], src[:, 0:C, 0:NT], AF.Square, keys_r, ["sq"])
        ps, pk = self.psC.next()
        for c in range(C):
            self.MM(ps[:, 0:NT], self.ones_b[:], sq[:, c, 0:NT], c == 0, c == C - 1, ["sq", "ones"], [pk])
        if self.rms_mode == "lnexp":
            self.ACT(sd[:, 0:NT], ps[:, 0:NT], AF.Ln, [pk], ["sd"], bias=self.eps_col[:, 0:1], scale=1.0 / nfeat)
            self.ACT(rstd[:, 0:NT], sd[:, 0:NT], AF.Exp, ["sd"], ["rstd"], scale=-0.5)
        else:
            self.ACT(sd[:, 0:NT], ps[:, 0:NT], AF.Sqrt, [pk], ["sd"], bias=self.eps_col[:, 0:1], scale=1.0 / nfeat)
            self.S_.op("dve", lambda e: e.reciprocal(out=rstd[:, 0:NT], in_=sd[:, 0:NT]), ["sd"], ["rstd"])
        for c in range(C):
            self.STT(dst[:, c, 0:NT], src[:, c, 0:NT], gains[:, c:c + 1], rstd[:, 0:NT], ALU.mult, ALU.mult,
                     keys_r + ["rstd", "gains"], [key_w])

    def cols_to_rows(self, src, dram_rows, R, rkeys, q="sp"):
        ps, pk = self.psC.next()
        stg, sk = self.tstage.next()
        self.TR(ps[0:R, 0:128], src, self.ident_f[:], list(rkeys) + ["ident"], [pk])
        self.CP("dve", stg[0:R, :], ps[0:R, 0:128], [pk], [sk])
        self.DMA(dram_rows, stg[0:R, :], [sk], [], q=q)

    def rows_to_cols(self, dram_rows, dst, R, wkeys, q="sp"):
        ps, pk = self.psC.next()
        stg, sk = self.tstage.next()
        self.DMA(stg[0:R, :], dram_rows, [], [sk], q=q)
        self.TR(ps[:, 0:R], stg[0:R, :], self.ident_f[0:R, 0:R], [sk, "ident"], [pk])
        self.CP("dve", dst, ps[:, 0:R], [pk], list(wkeys))

    def load_rows_T(self, st, rows, tag):
        stage = self.sb(st, tag + "_stg", [128, 128], F32)
        cols = self.sb(st, tag + "_cols", [128, 128], F32)
        self.MS("pool", stage[:], 0.0, [tag + "stg"])
        for ap, base in rows:
            n = ap.shape[0]
            self.DMA(stage[base:base + n, :], ap, [], [tag + "stg"])
        ps, pk = self.psC.next()
        self.TR(ps[:, 0:128], stage[:], self.ident_f[:], [tag + "stg", "ident"], [pk])
        self.CP("dve", cols[:], ps[:, 0:128], [pk], [tag + "cols"])
        return cols

    def set_rings(self, mix):
        rg = lambda ids: Ring([(self.ps[i], "ps%d" % i) for i in ids])
        if mix:
            self.psA, self.psS, self.psB, self.psC = rg((0, 1, 2)), rg((3, 4)), rg((5,)), rg((6, 7))
        else:
            self.psA, self.psC = rg((0, 1, 2, 3, 4, 5)), rg((6, 7))

    def setup(self, st):
        nc = self.nc
        self.ps = [st.enter_context(nc.psum_tensor("ps%d" % i, [128, 512], F32)) for i in range(8)]
        self.set_rings(True)
        self.ident_f = self.sb(st, "ident_f", [128, 128], F32)
        self.ones_b = self.sb(st, "ones_b", [128, 128], BF16)
        self.eps_col = self.sb(st, "eps_col", [128, 1], F32)
        self.tstage = Ring([(self.sb(st, "tstage%d" % i, [128, 128], F32), "tstage%d" % i) for i in range(2)])
        with contextlib.ExitStack() as t:
            onesf = self.sb(t, "onesf", [128, 128], F32)
            self.MS("pool", onesf[:], 1.0, ["onesf"])
            self.MS("pool", self.ones_b[:], 1.0, ["ones"])
            self.MS("pool", self.eps_col[:], EPS, ["eps"])
            self.S_.op("pool", lambda e: e.affine_select(out=self.ident_f[:], in_=onesf[:], pattern=[[-1, 128]],
                                                         compare_op=ALU.is_equal, fill=0.0, base=0,
                                                         channel_multiplier=1), ["onesf"], ["ident"])
            self.S_.barrier()

    def mix_phase(self, l, ph):
        nc, S_, I, O = self.nc, self.S_, self.I, self.O
        S, TS_, NSUB, NJ, JMIN, NSLOT = self.S, self.TSM, self.NSUB, self.NJ, self.JMIN, self.NSLOT
        TW = TS_ + 1
        rd, wr = (ph - 1) % 2, ph % 2
        self.rms_mode = "lnexp"
        self.set_rings(True)
        with contextlib.ExitStack() as st:
            sb = lambda n, s, d: self.sb(st, n, s, d)
            win = sb("win", [128, KC, DIN], BF16)
            wout = sb("wout", [128, KC, D], BF16)
            wglu = sb("wglu", [128, 2, 256], BF16)
            poolw = sb("poolw", [128, 2, 128], BF16)
            Bre = sb("Bre", [128, 8, 128], BF16)
            Bim = sb("Bim", [128, 8, 128], BF16)
            Cre = sb("Cre", [128, 8, 128], BF16)
            Cim = sb("Cim", [128, 8, 128], BF16)
            Tre = sb("Tre", [128, 8, TW], F32)
            Tim = sb("Tim", [128, 8, TW], F32)
            cs = sb("cs", [128, 8, TW], F32)
            sn = sb("sn", [128, 8, TW], F32)
            rho = sb("rho", [128, 8], F32)
            mask = sb("mask", [128, NJ, TS_], BF16)
            icnt_tab = sb("icnt_tab", [128, 2, 16], F32)
            icnt = sb("icnt", [128, 2], F32)
            for c in range(KC):
                self.DMA(win[:, c, :], I["w_in"][l, c * 128:(c + 1) * 128, :], [], ["win"], q="pool")
            for c in range(KC):
                self.DMA(wout[:, c, :], I["w_out"][l, c * 128:(c + 1) * 128, :], [], ["wout"], q="pool")
            for c in range(2):
                self.DMA(wglu[:, c, :], I["ssm_w_glu"][l, c * 128:(c + 1) * 128, :], [], ["wglu"], q="pool")
            self.MS("pool", poolw[:], 0.0, ["poolw"])
            for gi in range(4):
                h0 = (gi % 2) * 64
                self.DMA(poolw[h0:h0 + 64, gi // 2, h0:h0 + 64], I["pool_w"][l, gi], [], ["poolw"], q="pool")
            rows = [(I["norm1_g"][l].rearrange("(c p) -> c p", p=128), 0),
                    (I["out_norm_att"][l].rearrange("(c p) -> c p", p=128), 8),
                    (I["out_norm_ssm"][l].rearrange("(c p) -> c p", p=128), 12),
                    (I["out_norm_pool"][l].rearrange("(c p) -> c p", p=128), 14),
                    (I["pool_scale"][l].rearrange("(c p) -> c p", p=128), 16),
                    (I["ssm_d"][l].rearrange("(c p) -> c p", p=128), 18),
                    (I["ssm_b_glu"][l].rearrange("(c p) -> c p", p=128), 20)]
            pc = self.load_rows_T(st, rows, "pc")
            self.ckpt("m_pc")
            g1, gatt, gssm, gpool = pc[:, 0:8], pc[:, 8:12], pc[:, 12:14], pc[:, 14:16]
            pscale, dskip, bglu = pc[:, 16:18], pc[:, 18:20], pc[:, 20:22]
            nbglu = sb("nbglu", [128, 2], F32)
            self.TS("dve", nbglu[:], bglu, -1.0, None, ALU.mult, None, ["pccols"], ["nbglu"])
            with contextlib.ExitStack() as t:
                tb = lambda n, s, d: self.sb(t, n, s, d)
                arr = lambda nm: I[nm][l].rearrange("(gp gl) p -> gp (gl p)", gl=2)
                sc = self.load_rows_T(t, [(arr("ssm_a_re"), 0), (arr("ssm_a_im"), 32)], "sc")
                are, aim = sc[:, 0:8], sc[:, 32:40]
                dtr = tb("dtr", [128, 2], F32)
                dtrow = tb("dtrow", [128, 128], F32)
                self.MS("pool", dtr[:], 0.0, ["dtr"])
                self.MS("pool", dtrow[:], 0.0, ["dtrow"])
                self.DMA(dtr[0:8, :], I["ssm_log_dt"][l].rearrange("(gp gl) -> gp gl", gl=2), [], ["dtr"])
                self.ACT(dtr[0:8, :], dtr[0:8, :], AF.Exp, ["dtr"], ["dtr"])
                self.CP("dve", dtrow[0:8, :].rearrange("p (a b) -> p a b", b=64),
                        dtr[0:8, :].unsqueeze(2).to_broadcast([8, 2, 64]), ["dtr"], ["dtrow"])
                psd, pkd = self.psC.next()
                self.TR(psd[:, 0:128], dtrow[:], self.ident_f[:], ["dtrow", "ident"], [pkd])
                dtc = tb("dtc", [128, 8], F32)
                self.CP("dve", dtc[:], psd[:, 0:8], [pkd], ["dtc"])
                th = tb("th", [128, 8], F32)
                lre = tb("lre", [128, 8], F32)
                self.TT("dve", lre[:], are, dtc[:], ALU.mult, ["sccols", "dtc"], ["lre"])
                self.TT("dve", th[:], aim, dtc[:], ALU.mult, ["sccols", "dtc"], ["thX"])
                self.ACT(rho[:], lre[:], AF.Exp, ["lre"], ["rho"])
                s1, c1 = self.sincos(t, th[:], 8, "th")
                abre = tb("abre", [128, 8], F32)
                abim = tb("abim", [128, 8], F32)
                self.TT("dve", abre[:], rho[:], c1[:], ALU.mult, ["rho", "thc"], ["abre"])
                self.TT("dve", abim[:], rho[:], s1[:], ALU.mult, ["rho", "ths"], ["abim"])
                self.TS("dve", abre[:], abre[:], -1.0, None, ALU.add, None, ["abre"], ["abre"])
                den = tb("den", [128, 8], F32)
                t0 = tb("t0", [128, 8], F32)
                core_ = tb("core", [128, 8], F32)
                coim = tb("coim", [128, 8], F32)
                self.TT("dve", den[:], are, are, ALU.mult, ["sccols"], ["den"])
                self.TT("dve", t0[:], aim, aim, ALU.mult, ["sccols"], ["t0"])
                self.TT("dve", den[:], den[:], t0[:], ALU.add, ["den", "t0"], ["den"])
                self.S_.op("dve", lambda e: e.reciprocal(out=den[:], in_=den[:]), ["den"], ["den"])
                self.TT("dve", core_[:], abre[:], are, ALU.mult, ["abre", "sccols"], ["core"])
                self.TT("dve", t0[:], abim[:], aim, ALU.mult, ["abim", "sccols"], ["t0"])
                self.TT("dve", core_[:], core_[:], t0[:], ALU.add, ["core", "t0"], ["core"])
                self.TT("dve", core_[:], core_[:], den[:], ALU.mult, ["core", "den"], ["core"])
                self.TT("dve", coim[:], abim[:], are, ALU.mult, ["abim", "sccols"], ["coim"])
                self.TT("dve", t0[:], abre[:], aim, ALU.mult, ["abre", "sccols"], ["t0"])
                self.TT("dve", coim[:], coim[:], t0[:], ALU.subtract, ["coim", "t0"], ["coim"])
                self.TT("dve", coim[:], coim[:], den[:], ALU.mult, ["coim", "den"], ["coim"])
                self.ckpt("m_co")
                iot = tb("iot", [128, TW], F32)
                self.S_.op("pool", lambda e: e.iota(iot[:], pattern=[[1, TW]], base=0, channel_multiplier=0,
                                                    allow_small_or_imprecise_dtypes=True), [], ["iot"])
                ang = tb("ang", [128, 8, TW], F32)
                self.TT("dve", ang[:], th[:].unsqueeze(2).to_broadcast([128, 8, TW]),
                        iot[:].unsqueeze(1).to_broadcast([128, 8, TW]), ALU.mult, ["thX", "iot"], ["angX"])
                sA, cA = self.sincos(t, ang[:].rearrange("p a b -> p (a b)"), 8 * TW, "ang")
                self.CP("pool", sn[:].rearrange("p a b -> p (a b)"), sA[:], ["angs"], ["sn"])
                self.CP("pool", cs[:].rearrange("p a b -> p (a b)"), cA[:], ["angc"], ["cs"])
                tmpT = tb("tmpT", [128, 8, TW], F32)
                cob = lambda x: x[:].unsqueeze(2).to_broadcast([128, 8, TW])
                self.TT("dve", Tre[:], cs[:], cob(core_), ALU.mult, ["cs", "core"], ["Tre"])
                self.TT("dve", tmpT[:], sn[:], cob(coim), ALU.mult, ["sn", "coim"], ["tmpT"])
                self.TT("dve", Tre[:], Tre[:], tmpT[:], ALU.add, ["Tre", "tmpT"], ["Tre"])
                self.TT("dve", Tim[:], cs[:], cob(coim), ALU.mult, ["cs", "coim"], ["Tim"])
                self.TT("dve", tmpT[:], sn[:], cob(core_), ALU.mult, ["sn", "core"], ["tmpT"])
                self.TT("dve", Tim[:], Tim[:], tmpT[:], ALU.subtract, ["Tim", "tmpT"], ["Tim"])
                self.ckpt("m_tab")
                X4 = tb("X4", [128, 8, 128], F32)
                for nm, dst, isB in (("ssm_b_re", Bre, True), ("ssm_b_im", Bim, True),
                                     ("ssm_c_re", Cre, False), ("ssm_c_im", Cim, False)):
                    self.MS("pool", X4[:], 0.0, ["X4"])
                    for g in range(16):
                        gp, gl = g // 2, g % 2
                        r0 = (gp % 4) * 32
                        if isB:
                            self.DMA(X4[gl * 64:(gl + 1) * 64, gp, r0 + gl * 16:r0 + gl * 16 + 16], I[nm][l, g],
                                     [], ["X4"])
                        else:
                            self.DMA(X4[r0 + gl * 16:r0 + gl * 16 + 16, gp, gl * 64:(gl + 1) * 64], I[nm][l, g],
                                     [], ["X4"])
                    for gp in range(8):
                        psx, pkx = self.psA.next()
                        self.TR(psx[:, 0:128], X4[:, gp, :], self.ident_f[:], ["X4", "ident"], [pkx])
                        self.CP("act", dst[:, gp, :], psx[:, 0:128], [pkx], [nm])
                S_.barrier()
                self.ckpt("m_bc")
            with contextlib.ExitStack() as t:
                tb = lambda n, s, d: self.sb(t, n, s, d)
                NM = NJ * TS_
                v = tb("mv", [128, NM], F32)
                m0 = tb("m0", [128, NM], F32)
                cc = tb("mc", [128, NM], F32)
                qi = tb("mq", [128, NM], I32)
                rr = tb("mr", [128, NM], F32)
                self.S_.op("pool", lambda e: e.iota(v[:], pattern=[[128, NJ], [1, TS_]], base=128 * JMIN,
                                                    channel_multiplier=-1, allow_small_or_imprecise_dtypes=True),
                           [], ["mv"])
                self.TS("dve", m0[:], v[:], 0.0, None, ALU.is_ge, None, ["mv"], ["m0"])
                self.TS("dve", cc[:], v[:], 128.0, None, ALU.is_le, None, ["mv"], ["mc"])
                self.TT("dve", cc[:], cc[:], m0[:], ALU.mult, ["mc", "m0"], ["mc"])
                for dil, lim in ((4, 512.0), (16, 2048.0)):
                    self.TS("dve", qi[:], v[:], 1.0 / dil, None, ALU.mult, None, ["mv"], ["mq"])
                    self.STT(rr[:], qi[:], -float(dil), v[:], ALU.mult, ALU.add, ["mq", "mv"], ["mr"])
                    self.TS("dve", rr[:], rr[:], 0.0, None, ALU.is_equal, None, ["mr"], ["mr"])
                    self.TT("dve", rr[:], rr[:], m0[:], ALU.mult, ["mr", "m0"], ["mr"])
                    self.TS("dve", qi[:], v[:], lim, None, ALU.is_le, None, ["mv"], ["mq"])
                    self.TT("dve", rr[:], rr[:], qi[:], ALU.mult, ["mr", "mq"], ["mr"])
                    self.TT("dve", cc[:], cc[:], rr[:], ALU.add, ["mc", "mr"], ["mc"])
                self.CP("dve", mask[:].rearrange("p a b -> p (a b)"), cc[:], ["mc"], ["mask"])
                self.ckpt("m_mask")
                wcol = tb("wcol", [128, 2], F32)
                for ch, (wa, wb) in enumerate(((2.0, 4.0), (8.0, 16.0))):
                    self.MS("pool", wcol[0:64, ch:ch + 1], wa, ["wcol"])
                    self.MS("pool", wcol[64:128, ch:ch + 1], wb, ["wcol"])
                self.S_.op("dve", lambda e: e.reciprocal(out=icnt[:], in_=wcol[:]), ["wcol"], ["icnt"])
                i16 = tb("i16", [128, 16], F32)
                self.S_.op("pool", lambda e: e.iota(i16[:], pattern=[[1, 16]], base=1, channel_multiplier=0,
                                                    allow_small_or_imprecise_dtypes=True), [], ["i16"])
                self.TT("dve", icnt_tab[:], i16[:].unsqueeze(1).to_broadcast([128, 2, 16]),
                        wcol[:].unsqueeze(2).to_broadcast([128, 2, 16]), ALU.min, ["i16", "wcol"], ["icnt_tab"])
                self.S_.op("dve", lambda e: e.reciprocal(out=icnt_tab[:], in_=icnt_tab[:]), ["icnt_tab"], ["icnt_tab"])
                S_.barrier()
            xT = [sb("xT%d" % i, [128, KC, TS_], F32) for i in range(3)]
            xtoks = [sb("xtok%d" % i, [128, NSUB, D], F32) for i in range(2)]
            sq1 = sb("sq1", [128, KC, TS_], BF16)
            sd1 = sb("sd1", [128, TS_], F32)
            rstd1 = sb("rstd1", [128, TS_], F32)
            sq = sb("sq", [128, KC, TS_], BF16)
            sd = sb("sd", [128, TS_], F32)
            rstd = sb("rstd", [128, TS_], F32)
            hnT = sb("hnT", [128, KC, TS_], BF16)
            qT = sb("qT", [128, 4, TS_], BF16)
            qT_b = sb("qTb", [128, 4, TS_], BF16)
            KT = sb("KT", [128, 4, NSLOT * 128], BF16)
            Vr = sb("Vr", [128, NSLOT, 8, 65], BF16)
            usb = sb("usb", [128, 2, TS_], BF16)
            usf = sb("usf", [128, 2, TS_], F32)
            upl = sb("upl", [128, 2, 16 + TS_], F32)
            usb_b = sb("usbb", [128, 2, TS_], BF16)
            usf_b = sb("usfb", [128, 2, TS_], F32)
            upl_b = sb("uplb", [128, 2, 16 + TS_], F32)
            Pr = Ring([(sb("P%d" % i, [128, 2 * TS_], BF16), "P%d" % i) for i in range(6)])
            rec = sb("rec", [128, NSUB], F32)
            accs = sb("accs", [128, 8, NSUB * 65], F32)
            rec8 = sb("rec8", [128, 8, NSUB], F32)
            att = sb("att", [128, NSUB, 512], F32)
            att_b = sb("attb", [128, NSUB, 512], F32)
            ssa = sb("ssa", [128, NSUB], F32)
            junk = sb("junk", [128, 512], BF16)
            mixT = sb("mixT", [128, KC, TS_], BF16)
            mixT_b = sb("mixTb", [128, KC, TS_], BF16)

            class _B:
                pass
            bs = []
            for i_, (q_, ub_, uf_, up_, at_, mx_) in enumerate(((qT, usb, usf, upl, att, mixT),
                                                                 (qT_b, usb_b, usf_b, upl_b, att_b, mixT_b))):
                B_ = _B()
                B_.qT, B_.usb, B_.usf, B_.upl, B_.att, B_.mixT = q_, ub_, uf_, up_, at_, mx_
                B_.qk, B_.ubk, B_.ufk, B_.upk, B_.ak, B_.mk = ("qT%d" % i_, "usb%d" % i_, "usf%d" % i_, "upl%d" % i_,
                                                              "att%d" % i_, "mixT%d" % i_)
                bs.append(B_)
            Gre = sb("Gre", [128, 8, TS_], F32)
            Gim = sb("Gim", [128, 8, TS_], F32)
            ta = sb("ta", [128, 8, TS_], F32)
            tb2 = sb("tb2", [128, 8, TS_], F32)
            hre = sb("hre", [128, 8, TS_], BF16)
            himn = sb("himn", [128, 8, TS_], BF16)
            wtmp = Ring([(sb("wt%d" % i, [128, TS_], F32), "wt%d" % i) for i in range(6)])
            gst = sb("gst", [128, 2, 8], F32)
            gl_t = sb("gl_t", [128, 4, 8], F32)
            hl = sb("hl", [128, 2, 8], F32)
            yss = sb("yss", [128, 2, TS_], F32)
            gact = sb("gact", [128, 2, TS_], F32)
            gactb = sb("gactb", [128, 2, TS_], BF16)
            sgl = sb("sgl", [128, 2, TS_], F32)
            pl1 = sb("pl1", [128, 2, 16 + TS_], F32)
            pl2 = sb("pl2", [128, 2, 16 + TS_], F32)
            wsel = sb("wsel", [128, 2, TS_], F32)
            pooled = sb("pooled", [128, 2, TS_], BF16)
            pout = sb("pout", [128, 2, TS_], F32)
            t16 = sb("t16", [128, 2, 16], F32)
            kvst = Ring([(sb("kvst%d" % i, [128, 512], F32), "kvst%d" % i) for i in range(2)])
            kts = sb("kts", [128, 4, NS], BF16)
            h0c = sb("h0c", [128, 2, 8], F32)
            ups = sb("ups", [128, 2, NS], F32)
            kcs = Ring([(sb("kcs%d" % i, [128, 512], F32), "kcs%d" % i) for i in range(2)])
            rmstmp = (sq, sd, rstd)
            self.ckpt("m_icnt")
            if l == 0:
                print("SBUF remaining after mix alloc:", self.nc.sbuf_bytes_remaining)
            self.MS("pool", Vr[:], 1.0, ["vall"])
            S_.barrier()
            self.ckpt("m_work")

            def load_dma(dst, tok0, NT, from_tok, src_tok=None, xi=0):
                key = "xT%d" % xT.index(dst)
                if from_tok:
                    xtok = xtoks[xi % 2]
                    nsub = (NT + 127) // 128
                    w = min(128, NT)
                    for sub in range(nsub):
                        self.DMA(xtok[0:w, sub, :], src_tok[tok0 + sub * 128: tok0 + sub * 128 + w, :], [], ["xtok%d" % (xi % 2)])
                else:
                    self.DMA(dst[:, :, 0:NT], self.scr[rd, :, :, tok0:tok0 + NT].rearrange("c p t -> p c t"),
                             [], [key])
                return key

            def load_tr(dst, NT, from_tok, xi=0):
                key = "xT%d" % xT.index(dst)
                if from_tok:
                    xtok = xtoks[xi % 2]
                    nsub = (NT + 127) // 128
                    w = min(128, NT)
                    for c in range(KC):
                        ps, pk = self.psA.next()
                        for sub in range(nsub):
                            self.TR(ps[:, sub * 128: sub * 128 + w], xtok[0:w, sub, c * 128:(c + 1) * 128],
                                    self.ident_f[0:w, 0:w], ["xtok%d" % (xi % 2), "ident"], [pk])
                        self.CP("act", dst[:, c, 0:NT], ps[:, 0:NT], [pk], [key])
                return key

            def load_xT(dst, tok0, NT, from_tok, src_tok=None):
                load_dma(dst, tok0, NT, from_tok, src_tok)
                return load_tr(dst, NT, from_tok)

            def in_proj(xk, x, NT, tmp=None):
                self.rms_feat(x, KC, NT, g1, hnT, D, [xk], "hnT", tmp or rmstmp)
                self.ckpt("t_rms")
                outs = {}
                for oc in (0, 1, 2, 3, 4, 5, 6, 7, 12, 13, 14, 15):
                    ps, pk = self.psA.next()
                    for kc in range(KC):
                        self.MM(ps[:, 0:NT], win[:, kc, oc * 128:(oc + 1) * 128], hnT[:, kc, 0:NT], kc == 0,
                                kc == KC - 1, ["win", "hnT"], [pk])
                    outs[oc] = (ps, pk)
                    yield oc, ps, pk

            def attention(qcol0, NQ, ktiles, B, LA=4):
                nsub = (NQ + 127) // 128
                qw = min(128, NQ)
                pending = []

                def flush_one():
                    item = pending.pop(0)
                    if item[0] == "pv":
                        _, acc, acck, P, Pk, nk, slot, h, subs, first, last, pc0 = item
                        for i_, s_ in enumerate(subs):
                            self.MM(acc[0:qw, s_ * 65:(s_ + 1) * 65], P[0:nk, pc0 + s_ * 128: pc0 + s_ * 128 + qw],
                                    Vr[0:nk, slot, h, :], first and i_ == 0, last and i_ == len(subs) - 1,
                                    [Pk, "v%d" % slot, "vall"], [acck])
                    else:
                        _, acc, acck, h = item
                        self.CP("act", accs[0:qw, h, 0:nsub * 65], acc[0:qw, 0:nsub * 65], [acck], ["accs"])

                mm_i = [0]
                gq = []
                for h in range(8):
                    hp, hc = (h % 2) * 64, h // 2
                    acc, acck = self.psB.next()
                    plan = []
                    for (slot, nk, jj) in ktiles:
                        subs = [s_ for s_ in range(nsub) if 0 <= jj + s_ <= 16]
                        if subs:
                            plan.append((slot, nk, jj, subs))
                    groups = []
                    i_ = 0
                    while i_ < len(plan):
                        if (nsub == 1 and i_ + 1 < len(plan) and plan[i_][1] == 128 and plan[i_ + 1][1] == 128
                                and plan[i_ + 1][2] == plan[i_][2] - 1):
                            groups.append([plan[i_ + 1], plan[i_]])
                            i_ += 2
                        else:
                            groups.append([plan[i_]])
                            i_ += 1
                    npv = len(plan)
                    ipv = 0
                    for grp in groups:
                        ng = len(grp)
                        nk = grp[0][1]
                        scp, sck = self.psS.next()
                        P, Pk = Pr.next()
                        for gi_, (slot, nk_, jj, subs) in enumerate(grp):
                            self.MM(scp[0:nk, gi_ * NQ:(gi_ + 1) * NQ], KT[hp:hp + 64, hc, slot * 128: slot * 128 + nk],
                                    B.qT[hp:hp + 64, hc, qcol0:qcol0 + NQ], gi_ == 0, gi_ == ng - 1,
                                    ["kt%d" % slot, B.qk], [sck])
                        self.ACT(P[0:nk, 0:ng * NQ], scp[0:nk, 0:ng * NQ], AF.Exp, [sck], [Pk], scale=0.125)
                        m0_ = grp[0][2] - JMIN
                        mm_i[0] += 1
                        self.TT("pool",
                                P[0:nk, 0:ng * NQ].rearrange("p (a q) -> p a q", a=ng),
                                P[0:nk, 0:ng * NQ].rearrange("p (a q) -> p a q", a=ng),
                                mask[0:nk, m0_:m0_ + ng, 0:NQ], ALU.mult, [Pk, "mask"], [Pk])
                        for gi_, (slot, nk_, jj, subs) in enumerate(grp):
                            pending.append(("pv", acc, acck, P, Pk, nk, slot, h, subs, ipv == 0, ipv == npv - 1, gi_ * NQ))
                            ipv += 1
                        if ipv == npv:
                            pending.append(("fin", acc, acck, h))
                        gq.append(ng)
                        while len(gq) > LA:
                            for _ in range(gq.pop(0)):
                                while pending[0][0] != "pv":
                                    flush_one()
                                flush_one()
                        yield
                while pending:
                    flush_one()
                a4 = accs[0:qw, :, 0:nsub * 65].rearrange("p h (s e) -> p h s e", e=65)
                for s_ in range(nsub):
                    self.S_.op("dve", lambda e, s_=s_: e.reciprocal(out=rec8[0:qw, :, s_:s_ + 1], in_=a4[:, :, s_, 64:65]),
                               ["accs"], ["rec8"])
                    self.TT("dve", B.att[0:qw, s_, :].rearrange("p (h e) -> p h e", e=64), a4[:, :, s_, 0:64],
                            rec8[0:qw, :, s_:s_ + 1].to_broadcast([qw, 8, 64]), ALU.mult, ["accs", "rec8"], [B.ak])
                yield

            def interleave(gens, weights):
                alive = list(gens)
                wts = list(weights)
                while alive:
                    for i in range(len(alive) - 1, -1, -1):
                        pass
                    nxt_alive, nxt_w = [], []
                    for g_, w_ in zip(alive, wts):
                        ok = True
                        for _ in range(w_):
                            try:
                                next(g_)
                            except StopIteration:
                                ok = False
                                break
                        if ok:
                            nxt_alive.append(g_)
                            nxt_w.append(w_)
                    alive, wts = nxt_alive, nxt_w

            def sched_tile(att_g, n_att, others):
                late_start = int(0.5 * n_att)
                held = [False] * len(others)
                alive = [True] * len(others)
                i = 0

                def adv(j):
                    if not alive[j]:
                        return
                    try:
                        tag = next(others[j])
                        if tag == "late":
                            held[j] = True
                    except StopIteration:
                        alive[j] = False

                for _ in att_g:
                    i += 1
                    for j in range(len(others)):
                        if i % (3 + 2 * j) == 0 and alive[j] and not (held[j] and i < late_start):
                            adv(j)
                for j in range(len(others)):
                    while alive[j]:
                        adv(j)

            def att_finish(col0, NQ, B):
                nsub = (NQ + 127) // 128
                qw = min(128, NQ)
                for s_ in range(nsub):
                    self.ACT(junk[0:qw, :], B.att[0:qw, s_, :], AF.Square, [B.ak], ["junk", "ssa"], accum=ssa[0:qw, s_:s_ + 1])
                self.ACT(ssa[0:qw, 0:nsub], ssa[0:qw, 0:nsub], AF.Ln, ["junk", "ssa"], ["ssa"], bias=self.eps_col[0:qw, 0:1],
                         scale=1.0 / 512)
                self.ACT(ssa[0:qw, 0:nsub], ssa[0:qw, 0:nsub], AF.Exp, ["ssa"], ["ssa"], scale=-0.5)
                self.TT("dve", B.att[0:qw, 0:nsub, :], B.att[0:qw, 0:nsub, :],
                        ssa[0:qw, 0:nsub].unsqueeze(2).to_broadcast([qw, nsub, 512]), ALU.mult, [B.ak, "ssa"], [B.ak])
                for c in range(4):
                    ps, pk = self.psA.next()
                    for s_ in range(nsub):
                        self.TR(ps[:, s_ * 128: s_ * 128 + qw], B.att[0:qw, s_, c * 128:(c + 1) * 128],
                                self.ident_f[0:qw, 0:qw], [B.ak, "ident"], [pk])
                    self.ACT(B.mixT[:, c, col0:col0 + NQ], ps[:, 0:NQ], AF.Copy, [pk, "pccols"], [B.mk], scale=gatt[:, c:c + 1])

            def ssm(col0, NT, first_zero, B):
                for gp in range(8):
                    ch = gp // 4
                    pre, prk = self.psA.next()
                    self.MM(pre[:, 0:NT], Bre[:, gp, :], B.usb[:, ch, col0:col0 + NT], True, True, [B.ubk, "ssm_b_re"], [prk])
                    pim, pik = self.psA.next()
                    self.MM(pim[:, 0:NT], Bim[:, gp, :], B.usb[:, ch, col0:col0 + NT], True, True, [B.ubk, "ssm_b_im"], [pik])
                    wre, wrk = wtmp.next()
                    wim, wik = wtmp.next()
                    t1, t1k = wtmp.next()
                    self.TT("dve", wre[:, 0:NT], pre[:, 0:NT], Tre[:, gp, 0:NT], ALU.mult, [prk, "Tre"], [wrk])
                    self.TT("dve", t1[:, 0:NT], pim[:, 0:NT], Tim[:, gp, 0:NT], ALU.mult, [pik, "Tim"], [t1k])
                    self.TT("pool", wre[:, 0:NT], wre[:, 0:NT], t1[:, 0:NT], ALU.subtract, [wrk, t1k], [wrk])
                    t2, t2k = wtmp.next()
                    self.TT("dve", wim[:, 0:NT], pim[:, 0:NT], Tre[:, gp, 0:NT], ALU.mult, [pik, "Tre"], [wik])
                    self.TT("dve", t2[:, 0:NT], pre[:, 0:NT], Tim[:, gp, 0:NT], ALU.mult, [prk, "Tim"], [t2k])
                    self.TT("pool", wim[:, 0:NT], wim[:, 0:NT], t2[:, 0:NT], ALU.add, [wik, t2k], [wik])
                    for pl, (G, wv, wk) in enumerate(((Gre, wre, wrk), (Gim, wim, wik))):
                        self.S_.op("dve", lambda e, G=G, wv=wv, pl=pl, gp=gp: e.tensor_tensor_scan(
                            out=G[:, gp, 0:NT], data0=rho[:, gp:gp + 1].to_broadcast([128, NT]), data1=wv[:, 0:NT],
                            initial=gst[:, pl, gp:gp + 1], op0=ALU.mult, op1=ALU.add),
                            [wk, "rho", "gst"], ["G%d" % pl])
                    yield
                csn, snn = cs[:, :, 0:NT], sn[:, :, 0:NT]
                self.TT("dve", ta[:, :, 0:NT], Gre[:, :, 0:NT], csn, ALU.mult, ["G0", "cs"], ["ta"])
                self.TT("dve", tb2[:, :, 0:NT], Gim[:, :, 0:NT], snn, ALU.mult, ["G1", "sn"], ["tb2"])
                self.TT("dve", hre[:, :, 0:NT], ta[:, :, 0:NT], tb2[:, :, 0:NT], ALU.subtract, ["ta", "tb2"], ["hre"])
                yield
                self.TT("dve", ta[:, :, 0:NT], Gim[:, :, 0:NT], csn, ALU.mult, ["G1", "cs"], ["ta"])
                self.TT("dve", tb2[:, :, 0:NT], Gre[:, :, 0:NT], snn, ALU.mult, ["G0", "sn"], ["tb2"])
                self.STT(himn[:, :, 0:NT], ta[:, :, 0:NT], -1.0, tb2[:, :, 0:NT], ALU.mult, ALU.subtract,
                         ["ta", "tb2"], ["himn"])
                yield
                glr, gli = Gre[:, :, NT - 1], Gim[:, :, NT - 1]
                for dst, col in ((gst, NT), (hl, NT - 1)):
                    cN, sN = cs[:, :, col], sn[:, :, col]
                    self.TT("dve", gl_t[:, 0, :], glr, cN, ALU.mult, ["G0", "cs"], ["glt0"])
                    self.TT("dve", gl_t[:, 1, :], gli, sN, ALU.mult, ["G1", "sn"], ["glt1"])
                    self.TT("dve", gl_t[:, 2, :], gli, cN, ALU.mult, ["G1", "cs"], ["glt2"])
                    self.TT("dve", gl_t[:, 3, :], glr, sN, ALU.mult, ["G0", "sn"], ["glt3"])
                    dk = "gst" if dst is gst else "hl"
                    self.TT("dve", dst[:, 0, :], gl_t[:, 0, :], gl_t[:, 1, :], ALU.subtract, ["glt0", "glt1"], [dk])
                    self.TT("dve", dst[:, 1, :], gl_t[:, 2, :], gl_t[:, 3, :], ALU.add, ["glt2", "glt3"], [dk])
                if self.dbg and ph == 0 and col0 == 0 and not getattr(self, "_dumped", False):
                    self._dumped = True
                    dd = self.O["dbg"]
                    for i, (tt, kk) in enumerate(((Tre, "Tre"), (Tim, "Tim"), (cs, "cs"), (sn, "sn"))):
                        self.DMA(dd[5 + i, :, :, 0:TW].rearrange("c p t -> p c t"), tt[:, :, :], [kk], [])
                    self.DMA(dd[9, :, :, 0:NT].rearrange("c p t -> p c t"), Gre[:, :, 0:NT], ["G0"], [])
                    self.DMA(dd[10, :, :, 0:NT].rearrange("c p t -> p c t"), Gim[:, :, 0:NT], ["G1"], [])
                    self.DMA(dd[11, 0, :, 0:8], rho[:, :], ["rho"], [])
                    self.DMA(dd[11, 1, :, 0:NT], B.usf[:, 0, 0:NT], [B.ufk], [])
                    self.DMA(dd[11, 2, :, 0:NT], B.usf[:, 1, 0:NT], [B.ufk], [])
                yield "late"
                for ch in range(2):
                    yp_, ypk = self.psC.next()
                    for i, gp in enumerate(range(ch * 4, ch * 4 + 4)):
                        self.MM(yp_[:, 0:NT], Cre[:, gp, :], hre[:, gp, 0:NT], i == 0, False, ["hre", "ssm_c_re"], [ypk])
                        self.MM(yp_[:, 0:NT], Cim[:, gp, :], himn[:, gp, 0:NT], False, i == 3, ["himn", "ssm_c_im"], [ypk])
                    self.STT(yss[:, ch, 0:NT], B.usf[:, ch, col0:col0 + NT], dskip[:, ch:ch + 1], yp_[:, 0:NT], ALU.mult,
                             ALU.add, [ypk, B.ufk, "pccols"], ["yss"])
                    yield
                yv, gv, sv = yss[:, :, 0:NT], gact[:, :, 0:NT], sgl[:, :, 0:NT]
                self.TT("dve", gv, yv, yv, ALU.mult, ["yss"], ["gact"])
                self.TS("dve", gv, gv, 0.044715, 1.0, ALU.mult, ALU.add, ["gact"], ["gact"])
                self.TT("dve", gv, gv, yv, ALU.mult, ["gact", "yss"], ["gact"])
                self.ACT(sv, gv, AF.Exp, ["gact"], ["sgl"], scale=-1.5957691216057308)
                self.TS("dve", sv, sv, 1.0, None, ALU.add, None, ["sgl"], ["sgl"])
                self.S_.op("dve", lambda e, sv=sv: e.reciprocal(out=sv, in_=sv), ["sgl"], ["sgl"])
                self.TT("dve", gv, yv, sv, ALU.mult, ["yss", "sgl"], ["gact"])
                self.CP("dve", gactb[:, :, 0:NT], gv, ["gact"], ["gactb"])
                yield
                for oc in range(2):
                    zp, zk = self.psC.next()
                    for kc in range(2):
                        self.MM(zp[:, 0:NT], wglu[:, kc, oc * 128:(oc + 1) * 128], gactb[:, kc, 0:NT], kc == 0, kc == 1,
                                ["wglu", "gactb"], [zk])
                    self.ACT(sgl[:, oc, 0:NT], zp[:, 0:NT], AF.Exp, [zk], ["sgl"], bias=nbglu[:, oc:oc + 1], scale=-1.0)
                self.TS("dve", sv, sv, 1.0, None, ALU.add, None, ["sgl"], ["sgl"])
                self.S_.op("dve", lambda e, sv=sv: e.reciprocal(out=sv, in_=sv), ["sgl"], ["sgl"])
                self.TT("dve", yss[:, :, 0:NT], gv, sv, ALU.mult, ["gact", "sgl"], ["yss"])
                yield
                self.rms_feat(yss, 2, NT, gssm, _View(B.mixT, 4, col0), 256, ["yss"], B.mk, rmstmp)
                yield

            def pool_mix(col0, NT, first, B):
                W = 16 + NT
                self.TT("dve", pl1[:, :, 1:W], B.upl[:, :, 1:W], B.upl[:, :, 0:W - 1], ALU.add, [B.upk], ["pl1"])
                self.TT("dve", pl2[:, :, 3:W], pl1[:, :, 3:W], pl1[:, :, 1:W - 2], ALU.add, ["pl1"], ["pl2"])
                self.CP("dve", wsel[0:64, 0, 0:NT], pl1[0:64, 0, 16:W], ["pl1"], ["wsel"])
                self.CP("dve", wsel[64:128, 0, 0:NT], pl2[64:128, 0, 16:W], ["pl2"], ["wsel"])
                yield
                self.TT("dve", pl1[:, :, 7:W], pl2[:, :, 7:W], pl2[:, :, 3:W - 4], ALU.add, ["pl2", "wsel"], ["pl1"])
                self.TT("dve", pl2[:, :, 15:W], pl1[:, :, 15:W], pl1[:, :, 7:W - 8], ALU.add, ["pl1", "wsel"], ["pl2"])
                self.CP("dve", wsel[0:64, 1, 0:NT], pl1[0:64, 1, 16:W], ["pl1"], ["wsel"])
                self.CP("dve", wsel[64:128, 1, 0:NT], pl2[64:128, 1, 16:W], ["pl2"], ["wsel"])
                yield
                for ch in range(2):
                    self.STT(pooled[:, ch, 0:NT], wsel[:, ch, 0:NT], icnt[:, ch:ch + 1], B.upl[:, ch, 16:W], ALU.mult,
                             ALU.subtract, ["wsel", "icnt", B.upk], ["pooled"])
                if first:
                    self.TT("dve", t16[:], wsel[:, :, 0:16], icnt_tab[:], ALU.mult, ["wsel", "icnt_tab"], ["t16"])
                    self.TT("dve", pooled[:, :, 0:16], t16[:], B.upl[:, :, 16:32], ALU.subtract, ["t16", B.upk], ["pooled"])
                yield "late"
                for ch in range(2):
                    pp, ppk = self.psC.next()
                    self.MM(pp[:, 0:NT], poolw[:, ch, :], pooled[:, ch, 0:NT], True, True, ["poolw", "pooled"], [ppk])
                    self.ACT(pout[:, ch, 0:NT], pp[:, 0:NT], AF.Copy, [ppk, "pccols"], ["pout"], scale=pscale[:, ch:ch + 1])
                yield
                self.rms_feat(pout, 2, NT, gpool, _View(B.mixT, 6, col0), 256, ["pout"], B.mk, rmstmp)
                yield

            def out_proj(xk, x, NT, tok0, B):
                for oc in range(KC):
                    ps, pk = self.psA.next()
                    for kc in range(KC):
                        self.MM(ps[:, 0:NT], wout[:, kc, oc * 128:(oc + 1) * 128], B.mixT[:, kc, 0:NT], kc == 0,
                                kc == KC - 1, ["wout", B.mk], [pk])
                    self.TT("dve", x[:, oc, 0:NT], ps[:, 0:NT], x[:, oc, 0:NT], ALU.add, [pk, xk], [xk])
                    yield
                self.DMA(self.scr[wr, :, :, tok0:tok0 + NT].rearrange("c p t -> p c t"), x[:, :, 0:NT], [xk], [])
                if self.dbg:
                    self.DMA(self.O["dbg"][ph, :, :, tok0:tok0 + NT].rearrange("c p t -> p c t"), x[:, :, 0:NT], [xk], [])
                    if ph == 0:
                        self.DMA(self.O["dbg"][4, :, :, tok0:tok0 + NT].rearrange("c p t -> p c t"), B.mixT[:, :, 0:NT], [B.mk], [], q="pool")

            def kv_tokmajor(NT, cols, is_v, slots, out_aps):
                nsub = len(cols)
                for s_, (c0, w) in enumerate(cols):
                    ps, pk = self.psA.next()
                    off = 1024 if is_v else 512
                    for kc in range(KC):
                        self.MM(ps[0:w, 0:512], hnT[:, kc, c0:c0 + w], win[:, kc, off:off + 512], kc == 0, kc == KC - 1,
                                ["hnT", "win"], [pk])
                    if is_v and slots[s_] is not None:
                        self.CP("act", Vr[0:w, slots[s_], :, 0:64], ps[0:w, 0:512].rearrange("p (h e) -> p h e", e=64),
                                [pk], ["v%d" % slots[s_]])
                    if out_aps[s_] is not None:
                        stg, sk = kvst.next()
                        self.CP("act", stg[0:w, :], ps[0:w, 0:512], [pk], [sk])
                        self.DMA(out_aps[s_], stg[0:w, :], [sk], [])

            self.MS("pool", gst[:], 0.0, ["gst"])
            self.MS("pool", upl[:], 0.0, [bs[0].upk])
            self.MS("pool", upl_b[:], 0.0, [bs[1].upk])
            nst = S // TS_
            keep0 = S - self.NKEEP
            rmstmp1 = (sq1, sd1, rstd1)
            xkeys = {}

            def P1(sti):
                x, B = xT[sti % 3], bs[sti % 2]
                tok0 = sti * TS_
                xk = load_tr(x, TS_, ph == 0, sti)
                xkeys[sti] = xk
                if sti > 0:
                    Bp = bs[(sti - 1) % 2]
                    self.CP("dve", B.upl[:, :, 1:16], Bp.upl[:, :, TS_ + 1:TS_ + 16], [Bp.upk], [B.upk])
                yield
                a0 = tok0 // 128
                slots = [(a0 + s_) % NSLOT for s_ in range(NSUB)]
                for oc, ps, pk in in_proj(xk, x, TS_, rmstmp1):
                    if oc < 4:
                        self.CP("act", B.qT[:, oc, 0:TS_], ps[:, 0:TS_], [pk], [B.qk])
                    elif oc < 8:
                        for s_ in range(NSUB):
                            self.CP("act", KT[:, oc - 4, slots[s_] * 128:(slots[s_] + 1) * 128],
                                    ps[:, s_ * 128:(s_ + 1) * 128], [pk], ["kt%d" % slots[s_]])
                    elif oc < 14:
                        self.CP("act", B.usb[:, oc - 12, 0:TS_], ps[:, 0:TS_], [pk], [B.ubk])
                        self.CP("act", B.usf[:, oc - 12, 0:TS_], ps[:, 0:TS_], [pk], [B.ufk])
                    else:
                        self.CP("act", B.upl[:, oc - 14, 16:16 + TS_], ps[:, 0:TS_], [pk], [B.upk])
                    yield
                cols = [(s_ * 128, 128) for s_ in range(NSUB)]
                outs_v = [O["vp"][l, tok0 + s_ * 128 - keep0: tok0 + s_ * 128 - keep0 + 128, :]
                          if tok0 + s_ * 128 >= keep0 else None for s_ in range(NSUB)]
                outs_k = [O["kp"][l, tok0 + s_ * 128 - keep0: tok0 + s_ * 128 - keep0 + 128, :]
                          if tok0 + s_ * 128 >= keep0 else None for s_ in range(NSUB)]
                kv_tokmajor(TS_, cols, True, slots, outs_v)
                yield
                if any(o is not None for o in outs_k):
                    kv_tokmajor(TS_, cols, False, [None] * NSUB, outs_k)
                yield

            def P3(sti):
                x, B = xT[sti % 3], bs[sti % 2]
                att_finish(0, TS_, B)
                yield
                yield from out_proj(xkeys[sti], x, TS_, sti * TS_, B)

            def drain(g):
                for _ in g:
                    pass

            def sched2(att_g, n_att, sides):
                late_start = int(0.3 * n_att)
                st_ = [{"g": g, "p": p, "s": s0, "h": h, "held": False, "alive": True} for (g, p, s0, h) in sides]

                def adv(d):
                    try:
                        if next(d["g"]) == "late":
                            d["held"] = True
                    except StopIteration:
                        d["alive"] = False

                i = 0
                for _ in att_g:
                    i += 1
                    for d in st_:
                        if d["alive"] and i >= d["s"] and (i - d["s"]) % d["p"] == 0 and not (d["h"] and d["held"] and i < late_start):
                            adv(d)
                for d in st_:
                    while d["alive"]:
                        adv(d)

            for i_ in range(min(2, nst)):
                load_dma(xT[i_ % 3], i_ * TS_, TS_, ph == 0, I["xp"], i_)
            drain(P1(0))
            for sti in range(nst):
                B = bs[sti % 2]
                a0 = sti * TS_ // 128
                ktiles = [(a % NSLOT, 128, a0 - a) for a in range(max(0, a0 - 16), a0 + NSUB)]
                n_att = 8 * ((len(ktiles) + 1) // 2) + 1
                sides = []
                if sti > 0:
                    sides.append((P3(sti - 1), 2, 1, False))
                sides.append((ssm(0, TS_, sti == 0, B), 3, 2, True))
                sides.append((pool_mix(0, TS_, sti == 0, B), 6, 4, True))
                if sti + 1 < nst:
                    sides.append((P1(sti + 1), 2, 3, False))
                sched2(attention(0, TS_, ktiles, B), n_att, sides)
                if sti + 2 < nst:
                    load_dma(xT[(sti + 2) % 3], (sti + 2) * TS_, TS_, ph == 0, I["xp"], sti + 2)
            drain(P3(nst - 1))
            Bl = bs[(nst - 1) % 2]
            self.ckpt("m_prompt")
            for pl, nm in ((0, "rep"), (1, "imp")):
                self.cols_to_rows(hl[:, pl, :], O[nm][l].rearrange("(gp gl) p -> gp (gl p)", gl=2), 8, ["hl"])
            for ch in range(2):
                self.cols_to_rows(Bl.upl[:, ch, TS_ + 1:TS_ + 16], O["poolp"][l, :, ch * 128:(ch + 1) * 128], 15, [Bl.upk])

            self.ckpt("m_pout")
            xs_ = xT[nst % 3]
            B0 = bs[0]
            xk = load_xT(xs_, 0 if ph == 0 else S, NS, ph == 0, I["xs"])
            for oc, ps, pk in in_proj(xk, xs_, NS):
                if oc < 4:
                    self.CP("act", B0.qT[:, oc, 0:NS], ps[:, 0:NS], [pk], [B0.qk])
                elif oc < 8:
                    self.CP("act", kts[:, oc - 4, :], ps[:, 0:NS], [pk], ["kts"])
                elif oc < 14:
                    self.CP("act", B0.usb[:, oc - 12, 0:NS], ps[:, 0:NS], [pk], [B0.ubk])
                    self.CP("dve", B0.usf[:, oc - 12, 0:NS], ps[:, 0:NS], [pk], [B0.ufk])
                else:
                    self.CP("dve", ups[:, oc - 14, 0:NS], ps[:, 0:NS], [pk], ["ups"])
            for s in range(NSEQ):
                c0 = s * TD
                for m in range(16):
                    kc_, kck = kcs.next()
                    self.DMA(kc_[:], I["ck"][l, s, m * 128:(m + 1) * 128, :], [], [kck])
                    psk, pkk = self.psA.next()
                    for c in range(4):
                        self.TR(psk[:, c * 128:(c + 1) * 128], kc_[:, c * 128:(c + 1) * 128], self.ident_f[:],
                                [kck, "ident"], [pkk])
                    self.CP("act" if m % 2 else "dve", KT[:, :, m * 128:(m + 1) * 128],
                            psk[:, 0:512].rearrange("p (c k) -> p c k", k=128), [pkk], ["kt%d" % m])
                    vst, vsk = xtoks[m % 2], "xtok%d" % (m % 2)
                    self.DMA(vst[:, 0, 0:512], I["cv"][l, s, m * 128:(m + 1) * 128, :], [], [vsk], q="act")
                    self.CP("pool" if m % 2 else "dve", Vr[:, m, :, 0:64],
                            vst[:, 0, 0:512].rearrange("k (h e) -> k h e", e=64), [vsk], ["v%d" % m])
                self.CP("dve", KT[:, :, 16 * 128:16 * 128 + TD], kts[:, :, c0:c0 + TD], ["kts"], ["kt16"])
                kv_tokmajor(TD, [(c0, TD)], True, [16], [O["vs"][l, s]])
                kv_tokmajor(TD, [(c0, TD)], False, [None], [O["ks"][l, s]])
                ktiles = [(m, 128, 16 - m) for m in range(16)] + [(16, TD, 0)]
                interleave([attention(c0, TD, ktiles, B0)], [1])
                att_finish(c0, TD, B0)
                for pl, nm in ((0, "sre"), (1, "sim")):
                    self.rows_to_cols(I[nm][l, s].rearrange("(gp gl) p -> gp (gl p)", gl=2), h0c[:, pl, :], 8, ["h0c"])
                c1_, s1_ = cs[:, :, 1], sn[:, :, 1]
                self.TT("pool", gl_t[:, 0, :], h0c[:, 0, :], c1_, ALU.mult, ["h0c", "cs"], ["glt0"])
                self.TT("pool", gl_t[:, 1, :], h0c[:, 1, :], s1_, ALU.mult, ["h0c", "sn"], ["glt1"])
                self.TT("pool", gl_t[:, 2, :], h0c[:, 1, :], c1_, ALU.mult, ["h0c", "cs"], ["glt2"])
                self.TT("pool", gl_t[:, 3, :], h0c[:, 0, :], s1_, ALU.mult, ["h0c", "sn"], ["glt3"])
                self.TT("pool", gst[:, 0, :], gl_t[:, 0, :], gl_t[:, 1, :], ALU.subtract, ["glt0", "glt1"], ["gst"])
                self.TT("pool", gst[:, 1, :], gl_t[:, 2, :], gl_t[:, 3, :], ALU.add, ["glt2", "glt3"], ["gst"])
                interleave([ssm(c0, TD, False, B0)], [1])
                for pl, nm in ((0, "res"), (1, "ims")):
                    self.cols_to_rows(hl[:, pl, :], O[nm][l, s].rearrange("(gp gl) p -> gp (gl p)", gl=2), 8, ["hl"])
                for ch in range(2):
                    self.rows_to_cols(I["spool"][l, s, :, ch * 128:(ch + 1) * 128], B0.upl[:, ch, 1:16], 15, [B0.upk], q="act")
                self.CP("dve", B0.upl[:, :, 16:16 + TD], ups[:, :, c0:c0 + TD], ["ups"], [B0.upk])
                for ch in range(2):
                    self.cols_to_rows(B0.upl[:, ch, TD + 1:TD + 16], O["pools"][l, s, :, ch * 128:(ch + 1) * 128], 15, [B0.upk], q="act")
                interleave([pool_mix(c0, TD, False, B0)], [1])
            drain(out_proj(xk, xs_, NS, S, B0))
            S_.barrier()

    def ffn_phase(self, l, ph, last):
        nc, S_, I, O = self.nc, self.S_, self.I, self.O
        S, TS_ = self.S, self.TSF
        rd, wr = (ph - 1) % 2, ph % 2
        self.rms_mode = "pow"
        self.set_rings(False)
        with contextlib.ExitStack() as st:
            sb = lambda n, s, d: self.sb(st, n, s, d)
            wup = sb("wup", [128, KC, 2 * DFF], BF16)
            wdn = sb("wdn", [128, AC, D], BF16)
            for c in range(KC):
                for hh in range(4):
                    self.DMA(wup[:, c, hh * 1408:(hh + 1) * 1408], I["w_up"][l, c * 128:(c + 1) * 128, hh * 1408:(hh + 1) * 1408],
                             [], ["wup"], q="pool")
            for c in range(AC):
                self.DMA(wdn[:, c, :], I["w_down"][l, c * 128:(c + 1) * 128, :], [], ["wdn"], q="pool")
            pa = self.load_rows_T(st, [(I["norm2_g"][l].rearrange("(c p) -> c p", p=128), 0),
                                       (I["norm_f_g"].rearrange("(c p) -> c p", p=128), 8),
                                       (I["conv_b"][l].rearrange("(c p) -> c p", p=128), 16),
                                       (I["conv_w"][l, 0].rearrange("(c p) -> c p", p=128), 64)], "pa")
            pb = self.load_rows_T(st, [(I["conv_w"][l, 1].rearrange("(c p) -> c p", p=128), 0),
                                       (I["conv_w"][l, 2].rearrange("(c p) -> c p", p=128), 64)], "pb")
            g2, gf, cb, cw0, cw1, cw2 = pa[:, 0:8], pa[:, 8:16], pa[:, 16:60], pa[:, 64:108], pb[:, 0:44], pb[:, 64:108]
            xT = [sb("fx%d" % i, [128, KC, TS_], F32) for i in range(2)]
            sq = sb("fsq", [128, KC, TS_], BF16)
            sd = sb("fsd", [128, TS_], F32)
            rstd = sb("frstd", [128, TS_], F32)
            hnT = sb("fhn", [128, KC, TS_], BF16)
            actT = sb("factT", [128, AC, TS_], BF16)
            raws = Ring([(sb("raw%d" % i, [128, TS_ + 2 * NSEQ], F32), "raw%d" % i) for i in range(6)])
            cvs = Ring([(sb("cv%d" % i, [128, TS_], F32), "cv%d" % i) for i in range(6)])
            sgs = Ring([(sb("sg%d" % i, [128, TS_], F32), "sg%d" % i) for i in range(2)])
            uptail = sb("uptail", [128, FC, 2], F32)
            stail = sb("stail", [128, FC, NSEQ, 2], F32)
            ytok = Ring([(sb("ytok%d" % i, [128, D], F32), "ytok%d" % i) for i in range(1)]) if last else None
            rmstmp = (sq, sd, rstd)
            if l == 0:
                print("SBUF remaining after ffn alloc:", self.nc.sbuf_bytes_remaining)
            self.MS("pool", uptail[:], 0.0, ["uptail"])
            for s_ in range(NSEQ):
                for r_ in range(2):
                    self.rows_to_cols(I["sconv"][l, s_, r_].rearrange("(c p) -> c p", p=128), stail[:, :, s_, r_], FC, ["stail"],
                                      q="act")
            S_.barrier()

            def load_x(dst, tok0, NT):
                key = "fx%d" % xT.index(dst)
                self.DMA(dst[:, :, 0:NT], self.scr[rd, :, :, tok0:tok0 + NT].rearrange("c p t -> p c t"), [], [key])
                return key

            hn2 = [hnT, sb("fhnb", [128, KC, TS_], BF16)]
            act2 = [actT, sb("factTb", [128, AC, TS_], BF16)]
            dstate = {"d": None}

            def stA(xk, x, NT, bi):
                self.rms_feat(x, KC, NT, g2, hn2[bi], D, [xk], "fhn%d" % bi, rmstmp)

            def stB(NT, nseq, tl, tail, tailk, bi, lo, hi):
                hn_, hk_, at_, ak_ = hn2[bi], "fhn%d" % bi, act2[bi], "actT%d" % bi
                for fc in range(lo, hi):
                    st_ = []
                    for cidx in (fc, fc + AC):
                        ps, pk = self.psA.next()
                        for kc in range(KC):
                            self.MM(ps[:, 0:NT], wup[:, kc, cidx * 128:(cidx + 1) * 128], hn_[:, kc, 0:NT], kc == 0,
                                    kc == KC - 1, ["wup", hk_], [pk])
                        raw, rk = raws.next()
                        r3 = raw[:, 0:nseq * (tl + 2)].rearrange("p (s t) -> p s t", t=tl + 2)
                        tl_ap = tail[:, cidx, :].unsqueeze(1) if nseq == 1 else tail[:, cidx, :, :]
                        cv, ck = cvs.next()
                        c3 = cv[:, 0:NT].rearrange("p (s t) -> p s t", t=tl)
                        p3 = ps[:, 0:NT].rearrange("p (s t) -> p s t", t=tl)
                        tk_ = "%s%d" % (tailk, cidx)
                        self.CP("pool", r3[:, :, 0:2], tl_ap, [tailk, tk_], [rk])
                        self.CP("act", r3[:, :, 2:2 + tl], p3, [pk], [rk])
                        self.ACT(c3, p3, AF.Identity, [pk, "pacols", "pbcols"], [ck], bias=cb[:, cidx:cidx + 1],
                                 scale=cw2[:, cidx:cidx + 1])
                        self.CP("pool", tl_ap, r3[:, :, tl:tl + 2], [rk], [tk_])
                        st_.append((cidx, r3, c3, rk, ck, cv))
                    for (cidx, r3, c3, rk, ck, cv) in st_:
                        self.STT(c3, r3[:, :, 1:1 + tl], cw1[:, cidx:cidx + 1], c3, ALU.mult, ALU.add, [rk, ck, "pbcols"], [ck])
                    for (cidx, r3, c3, rk, ck, cv) in st_:
                        self.STT(c3, r3[:, :, 0:tl], cw0[:, cidx:cidx + 1], c3, ALU.mult, ALU.add, [rk, ck, "pacols"], [ck])

                    def tail_part(fc=fc, st_=st_, at_=at_, ak_=ak_, NT=NT):
                        (_, _, _, _, cak, ca), (_, _, _, _, cgk, cg) = st_
                        sg, sgk = sgs.next()
                        self.ACT(sg[:, 0:NT], cg[:, 0:NT], AF.Silu, [cgk], [sgk])
                        self.TT("dve", at_[:, fc, 0:NT], sg[:, 0:NT], ca[:, 0:NT], ALU.mult, [sgk, cak], [ak_])

                    if dstate["d"] is not None:
                        dstate["d"]()
                    dstate["d"] = tail_part

            def flushB():
                if dstate["d"] is not None:
                    dstate["d"]()
                    dstate["d"] = None

            def stC(xk, x, tok0, NT, nseq, bi):
                at_, ak_ = act2[bi], "actT%d" % bi
                for oc in range(KC):
                    ps, pk = self.psA.next()
                    for kc in range(AC):
                        self.MM(ps[:, 0:NT], wdn[:, kc, oc * 128:(oc + 1) * 128], at_[:, kc, 0:NT], kc == 0, kc == AC - 1,
                                ["wdn", ak_], [pk])
                    self.TT("dve", x[:, oc, 0:NT], ps[:, 0:NT], x[:, oc, 0:NT], ALU.add, [pk, xk], [xk])
                if self.dbg:
                    self.DMA(self.O["dbg"][ph, :, :, tok0:tok0 + NT].rearrange("c p t -> p c t"), x[:, :, 0:NT], [xk], [])
                if not last:
                    self.DMA(self.scr[wr, :, :, tok0:tok0 + NT].rearrange("c p t -> p c t"), x[:, :, 0:NT], [xk], [])
                    return
                ynT = at_[:, :, :].rearrange("p a b -> p (a b)").bitcast(F32)[:, 0:KC * TS_].rearrange("p (c t) -> p c t", t=TS_)
                self.rms_feat(x, KC, NT, gf, ynT, D, [xk], ak_, rmstmp)
                nsub = (NT + 127) // 128
                w = min(128, NT)
                ydst = O["yp"] if nseq == 1 else O["ys"]
                t0_ = tok0 if nseq == 1 else 0
                for s_ in range(nsub):
                    yt, ytk = ytok.next()
                    for half in range(2):
                        ps, pk = self.psA.next()
                        for c4 in range(4):
                            c = half * 4 + c4
                            self.TR(ps[0:w, c4 * 128:(c4 + 1) * 128], ynT[:, c, s_ * 128: s_ * 128 + w], self.ident_f[:],
                                    [ak_, "ident"], [pk])
                        self.CP("act" if half else "dve", yt[0:w, half * 512:(half + 1) * 512], ps[0:w, 0:512], [pk], [ytk])
                    self.DMA(ydst[t0_ + s_ * 128: t0_ + s_ * 128 + w, :], yt[0:w, :], [ytk], [])

            nst = S // TS_
            NPRE = 4
            xk_of = {0: load_x(xT[0], 0, TS_)}
            stA(xk_of[0], xT[0], TS_, 0)
            stB(TS_, 1, TS_, uptail, "uptail", 0, 0, NPRE)
            for sti in range(nst):
                bi = sti % 2
                x, xk = xT[bi], xk_of[sti]
                if sti + 1 < nst:
                    xk_of[sti + 1] = load_x(xT[(sti + 1) % 2], (sti + 1) * TS_, TS_)
                stB(TS_, 1, TS_, uptail, "uptail", bi, NPRE, 10)
                if sti + 1 < nst:
                    stA(xk_of[sti + 1], xT[(sti + 1) % 2], TS_, 1 - bi)
                stB(TS_, 1, TS_, uptail, "uptail", bi, 10, AC)
                if sti + 1 < nst:
                    stB(TS_, 1, TS_, uptail, "uptail", 1 - bi, 0, NPRE)
                else:
                    flushB()
                stC(xk, x, sti * TS_, TS_, 1, bi)
            allup = ["uptail"] + ["uptail%d" % c for c in range(FC)]
            for r_ in range(2):
                self.cols_to_rows(uptail[:, :, r_], O["convp"][l, r_].rearrange("(c p) -> c p", p=128), FC, allup, q="act")
            xs_ = xT[nst % 2]
            xks = load_x(xs_, S, NS)
            stA(xks, xs_, NS, 0)
            stB(NS, NSEQ, TD, stail, "stail", 0, 0, AC)
            flushB()
            stC(xks, xs_, S, NS, NSEQ, 0)
            allst = ["stail"] + ["stail%d" % c for c in range(FC)]
            for s_ in range(NSEQ):
                for r_ in range(2):
                    self.cols_to_rows(stail[:, :, s_, r_], O["convs"][l, s_, r_].rearrange("(c p) -> c p", p=128), FC, allst,
                                      q="act")
            S_.barrier()

    def build(self):
        self.declare()
        with contextlib.ExitStack() as st:
            self.S_ = Sched(self.nc, st)
            try:
                self.setup(st)
                self.ckpt("setup")
                ph = 0
                for l in range(2):
                    self.mix_phase(l, ph)
                    self.ckpt("mix%d" % l)
                    ph += 1
                    self.ffn_phase(l, ph, last=(l == 1))
                    self.ckpt("ffn%d" % l)
                    ph += 1
            except _Stop:
                self.S_.barrier()
            self.S_.emit()
        return self.nc


class _View:
    def __init__(self, t, c0, col0):
        self.t, self.c0, self.col0 = t, c0, col0

    def __getitem__(self, key):
        p, c, cols = key
        return self.t[p, self.c0 + c, self.col0 + cols.start: self.col0 + cols.stop]


_NC_CACHE = {}


def _get_nc():
    if "nc" not in _NC_CACHE:
        _NC_CACHE["nc"] = Builder().build()
    return _NC_CACHE["nc"]


_WNAMES = ("norm1_g", "w_in", "ssm_log_dt", "ssm_a_re", "ssm_a_im", "ssm_b_re", "ssm_b_im", "ssm_c_re", "ssm_c_im",
           "ssm_d", "ssm_w_glu", "ssm_b_glu", "pool_w", "pool_scale", "out_norm_att", "out_norm_ssm", "out_norm_pool",
           "w_out", "norm2_g", "w_up", "conv_w", "conv_b", "w_down", "norm_f_g")


def make_in_maps(inp, S=8192):
    f = lambda a: np.ascontiguousarray(np.asarray(a, dtype=np.float32))
    maps = []
    zeros_p = np.zeros((S, D), np.float32)
    for c in range(NCORES):
        m = {}
        m["xp"] = f(inp["x_prompt"][c]) if c < inp["x_prompt"].shape[0] else zeros_p
        sl = slice(c * NSEQ, (c + 1) * NSEQ)
        m["xs"] = f(inp["x_sample"][sl]).reshape(NS, D)
        m["ck"] = f(inp["cache_k"][:, sl]).reshape(2, NSEQ, PAST, 512)
        m["cv"] = f(inp["cache_v"][:, sl]).reshape(2, NSEQ, PAST, 512)
        m["sre"] = f(inp["state_ssm_re"][:, sl])
        m["sim"] = f(inp["state_ssm_im"][:, sl])
        m["spool"] = f(inp["state_pool"][:, sl])
        m["sconv"] = f(inp["state_conv"][:, sl])
        for n in _WNAMES:
            m[n] = f(inp[n])
        maps.append(m)
    return maps


def gather(res, S=8192, nkeep=2048, nb=2):
    r = res
    cat = lambda n: np.concatenate([r[c][n] for c in range(NCORES)], axis=1)
    y_prompt = np.stack([r[c]["yp"] for c in range(nb)], 0)
    y_sample = np.concatenate([r[c]["ys"].reshape(NSEQ, TD, D) for c in range(NCORES)], 0)
    pk = lambda n: np.stack([r[c][n] for c in range(nb)], 1)
    new_k_prompt = pk("kp").reshape(2, nb, nkeep, 8, 64)
    new_v_prompt = pk("vp").reshape(2, nb, nkeep, 8, 64)
    outs = (y_prompt, y_sample, new_k_prompt, new_v_prompt, pk("rep"), pk("imp"), pk("poolp"), pk("convp"),
            cat("ks").reshape(2, NCORES * NSEQ, TD, 8, 64), cat("vs").reshape(2, NCORES * NSEQ, TD, 8, 64),
            cat("res"), cat("ims"), cat("pools"), cat("convs"))
    return tuple(np.ascontiguousarray(o, dtype=np.float32) for o in outs)


def kernel(**inputs):
    nc = _get_nc()
    in_maps = make_in_maps(inputs)
    res = run_bass_kernel_spmd(nc, in_maps, core_ids=list(range(NCORES)))
    return gather(res.results)
```
